# Optimizing a Trainium2 kernel written in Bass

```python
import math
import jax
import jax.numpy as jnp
from jax import lax
import numpy as np

D_MODEL = 1024
BATCH = 8
SEQ = 4096
DEPTH = 4

GRID_W = 64
CTX_LEN = 256
GROUP_W = 256
N_GROUPS = 4
MIX_W = GROUP_W * N_GROUPS
N_IN_SLICES = 14
IN_W = GROUP_W * N_IN_SLICES
LRU_HEADS = 4
LRU_BLOCK = GROUP_W // LRU_HEADS
LRU_CONV = 4
LRU_C = 8.0
HGRN_HEADS = 4
HGRN_HEAD_DIM = GROUP_W // HGRN_HEADS
HGRN_CHUNK = 64
LB_FLOOR = 1e-30
CONV_K = 31
DIFF_HEADS = 4
DIFF_HEAD_DIM = GROUP_W // (2 * DIFF_HEADS)
Q_BLOCK = 128
ROPE_THETA = 10000.0
RMS_EPS = 1e-6
LN_EPS = 1e-5

kernel_name = 'hybrid_parallel_heads_dit_block'


def rms_norm(x, g):
    xf = x.astype(jnp.float32)
    y = xf * lax.rsqrt(jnp.mean(xf * xf, axis=-1, keepdims=True) + RMS_EPS)
    return (y * g.astype(jnp.float32)).astype(x.dtype)


def layer_norm(x, g, b):
    xf = x.astype(jnp.float32)
    mu = jnp.mean(xf, axis=-1, keepdims=True)
    var = jnp.mean(jnp.square(xf - mu), axis=-1, keepdims=True)
    y = (xf - mu) * lax.rsqrt(var + LN_EPS) * g.astype(jnp.float32) + b.astype(jnp.float32)
    return y.astype(x.dtype)


def depthwise_conv(x, w, b, pad_left, pad_right):
    y = lax.conv_general_dilated(
        x, w[:, None, :].astype(x.dtype), window_strides=(1,),
        padding=[(pad_left, pad_right)], dimension_numbers=('NWC', 'WIO', 'NWC'),
        feature_group_count=x.shape[-1])
    return y + b.astype(x.dtype)


def axial_rope_tables(n_tokens):
    rows = n_tokens // GRID_W
    row = jnp.repeat(jnp.arange(rows, dtype=jnp.float32), GRID_W)
    col = jnp.tile(jnp.arange(GRID_W, dtype=jnp.float32), rows)
    n_freq = DIFF_HEAD_DIM // 4
    inv_freq = ROPE_THETA ** (-jnp.arange(n_freq, dtype=jnp.float32) / n_freq)
    ang = jnp.concatenate([row[:, None] * inv_freq, col[:, None] * inv_freq], axis=-1)
    return jnp.cos(ang), jnp.sin(ang)


def apply_rope(x, cos, sin):
    half = x.shape[-1] // 2
    x1, x2 = x[..., :half], x[..., half:]
    c = cos[None, :, None, :].astype(x.dtype)
    s = sin[None, :, None, :].astype(x.dtype)
    return jnp.concatenate([x1 * c - x2 * s, x1 * s + x2 * c], axis=-1)


def block_diag(x, w, b):
    bsz, l, _ = x.shape
    y = jnp.einsum('blhi,hij->blhj', x.reshape(bsz, l, LRU_HEADS, LRU_BLOCK), w)
    return y.reshape(bsz, l, GROUP_W) + b


def linear_scan(a, u, h0):
    def combine(left, right):
        return left[0] * right[0], right[0] * left[1] + right[1]
    a_cum, h = lax.associative_scan(combine, (a, u), axis=1)
    return h + a_cum * h0[:, None, :]


def rglru_direction(x_c, x_l, w_r, b_r, w_i, b_i, lam):
    def coeffs(x):
        xf = x.astype(jnp.float32)
        r = jax.nn.sigmoid(block_diag(xf, w_r, b_r))
        i = jax.nn.sigmoid(block_diag(xf, w_i, b_i))
        log_a = -LRU_C * r * jax.nn.softplus(-lam.astype(jnp.float32))
        return jnp.exp(log_a), jnp.sqrt(jnp.maximum(-jnp.expm1(2.0 * log_a), 0.0)) * (i * xf)
    a_c, u_c = coeffs(x_c)
    h_c = linear_scan(a_c, u_c, jnp.zeros_like(u_c[:, 0]))
    a_l, u_l = coeffs(x_l)
    h_l = linear_scan(a_l, u_l, h_c[:, -1])
    return h_c, h_l


def rglru_mixer(x_l, x_c, conv_w, conv_b, w_r, b_r, w_i, b_i, lam):
    pad_l, pad_r = LRU_CONV // 2, LRU_CONV - 1 - LRU_CONV // 2
    xc_l = depthwise_conv(x_l, conv_w, conv_b, pad_l, pad_r)
    xc_c = depthwise_conv(x_c, conv_w, conv_b, pad_l, pad_r)
    hf_c, hf_l = rglru_direction(xc_c, xc_l, w_r[0], b_r[0], w_i[0], b_i[0], lam[0])
    hb_c, hb_l = rglru_direction(jnp.flip(xc_c, 1), jnp.flip(xc_l, 1), w_r[1], b_r[1], w_i[1], b_i[1], lam[1])
    y_l = (hf_l + jnp.flip(hb_l, 1)).astype(x_l.dtype)
    y_c = (hf_c + jnp.flip(hb_c, 1)).astype(x_c.dtype)
    return y_l, y_c


def log_forget(z, lb):
    zf = z.astype(jnp.float32)
    lbf = lb.astype(jnp.float32)
    return jnp.logaddexp(jnp.log(jnp.maximum(lbf, LB_FLOOR)), jnp.log1p(-lbf) + jax.nn.log_sigmoid(zf))


def gla_chunk_scan(q, k, v, log_f, s0):
    b, l, h, _ = q.shape
    dv = v.shape[-1]
    n = l // HGRN_CHUNK

    def to_chunks(t):
        return t.astype(jnp.float32).reshape(b, n, HGRN_CHUNK, h, t.shape[-1]).transpose(1, 0, 3, 2, 4)

    incl = jnp.tril(jnp.ones((HGRN_CHUNK, HGRN_CHUNK), dtype=bool))[:, :, None]

    def step(state, chunk):
        qc, kc, vc, gc = chunk
        g_cum = jnp.cumsum(gc, axis=2)
        o_inter = jnp.einsum('bhcd,bhde->bhce', qc * jnp.exp(g_cum), state)
        rel = g_cum[:, :, :, None, :] - g_cum[:, :, None, :, :]
        decay = jnp.where(incl, jnp.exp(jnp.minimum(rel, 0.0)), 0.0)
        scores = jnp.einsum('bhid,bhjd,bhijd->bhij', qc, kc, decay)
        o = o_inter + jnp.einsum('bhij,bhje->bhie', scores, vc)
        g_end = g_cum[:, :, -1:, :]
        state = (jnp.exp(g_end[:, :, 0, :, None]) * state
                 + jnp.einsum('bhcd,bhce->bhde', kc * jnp.exp(g_end - g_cum), vc))
        return state, o

    state, o = lax.scan(step, s0.astype(jnp.float32), tuple(to_chunks(t) for t in (q, k, v, log_f)))
    return o.transpose(1, 0, 3, 2, 4).reshape(b, l, h, dv), state


def hgrn2_mixer(q_l, i_l, zf_l, zb_l, q_c, i_c, zf_c, zb_c, lb, norm_g):
    def heads(t):
        return t.reshape(t.shape[0], t.shape[1], HGRN_HEADS, HGRN_HEAD_DIM)
    bsz = q_l.shape[0]
    qh_l, qh_c = heads(jax.nn.silu(q_l)), heads(jax.nn.silu(q_c))
    ih_l, ih_c = heads(i_l), heads(i_c)
    outs_l, outs_c = [], []
    for d, (z_l, z_c) in enumerate(((zf_l, zf_c), (zb_l, zb_c))):
        g_l, g_c = heads(log_forget(z_l, lb[d])), heads(log_forget(z_c, lb[d]))
        seq_c = (qh_c, -jnp.expm1(g_c), ih_c, g_c)
        seq_l = (qh_l, -jnp.expm1(g_l), ih_l, g_l)
        if d == 1:
            seq_c = tuple(jnp.flip(t, 1) for t in seq_c)
            seq_l = tuple(jnp.flip(t, 1) for t in seq_l)
        s_zero = jnp.zeros((bsz, HGRN_HEADS, HGRN_HEAD_DIM, HGRN_HEAD_DIM), jnp.float32)
        o_c, s_c = gla_chunk_scan(*seq_c, s_zero)
        o_l, _ = gla_chunk_scan(*seq_l, s_c)
        if d == 1:
            o_c, o_l = jnp.flip(o_c, 1), jnp.flip(o_l, 1)
        outs_c.append(o_c)
        outs_l.append(o_l)

    def finish(o, like):
        return rms_norm(o, norm_g).reshape(o.shape[0], o.shape[1], GROUP_W).astype(like.dtype)
    return finish(outs_l[0] + outs_l[1], q_l), finish(outs_c[0] + outs_c[1], q_c)


def conformer_conv(v, g, w, b, ln_g, ln_b):
    y = v * jax.nn.sigmoid(g)
    y = depthwise_conv(y, w, b, CONV_K // 2, CONV_K // 2)
    return jax.nn.silu(layer_norm(y, ln_g, ln_b))


def diff_attention_mixer(q_l, k_l, v_l, q_c, k_c, v_c, cos, sin, lam, lam_init, norm_g, with_ctx):
    bsz, l, _ = q_l.shape
    h2 = 2 * DIFF_HEADS

    def sub_heads(t):
        return t.reshape(t.shape[0], t.shape[1], h2, DIFF_HEAD_DIM)

    def val_heads(t):
        return t.reshape(t.shape[0], t.shape[1], DIFF_HEADS, 2 * DIFF_HEAD_DIM)

    scale = DIFF_HEAD_DIM ** -0.5

    def attend(q, keys, vals):
        s = jnp.einsum('bqhd,bkhd->bhqk', q, keys).astype(jnp.float32) * scale
        p = jax.nn.softmax(s, axis=-1)
        p = p.reshape(p.shape[0], DIFF_HEADS, 2, p.shape[2], p.shape[3])
        a = (p[:, :, 0] - lam * p[:, :, 1]).astype(vals.dtype)
        return jnp.einsum('bhqk,bkhe->bqhe', a, vals)

    def finish(o):
        return (rms_norm(o, norm_g) * (1.0 - lam_init)).reshape(o.shape[0], o.shape[1], GROUP_W)

    kh_c, vh_c = sub_heads(k_c), val_heads(v_c)
    keys = jnp.concatenate([kh_c, apply_rope(sub_heads(k_l), cos, sin)], axis=1)
    vals = jnp.concatenate([vh_c, val_heads(v_l)], axis=1)
    n_blk = l // Q_BLOCK
    q_rot = apply_rope(sub_heads(q_l), cos, sin)
    q_blocks = q_rot.reshape(bsz, n_blk, Q_BLOCK, h2, DIFF_HEAD_DIM).transpose(1, 0, 2, 3, 4)
    o_l = lax.map(lambda qb: attend(qb, keys, vals), q_blocks)
    o_l = o_l.transpose(1, 0, 2, 3, 4).reshape(bsz, l, DIFF_HEADS, 2 * DIFF_HEAD_DIM)
    y_l = finish(o_l)
    y_c = finish(attend(sub_heads(q_c), kh_c, vh_c)) if with_ctx else None
    return y_l, y_c


def hybrid_layer(x, xc, c, c_ctx, layer, with_ctx, lb, cos, sin,
                 w_mod, b_mod, g_pre, g_post, w_in, w_out,
                 lru_conv_w, lru_conv_b, lru_w_r, lru_b_r, lru_w_i, lru_b_i, lru_lambda,
                 hgrn_norm_g, conf_conv_w, conf_conv_b, conf_ln_g, conf_ln_b,
                 lam_q1, lam_k1, lam_q2, lam_k2, diff_norm_g):
    mod = jax.nn.silu(c) @ w_mod + b_mod
    mod_c = jax.nn.silu(c_ctx) @ w_mod + b_mod
    shift, scale, gate = jnp.split(mod, 3, axis=-1)
    shift_c, scale_c, gate_c = jnp.split(mod_c, 3, axis=-1)
    h = rms_norm(x, g_pre) * (1.0 + scale[:, None]) + shift[:, None]
    hc = rms_norm(xc, g_pre) * (1.0 + scale_c) + shift_c
    u = jnp.split(h @ w_in, N_IN_SLICES, axis=-1)
    uc = jnp.split(hc @ w_in, N_IN_SLICES, axis=-1)

    ya, ya_c = rglru_mixer(u[0], uc[0], lru_conv_w, lru_conv_b, lru_w_r, lru_b_r, lru_w_i, lru_b_i, lru_lambda)
    yb, yb_c = hgrn2_mixer(u[2], u[3], u[4], u[5], uc[2], uc[3], uc[4], uc[5], lb, hgrn_norm_g)
    yc = conformer_conv(u[7], u[8], conf_conv_w, conf_conv_b, conf_ln_g, conf_ln_b)
    lam_init = 0.8 - 0.6 * math.exp(-0.3 * layer)
    lam = (jnp.exp(jnp.sum(lam_q1.astype(jnp.float32) * lam_k1.astype(jnp.float32)))
           - jnp.exp(jnp.sum(lam_q2.astype(jnp.float32) * lam_k2.astype(jnp.float32))) + lam_init)
    yd, yd_c = diff_attention_mixer(u[10], u[11], u[12], uc[10], uc[11], uc[12], cos, sin,
                                    lam, lam_init, diff_norm_g, with_ctx)

    def merge(ys, gs):
        return jnp.concatenate([y * jax.nn.silu(g) for y, g in zip(ys, gs)], axis=-1) @ w_out

    x = x + gate[:, None] * rms_norm(merge((ya, yb, yc, yd), (u[1], u[6], u[9], u[13])), g_post)
    if with_ctx:
        yc_c = conformer_conv(uc[7], uc[8], conf_conv_w, conf_conv_b, conf_ln_g, conf_ln_b)
        xc = xc + gate_c * rms_norm(merge((ya_c, yb_c, yc_c, yd_c), (uc[1], uc[6], uc[9], uc[13])), g_post)
    return x, xc


def setup_inputs(seed: int = 0) -> dict:
    key = jax.random.key(seed)
    ks = jax.random.split(key, 28)
    f32 = jnp.float32
    D = D_MODEL

    def nrm(k, shape, s):
        return jax.random.normal(k, shape, f32) * s

    a_pow = jax.random.uniform(ks[16], (DEPTH, 2, GROUP_W), f32, 0.9, 0.999)
    sig = a_pow ** (1.0 / LRU_C)
    return {
        'x': nrm(ks[0], (BATCH, SEQ, D), 1.0),
        'c': nrm(ks[1], (BATCH, D), 1.0),
        'ctx': nrm(ks[2], (BATCH, CTX_LEN, D), 1.0),
        'c_ctx': nrm(ks[3], (D,), 1.0),
        'w_mod': nrm(ks[4], (DEPTH, D, 3 * D), 0.5 * D ** -0.5),
        'b_mod': nrm(ks[5], (DEPTH, 3 * D), 0.02),
        'g_pre': 1.0 + nrm(ks[6], (DEPTH, D), 0.02),
        'g_post': 1.0 + nrm(ks[7], (DEPTH, D), 0.02),
        'w_in': nrm(ks[8], (DEPTH, D, IN_W), D ** -0.5),
        'w_out': nrm(ks[9], (DEPTH, MIX_W, D), MIX_W ** -0.5),
        'lru_conv_w': nrm(ks[10], (DEPTH, LRU_CONV, GROUP_W), LRU_CONV ** -0.5),
        'lru_conv_b': nrm(ks[11], (DEPTH, GROUP_W), 0.02),
        'lru_w_r': nrm(ks[12], (DEPTH, 2, LRU_HEADS, LRU_BLOCK, LRU_BLOCK), LRU_BLOCK ** -0.5),
        'lru_b_r': nrm(ks[13], (DEPTH, 2, GROUP_W), 0.02),
        'lru_w_i': nrm(ks[14], (DEPTH, 2, LRU_HEADS, LRU_BLOCK, LRU_BLOCK), LRU_BLOCK ** -0.5),
        'lru_b_i': nrm(ks[15], (DEPTH, 2, GROUP_W), 0.02),
        'lru_lambda': jnp.log(sig) - jnp.log1p(-sig),
        'hgrn_lb': nrm(ks[17], (DEPTH, 2, GROUP_W), 1.0),
        'hgrn_norm_g': 1.0 + nrm(ks[18], (DEPTH, HGRN_HEAD_DIM), 0.02),
        'conf_conv_w': nrm(ks[19], (DEPTH, CONV_K, GROUP_W), CONV_K ** -0.5),
        'conf_conv_b': nrm(ks[20], (DEPTH, GROUP_W), 0.02),
        'conf_ln_g': 1.0 + nrm(ks[21], (DEPTH, GROUP_W), 0.02),
        'conf_ln_b': nrm(ks[22], (DEPTH, GROUP_W), 0.02),
        'diff_lam_q1': nrm(ks[23], (DEPTH, DIFF_HEAD_DIM), 0.1),
        'diff_lam_k1': nrm(ks[24], (DEPTH, DIFF_HEAD_DIM), 0.1),
        'diff_lam_q2': nrm(ks[25], (DEPTH, DIFF_HEAD_DIM), 0.1),
        'diff_lam_k2': nrm(ks[26], (DEPTH, DIFF_HEAD_DIM), 0.1),
        'diff_norm_g': 1.0 + nrm(ks[27], (DEPTH, 2 * DIFF_HEAD_DIM), 0.02),
    }


def reference(x, c, ctx, c_ctx, w_mod, b_mod, g_pre, g_post, w_in, w_out,
              lru_conv_w, lru_conv_b, lru_w_r, lru_b_r, lru_w_i, lru_b_i, lru_lambda,
              hgrn_lb, hgrn_norm_g, conf_conv_w, conf_conv_b, conf_ln_g, conf_ln_b,
              diff_lam_q1, diff_lam_k1, diff_lam_q2, diff_lam_k2, diff_norm_g):
    cos, sin = axial_rope_tables(x.shape[1])
    p = jax.nn.softmax(hgrn_lb.astype(jnp.float32), axis=0)
    lower_bounds = jnp.cumsum(p, axis=0) - p[0]
    xc = ctx
    for layer in range(DEPTH):
        x, xc = hybrid_layer(
            x, xc, c, c_ctx, layer, layer < DEPTH - 1, lower_bounds[layer], cos, sin,
            w_mod[layer], b_mod[layer], g_pre[layer], g_post[layer], w_in[layer], w_out[layer],
            lru_conv_w[layer], lru_conv_b[layer], lru_w_r[layer], lru_b_r[layer],
            lru_w_i[layer], lru_b_i[layer], lru_lambda[layer],
            hgrn_norm_g[layer], conf_conv_w[layer], conf_conv_b[layer], conf_ln_g[layer], conf_ln_b[layer],
            diff_lam_q1[layer], diff_lam_k1[layer], diff_lam_q2[layer], diff_lam_k2[layer], diff_norm_g[layer])
    return x
```

```python
import math
import numpy as np
import ml_dtypes
from contextlib import ExitStack
import concourse.bass as bass
import concourse.mybir as mybir
from concourse.bass_utils import run_bass_kernel_spmd

F32 = mybir.dt.float32
BF16 = mybir.dt.bfloat16
ALU = mybir.AluOpType
AF = mybir.ActivationFunctionType
AX = mybir.AxisListType

D = 1024
LC = 256
LL = 4096
T = LC + LL
NT = T // 128
DEPTH = 4
RMS_EPS = 1e-6
LN_EPS = 1e-5
BLK = [(0, 256)] + [(256 + 512 * j, 512) for j in range(8)]
NCH = T // 64


import os as _os
class _Stop(Exception):
    pass


def _ck(n):
    if int(_os.environ.get('BSTOP', '99')) == n:
        raise _Stop()


class Sched:
    ENG = ('pe', 'act', 'dve', 'pool', 'sp')

    def __init__(s, nc, es, ndsem=24):
        s.nc = nc
        s.eng = dict(pe=nc.tensor, act=nc.scalar, dve=nc.vector, pool=nc.gpsimd, sp=nc.sync)
        s.sem = {e: es.enter_context(nc.semaphore("sem_" + e)) for e in s.ENG}
        s.cnt = {e: 0 for e in s.ENG}
        s.seen = {e: {} for e in s.ENG}
        s.dsem = [es.enter_context(nc.semaphore(f"dsem{i}")) for i in range(ndsem)]
        s.dcnt = [0] * ndsem
        s.dnext = 0
        s.lastw = {}
        s.readers = {}
        s.unsig = False

    def _need(s, e, tok):
        kind, who, val = tok
        if kind == 'e' and who == e and e == 'pe':
            return
        key = (kind, who)
        if s.seen[e].get(key, 0) >= val:
            return
        if kind == 'e':
            assert val <= s.cnt[who], f"wait on unsignaled op {tok} cnt={s.cnt[who]}"
            s.eng[e].wait_ge(s.sem[who], val)
        else:
            s.eng[e].wait_ge(s.dsem[who], val)
        s.seen[e][key] = val

    def _deps(s, e, reads, writes):
        for r in reads:
            t = s.lastw.get(r)
            if t is not None:
                s._need(e, t)
        for w in writes:
            t = s.lastw.get(w)
            if t is not None and not (t[0] == 'e' and t[1] == e):
                s._need(e, t)
            for (k, who), val in s.readers.get(w, {}).items():
                if not (k == 'e' and who == e):
                    s._need(e, (k, who, val))

    def _reg(s, tok, reads, writes):
        for r in reads:
            d = s.readers.setdefault(r, {})
            key = (tok[0], tok[1])
            if d.get(key, 0) < tok[2]:
                d[key] = tok[2]
        for w in writes:
            s.lastw[w] = tok
            s.readers[w] = {}

    def op(s, e, fn, reads=(), writes=(), sig=True):
        s._deps(e, reads, writes)
        ins = fn(s.eng[e])
        if sig:
            s.cnt[e] += 1
            ins.then_inc(s.sem[e], 1)
            tok = ('e', e, s.cnt[e])
            if e == 'pe':
                s.unsig = False
        else:
            assert e == 'pe'
            tok = ('e', e, s.cnt[e] + 1)
            s.unsig = True
        s._reg(tok, reads, writes)

    def dma(s, q, out, in_, reads=(), writes=()):
        s._deps(q, reads, writes)
        i = s.dnext
        s.dnext = (s.dnext + 1) % len(s.dsem)
        if s.dcnt[i] > 0:
            s._need(q, ('d', i, 16 * s.dcnt[i]))
        s.dcnt[i] += 1
        s.eng[q].dma_start(out=out, in_=in_).then_inc(s.dsem[i], 16)
        s._reg(('d', i, 16 * s.dcnt[i]), reads, writes)

    def barrier(s):
        assert not s.unsig
        toks = [('e', e, s.cnt[e]) for e in s.ENG if s.cnt[e] > 0]
        toks += [('d', i, 16 * s.dcnt[i]) for i in range(len(s.dsem)) if s.dcnt[i] > 0]
        for e in s.ENG:
            for t in toks:
                s._need(e, t)
        s.lastw.clear()
        s.readers.clear()


def build(nc, nlayers=DEPTH, dbg=False, stages="HABCDO"):
    es = ExitStack()
    S = Sched(nc, es)

    def din(name, shape, dt=F32):
        return nc.dram_tensor(name, list(shape), dt, kind="ExternalInput").ap()

    x_in = din("x_b", [LL, D])
    c_in = din("ctx_b", [LC, D])
    cfm = din("cfm", [128, 16])
    w_mod = din("w_mod", [DEPTH, D, 3 * D])
    b_mod = din("b_mod", [DEPTH, 3 * D])
    b_mod_fm = din("b_mod_fm", [DEPTH, 128, 24])
    g_pre_fm = din("g_pre_fm", [DEPTH, 128, 8])
    g_post = din("g_post", [DEPTH, D])
    w_in = din("w_in", [DEPTH, D, 3584])
    w_out = din("w_out", [DEPTH, D, D])
    lru_cw = din("lru_cw", [DEPTH, 128, 8])
    lru_cb = din("lru_cb", [DEPTH, 128, 2])
    lru_bd = din("lru_bd", [DEPTH, 2, 4, 128, 128])
    lru_b = din("lru_b", [DEPTH, 128, 8])
    lru_lam = din("lru_lam", [DEPTH, 128, 4])
    hg_lb = din("hg_lb", [128, 16])
    hg_g = din("hg_g", [DEPTH, 128, 1])
    cf_w = din("cf_w", [DEPTH, 128, 62])
    cf_b = din("cf_b", [DEPTH, 128, 6])
    df_lam = din("df_lam", [DEPTH, 128])
    df_g = din("df_g", [DEPTH, 128, 1])
    c_ident = din("c_ident", [128, 128], BF16)
    c_cos = din("c_cos", [128, LL])
    c_sin = din("c_sin", [128, LL])
    c_mask = din("c_mask", [128, 128])
    c_bd64 = din("c_bd64", [128, 128], BF16)

    out = nc.dram_tensor("out", [LL, D], F32, kind="ExternalOutput").ap()
    xcs = nc.dram_tensor("xcs", [LC, D], F32).ap()
    ymix = nc.dram_tensor("ymix", [D, T], BF16, kind="ExternalOutput" if dbg else "Internal").ap()
    ggd = nc.dram_tensor("ggd", [DEPTH, 2, D], F32, kind="ExternalOutput" if dbg else "Internal").ap()

    uid = [0]

    def sb(name, shape, dt=F32, ctx=None):
        uid[0] += 1
        return (ctx or es).enter_context(nc.sbuf_tensor(f"{name}_{uid[0]}", list(shape), dt))

    PS = [es.enter_context(nc.psum_tensor(f"ps{i}", [128, 512], F32)) for i in range(8)]

    def pk(i):
        return ('ps', i)

    hT = sb("hT", [128, 8, T], BF16)
    identb = sb("identb", [128, 128], BF16)
    bd64 = sb("bd64", [128, 128], BF16)
    ones64 = sb("ones64", [128, 64], BF16)
    ones256 = sb("ones256", [128, 128], BF16)
    maskt = sb("maskt", [128, 128], F32)
    GS = sb("GS", [128, DEPTH, 4, 8], F32)
    LBt = sb("LBt", [128, 3, 4, 4], F32)
    epsr = sb("epsr", [128, 1], F32)
    epsl = sb("epsl", [128, 1], F32)
    onec = sb("onec", [128, 1], F32)

    S.dma('sp', identb[:], c_ident[:, :], writes=['identb'])
    S.dma('sp', bd64[:], c_bd64[:, :], writes=['bd64'])
    S.dma('sp', maskt[:], c_mask[:, :], writes=['maskt'])
    S.op('dve', lambda e: e.memset(ones64[:], 1.0), writes=['ones64'])
    S.op('dve', lambda e: e.memset(ones256[:], 1.0 / 256.0), writes=['ones256'])
    S.op('dve', lambda e: e.memset(epsr[:], RMS_EPS), writes=['epsr'])
    S.op('dve', lambda e: e.memset(epsl[:], LN_EPS), writes=['epsl'])
    S.op('dve', lambda e: e.memset(onec[:], 1.0), writes=['onec'])
    for i_ in range(8):
        S.op('dve', lambda e, i_=i_: e.memset(PS[i_][:, :], 0.0), writes=[pk(i_)])

    def act(out_, in_, func, reads, writes, bias=None, scale=None):
        kw = {}
        if bias is not None:
            kw['bias'] = bias
        if scale is not None:
            kw['scale'] = scale
        S.op('act', lambda e: e.activation(out=out_, in_=in_, func=func, **kw), reads=reads, writes=writes)

    def rstd_from_ss(rs, ss, n_inv, eps_ap, key_rs, key_ss):
        act(rs, ss, AF.Ln, [key_ss], [key_rs], bias=eps_ap, scale=n_inv)
        act(rs, rs, AF.Exp, [key_rs], [key_rs], scale=-0.5)

    def prologue():
        with ExitStack() as cx:
            cf = sb("cf", [128, 16], F32, cx)
            sc = sb("sc", [128, 16], F32, cx)
            rep = sb("rep", [128, 2, 8, 128], F32, cx)
            wms = [sb(f"wm{i}", [128, 8, 512], F32, cx) for i in range(2)]
            bfm = sb("bfm", [128, 24], F32, cx)
            gpf = sb("gpf", [128, 8], F32, cx)
            bg = sb("bg", [128, 1024], F32, cx)
            gp = sb("gp", [128, 1024], F32, cx)
            tmpg = sb("tmpg", [128, 1024], F32, cx)
            tmp = sb("ptmp", [128, 16], F32, cx)
            hl = sb("hl", [128, 16], F32, cx)
            hs = sb("hs", [128, 4], F32, cx)
            S.dma('sp', cf[:], cfm[:, :], writes=['cf'])
            S.dma('sp', hl[:], hg_lb[:, :], writes=['hl'])
            act(sc[:], cf[:], AF.Silu, ['cf'], ['sc'])
            for j in range(2):
                src = sc[:].rearrange("p (k j) -> p k j", j=2)[:, :, j:j + 1].broadcast_to([128, 8, 128])
                S.op('dve', lambda e, j=j, src=src: e.tensor_copy(out=rep[:, j, :, :], in_=src), reads=['sc'], writes=['rep'])
            act(hl[:], hl[:], AF.Exp, ['hl'], ['hl'])
            hl3 = hl[:].rearrange("p (a l) -> p a l", l=4)
            S.op('dve', lambda e: e.reduce_sum(out=hs[:], in_=hl3, axis=AX.X), reads=['hl'], writes=['hs'])
            S.op('dve', lambda e: e.reciprocal(out=hs[:], in_=hs[:]), reads=['hs'], writes=['hs'])
            S.op('dve', lambda e: e.tensor_tensor(out=hl3, in0=hl3, in1=hs[:].unsqueeze(2).broadcast_to([128, 4, 4]), op=ALU.mult),
                 reads=['hl', 'hs'], writes=['hl'])
            S.op('dve', lambda e: e.memset(LBt[:, 0, :, 0:1], 0.0), writes=['LBt'])
            S.op('dve', lambda e: e.tensor_copy(out=LBt[:, 0, :, 1:2], in_=hl3[:, :, 1:2]), reads=['hl', 'LBt'], writes=['LBt'])
            for l in (2, 3):
                S.op('dve', lambda e, l=l: e.tensor_tensor(out=LBt[:, 0, :, l:l + 1], in0=LBt[:, 0, :, l - 1:l], in1=hl3[:, :, l:l + 1], op=ALU.add),
                     reads=['hl', 'LBt'], writes=['LBt'])
            S.op('dve', lambda e: e.tensor_scalar(out=LBt[:, 1, :, :], in0=LBt[:, 0, :, :], scalar1=-1.0, scalar2=1.0, op0=ALU.mult, op1=ALU.add),
                 reads=['LBt'], writes=['LBt'])
            S.op('dve', lambda e: e.tensor_scalar(out=LBt[:, 2, :, :], in0=LBt[:, 1, :, :], scalar1=-1.0, scalar2=None, op0=ALU.mult),
                 reads=['LBt'], writes=['LBt'])
            for l in range(nlayers):
                S.dma('sp', bfm[:], b_mod_fm[l], writes=['bfm'])
                S.dma('sp', gpf[:], g_pre_fm[l], writes=['gpf'])
                S.dma('sp', bg[:], b_mod[l:l + 1, 2048:3072].partition_broadcast(128), writes=['bg'])
                S.dma('sp', gp[:], g_post[l:l + 1, :].partition_broadcast(128), writes=['gp'])
                wv = w_mod[l].rearrange("(k p) n -> p k n", p=128)
                for sl in range(6):
                    wm = wms[sl % 2]
                    wk = f'wm{sl % 2}'
                    S.dma('sp', wm[:], wv[:, :, sl * 512:(sl + 1) * 512], writes=[wk])
                    if sl < 4:
                        for c in range(4):
                            cc = sl * 4 + c
                            for k in range(8):
                                S.op('pe', lambda e, k=k, c=c, cc=cc, wm=wm: e.matmul(PS[0][:, 2 * cc:2 * cc + 2], lhsT=wm[:, k, c * 128:(c + 1) * 128],
                                                                                   rhs=sc[:, 2 * k:2 * k + 2], start=(k == 0), stop=(k == 7)),
                                     reads=[wk, 'sc'], writes=[pk(0)], sig=(k == 7))
                    else:
                        half = sl - 4
                        for j in range(2):
                            for k in range(8):
                                S.op('pe', lambda e, k=k, j=j, half=half, wm=wm: e.matmul(PS[1 + 2 * j + half][:, :], lhsT=rep[:, j, k, :], rhs=wm[:, k, :],
                                                                                        start=(k == 0), stop=(k == 7)),
                                     reads=[wk, 'rep'], writes=[pk(1 + 2 * j + half)], sig=(k == 7))
                psv = PS[0][:, 0:32].rearrange("p (w k j) -> p w k j", w=2, k=8, j=2)
                for j in range(2):
                    S.op('dve', lambda e, j=j: e.tensor_tensor(out=GS[:, l, 1 + 2 * j, :], in0=psv[:, 0, :, j], in1=bfm[:, 0:8], op=ALU.add),
                         reads=[pk(0), 'bfm'], writes=['GS'])
                    S.op('dve', lambda e, j=j: e.tensor_tensor(out=tmp[:, 0:8], in0=psv[:, 1, :, j], in1=bfm[:, 8:16], op=ALU.add),
                         reads=[pk(0), 'bfm'], writes=['ptmp'])
                    S.op('dve', lambda e, j=j: e.scalar_tensor_tensor(out=GS[:, l, 2 * j, :], in0=tmp[:, 0:8], scalar=1.0, in1=gpf[:], op0=ALU.add, op1=ALU.mult),
                         reads=['ptmp', 'gpf'], writes=['GS'])
                for j in range(2):
                    for half in range(2):
                        S.op('dve', lambda e, j=j, half=half: e.tensor_tensor(out=tmpg[:, half * 512:(half + 1) * 512], in0=PS[1 + 2 * j + half][:, :],
                                                                            in1=bg[:, half * 512:(half + 1) * 512], op=ALU.add),
                             reads=[pk(1 + 2 * j + half), 'bg'], writes=['tmpg'])
                    S.op('dve', lambda e: e.tensor_tensor(out=tmpg[:], in0=tmpg[:], in1=gp[:], op=ALU.mult), reads=['tmpg', 'gp'], writes=['tmpg'])
                    S.dma('pool', ggd[l, j:j + 1, :], tmpg[0:1, :], reads=['tmpg'], writes=['ggd'])
        S.barrier()

    def load_w(dst, l, src, col0, ncols, stgs, dkey, off=0, slab=256):
        wv = src[l].rearrange("(k p) n -> p k n", p=128)
        i = 0
        for c in range(0, ncols, slab):
            n = min(slab, ncols - c)
            st, sk = stgs[i % len(stgs)]
            i += 1
            S.dma('sp', st[:, :, 0:n], wv[:, :, col0 + c:col0 + c + n], writes=[sk])
            S.op('pool', lambda e, st=st, c=c, n=n: e.tensor_copy(out=dst[:, :, off + c:off + c + n], in_=st[:, :, 0:n]), reads=[sk], writes=[dkey])

    def proj_fm(b, W, wkey, wc0, t0, n, wn=128):
        for k in range(8):
            S.op('pe', lambda e, k=k: e.matmul(PS[b][0:wn, 0:n], lhsT=W[:, k, wc0:wc0 + wn], rhs=hT[:, k, t0:t0 + n], start=(k == 0), stop=(k == 7)),
                 reads=[wkey, 'hT'], writes=[pk(b)], sig=(k == 7))

    def proj_tm(ps_ap, b, W, wkey, wc0, ncols, tile):
        for k in range(8):
            S.op('pe', lambda e, k=k: e.matmul(ps_ap, lhsT=hT[:, k, tile * 128:(tile + 1) * 128], rhs=W[:, k, wc0:wc0 + ncols], start=(k == 0), stop=(k == 7)),
                 reads=[wkey, 'hT'], writes=[pk(b)], sig=(k == 7))

    def res_src(l, i):
        if i < 2:
            base = c_in if l == 0 else xcs
            return base[i * 128:(i + 1) * 128, :]
        base = x_in if l == 0 else out
        return base[(i - 2) * 128:(i - 1) * 128, :]

    def res_dst(i):
        if i < 2:
            return xcs[i * 128:(i + 1) * 128, :]
        return out[(i - 2) * 128:(i - 1) * 128, :]

    def stageH(l):
        with ExitStack() as cx:
            xts = [sb(f"hx{i}", [128, 1024], F32, cx) for i in range(2)]
            sq = sb("hsq", [128, 1024], F32, cx)
            xss = [sb(f"hxs{i}", [128, 1024], BF16, cx) for i in range(2)]
            st = sb("hst", [128, 4], F32, cx)
            for i in range(NT):
                p = i % 2
                xt, xs = xts[p], xss[p]
                jj = 0 if i >= 2 else 2
                S.dma('sp', xt[:], res_src(l, i), reads=[('res', i)], writes=[f'hx{p}'])
                act(sq[:], xt[:], AF.Square, [f'hx{p}'], ['hsq'])
                S.op('dve', lambda e, p=p: e.reduce_sum(out=st[:, p:p + 1], in_=sq[:], axis=AX.X), reads=['hsq'], writes=[f'hss{p}'])
                rstd_from_ss(st[:, 2 + p:3 + p], st[:, p:p + 1], 1.0 / D, epsr[:], f'hrs{p}', f'hss{p}')
                S.op('dve', lambda e, p=p, xt=xt, xs=xs: e.tensor_scalar(out=xs[:], in0=xt[:], scalar1=st[:, 2 + p:3 + p], scalar2=None, op0=ALU.mult),
                     reads=[f'hx{p}', f'hrs{p}'], writes=[f'hxs{p}'])
                b = 6 + p
                psb = PS[b][:].bitcast(BF16)
                for k in range(8):
                    S.op('pe', lambda e, k=k, xs=xs, psb=psb: e.transpose(out=psb[:, k * 128:(k + 1) * 128], in_=xs[:, k * 128:(k + 1) * 128], identity=identb[:]),
                         reads=[f'hxs{p}', 'identb'], writes=[pk(b)], sig=(k == 7))
                for k in range(8):
                    S.op('dve', lambda e, k=k, psb=psb, i=i, jj=jj: e.tensor_scalar(out=hT[:, k, i * 128:(i + 1) * 128], in0=psb[:, k * 128:(k + 1) * 128],
                                                                             scalar1=GS[:, l, jj, k:k + 1], scalar2=GS[:, l, jj + 1, k:k + 1],
                                                                             op0=ALU.mult, op1=ALU.add),
                         reads=[pk(b), 'GS'], writes=['hT'])
        S.barrier()

    def stageO(l, last, fuse_next=False):
        with ExitStack() as cx:
            wo = sb("wo", [128, 8, 1024], BF16, cx)
            stgs = [(sb(f"ostg{i}", [128, 8, 256], F32, cx), f'ostg{i}') for i in range(2)]
            GG = [sb(f"GG{j}", [128, 1024], F32, cx) for j in range(2)]
            yms = [sb(f"oym{i}", [128, 8, 512], BF16, cx) for i in range(2)]
            xts = [sb(f"ox{i}", [128, 1024], F32, cx) for i in range(2)]
            sq = sb("osq", [128, 1024], F32, cx)
            tts = [sb(f"ot{i}", [128, 1024], F32, cx) for i in range(2)]
            st = sb("ost", [128, 8], F32, cx)
            if fuse_next:
                sq2 = sb("osq2", [128, 1024], F32, cx)
                xss = [sb(f"oxs{i}", [128, 1024], BF16, cx) for i in range(2)]
            load_w(wo, l, w_out, 0, 1024, stgs, 'wo')
            for j in range(2):
                S.dma('sp', GG[j][:], ggd[l, j:j + 1, :].partition_broadcast(128), reads=['ggd'], writes=[f'GG{j}'])
            ymv = ymix.rearrange("(k p) t -> p k t", p=128)
            it = 0
            for bi, (t0, n) in enumerate(BLK):
                if last and bi == 0:
                    continue
                ym = yms[bi % 2]
                yk = f'oym{bi % 2}'
                S.dma('sp', ym[:, :, 0:n], ymv[:, :, t0:t0 + n], reads=['ymix'], writes=[yk])
                for tt in range(n // 128):
                    i = (t0 // 128) + tt
                    j = 0 if i >= 2 else 1
                    p = it % 2
                    it += 1
                    xt, tq = xts[p], tts[p]
                    S.dma('sp', xt[:], res_src(l, i), reads=[('res', i)], writes=[f'ox{p}'])
                    for half in range(2):
                        b = 2 * p + half
                        for k in range(8):
                            S.op('pe', lambda e, k=k, b=b, half=half, ym=ym, tt=tt: e.matmul(PS[b][:, :], lhsT=ym[:, k, tt * 128:(tt + 1) * 128],
                                                                                       rhs=wo[:, k, half * 512:(half + 1) * 512], start=(k == 0), stop=(k == 7)),
                                 reads=[yk, 'wo'], writes=[pk(b)], sig=(k == 7))
                        act(sq[:, half * 512:(half + 1) * 512], PS[b][:, :], AF.Square, [pk(b)], ['osq'])
                    S.op('dve', lambda e, p=p: e.reduce_sum(out=st[:, p:p + 1], in_=sq[:], axis=AX.X), reads=['osq'], writes=[f'oss{p}'])
                    rstd_from_ss(st[:, 2 + p:3 + p], st[:, p:p + 1], 1.0 / D, epsr[:], f'ors{p}', f'oss{p}')
                    for half in range(2):
                        b = 2 * p + half
                        S.op('dve', lambda e, b=b, half=half, tq=tq, p=p, j=j: e.scalar_tensor_tensor(out=tq[:, half * 512:(half + 1) * 512], in0=PS[b][:, :],
                                                                                                scalar=st[:, 2 + p:3 + p], in1=GG[j][:, half * 512:(half + 1) * 512],
                                                                                                op0=ALU.mult, op1=ALU.mult),
                             reads=[pk(b), f'ors{p}', f'GG{j}'], writes=[f'ot{p}'])
                    S.op('pool', lambda e, tq=tq, xt=xt: e.tensor_tensor(out=tq[:], in0=tq[:], in1=xt[:], op=ALU.add), reads=[f'ot{p}', f'ox{p}'], writes=[f'ot{p}'])
                    S.dma('pool', res_dst(i), tq[:], reads=[f'ot{p}'], writes=[('res', i)])
                    if fuse_next:
                        ln = l + 1
                        jj = 0 if i >= 2 else 2
                        xs = xss[p]
                        act(sq2[:], tq[:], AF.Square, [f'ot{p}'], ['osq2'])
                        S.op('dve', lambda e, p=p: e.reduce_sum(out=st[:, 4 + p:5 + p], in_=sq2[:], axis=AX.X), reads=['osq2'], writes=[f'oss2{p}'])
                        rstd_from_ss(st[:, 6 + p:7 + p], st[:, 4 + p:5 + p], 1.0 / D, epsr[:], f'ors2{p}', f'oss2{p}')
                        S.op('pool', lambda e, p=p, tq=tq, xs=xs: e.tensor_scalar(out=xs[:], in0=tq[:], scalar1=st[:, 6 + p:7 + p], scalar2=None, op0=ALU.mult),
                             reads=[f'ot{p}', f'ors2{p}'], writes=[f'oxs{p}'])
                        for k in range(8):
                            b = (4 + p) if k < 4 else (6 + p)
                            psb = PS[b][:].bitcast(BF16)
                            S.op('pe', lambda e, k=k, xs=xs, psb=psb: e.transpose(out=psb[:, (k % 4) * 128:(k % 4 + 1) * 128], in_=xs[:, k * 128:(k + 1) * 128], identity=identb[:]),
                                 reads=[f'oxs{p}', 'identb'], writes=[pk(b)], sig=(k % 4 == 3))
                        for k in range(8):
                            b = (4 + p) if k < 4 else (6 + p)
                            psb = PS[b][:].bitcast(BF16)
                            if k < 4:
                                act(hT[:, k, i * 128:(i + 1) * 128], psb[:, (k % 4) * 128:(k % 4 + 1) * 128], AF.Identity, [pk(b), 'GS'], [('hTa', k)],
                                    bias=GS[:, ln, jj + 1, k:k + 1], scale=GS[:, ln, jj, k:k + 1])
                            else:
                                S.op('dve', lambda e, k=k, psb=psb, i=i, jj=jj, ln=ln: e.tensor_scalar(out=hT[:, k, i * 128:(i + 1) * 128], in0=psb[:, (k % 4) * 128:(k % 4 + 1) * 128],
                                                                                            scalar1=GS[:, ln, jj, k:k + 1], scalar2=GS[:, ln, jj + 1, k:k + 1],
                                                                                            op0=ALU.mult, op1=ALU.add),
                                     reads=[pk(b), 'GS'], writes=[('hTd', k)])
        S.barrier()

    NG = T + 3

    def stageA(l):
        with ExitStack() as cx:
            W = sb("aW", [128, 8, 512], BF16, cx)
            stgs = [(sb(f"astg{i}", [128, 8, 256], F32, cx), f'astg{i}') for i in range(2)]
            bdst = sb("abdst", [128, 4, 128], F32, cx)
            bd = sb("abd", [128, 4, 128], BF16, cx)
            cw = sb("acw", [128, 8], F32, cx)
            cb = sb("acb", [128, 2], F32, cx)
            lbias = sb("alb", [128, 8], F32, cx)
            lam = sb("alam", [128, 4], F32, cx)
            B = [sb(f"aB{i}", [128, NG + 3], F32, cx) for i in range(5)]
            XCb = sb("aXCb", [128, NG], BF16, cx)
            SGt = sb("aSG", [128, T], BF16, cx)
            Yb = XCb
            load_w(W, l, w_in, 0, 512, stgs, 'aW')
            S.dma('sp', cw[:], lru_cw[l], writes=['acw'])
            S.dma('sp', cb[:], lru_cb[l], writes=['acb'])
            S.dma('sp', lbias[:], lru_b[l], writes=['alb'])
            S.dma('sp', lam[:], lru_lam[l], writes=['alam'])
            act(lam[:], lam[:], AF.Exp, ['alam'], ['alam'], scale=-1.0)
            act(lam[:], lam[:], AF.Ln, ['alam'], ['alam'], bias=onec[:], scale=1.0)
            S.op('dve', lambda e: e.tensor_scalar(out=lam[:], in0=lam[:], scalar1=-8.0, scalar2=None, op0=ALU.mult), reads=['alam'], writes=['alam'])
            for pc in range(2):
                UX, XC = B[0], B[4]
                for (a, b_) in ((0, 2), (258, 261), (NG + 2, NG + 3)):
                    S.op('dve', lambda e, a=a, b_=b_: e.memset(UX[:, a:b_], 0.0), writes=['aB0'])
                for bi, (t0, n) in enumerate(BLK):
                    ux0 = 2 + t0 if t0 < 256 else 261 + (t0 - 256)
                    b = bi % 2
                    proj_fm(b, W, 'aW', pc * 128, t0, n)
                    S.op('dve', lambda e, b=b, ux0=ux0, n=n: e.tensor_copy(out=UX[:, ux0:ux0 + n], in_=PS[b][:, 0:n]), reads=[pk(b)], writes=['aB0'])
                    b2 = 2 + bi % 2
                    proj_fm(b2, W, 'aW', 256 + pc * 128, t0, n)
                    act(SGt[:, t0:t0 + n], PS[b2][:, 0:n], AF.Silu, [pk(b2)], ['aSG'])
                S.op('dve', lambda e: e.tensor_scalar(out=XC[:, 0:NG], in0=UX[:, 0:NG], scalar1=cw[:, pc * 4:pc * 4 + 1], scalar2=cb[:, pc:pc + 1],
                                                      op0=ALU.mult, op1=ALU.add), reads=['aB0', 'acw', 'acb'], writes=['aB4'])
                for k in range(1, 4):
                    S.op('dve', lambda e, k=k: e.scalar_tensor_tensor(out=XC[:, 0:NG], in0=UX[:, k:k + NG], scalar=cw[:, pc * 4 + k:pc * 4 + k + 1], in1=XC[:, 0:NG],
                                                                      op0=ALU.mult, op1=ALU.add), reads=['aB0', 'aB4', 'acw'], writes=['aB4'])
                S.op('pool', lambda e: e.tensor_copy(out=XCb[:], in_=XC[:, 0:NG]), reads=['aB4'], writes=['aXCb'])
                S.dma('sp', bdst[:], lru_bd[l, pc].rearrange("w a b -> a w b"), writes=['abdst'])
                S.op('pool', lambda e: e.tensor_copy(out=bd[:], in_=bdst[:]), reads=['abdst'], writes=['abd'])
                for dr in range(2):
                    R, I, A = B[0], B[1], B[2]
                    for gi, g0 in enumerate(range(0, NG, 512)):
                        n = min(512, NG - g0)
                        for wh, (dst, dk) in enumerate(((R, 'aB0'), (I, 'aB1'))):
                            b = 2 * (gi % 2) + wh
                            S.op('pe', lambda e, b=b, wh=wh, g0=g0, n=n: e.matmul(PS[b][:, 0:n], lhsT=bd[:, 2 * dr + wh, :], rhs=XCb[:, g0:g0 + n], start=True, stop=True),
                                 reads=['abd', 'aXCb'], writes=[pk(b)])
                            bi_ = wh * 4 + dr * 2 + pc
                            act(dst[:, g0:g0 + n], PS[b][:, 0:n], AF.Sigmoid, [pk(b), 'alb'], [dk], bias=lbias[:, bi_:bi_ + 1], scale=1.0)
                    ci = dr * 2 + pc
                    act(A[:, 0:NG], R[:, 0:NG], AF.Exp, ['aB0', 'alam'], ['aB2'], scale=lam[:, ci:ci + 1])
                    act(R[:, 0:NG], A[:, 0:NG], AF.Square, ['aB2'], ['aB0'])
                    act(R[:, 0:NG], R[:, 0:NG], AF.Ln, ['aB0'], ['aB0'], bias=onec[:], scale=-1.0)
                    act(R[:, 0:NG], R[:, 0:NG], AF.Exp, ['aB0'], ['aB0'], scale=0.5)
                    S.op('dve', lambda e: e.tensor_tensor(out=I[:, 0:NG], in0=I[:, 0:NG], in1=XC[:, 0:NG], op=ALU.mult), reads=['aB1', 'aB4'], writes=['aB1'])
                    S.op('dve', lambda e: e.tensor_tensor(out=I[:, 0:NG], in0=I[:, 0:NG], in1=R[:, 0:NG], op=ALU.mult), reads=['aB1', 'aB0'], writes=['aB1'])
                    H, hk = (B[3], 'aB3') if dr == 0 else (B[0], 'aB0')
                    if dr == 0:
                        S.op('dve', lambda e, H=H: e.tensor_tensor_scan(out=H[:, 0:256], data0=A[:, 0:256], data1=I[:, 0:256], initial=0.0, op0=ALU.mult, op1=ALU.add),
                             reads=['aB2', 'aB1'], writes=[hk])
                        S.op('dve', lambda e, H=H: e.tensor_tensor_scan(out=H[:, 259:NG], data0=A[:, 259:NG], data1=I[:, 259:NG], initial=H[:, 255:256],
                                                                         op0=ALU.mult, op1=ALU.add), reads=['aB2', 'aB1', hk], writes=[hk])
                    else:
                        rv = lambda X, a, b_: X[:, a:b_][:, ::-1]
                        S.op('dve', lambda e, H=H: e.tensor_tensor_scan(out=rv(H, 0, 256), data0=rv(A, 0, 256), data1=rv(I, 0, 256), initial=0.0, op0=ALU.mult, op1=ALU.add),
                             reads=['aB2', 'aB1'], writes=[hk])
                        S.op('dve', lambda e, H=H: e.tensor_tensor_scan(out=rv(H, 259, NG), data0=rv(A, 259, NG), data1=rv(I, 259, NG), initial=H[:, 0:1],
                                                                         op0=ALU.mult, op1=ALU.add), reads=['aB2', 'aB1', hk], writes=[hk])
                        S.op('dve', lambda e: e.tensor_tensor(out=B[3][:, 0:NG], in0=B[3][:, 0:NG], in1=B[0][:, 0:NG], op=ALU.add), reads=['aB3', 'aB0'], writes=['aB3'])
                S.op('dve', lambda e: e.tensor_tensor(out=Yb[:, 0:256], in0=B[3][:, 0:256], in1=SGt[:, 0:256], op=ALU.mult), reads=['aB3', 'aSG'], writes=['aXCb'])
                S.op('dve', lambda e: e.tensor_tensor(out=Yb[:, 256:T], in0=B[3][:, 259:NG], in1=SGt[:, 256:T], op=ALU.mult), reads=['aB3', 'aSG'], writes=['aXCb'])
                S.dma('pool', ymix[pc * 128:(pc + 1) * 128, :], Yb[:, 0:T], reads=['aXCb'], writes=['ymix'])
        S.barrier()

    NP = T + 60

    def stageC(l, last):
        with ExitStack() as cx:
            W = sb("cW", [128, 8, 768], BF16, cx)
            stgs = [(sb(f"cstg{i}", [128, 8, 256], F32, cx), f'cstg{i}') for i in range(2)]
            Yp = [sb(f"cYp{i}", [128, NP], BF16, cx) for i in range(2)]
            SG = sb("cSG", [128, 2, T], BF16, cx)
            Dg = sb("cDg", [128, 2, 31, 128], BF16, cx)
            cwt = sb("ccw", [128, 62], F32, cx)
            cbt = sb("ccb", [128, 6], F32, cx)
            sgm = [sb(f"csgm{i}", [128, 512], F32, cx) for i in range(2)]
            Cf = sb("cCf", [128, 2, 512], F32, cx)
            Cb = sb("cCb", [128, 2, 512], BF16, cx)
            Cq = sb("cCq", [128, 2, 512], BF16, cx)
            mean = sb("cmean", [128, 512], F32, cx)
            var = sb("cvar", [128, 512], F32, cx)
            dd = [sb(f"cdd{i}", [128, 512], F32, cx) for i in range(2)]
            yo = [sb(f"cyo{i}", [128, 512], BF16, cx) for i in range(2)]
            load_w(W, l, w_in, 1792, 768, stgs, 'cW')
            S.dma('sp', cwt[:], cf_w[l], writes=['ccw'])
            S.dma('sp', cbt[:], cf_b[l], writes=['ccb'])
            for pc in range(2):
                i0 = identb[:].unsqueeze(1).broadcast_to([128, 31, 128])
                i1 = cwt[:, pc * 31:(pc + 1) * 31].unsqueeze(2).broadcast_to([128, 31, 128])
                S.op('dve', lambda e, pc=pc, i0=i0, i1=i1: e.tensor_tensor(out=Dg[:, pc, :, :], in0=i0, in1=i1, op=ALU.mult), reads=['identb', 'ccw'], writes=['cDg'])
                S.op('pool', lambda e, pc=pc: e.memset(Yp[pc][:], 0.0), writes=[f'cYp{pc}'])
            for bi, (t0, n) in enumerate(BLK):
                p0 = 15 + t0 if t0 < 256 else 301 + (t0 - 256)
                for pc in range(2):
                    q = (bi * 2 + pc) % 2
                    proj_fm(0 + q, W, 'cW', pc * 128, t0, n)
                    proj_fm(2 + q, W, 'cW', 256 + pc * 128, t0, n)
                    act(sgm[q][:, 0:n], PS[2 + q][:, 0:n], AF.Sigmoid, [pk(2 + q)], [f'csgm{q}'])
                    S.op('dve', lambda e, pc=pc, q=q, p0=p0, n=n: e.tensor_tensor(out=Yp[pc][:, p0:p0 + n], in0=PS[q][:, 0:n], in1=sgm[q][:, 0:n], op=ALU.mult),
                         reads=[pk(q), f'csgm{q}'], writes=[f'cYp{pc}'])
                    proj_fm(4 + q, W, 'cW', 512 + pc * 128, t0, n)
                    act(SG[:, pc, t0:t0 + n], PS[4 + q][:, 0:n], AF.Silu, [pk(4 + q)], ['cSG'])
            for bi, (t0, n) in enumerate(BLK):
                if last and bi == 0:
                    continue
                p0 = 15 + t0 if t0 < 256 else 301 + (t0 - 256)
                for pc in range(2):
                    b = pc
                    for k in range(31):
                        S.op('pe', lambda e, k=k, pc=pc, b=b: e.matmul(PS[b][:, 0:n], lhsT=Dg[:, pc, k, :], rhs=Yp[pc][:, p0 + k - 15:p0 + k - 15 + n], start=(k == 0), stop=(k == 30)),
                             reads=['cDg', f'cYp{pc}'], writes=[pk(b)], sig=(k == 30))
                    S.op('dve', lambda e, pc=pc, b=b: e.tensor_scalar(out=Cf[:, pc, 0:n], in0=PS[b][:, 0:n], scalar1=cbt[:, pc:pc + 1], scalar2=None, op0=ALU.add),
                         reads=[pk(b), 'ccb'], writes=['cCf'])
                    S.op('pool', lambda e, pc=pc: e.tensor_copy(out=Cb[:, pc, 0:n], in_=Cf[:, pc, 0:n]), reads=['cCf'], writes=['cCb'])
                    S.op('pool', lambda e, pc=pc: e.tensor_tensor(out=Cq[:, pc, 0:n], in0=Cf[:, pc, 0:n], in1=Cf[:, pc, 0:n], op=ALU.mult), reads=['cCf'], writes=['cCq'])
                for pc in range(2):
                    S.op('pe', lambda e, pc=pc: e.matmul(PS[2][:, 0:n], lhsT=ones256[:], rhs=Cb[:, pc, 0:n], start=(pc == 0), stop=(pc == 1)),
                         reads=['ones256', 'cCb'], writes=[pk(2)], sig=(pc == 1))
                for pc in range(2):
                    S.op('pe', lambda e, pc=pc: e.matmul(PS[3][:, 0:n], lhsT=ones256[:], rhs=Cq[:, pc, 0:n], start=(pc == 0), stop=(pc == 1)),
                         reads=['ones256', 'cCq'], writes=[pk(3)], sig=(pc == 1))
                S.op('dve', lambda e: e.tensor_copy(out=mean[:, 0:n], in_=PS[2][:, 0:n]), reads=[pk(2)], writes=['cmean'])
                S.op('dve', lambda e: e.tensor_tensor(out=var[:, 0:n], in0=mean[:, 0:n], in1=mean[:, 0:n], op=ALU.mult), reads=['cmean'], writes=['cvar'])
                S.op('dve', lambda e: e.tensor_tensor(out=var[:, 0:n], in0=PS[3][:, 0:n], in1=var[:, 0:n], op=ALU.subtract), reads=[pk(3), 'cvar'], writes=['cvar'])
                act(var[:, 0:n], var[:, 0:n], AF.Ln, ['cvar'], ['cvar'], bias=epsl[:], scale=1.0)
                act(var[:, 0:n], var[:, 0:n], AF.Exp, ['cvar'], ['cvar'], scale=-0.5)
                for pc in range(2):
                    d_, y_ = dd[pc], yo[pc]
                    S.op('dve', lambda e, pc=pc, d_=d_: e.tensor_tensor(out=d_[:, 0:n], in0=Cf[:, pc, 0:n], in1=mean[:, 0:n], op=ALU.subtract), reads=['cCf', 'cmean'], writes=[f'cdd{pc}'])
                    S.op('dve', lambda e, pc=pc, d_=d_: e.tensor_tensor(out=d_[:, 0:n], in0=d_[:, 0:n], in1=var[:, 0:n], op=ALU.mult), reads=[f'cdd{pc}', 'cvar'], writes=[f'cdd{pc}'])
                    act(d_[:, 0:n], d_[:, 0:n], AF.Silu, [f'cdd{pc}', 'ccb'], [f'cdd{pc}'], bias=cbt[:, 4 + pc:5 + pc], scale=cbt[:, 2 + pc:3 + pc])
                    S.op('dve', lambda e, pc=pc, d_=d_, y_=y_: e.tensor_tensor(out=y_[:, 0:n], in0=d_[:, 0:n], in1=SG[:, pc, t0:t0 + n], op=ALU.mult),
                         reads=[f'cdd{pc}', 'cSG'], writes=[f'cyo{pc}'])
                    S.dma('pool', ymix[512 + pc * 128:512 + (pc + 1) * 128, t0:t0 + n], y_[:, 0:n], reads=[f'cyo{pc}'], writes=['ymix'])
        S.barrier()

    def stageD(l, last):
        lam_init = 0.8 - 0.6 * math.exp(-0.3 * l)
        scale = 32 ** -0.5
        with ExitStack() as cx:
            KT = sb("dKT", [128, 2, T], BF16, cx)
            QT = sb("dQT", [128, 2, T], BF16, cx)
            V = sb("dV", [128, NT, 384], BF16, cx)
            SG = sb("dSG", [128, 2, T], BF16, cx)
            lmt = sb("dlm", [128, 128], F32, cx)
            lms = sb("dls", [128, 4], F32, cx)
            gn = sb("dgn", [128, 1], F32, cx)
            S.dma('sp', lmt[:], df_lam[l:l + 1, :].partition_broadcast(128), writes=['dlm'])
            S.dma('sp', gn[:], df_g[l], writes=['dgn'])
            lm4 = lmt[:].rearrange("p (a d) -> p a d", d=32)
            S.op('dve', lambda e: e.tensor_tensor(out=lm4[:, 0, :], in0=lm4[:, 0, :], in1=lm4[:, 1, :], op=ALU.mult), reads=['dlm'], writes=['dlm'])
            S.op('dve', lambda e: e.tensor_tensor(out=lm4[:, 2, :], in0=lm4[:, 2, :], in1=lm4[:, 3, :], op=ALU.mult), reads=['dlm'], writes=['dlm'])
            S.op('dve', lambda e: e.reduce_sum(out=lms[:, 0:1], in_=lm4[:, 0, :], axis=AX.X), reads=['dlm'], writes=['dls'])
            S.op('dve', lambda e: e.reduce_sum(out=lms[:, 1:2], in_=lm4[:, 2, :], axis=AX.X), reads=['dlm'], writes=['dls'])
            act(lms[:, 0:2], lms[:, 0:2], AF.Exp, ['dls'], ['dls'])
            S.op('dve', lambda e: e.tensor_tensor(out=lms[:, 2:3], in0=lms[:, 1:2], in1=lms[:, 0:1], op=ALU.subtract), reads=['dls'], writes=['dls'])
            S.op('dve', lambda e: e.tensor_scalar(out=lms[:, 2:3], in0=lms[:, 2:3], scalar1=-lam_init, scalar2=None, op0=ALU.add), reads=['dls'], writes=['dls'])
            S.op('dve', lambda e: e.tensor_scalar(out=gn[:], in0=gn[:], scalar1=(1.0 - lam_init), scalar2=None, op0=ALU.mult), reads=['dgn'], writes=['dgn'])
            with ExitStack() as c1:
                W = sb("dW", [128, 8, 1024], BF16, c1)
                Wsw = sb("dWsw", [128, 8, 512], BF16, c1)
                stgs = [(sb(f"dstg{i}", [128, 8, 128], F32, c1), f'dstg{i}') for i in range(2)]
                cs = [sb(f"dcs{i}", [128, 512], F32, c1) for i in range(2)]
                sn = [sb(f"dsn{i}", [128, 512], F32, c1) for i in range(2)]
                t1 = [sb(f"dt1{i}", [128, 512], F32, c1) for i in range(2)]
                t2 = [sb(f"dt2{i}", [128, 512], F32, c1) for i in range(2)]
                S.op('pool', lambda e: e.memset(V[:], 1.0), writes=['dV'])
                load_w(W, l, w_in, 2560, 1024, stgs, 'dW', slab=128)
                for k in range(8):
                    wv_ = W[:, k, 0:512].rearrange("p (g two j) -> p g two j", two=2, j=16)
                    sv_ = Wsw[:, k, :].rearrange("p (g two j) -> p g two j", two=2, j=16)
                    for h in range(2):
                        S.op('pool', lambda e, wv_=wv_, sv_=sv_, h=h: e.tensor_copy(out=sv_[:, :, 1 - h, :], in_=wv_[:, :, h, :]), reads=['dW'], writes=['dWsw'])
                ci = 0
                for bi, (t0, n) in enumerate(BLK):
                    lat = t0 >= 256
                    cp = bi % 2
                    if lat:
                        S.dma('sp', cs[cp][:, 0:n], c_cos[:, t0 - 256:t0 - 256 + n], writes=[f'dcs{cp}'])
                        S.dma('sp', sn[cp][:, 0:n], c_sin[:, t0 - 256:t0 - 256 + n], writes=[f'dsn{cp}'])
                    for which, (dst, dk) in enumerate(((QT, 'dQT'), (KT, 'dKT'))):
                        for ck in range(2):
                            q = ci % 2
                            ci += 1
                            proj_fm(q, W, 'dW', which * 256 + ck * 128, t0, n)
                            if not lat:
                                S.op('dve', lambda e, dst=dst, ck=ck, q=q: e.tensor_copy(out=dst[:, ck, t0:t0 + n], in_=PS[q][:, 0:n]), reads=[pk(q)], writes=[dk])
                            else:
                                proj_fm(2 + q, Wsw, 'dWsw', which * 256 + ck * 128, t0, n)
                                S.op('dve', lambda e, q=q: e.tensor_tensor(out=t1[q][:, 0:n], in0=PS[q][:, 0:n], in1=cs[cp][:, 0:n], op=ALU.mult),
                                     reads=[pk(q), f'dcs{cp}'], writes=[f'dt1{q}'])
                                S.op('dve', lambda e, q=q: e.tensor_tensor(out=t2[q][:, 0:n], in0=PS[2 + q][:, 0:n], in1=sn[cp][:, 0:n], op=ALU.mult),
                                     reads=[pk(2 + q), f'dsn{cp}'], writes=[f'dt2{q}'])
                                S.op('pool', lambda e, dst=dst, ck=ck, q=q: e.tensor_tensor(out=dst[:, ck, t0:t0 + n], in0=t1[q][:, 0:n], in1=t2[q][:, 0:n], op=ALU.add),
                                     reads=[f'dt1{q}', f'dt2{q}'], writes=[dk])
                    for ck in range(2):
                        b = 4 + ck
                        proj_fm(b, W, 'dW', 768 + ck * 128, t0, n)
                        act(SG[:, ck, t0:t0 + n], PS[b][:, 0:n], AF.Silu, [pk(b)], ['dSG'])
                    for tt in range(n // 128):
                        i = t0 // 128 + tt
                        b = 6 + i % 2
                        proj_tm(PS[b][:, 0:256], b, W, 'dW', 512, 256, i)
                        vv = V[:, i, :].rearrange("p (g c) -> p g c", c=192)
                        pv4 = PS[b][:, 0:256].rearrange("p (g r c) -> p g r c", r=2, c=64)
                        for r_ in range(2):
                            S.op('dve', lambda e, vv=vv, pv4=pv4, r_=r_: e.tensor_copy(out=vv[:, :, 128 * r_:128 * r_ + 64], in_=pv4[:, :, r_, :]), reads=[pk(b)], writes=['dV'])
            S.barrier()
            with ExitStack() as c2:
                Pb = [sb(f"dP{i}", [128, 512], BF16, c2) for i in range(6)]
                Qz = [sb(f"dQz{i}", [128, 4, 512], BF16, c2) for i in range(2)]
                RL = [sb(f"dRL{i}", [128, 512], F32, c2) for i in range(2)]
                Nn = [sb(f"dN{i}", [128, 512], F32, c2) for i in range(2)]
                Oc = [sb(f"dOc{i}", [128, 512], F32, c2) for i in range(4)]
                Oh = sb("dOh", [128, 512], F32, c2)
                Osq = sb("dOsq", [128, 512], BF16, c2)
                rs = sb("drs", [128, 512], F32, c2)
                Yo = [sb(f"dYo{i}", [128, 512], BF16, c2) for i in range(2)]
                pending = []
                state = dict(n=0)
                for i_ in range(2):
                    S.op('pool', lambda e, i_=i_: e.memset(Qz[i_][:], 0.0), writes=[f'dQz{i_}'])

                def prep_q(pi, q0, nq, hp):
                    pp = pi % 2
                    for s_ in range(4):
                        S.op('pool', lambda e, s_=s_: e.tensor_copy(out=Qz[pp][32 * s_:32 * s_ + 32, s_, 0:nq], in_=QT[32 * s_:32 * s_ + 32, hp, q0:q0 + nq]),
                             reads=['dQT'], writes=[f'dQz{pp}'])

                def finalize(q0, nq, hp, yp):
                    steps = []
                    for s_ in range(4):
                        steps.append(lambda s_=s_: S.op('dve', lambda e: e.tensor_copy(out=Oc[s_][:, 0:nq], in_=PS[3 + s_][:, 0:nq]), reads=[pk(3 + s_)], writes=[f'dOc{s_}']))
                    for s_ in range(4):
                        hh, w = s_ // 2, s_ % 2
                        lo, ll = 64 * hh, 64 * (1 - hh)
                        steps.append(lambda s_=s_, w=w, lo=lo, ll=ll: S.op('dve', lambda e: e.reciprocal(out=RL[w][lo:lo + 64, 0:nq], in_=Oc[s_][ll:ll + 64, 0:nq]),
                                                                     reads=[f'dOc{s_}'], writes=[f'dRL{w}']))
                        steps.append(lambda s_=s_, w=w, lo=lo: S.op('dve', lambda e: e.tensor_tensor(out=Nn[w][lo:lo + 64, 0:nq], in0=Oc[s_][lo:lo + 64, 0:nq], in1=RL[w][lo:lo + 64, 0:nq], op=ALU.mult),
                                                              reads=[f'dOc{s_}', f'dRL{w}'], writes=[f'dN{w}']))
                    steps.append(lambda: S.op('dve', lambda e: e.scalar_tensor_tensor(out=Oh[:, 0:nq], in0=Nn[1][:, 0:nq], scalar=lms[:, 2:3], in1=Nn[0][:, 0:nq], op0=ALU.mult, op1=ALU.add),
                                              reads=['dN0', 'dN1', 'dls'], writes=['dOh']))
                    steps.append(lambda: S.op('pool', lambda e: e.tensor_tensor(out=Osq[:, 0:nq], in0=Oh[:, 0:nq], in1=Oh[:, 0:nq], op=ALU.mult), reads=['dOh'], writes=['dOsq']))
                    steps.append(lambda: S.op('pe', lambda e: e.matmul(PS[7][:, 0:nq], lhsT=bd64[:], rhs=Osq[:, 0:nq], start=True, stop=True), reads=['bd64', 'dOsq'], writes=[pk(7)]))
                    steps.append(lambda: rstd_from_ss(rs[:, 0:nq], PS[7][:, 0:nq], 1.0 / 64, epsr[:], 'drs', pk(7)))
                    steps.append(lambda: S.op('dve', lambda e: e.tensor_tensor(out=Oh[:, 0:nq], in0=Oh[:, 0:nq], in1=rs[:, 0:nq], op=ALU.mult), reads=['dOh', 'drs'], writes=['dOh']))
                    steps.append(lambda: S.op('dve', lambda e: e.scalar_tensor_tensor(out=Yo[yp][:, 0:nq], in0=Oh[:, 0:nq], scalar=gn[:, 0:1], in1=SG[:, hp, q0:q0 + nq], op0=ALU.mult, op1=ALU.mult),
                                              reads=['dOh', 'dgn', 'dSG'], writes=[f'dYo{yp}']))
                    steps.append(lambda: S.dma('pool', ymix[768 + hp * 128:768 + (hp + 1) * 128, q0:q0 + nq], Yo[yp][:, 0:nq], reads=[f'dYo{yp}'], writes=['ymix']))
                    return steps

                passes = []
                if not last:
                    for hp in range(2):
                        passes.append((0, 256, [0, 1], hp))
                for qb in range(8):
                    for hp in range(2):
                        passes.append((256 + qb * 512, 512, list(range(NT)), hp))

                def attn_pass(pi):
                    q0, nq, kts, hp = passes[pi]
                    pp = pi % 2
                    seq = [(kt, s_) for kt in kts for s_ in range(4)]
                    nseq = len(seq)
                    base = state['n']

                    def qk(m):
                        kt, s_ = seq[m]
                        b_ = (base + m) % 3
                        S.op('pe', lambda e: e.matmul(PS[b_][:, 0:nq], lhsT=KT[:, hp, kt * 128:(kt + 1) * 128], rhs=Qz[pp][:, s_, 0:nq], start=True, stop=True),
                             reads=['dKT', f'dQz{pp}'], writes=[pk(b_)])

                    def ex(m):
                        g = base + m
                        act(Pb[g % 6][:, 0:nq], PS[g % 3][:, 0:nq], AF.Exp, [pk(g % 3)], [f'dP{g % 6}'], scale=scale)

                    def pv(m):
                        kt, s_ = seq[m]
                        pb = (base + m) % 6
                        hh = s_ // 2
                        c0 = hp * 192 + hh * 64
                        S.op('pe', lambda e: e.matmul(PS[3 + s_][:, 0:nq], lhsT=V[:, kt, c0:c0 + 128], rhs=Pb[pb][:, 0:nq], start=(kt == kts[0]), stop=(kt == kts[-1])),
                             reads=['dV', f'dP{pb}'], writes=[pk(3 + s_)])

                    for m in range(min(3, nseq)):
                        qk(m)
                    if pi + 1 < len(passes):
                        prep_q(pi + 1, passes[pi + 1][0], passes[pi + 1][1], passes[pi + 1][3])
                    for m in range(nseq):
                        ex(m)
                        pv(m)
                        if m + 3 < nseq:
                            qk(m + 3)
                        if pending and m % 2 == 1:
                            pending.pop(0)()
                    state['n'] = base + nseq
                    while pending:
                        pending.pop(0)()
                    fs = finalize(q0, nq, hp, pi % 2)
                    for f_ in fs[:4]:
                        f_()
                    pending.extend(fs[4:])

                prep_q(0, passes[0][0], passes[0][1], passes[0][3])
                for pi in range(len(passes)):
                    attn_pass(pi)
                while pending:
                    pending.pop(0)()
        S.barrier()

    def stageB(l):
        with ExitStack() as cx:
            W = sb("bW", [128, 8, 640], BF16, cx)
            stgs = [(sb(f"bstg{i}", [128, 8, 128], F32, cx), f'bstg{i}') for i in range(2)]
            Vt = sb("bV", [128, NT, 128], BF16, cx)
            OT = sb("bOT", [128, T], F32, cx)
            QTl = sb("bQT", [128, T], BF16, cx)
            KTl = sb("bKT", [128, T], BF16, cx)
            Kt = sb("bKt", [128, NT, 128], BF16, cx)
            Sb = sb("bSb", [128, NCH, 64], BF16, cx)
            KH = Sb[:].rearrange("p c e -> p (c e)")
            M0 = sb("bM0", [128, T], BF16, cx)
            Gc = sb("bGc", [128, T], F32, cx)
            Bt = Gc[:].rearrange("p (c e) -> p c e", e=64)
            T1 = [sb(f"bT1{i}", [128, 512], F32, cx) for i in range(2)]
            T2 = [sb(f"bT2{i}", [128, 512], F32, cx) for i in range(2)]
            sm = sb("bsm", [128, 6, NCH], F32, cx)
            SCm = [sb(f"bSC{i}", [128, 256], BF16, cx) for i in range(4)]
            osq = sb("bosq", [128, 512], BF16, cx)
            ors = sb("bors", [128, 512], F32, cx)
            oy = [sb(f"boy{i}", [128, 512], BF16, cx) for i in range(2)]
            hgn = sb("bhgn", [128, 1], F32, cx)
            S.dma('sp', hgn[:], hg_g[l], writes=['bhgn'])
            groups = [[0, 1]] + [list(range(2 + 8 * g, 10 + 8 * g)) for g in range(4)]
            for pc in range(2):
                for j in range(5):
                    load_w(W, l, w_in, 512 + j * 256 + pc * 128, 128, stgs, 'bW', off=j * 128, slab=128)
                S.op('pool', lambda e: e.memset(OT[:], 0.0), writes=['bOT'])
                for i in range(NT):
                    b = 6 + i % 2
                    proj_tm(PS[b][:, 0:128], b, W, 'bW', 128, 128, i)
                    S.op('dve', lambda e, b=b, i=i: e.tensor_copy(out=Vt[:, i, :], in_=PS[b][:, 0:128]), reads=[pk(b)], writes=['bV'])
                _ck(1)
                for dr in range(2):
                    li = (pc * 2 + dr)
                    lb_ap = LBt[:, 0, li, l:l + 1]
                    oml_ap = LBt[:, 1, li, l:l + 1]
                    S.op('pool', lambda e: e.memset(M0[:], 1.0), writes=['bM0'])
                    m3 = M0[:].rearrange("p (c j) -> p c j", j=64)
                    zc = 0 if dr == 0 else 63
                    S.op('pool', lambda e, zc=zc: e.memset(m3[:, :, zc:zc + 1], 0.0), writes=['bM0'])
                    for bi, (t0, n) in enumerate(BLK):
                        q = bi % 2
                        proj_fm(q, W, 'bW', (2 + dr) * 128, t0, n)
                        act(T1[q][:, 0:n], PS[q][:, 0:n], AF.Exp, [pk(q)], [f'bT1{q}'], scale=-1.0)
                        act(T2[q][:, 0:n], T1[q][:, 0:n], AF.Ln, [f'bT1{q}', 'LBt'], [f'bT2{q}'], bias=onec[:], scale=lb_ap)
                        act(T1[q][:, 0:n], T1[q][:, 0:n], AF.Ln, [f'bT1{q}'], [f'bT1{q}'], bias=onec[:], scale=1.0)
                        S.op('dve', lambda e, q=q, t0=t0, n=n: e.tensor_tensor(out=Gc[:, t0:t0 + n], in0=T2[q][:, 0:n], in1=T1[q][:, 0:n], op=ALU.subtract),
                             reads=[f'bT1{q}', f'bT2{q}'], writes=['bGc'])
                    if dr == 0:
                        S.op('dve', lambda e: e.tensor_tensor_scan(out=Gc[:], data0=M0[:], data1=Gc[:], initial=0.0, op0=ALU.mult, op1=ALU.add),
                             reads=['bGc', 'bM0'], writes=['bGc'])
                    else:
                        S.op('dve', lambda e: e.tensor_tensor_scan(out=Gc[:][:, ::-1], data0=M0[:][:, ::-1], data1=Gc[:][:, ::-1], initial=0.0, op0=ALU.mult, op1=ALU.add),
                             reads=['bGc', 'bM0'], writes=['bGc'])
                    _ck(2)
                    g3 = Gc[:].rearrange("p (c j) -> p c j", j=64)
                    mid = 31 if dr == 0 else 32
                    end = 63 if dr == 0 else 0
                    S.op('dve', lambda e: e.tensor_copy(out=sm[:, 0, :], in_=g3[:, :, mid]), reads=['bGc'], writes=['bsm'])
                    S.op('dve', lambda e: e.tensor_copy(out=sm[:, 1, :], in_=g3[:, :, end]), reads=['bGc'], writes=['bsm'])
                    S.op('dve', lambda e: e.tensor_tensor(out=sm[:, 5, :], in0=sm[:, 1, :], in1=sm[:, 0, :], op=ALU.subtract), reads=['bsm'], writes=['bsm'])
                    act(sm[:, 2, :], sm[:, 1, :], AF.Exp, ['bsm'], ['bsm'])
                    act(sm[:, 3, :], sm[:, 5, :], AF.Exp, ['bsm'], ['bsm'])
                    act(sm[:, 4, :], sm[:, 0, :], AF.Exp, ['bsm'], ['bsm'])
                    S.op('dve', lambda e: e.tensor_tensor(out=g3, in0=g3, in1=sm[:, 0, :].unsqueeze(2).broadcast_to([128, NCH, 64]), op=ALU.subtract),
                         reads=['bGc', 'bsm'], writes=['bGc'])
                    _ck(3)
                    for bi, (t0, n) in enumerate(BLK):
                        q = bi % 2
                        c0, nc_ = t0 // 64, n // 64
                        proj_fm(q, W, 'bW', 0, t0, n)
                        act(T1[q][:, 0:n], PS[q][:, 0:n], AF.Exp, [pk(q)], [f'bT1{q}'], scale=-1.0)
                        act(T1[q][:, 0:n], T1[q][:, 0:n], AF.Ln, [f'bT1{q}'], [f'bT1{q}'], bias=onec[:], scale=1.0)
                        S.op('dve', lambda e, q=q, t0=t0, n=n: e.tensor_tensor(out=T1[q][:, 0:n], in0=Gc[:, t0:t0 + n], in1=T1[q][:, 0:n], op=ALU.subtract),
                             reads=['bGc', f'bT1{q}'], writes=[f'bT1{q}'])
                        act(T1[q][:, 0:n], T1[q][:, 0:n], AF.Exp, [f'bT1{q}'], [f'bT1{q}'])
                        S.op('dve', lambda e, q=q, t0=t0, n=n: e.tensor_tensor(out=QTl[:, t0:t0 + n], in0=PS[q][:, 0:n], in1=T1[q][:, 0:n], op=ALU.mult),
                             reads=[pk(q), f'bT1{q}'], writes=['bQT'])
                        proj_fm(2 + q, W, 'bW', (2 + dr) * 128, t0, n)
                        act(T2[q][:, 0:n], PS[2 + q][:, 0:n], AF.Exp, [pk(2 + q)], [f'bT2{q}'])
                        act(T2[q][:, 0:n], T2[q][:, 0:n], AF.Ln, [f'bT2{q}'], [f'bT2{q}'], bias=onec[:], scale=1.0)
                        S.op('dve', lambda e, q=q, t0=t0, n=n: e.tensor_tensor(out=T2[q][:, 0:n], in0=Gc[:, t0:t0 + n], in1=T2[q][:, 0:n], op=ALU.add),
                             reads=['bGc', f'bT2{q}'], writes=[f'bT2{q}'])
                        act(T2[q][:, 0:n], T2[q][:, 0:n], AF.Exp, [f'bT2{q}'], [f'bT2{q}'], scale=-1.0)
                        S.op('dve', lambda e, q=q, t0=t0, n=n: e.tensor_scalar(out=KTl[:, t0:t0 + n], in0=T2[q][:, 0:n], scalar1=oml_ap, scalar2=None, op0=ALU.mult),
                             reads=[f'bT2{q}', 'LBt'], writes=['bKT'])
                        kv3 = KTl[:, t0:t0 + n].rearrange("p (c j) -> p c j", j=64)
                        kh3 = KH[:, t0:t0 + n].rearrange("p (c j) -> p c j", j=64)
                        S.op('dve', lambda e, kv3=kv3, kh3=kh3, c0=c0, nc_=nc_: e.tensor_tensor(out=kh3, in0=kv3, in1=sm[:, 3, c0:c0 + nc_].unsqueeze(2).broadcast_to([128, nc_, 64]), op=ALU.mult),
                             reads=['bKT', 'bsm'], writes=['bSb'])
                    _ck(4)
                    for i in range(NT):
                        b = 4 + i % 2
                        psb = PS[b][:].bitcast(BF16)
                        S.op('pe', lambda e, i=i, psb=psb: e.transpose(out=psb[:, 0:128], in_=KH[:, i * 128:(i + 1) * 128], identity=identb[:]),
                             reads=['bSb', 'identb'], writes=[pk(b)])
                        S.op('dve', lambda e, i=i, psb=psb: e.tensor_copy(out=Kt[:, i, :], in_=psb[:, 0:128]), reads=[pk(b)], writes=['bKt'])
                    _ck(5)
                    for g, tiles in enumerate(groups):
                        for ti, i in enumerate(tiles):
                            for cp in range(2):
                                bb = 2 * (g % 2) + cp
                                for h2 in range(2):
                                    S.op('pe', lambda e, ti=ti, i=i, cp=cp, h2=h2, bb=bb: e.matmul(PS[bb][64 * h2:64 * h2 + 64, ti * 64:(ti + 1) * 64],
                                                                                               lhsT=Kt[64 * cp:64 * cp + 64, i, 64 * h2:64 * h2 + 64],
                                                                                               rhs=Vt[64 * cp:64 * cp + 64, i, 64 * h2:64 * h2 + 64],
                                                                                               start=True, stop=True, tile_position=(64 * cp, 64 * h2)),
                                         reads=['bKt', 'bV'], writes=[pk(bb)], sig=(ti == len(tiles) - 1 and h2 == 1))
                        nt_ = len(tiles)
                        for cp in range(2):
                            bb = 2 * (g % 2) + cp
                            cfirst = 2 * tiles[0] + cp
                            if dr == 0:
                                dst = Bt[:, cfirst:cfirst + 2 * (nt_ - 1) + 1:2, :]
                            else:
                                pfirst = (3 - cfirst) if g == 0 else (71 - cfirst)
                                stop = pfirst - 2 * (nt_ - 1) - 1
                                dst = Bt[:, pfirst:(stop if stop >= 0 else None):-2, :]
                            S.op('dve', lambda e, bb=bb, dst=dst, nt_=nt_: e.tensor_copy(out=dst, in_=PS[bb][:, 0:nt_ * 64].rearrange("p (t e) -> p t e", e=64)),
                                 reads=[pk(bb)], writes=['bGc'])
                    _ck(6)
                    if dr == 0:
                        lam_ap = sm[:, 2, :]
                    else:
                        S.op('dve', lambda e: e.tensor_copy(out=sm[:, 5, 0:4], in_=sm[:, 2, 0:4][:, ::-1]), reads=['bsm'], writes=['bsm'])
                        S.op('dve', lambda e: e.tensor_copy(out=sm[:, 5, 4:NCH], in_=sm[:, 2, 4:NCH][:, ::-1]), reads=['bsm'], writes=['bsm'])
                        lam_ap = sm[:, 5, :]
                    for e_ in range(64):
                        S.op('dve', lambda e, e_=e_: e.tensor_tensor_scan(out=Bt[:, :, e_], data0=lam_ap, data1=Bt[:, :, e_], initial=0.0, op0=ALU.mult, op1=ALU.add),
                             reads=['bsm', 'bGc'], writes=['bGc'])
                    _ck(7)
                    rho = sm[:, 4, :]
                    if dr == 0:
                        S.op('dve', lambda e: e.memset(Sb[:, 0:1, :], 0.0), writes=['bSb'])
                        S.op('dve', lambda e: e.tensor_tensor(out=Sb[:, 1:NCH, :], in0=Bt[:, 0:NCH - 1, :], in1=rho[:, 1:NCH].unsqueeze(2).broadcast_to([128, NCH - 1, 64]), op=ALU.mult),
                             reads=['bGc', 'bsm'], writes=['bSb'])
                    else:
                        S.op('dve', lambda e: e.memset(Sb[:, 3:4, :], 0.0), writes=['bSb'])
                        S.op('dve', lambda e: e.tensor_tensor(out=Sb[:, 0:3, :], in0=Bt[:, 2::-1, :], in1=rho[:, 0:3].unsqueeze(2).broadcast_to([128, 3, 64]), op=ALU.mult),
                             reads=['bGc', 'bsm'], writes=['bSb'])
                        S.op('dve', lambda e: e.tensor_tensor(out=Sb[:, 4:NCH, :], in0=Bt[:, 66:2:-1, :], in1=rho[:, 4:NCH].unsqueeze(2).broadcast_to([128, NCH - 4, 64]), op=ALU.mult),
                             reads=['bGc', 'bsm'], writes=['bSb'])
                    _ck(8)
                    mk = maskt[:, 64 * dr:64 * dr + 64]
                    for tgi, tg in enumerate(range(0, NT, 4)):
                        tiles = list(range(tg, min(tg + 4, NT)))
                        nt_ = len(tiles)
                        q = tgi % 2
                        for ti, i in enumerate(tiles):
                            for cp in range(2):
                                c = 2 * i + cp
                                for h2 in range(2):
                                    bb = 2 * q + h2
                                    for jb in range(2):
                                        full = (jb == 0) if dr == 0 else (jb == 1)
                                        i0, ni = (0, 64) if full else ((32, 32) if dr == 0 else (0, 32))
                                        S.op('pe', lambda e, c=c, cp=cp, h2=h2, bb=bb, ti=ti, jb=jb, i0=i0, ni=ni: e.matmul(
                                            PS[bb][64 * cp + 32 * jb:64 * cp + 32 * jb + 32, ti * 64 + i0:ti * 64 + i0 + ni],
                                            lhsT=KTl[64 * h2:64 * h2 + 64, c * 64 + 32 * jb:c * 64 + 32 * jb + 32],
                                            rhs=QTl[64 * h2:64 * h2 + 64, c * 64 + i0:c * 64 + i0 + ni],
                                            start=True, stop=True, tile_position=(64 * h2, 64 * cp + 32 * jb)),
                                             reads=['bKT', 'bQT'], writes=[pk(bb)], sig=(ti == nt_ - 1 and cp == 1 and jb == 1))
                        for h2 in range(2):
                            bb = 2 * q + h2
                            scv = SCm[2 * q + h2][:, 0:nt_ * 64].rearrange("p (t i) -> p t i", i=64)
                            S.op('dve', lambda e, bb=bb, scv=scv, nt_=nt_: e.tensor_tensor(out=scv, in0=PS[bb][:, 0:nt_ * 64].rearrange("p (t i) -> p t i", i=64),
                                                                                       in1=mk.unsqueeze(1).broadcast_to([128, nt_, 64]), op=ALU.mult),
                                 reads=[pk(bb), 'maskt'], writes=[f'bSC{2 * q + h2}'])
                        for ti, i in enumerate(tiles):
                            for cp in range(2):
                                for h2 in range(2):
                                    S.op('pe', lambda e, cp=cp, h2=h2, ti=ti, i=i, q=q: e.matmul(PS[4 + cp][64 * h2:64 * h2 + 64, ti * 64:(ti + 1) * 64],
                                                                                             lhsT=Vt[64 * cp:64 * cp + 64, i, 64 * h2:64 * h2 + 64],
                                                                                             rhs=SCm[2 * q + h2][64 * cp:64 * cp + 64, ti * 64:(ti + 1) * 64],
                                                                                             start=True, stop=True, tile_position=(64 * cp, 64 * h2)),
                                         reads=['bV', f'bSC{2 * q + h2}'], writes=[pk(4 + cp)], sig=(ti == nt_ - 1 and h2 == 1))
                        for ti, i in enumerate(tiles):
                            for cp in range(2):
                                c = 2 * i + cp
                                for h2 in range(2):
                                    S.op('pe', lambda e, c=c, cp=cp, h2=h2, ti=ti: e.matmul(PS[6][64 * h2:64 * h2 + 64, (ti * 2 + cp) * 64:(ti * 2 + cp + 1) * 64],
                                                                                       lhsT=Sb[64 * h2:64 * h2 + 64, c, :],
                                                                                       rhs=QTl[64 * h2:64 * h2 + 64, c * 64:(c + 1) * 64],
                                                                                       start=True, stop=True, tile_position=(64 * h2, 64 * h2)),
                                         reads=['bSb', 'bQT'], writes=[pk(6)], sig=(ti == nt_ - 1 and cp == 1 and h2 == 1))
                        otv = OT[:, tg * 128:(tg + nt_) * 128].rearrange("p (t c i) -> p t c i", c=2, i=64)
                        for cp in range(2):
                            S.op('dve', lambda e, cp=cp, otv=otv, nt_=nt_: e.tensor_tensor(out=otv[:, :, cp, :], in0=PS[4 + cp][:, 0:nt_ * 64].rearrange("p (t i) -> p t i", i=64),
                                                                                       in1=otv[:, :, cp, :], op=ALU.add),
                                 reads=[pk(4 + cp), 'bOT'], writes=['bOT'])
                        S.op('dve', lambda e, tg=tg, nt_=nt_: e.tensor_tensor(out=OT[:, tg * 128:(tg + nt_) * 128], in0=PS[6][:, 0:nt_ * 128], in1=OT[:, tg * 128:(tg + nt_) * 128], op=ALU.add),
                             reads=[pk(6), 'bOT'], writes=['bOT'])
                _ck(9)
                for bi, (t0, n) in enumerate(BLK):
                    q = bi % 2
                    S.op('pool', lambda e, t0=t0, n=n: e.tensor_tensor(out=osq[:, 0:n], in0=OT[:, t0:t0 + n], in1=OT[:, t0:t0 + n], op=ALU.mult), reads=['bOT'], writes=['bosq'])
                    S.op('pe', lambda e, n=n: e.matmul(PS[4][:, 0:n], lhsT=bd64[:], rhs=osq[:, 0:n], start=True, stop=True), reads=['bd64', 'bosq'], writes=[pk(4)])
                    rstd_from_ss(ors[:, 0:n], PS[4][:, 0:n], 1.0 / 64, epsr[:], 'bors', pk(4))
                    proj_fm(5, W, 'bW', 4 * 128, t0, n)
                    act(T1[q][:, 0:n], PS[5][:, 0:n], AF.Exp, [pk(5)], [f'bT1{q}'], scale=-1.0)
                    act(T1[q][:, 0:n], T1[q][:, 0:n], AF.Ln, [f'bT1{q}'], [f'bT1{q}'], bias=onec[:], scale=1.0)
                    act(T1[q][:, 0:n], T1[q][:, 0:n], AF.Exp, [f'bT1{q}'], [f'bT1{q}'], scale=-1.0)
                    S.op('dve', lambda e, q=q, n=n: e.tensor_tensor(out=T1[q][:, 0:n], in0=PS[5][:, 0:n], in1=T1[q][:, 0:n], op=ALU.mult), reads=[pk(5), f'bT1{q}'], writes=[f'bT1{q}'])
                    S.op('dve', lambda e, t0=t0, n=n: e.tensor_tensor(out=ors[:, 0:n], in0=ors[:, 0:n], in1=OT[:, t0:t0 + n], op=ALU.mult), reads=['bors', 'bOT'], writes=['bors'])
                    S.op('dve', lambda e, q=q, n=n: e.scalar_tensor_tensor(out=oy[q][:, 0:n], in0=ors[:, 0:n], scalar=hgn[:, 0:1], in1=T1[q][:, 0:n], op0=ALU.mult, op1=ALU.mult),
                         reads=['bors', 'bhgn', f'bT1{q}'], writes=[f'boy{q}'])
                    S.dma('pool', ymix[256 + pc * 128:256 + (pc + 1) * 128, t0:t0 + n], oy[q][:, 0:n], reads=[f'boy{q}'], writes=['ymix'])
        S.barrier()

    S.barrier()
    prologue()
    for l in range(nlayers):
        last = (l == DEPTH - 1)
        if 'H' in stages and (l == 0 or 'O' not in stages):
            stageH(l)
        if 'A' in stages:
            stageA(l)
        if 'B' in stages:
            try:
                stageB(l)
            except _Stop:
                S.barrier()
        if 'C' in stages:
            stageC(l, last)
        if 'D' in stages:
            stageD(l, last)
        if 'O' in stages:
            stageO(l, last, fuse_next=(l + 1 < nlayers and 'H' in stages))
    S.barrier()
    if 'BSTOP' not in _os.environ:
        es.close()
    return nc


def _consts():
    ident = np.eye(128, dtype=np.float32).astype(ml_dtypes.bfloat16)
    n_freq = 8
    inv_freq = (10000.0 ** (-np.arange(n_freq, dtype=np.float32) / n_freq)).astype(np.float32)
    row = np.repeat(np.arange(64, dtype=np.float32), 64)
    col = np.tile(np.arange(64, dtype=np.float32), 64)
    ang = np.concatenate([row[:, None] * inv_freq, col[:, None] * inv_freq], axis=-1).astype(np.float32)
    cos, sin = np.cos(ang).astype(np.float32), np.sin(ang).astype(np.float32)
    c32 = np.concatenate([cos, cos], axis=1).T
    s32 = np.concatenate([-sin, sin], axis=1).T
    cosT = np.ascontiguousarray(np.tile(c32, (4, 1)))
    sinT = np.ascontiguousarray(np.tile(s32, (4, 1)))
    j = np.arange(64)[:, None]
    i = np.arange(64)[None, :]
    fwd = (j <= i).astype(np.float32)
    bwd = (j >= i).astype(np.float32)
    mask = np.concatenate([np.tile(fwd, (2, 1)), np.tile(bwd, (2, 1))], axis=1)
    bd = np.zeros((128, 128), np.float32)
    bd[:64, :64] = 1
    bd[64:, 64:] = 1
    return dict(c_ident=ident, c_cos=cosT, c_sin=sinT, c_mask=np.ascontiguousarray(mask), c_bd64=bd.astype(ml_dtypes.bfloat16))


def _layout(inp, b):
    f = lambda a: np.ascontiguousarray(np.asarray(a, dtype=np.float32))
    m = {}
    m["x_b"] = f(inp["x"][b])
    m["ctx_b"] = f(inp["ctx"][b])
    cc = np.stack([np.asarray(inp["c"][b]), np.asarray(inp["c_ctx"])], -1)
    m["cfm"] = f(cc.reshape(8, 128, 2).transpose(1, 0, 2).reshape(128, 16))
    m["w_mod"] = f(inp["w_mod"])
    m["b_mod"] = f(inp["b_mod"])
    m["b_mod_fm"] = f(np.asarray(inp["b_mod"]).reshape(DEPTH, 24, 128).transpose(0, 2, 1))
    m["g_pre_fm"] = f(np.asarray(inp["g_pre"]).reshape(DEPTH, 8, 128).transpose(0, 2, 1))
    m["g_post"] = f(inp["g_post"])
    m["w_in"] = f(inp["w_in"])
    m["w_out"] = f(inp["w_out"])
    m["lru_cw"] = f(np.asarray(inp["lru_conv_w"]).reshape(DEPTH, 4, 2, 128).transpose(0, 3, 2, 1).reshape(DEPTH, 128, 8))
    m["lru_cb"] = f(np.asarray(inp["lru_conv_b"]).reshape(DEPTH, 2, 128).transpose(0, 2, 1))
    wr, wi = np.asarray(inp["lru_w_r"]), np.asarray(inp["lru_w_i"])
    bd = np.zeros((DEPTH, 2, 4, 128, 128), np.float32)
    for pc in range(2):
        for dr in range(2):
            for wh, w in enumerate((wr, wi)):
                for h2 in range(2):
                    bd[:, pc, 2 * dr + wh, 64 * h2:64 * h2 + 64, 64 * h2:64 * h2 + 64] = w[:, dr, 2 * pc + h2]
    m["lru_bd"] = bd
    br, bi_ = np.asarray(inp["lru_b_r"]), np.asarray(inp["lru_b_i"])
    lb = np.stack([br, bi_], 1).reshape(DEPTH, 2, 2, 2, 128)
    m["lru_b"] = f(lb.transpose(0, 4, 1, 2, 3).reshape(DEPTH, 128, 8))
    m["lru_lam"] = f(np.asarray(inp["lru_lambda"]).reshape(DEPTH, 2, 2, 128).transpose(0, 3, 1, 2).reshape(DEPTH, 128, 4))
    hl = np.asarray(inp["hgrn_lb"]).reshape(DEPTH, 2, 2, 128)
    m["hg_lb"] = f(hl.transpose(3, 2, 1, 0).reshape(128, 16))
    m["hg_g"] = f(np.tile(np.asarray(inp["hgrn_norm_g"]), (1, 2)).reshape(DEPTH, 128, 1))
    m["cf_w"] = f(np.asarray(inp["conf_conv_w"]).reshape(DEPTH, 31, 2, 128).transpose(0, 3, 2, 1).reshape(DEPTH, 128, 62))
    cb = np.stack([np.asarray(inp["conf_conv_b"]), np.asarray(inp["conf_ln_g"]), np.asarray(inp["conf_ln_b"])], 1).reshape(DEPTH, 3, 2, 128)
    m["cf_b"] = f(cb.transpose(0, 3, 1, 2).reshape(DEPTH, 128, 6))
    m["df_lam"] = f(np.concatenate([np.asarray(inp[k]) for k in ("diff_lam_q1", "diff_lam_k1", "diff_lam_q2", "diff_lam_k2")], axis=1))
    m["df_g"] = f(np.tile(np.asarray(inp["diff_norm_g"]), (1, 2)).reshape(DEPTH, 128, 1))
    return m


def kernel(**inputs):
    n = 8
    nc = bass.Bass("TRN2", target_bir_lowering=False)
    build(nc)
    consts = _consts()
    in_maps = []
    for b in range(n):
        m = _layout(inputs, b)
        m.update(consts)
        in_maps.append(m)
    res = run_bass_kernel_spmd(nc, in_maps, core_ids=list(range(n)))
    return np.stack([np.asarray(r["out"], dtype=np.float32) for r in res.results], axis=0)
```

```python
import math
import numpy as np
import ml_dtypes
from contextlib import ExitStack
import concourse.bass as bass
import concourse.mybir as mybir
from concourse.bass_utils import run_bass_kernel_spmd

F32 = mybir.dt.float32
BF16 = mybir.dt.bfloat16
ALU = mybir.AluOpType
AF = mybir.ActivationFunctionType
AX = mybir.AxisListType

D = 1024
LC = 256
LL = 4096
T = LC + LL
NT = T // 128
DEPTH = 4
RMS_EPS = 1e-6
LN_EPS = 1e-5
BLK = [(0, 256)] + [(256 + 512 * j, 512) for j in range(8)]
NCH = T // 64


import os as _os
class _Stop(Exception):
    pass


def _ck(n):
    if int(_os.environ.get('BSTOP', '99')) == n:
        raise _Stop()


class Sched:
    ENG = ('pe', 'act', 'dve', 'pool', 'sp')

    def __init__(s, nc, es, ndsem=24):
        s.nc = nc
        s.eng = dict(pe=nc.tensor, act=nc.scalar, dve=nc.vector, pool=nc.gpsimd, sp=nc.sync)
        s.sem = {e: es.enter_context(nc.semaphore("sem_" + e)) for e in s.ENG}
        s.cnt = {e: 0 for e in s.ENG}
        s.seen = {e: {} for e in s.ENG}
        s.dsem = [es.enter_context(nc.semaphore(f"dsem{i}")) for i in range(ndsem)]
        s.dcnt = [0] * ndsem
        s.dnext = 0
        s.lastw = {}
        s.readers = {}
        s.unsig = False

    def _need(s, e, tok):
        kind, who, val = tok
        if kind == 'e' and who == e and e == 'pe':
            return
        key = (kind, who)
        if s.seen[e].get(key, 0) >= val:
            return
        if kind == 'e':
            assert val <= s.cnt[who], f"wait on unsignaled op {tok} cnt={s.cnt[who]}"
            s.eng[e].wait_ge(s.sem[who], val)
        else:
            s.eng[e].wait_ge(s.dsem[who], val)
        s.seen[e][key] = val

    def _deps(s, e, reads, writes):
        for r in reads:
            t = s.lastw.get(r)
            if t is not None:
                s._need(e, t)
        for w in writes:
            t = s.lastw.get(w)
            if t is not None and not (t[0] == 'e' and t[1] == e):
                s._need(e, t)
            for (k, who), val in s.readers.get(w, {}).items():
                if not (k == 'e' and who == e):
                    s._need(e, (k, who, val))

    def _reg(s, tok, reads, writes):
        for r in reads:
            d = s.readers.setdefault(r, {})
            key = (tok[0], tok[1])
            if d.get(key, 0) < tok[2]:
                d[key] = tok[2]
        for w in writes:
            s.lastw[w] = tok
            s.readers[w] = {}

    def op(s, e, fn, reads=(), writes=(), sig=True):
        s._deps(e, reads, writes)
        ins = fn(s.eng[e])
        if sig:
            s.cnt[e] += 1
            ins.then_inc(s.sem[e], 1)
            tok = ('e', e, s.cnt[e])
            if e == 'pe':
                s.unsig = False
        else:
            assert e == 'pe'
            tok = ('e', e, s.cnt[e] + 1)
            s.unsig = True
        s._reg(tok, reads, writes)

    def dma(s, q, out, in_, reads=(), writes=()):
        s._deps(q, reads, writes)
        i = s.dnext
        s.dnext = (s.dnext + 1) % len(s.dsem)
        if s.dcnt[i] > 0:
            s._need(q, ('d', i, 16 * s.dcnt[i]))
        s.dcnt[i] += 1
        s.eng[q].dma_start(out=out, in_=in_).then_inc(s.dsem[i], 16)
        s._reg(('d', i, 16 * s.dcnt[i]), reads, writes)

    def barrier(s):
        assert not s.unsig
        toks = [('e', e, s.cnt[e]) for e in s.ENG if s.cnt[e] > 0]
        toks += [('d', i, 16 * s.dcnt[i]) for i in range(len(s.dsem)) if s.dcnt[i] > 0]
        for e in s.ENG:
            for t in toks:
                s._need(e, t)
        s.lastw.clear()
        s.readers.clear()


def build(nc, nlayers=DEPTH, dbg=False, stages="HABCDO"):
    es = ExitStack()
    S = Sched(nc, es)

    def din(name, shape, dt=F32):
        return nc.dram_tensor(name, list(shape), dt, kind="ExternalInput").ap()

    x_in = din("x_b", [LL, D])
    c_in = din("ctx_b", [LC, D])
    cfm = din("cfm", [128, 16])
    w_mod = din("w_mod", [DEPTH, D, 3 * D])
    b_mod = din("b_mod", [DEPTH, 3 * D])
    b_mod_fm = din("b_mod_fm", [DEPTH, 128, 24])
    g_pre_fm = din("g_pre_fm", [DEPTH, 128, 8])
    g_post = din("g_post", [DEPTH, D])
    w_in = din("w_in", [DEPTH, D, 3584])
    w_out = din("w_out", [DEPTH, D, D])
    lru_cw = din("lru_cw", [DEPTH, 128, 8])
    lru_cb = din("lru_cb", [DEPTH, 128, 2])
    lru_bd = din("lru_bd", [DEPTH, 2, 4, 128, 128])
    lru_b = din("lru_b", [DEPTH, 128, 8])
    lru_lam = din("lru_lam", [DEPTH, 128, 4])
    hg_lb = din("hg_lb", [128, 16])
    hg_g = din("hg_g", [DEPTH, 128, 1])
    cf_w = din("cf_w", [DEPTH, 128, 62])
    cf_b = din("cf_b", [DEPTH, 128, 6])
    df_lam = din("df_lam", [DEPTH, 128])
    df_g = din("df_g", [DEPTH, 128, 1])
    c_ident = din("c_ident", [128, 128], BF16)
    c_cos = din("c_cos", [128, LL])
    c_sin = din("c_sin", [128, LL])
    c_mask = din("c_mask", [128, 128])
    c_bd64 = din("c_bd64", [128, 128], BF16)

    out = nc.dram_tensor("out", [LL, D], F32, kind="ExternalOutput").ap()
    xcs = nc.dram_tensor("xcs", [LC, D], F32).ap()
    ymix = nc.dram_tensor("ymix", [D, T], BF16, kind="ExternalOutput" if dbg else "Internal").ap()
    ggd = nc.dram_tensor("ggd", [DEPTH, 2, D], F32, kind="ExternalOutput" if dbg else "Internal").ap()

    uid = [0]

    def sb(name, shape, dt=F32, ctx=None):
        uid[0] += 1
        return (ctx or es).enter_context(nc.sbuf_tensor(f"{name}_{uid[0]}", list(shape), dt))

    PS = [es.enter_context(nc.psum_tensor(f"ps{i}", [128, 512], F32)) for i in range(8)]

    def pk(i):
        return ('ps', i)

    hT = sb("hT", [128, 8, T], BF16)
    identb = sb("identb", [128, 128], BF16)
    bd64 = sb("bd64", [128, 128], BF16)
    ones64 = sb("ones64", [128, 64], BF16)
    ones256 = sb("ones256", [128, 128], BF16)
    maskt = sb("maskt", [128, 128], F32)
    GS = sb("GS", [128, DEPTH, 4, 8], F32)
    LBt = sb("LBt", [128, 3, 4, 4], F32)
    epsr = sb("epsr", [128, 1], F32)
    epsl = sb("epsl", [128, 1], F32)
    onec = sb("onec", [128, 1], F32)

    S.dma('sp', identb[:], c_ident[:, :], writes=['identb'])
    S.dma('sp', bd64[:], c_bd64[:, :], writes=['bd64'])
    S.dma('sp', maskt[:], c_mask[:, :], writes=['maskt'])
    S.op('dve', lambda e: e.memset(ones64[:], 1.0), writes=['ones64'])
    S.op('dve', lambda e: e.memset(ones256[:], 1.0 / 256.0), writes=['ones256'])
    S.op('dve', lambda e: e.memset(epsr[:], RMS_EPS), writes=['epsr'])
    S.op('dve', lambda e: e.memset(epsl[:], LN_EPS), writes=['epsl'])
    S.op('dve', lambda e: e.memset(onec[:], 1.0), writes=['onec'])
    for i_ in range(8):
        S.op('dve', lambda e, i_=i_: e.memset(PS[i_][:, :], 0.0), writes=[pk(i_)])

    def act(out_, in_, func, reads, writes, bias=None, scale=None):
        kw = {}
        if bias is not None:
            kw['bias'] = bias
        if scale is not None:
            kw['scale'] = scale
        S.op('act', lambda e: e.activation(out=out_, in_=in_, func=func, **kw), reads=reads, writes=writes)

    def rstd_from_ss(rs, ss, n_inv, eps_ap, key_rs, key_ss):
        act(rs, ss, AF.Ln, [key_ss], [key_rs], bias=eps_ap, scale=n_inv)
        act(rs, rs, AF.Exp, [key_rs], [key_rs], scale=-0.5)

    def prologue():
        with ExitStack() as cx:
            cf = sb("cf", [128, 16], F32, cx)
            sc = sb("sc", [128, 16], F32, cx)
            rep = sb("rep", [128, 2, 8, 128], F32, cx)
            wms = [sb(f"wm{i}", [128, 8, 512], F32, cx) for i in range(2)]
            bfm = sb("bfm", [128, 24], F32, cx)
            gpf = sb("gpf", [128, 8], F32, cx)
            bg = sb("bg", [128, 1024], F32, cx)
            gp = sb("gp", [128, 1024], F32, cx)
            tmpg = sb("tmpg", [128, 1024], F32, cx)
            tmp = sb("ptmp", [128, 16], F32, cx)
            hl = sb("hl", [128, 16], F32, cx)
            hs = sb("hs", [128, 4], F32, cx)
            S.dma('sp', cf[:], cfm[:, :], writes=['cf'])
            S.dma('sp', hl[:], hg_lb[:, :], writes=['hl'])
            act(sc[:], cf[:], AF.Silu, ['cf'], ['sc'])
            for j in range(2):
                src = sc[:].rearrange("p (k j) -> p k j", j=2)[:, :, j:j + 1].broadcast_to([128, 8, 128])
                S.op('dve', lambda e, j=j, src=src: e.tensor_copy(out=rep[:, j, :, :], in_=src), reads=['sc'], writes=['rep'])
            act(hl[:], hl[:], AF.Exp, ['hl'], ['hl'])
            hl3 = hl[:].rearrange("p (a l) -> p a l", l=4)
            S.op('dve', lambda e: e.reduce_sum(out=hs[:], in_=hl3, axis=AX.X), reads=['hl'], writes=['hs'])
            S.op('dve', lambda e: e.reciprocal(out=hs[:], in_=hs[:]), reads=['hs'], writes=['hs'])
            S.op('dve', lambda e: e.tensor_tensor(out=hl3, in0=hl3, in1=hs[:].unsqueeze(2).broadcast_to([128, 4, 4]), op=ALU.mult),
                 reads=['hl', 'hs'], writes=['hl'])
            S.op('dve', lambda e: e.memset(LBt[:, 0, :, 0:1], 0.0), writes=['LBt'])
            S.op('dve', lambda e: e.tensor_copy(out=LBt[:, 0, :, 1:2], in_=hl3[:, :, 1:2]), reads=['hl', 'LBt'], writes=['LBt'])
            for l in (2, 3):
                S.op('dve', lambda e, l=l: e.tensor_tensor(out=LBt[:, 0, :, l:l + 1], in0=LBt[:, 0, :, l - 1:l], in1=hl3[:, :, l:l + 1], op=ALU.add),
                     reads=['hl', 'LBt'], writes=['LBt'])
            S.op('dve', lambda e: e.tensor_scalar(out=LBt[:, 1, :, :], in0=LBt[:, 0, :, :], scalar1=-1.0, scalar2=1.0, op0=ALU.mult, op1=ALU.add),
                 reads=['LBt'], writes=['LBt'])
            S.op('dve', lambda e: e.tensor_scalar(out=LBt[:, 2, :, :], in0=LBt[:, 1, :, :], scalar1=-1.0, scalar2=None, op0=ALU.mult),
                 reads=['LBt'], writes=['LBt'])
            for l in range(nlayers):
                S.dma('sp', bfm[:], b_mod_fm[l], writes=['bfm'])
                S.dma('sp', gpf[:], g_pre_fm[l], writes=['gpf'])
                S.dma('sp', bg[:], b_mod[l:l + 1, 2048:3072].partition_broadcast(128), writes=['bg'])
                S.dma('sp', gp[:], g_post[l:l + 1, :].partition_broadcast(128), writes=['gp'])
                wv = w_mod[l].rearrange("(k p) n -> p k n", p=128)
                for sl in range(6):
                    wm = wms[sl % 2]
                    wk = f'wm{sl % 2}'
                    S.dma('sp', wm[:], wv[:, :, sl * 512:(sl + 1) * 512], writes=[wk])
                    if sl < 4:
                        for c in range(4):
                            cc = sl * 4 + c
                            for k in range(8):
                                S.op('pe', lambda e, k=k, c=c, cc=cc, wm=wm: e.matmul(PS[0][:, 2 * cc:2 * cc + 2], lhsT=wm[:, k, c * 128:(c + 1) * 128],
                                                                                   rhs=sc[:, 2 * k:2 * k + 2], start=(k == 0), stop=(k == 7)),
                                     reads=[wk, 'sc'], writes=[pk(0)], sig=(k == 7))
                    else:
                        half = sl - 4
                        for j in range(2):
                            for k in range(8):
                                S.op('pe', lambda e, k=k, j=j, half=half, wm=wm: e.matmul(PS[1 + 2 * j + half][:, :], lhsT=rep[:, j, k, :], rhs=wm[:, k, :],
                                                                                        start=(k == 0), stop=(k == 7)),
                                     reads=[wk, 'rep'], writes=[pk(1 + 2 * j + half)], sig=(k == 7))
                psv = PS[0][:, 0:32].rearrange("p (w k j) -> p w k j", w=2, k=8, j=2)
                for j in range(2):
                    S.op('dve', lambda e, j=j: e.tensor_tensor(out=GS[:, l, 1 + 2 * j, :], in0=psv[:, 0, :, j], in1=bfm[:, 0:8], op=ALU.add),
                         reads=[pk(0), 'bfm'], writes=['GS'])
                    S.op('dve', lambda e, j=j: e.tensor_tensor(out=tmp[:, 0:8], in0=psv[:, 1, :, j], in1=bfm[:, 8:16], op=ALU.add),
                         reads=[pk(0), 'bfm'], writes=['ptmp'])
                    S.op('dve', lambda e, j=j: e.scalar_tensor_tensor(out=GS[:, l, 2 * j, :], in0=tmp[:, 0:8], scalar=1.0, in1=gpf[:], op0=ALU.add, op1=ALU.mult),
                         reads=['ptmp', 'gpf'], writes=['GS'])
                for j in range(2):
                    for half in range(2):
                        S.op('dve', lambda e, j=j, half=half: e.tensor_tensor(out=tmpg[:, half * 512:(half + 1) * 512], in0=PS[1 + 2 * j + half][:, :],
                                                                            in1=bg[:, half * 512:(half + 1) * 512], op=ALU.add),
                             reads=[pk(1 + 2 * j + half), 'bg'], writes=['tmpg'])
                    S.op('dve', lambda e: e.tensor_tensor(out=tmpg[:], in0=tmpg[:], in1=gp[:], op=ALU.mult), reads=['tmpg', 'gp'], writes=['tmpg'])
                    S.dma('pool', ggd[l, j:j + 1, :], tmpg[0:1, :], reads=['tmpg'], writes=['ggd'])
        S.barrier()

    def load_w(dst, l, src, col0, ncols, stgs, dkey, off=0, slab=256):
        wv = src[l].rearrange("(k p) n -> p k n", p=128)
        i = 0
        for c in range(0, ncols, slab):
            n = min(slab, ncols - c)
            st, sk = stgs[i % len(stgs)]
            i += 1
            S.dma('sp', st[:, :, 0:n], wv[:, :, col0 + c:col0 + c + n], writes=[sk])
            if i % 2 == 0:
                S.op('dve', lambda e, st=st, c=c, n=n: e.tensor_copy(out=dst[:, :, off + c:off + c + n], in_=st[:, :, 0:n]), reads=[sk], writes=[dkey])
            else:
                act(dst[:, :, off + c:off + c + n], st[:, :, 0:n], AF.Identity, [sk], [dkey])

    def proj_fm(b, W, wkey, wc0, t0, n, wn=128):
        for k in range(8):
            S.op('pe', lambda e, k=k: e.matmul(PS[b][0:wn, 0:n], lhsT=W[:, k, wc0:wc0 + wn], rhs=hT[:, k, t0:t0 + n], start=(k == 0), stop=(k == 7)),
                 reads=[wkey, 'hT'], writes=[pk(b)], sig=(k == 7))

    def proj_tm(ps_ap, b, W, wkey, wc0, ncols, tile):
        for k in range(8):
            S.op('pe', lambda e, k=k: e.matmul(ps_ap, lhsT=hT[:, k, tile * 128:(tile + 1) * 128], rhs=W[:, k, wc0:wc0 + ncols], start=(k == 0), stop=(k == 7)),
                 reads=[wkey, 'hT'], writes=[pk(b)], sig=(k == 7))

    def res_src(l, i):
        if i < 2:
            base = c_in if l == 0 else xcs
            return base[i * 128:(i + 1) * 128, :]
        base = x_in if l == 0 else out
        return base[(i - 2) * 128:(i - 1) * 128, :]

    def res_dst(i):
        if i < 2:
            return xcs[i * 128:(i + 1) * 128, :]
        return out[(i - 2) * 128:(i - 1) * 128, :]

    def stageH(l):
        with ExitStack() as cx:
            xts = [sb(f"hx{i}", [128, 1024], F32, cx) for i in range(2)]
            sq = sb("hsq", [128, 1024], F32, cx)
            xss = [sb(f"hxs{i}", [128, 1024], BF16, cx) for i in range(2)]
            st = sb("hst", [128, 4], F32, cx)
            for i in range(NT):
                p = i % 2
                xt, xs = xts[p], xss[p]
                jj = 0 if i >= 2 else 2
                S.dma('sp', xt[:], res_src(l, i), reads=[('res', i)], writes=[f'hx{p}'])
                act(sq[:], xt[:], AF.Square, [f'hx{p}'], ['hsq'])
                S.op('dve', lambda e, p=p: e.reduce_sum(out=st[:, p:p + 1], in_=sq[:], axis=AX.X), reads=['hsq'], writes=[f'hss{p}'])
                rstd_from_ss(st[:, 2 + p:3 + p], st[:, p:p + 1], 1.0 / D, epsr[:], f'hrs{p}', f'hss{p}')
                S.op('dve', lambda e, p=p, xt=xt, xs=xs: e.tensor_scalar(out=xs[:], in0=xt[:], scalar1=st[:, 2 + p:3 + p], scalar2=None, op0=ALU.mult),
                     reads=[f'hx{p}', f'hrs{p}'], writes=[f'hxs{p}'])
                b = 6 + p
                psb = PS[b][:].bitcast(BF16)
                for k in range(8):
                    S.op('pe', lambda e, k=k, xs=xs, psb=psb: e.transpose(out=psb[:, k * 128:(k + 1) * 128], in_=xs[:, k * 128:(k + 1) * 128], identity=identb[:]),
                         reads=[f'hxs{p}', 'identb'], writes=[pk(b)], sig=(k == 7))
                for k in range(8):
                    S.op('dve', lambda e, k=k, psb=psb, i=i, jj=jj: e.tensor_scalar(out=hT[:, k, i * 128:(i + 1) * 128], in0=psb[:, k * 128:(k + 1) * 128],
                                                                             scalar1=GS[:, l, jj, k:k + 1], scalar2=GS[:, l, jj + 1, k:k + 1],
                                                                             op0=ALU.mult, op1=ALU.add),
                         reads=[pk(b), 'GS'], writes=['hT'])
        S.barrier()

    def stageO(l, last, fuse_next=False):
        with ExitStack() as cx:
            wo = sb("wo", [128, 8, 1024], BF16, cx)
            stgs = [(sb(f"ostg{i}", [128, 8, 256], F32, cx), f'ostg{i}') for i in range(2)]
            GG = [sb(f"GG{j}", [128, 1024], F32, cx) for j in range(2)]
            yms = [sb(f"oym{i}", [128, 8, 512], BF16, cx) for i in range(2)]
            xts = [sb(f"ox{i}", [128, 1024], F32, cx) for i in range(2)]
            sq = sb("osq", [128, 1024], F32, cx)
            tts = [sb(f"ot{i}", [128, 1024], F32, cx) for i in range(2)]
            st = sb("ost", [128, 8], F32, cx)
            if fuse_next:
                sq2 = sb("osq2", [128, 1024], F32, cx)
                xss = [sb(f"oxs{i}", [128, 1024], BF16, cx) for i in range(2)]
            load_w(wo, l, w_out, 0, 1024, stgs, 'wo')
            for j in range(2):
                S.dma('sp', GG[j][:], ggd[l, j:j + 1, :].partition_broadcast(128), reads=['ggd'], writes=[f'GG{j}'])
            ymv = ymix.rearrange("(k p) t -> p k t", p=128)
            it = 0
            hq = [None, None]
            for bi, (t0, n) in enumerate(BLK):
                if last and bi == 0:
                    continue
                ym = yms[bi % 2]
                yk = f'oym{bi % 2}'
                S.dma('sp', ym[:, :, 0:n], ymv[:, :, t0:t0 + n], reads=['ymix'], writes=[yk])
                for tt in range(n // 128):
                    i = (t0 // 128) + tt
                    j = 0 if i >= 2 else 1
                    p = it % 2
                    it += 1
                    xt, tq = xts[p], tts[p]
                    S.dma('sp', xt[:], res_src(l, i), reads=[('res', i)], writes=[f'ox{p}'])
                    for half in range(2):
                        b = 2 * p + half
                        for k in range(8):
                            S.op('pe', lambda e, k=k, b=b, half=half, ym=ym, tt=tt: e.matmul(PS[b][:, :], lhsT=ym[:, k, tt * 128:(tt + 1) * 128],
                                                                                       rhs=wo[:, k, half * 512:(half + 1) * 512], start=(k == 0), stop=(k == 7)),
                                 reads=[yk, 'wo'], writes=[pk(b)], sig=(k == 7))
                        act(sq[:, half * 512:(half + 1) * 512], PS[b][:, :], AF.Square, [pk(b)], ['osq'])
                    S.op('dve', lambda e, p=p: e.reduce_sum(out=st[:, p:p + 1], in_=sq[:], axis=AX.X), reads=['osq'], writes=[f'oss{p}'])
                    rstd_from_ss(st[:, 2 + p:3 + p], st[:, p:p + 1], 1.0 / D, epsr[:], f'ors{p}', f'oss{p}')
                    for half in range(2):
                        b = 2 * p + half
                        S.op('dve', lambda e, b=b, half=half, tq=tq, p=p, j=j: e.scalar_tensor_tensor(out=tq[:, half * 512:(half + 1) * 512], in0=PS[b][:, :],
                                                                                                scalar=st[:, 2 + p:3 + p], in1=GG[j][:, half * 512:(half + 1) * 512],
                                                                                                op0=ALU.mult, op1=ALU.mult),
                             reads=[pk(b), f'ors{p}', f'GG{j}'], writes=[f'ot{p}'])
                    S.op('pool', lambda e, tq=tq, xt=xt: e.tensor_tensor(out=tq[:], in0=tq[:], in1=xt[:], op=ALU.add), reads=[f'ot{p}', f'ox{p}'], writes=[f'ot{p}'])
                    S.dma('pool', res_dst(i), tq[:], reads=[f'ot{p}'], writes=[('res', i)])
                    if fuse_next:
                        def hpart(i=i, p=p, tq=tq):
                            ln = l + 1
                            jj = 0 if i >= 2 else 2
                            xs = xss[p]
                            act(sq2[:], tq[:], AF.Square, [f'ot{p}'], ['osq2'])
                            S.op('dve', lambda e, p=p: e.reduce_sum(out=st[:, 4 + p:5 + p], in_=sq2[:], axis=AX.X), reads=['osq2'], writes=[f'oss2{p}'])
                            rstd_from_ss(st[:, 6 + p:7 + p], st[:, 4 + p:5 + p], 1.0 / D, epsr[:], f'ors2{p}', f'oss2{p}')
                            act(xs[:], tq[:], AF.Identity, [f'ot{p}', f'ors2{p}'], [f'oxs{p}'], scale=st[:, 6 + p:7 + p])
                        def hpartB(i=i, p=p):
                            ln = l + 1
                            jj = 0 if i >= 2 else 2
                            xs = xss[p]
                            for k in range(8):
                                b = (4 + p) if k < 4 else (6 + p)
                                psb = PS[b][:].bitcast(BF16)
                                S.op('pe', lambda e, k=k, xs=xs, psb=psb: e.transpose(out=psb[:, (k % 4) * 128:(k % 4 + 1) * 128], in_=xs[:, k * 128:(k + 1) * 128], identity=identb[:]),
                                     reads=[f'oxs{p}', 'identb'], writes=[pk(b)], sig=(k % 4 == 3))
                            for k in range(8):
                                b = (4 + p) if k < 4 else (6 + p)
                                psb = PS[b][:].bitcast(BF16)
                                if k < 4:
                                    act(hT[:, k, i * 128:(i + 1) * 128], psb[:, (k % 4) * 128:(k % 4 + 1) * 128], AF.Identity, [pk(b), 'GS'], [('hTa', k)],
                                        bias=GS[:, ln, jj + 1, k:k + 1], scale=GS[:, ln, jj, k:k + 1])
                                else:
                                    S.op('dve', lambda e, k=k, psb=psb, i=i, jj=jj, ln=ln: e.tensor_scalar(out=hT[:, k, i * 128:(i + 1) * 128], in0=psb[:, (k % 4) * 128:(k % 4 + 1) * 128],
                                                                                                scalar1=GS[:, ln, jj, k:k + 1], scalar2=GS[:, ln, jj + 1, k:k + 1],
                                                                                                op0=ALU.mult, op1=ALU.add),
                                         reads=[pk(b), 'GS'], writes=[('hTd', k)])
                        if hq[1] is not None:
                            hq[1]()
                            hq[1] = None
                        if hq[0] is not None:
                            hq[0][0]()
                            hq[1] = hq[0][1]
                        hq[0] = (hpart, hpartB)
            if fuse_next:
                if hq[1] is not None:
                    hq[1]()
                if hq[0] is not None:
                    hq[0][0]()
                    hq[0][1]()
        S.barrier()

    NG = T + 3

    def stageA(l):
        with ExitStack() as cx:
            W = sb("aW", [128, 8, 512], BF16, cx)
            stgs = [(sb(f"astg{i}", [128, 8, 256], F32, cx), f'astg{i}') for i in range(2)]
            bdst = sb("abdst", [128, 4, 128], F32, cx)
            bd = sb("abd", [128, 4, 128], BF16, cx)
            cw = sb("acw", [128, 8], F32, cx)
            cb = sb("acb", [128, 2], F32, cx)
            lbias = sb("alb", [128, 8], F32, cx)
            lam = sb("alam", [128, 4], F32, cx)
            B = [sb(f"aB{i}", [128, NG + 3], F32, cx) for i in range(5)]
            XCb = sb("aXCb", [128, NG], BF16, cx)
            SGt = sb("aSG", [128, T], BF16, cx)
            Yb = XCb
            load_w(W, l, w_in, 0, 512, stgs, 'aW')
            S.dma('sp', cw[:], lru_cw[l], writes=['acw'])
            S.dma('sp', cb[:], lru_cb[l], writes=['acb'])
            S.dma('sp', lbias[:], lru_b[l], writes=['alb'])
            S.dma('sp', lam[:], lru_lam[l], writes=['alam'])
            act(lam[:], lam[:], AF.Exp, ['alam'], ['alam'], scale=-1.0)
            act(lam[:], lam[:], AF.Ln, ['alam'], ['alam'], bias=onec[:], scale=1.0)
            S.op('dve', lambda e: e.tensor_scalar(out=lam[:], in0=lam[:], scalar1=-8.0, scalar2=None, op0=ALU.mult), reads=['alam'], writes=['alam'])
            for pc in range(2):
                UX, XC = B[0], B[4]
                for (a, b_) in ((0, 2), (258, 261), (NG + 2, NG + 3)):
                    S.op('dve', lambda e, a=a, b_=b_: e.memset(UX[:, a:b_], 0.0), writes=['aB0'])
                for bi, (t0, n) in enumerate(BLK):
                    ux0 = 2 + t0 if t0 < 256 else 261 + (t0 - 256)
                    b = bi % 2
                    proj_fm(b, W, 'aW', pc * 128, t0, n)
                    S.op('dve', lambda e, b=b, ux0=ux0, n=n: e.tensor_copy(out=UX[:, ux0:ux0 + n], in_=PS[b][:, 0:n]), reads=[pk(b)], writes=['aB0'])
                    b2 = 2 + bi % 2
                    proj_fm(b2, W, 'aW', 256 + pc * 128, t0, n)
                    act(SGt[:, t0:t0 + n], PS[b2][:, 0:n], AF.Silu, [pk(b2)], ['aSG'])
                S.op('dve', lambda e: e.tensor_scalar(out=XC[:, 0:NG], in0=UX[:, 0:NG], scalar1=cw[:, pc * 4:pc * 4 + 1], scalar2=cb[:, pc:pc + 1],
                                                      op0=ALU.mult, op1=ALU.add), reads=['aB0', 'acw', 'acb'], writes=['aB4'])
                for k in range(1, 4):
                    S.op('dve', lambda e, k=k: e.scalar_tensor_tensor(out=XC[:, 0:NG], in0=UX[:, k:k + NG], scalar=cw[:, pc * 4 + k:pc * 4 + k + 1], in1=XC[:, 0:NG],
                                                                      op0=ALU.mult, op1=ALU.add), reads=['aB0', 'aB4', 'acw'], writes=['aB4'])
                S.op('pool', lambda e: e.tensor_copy(out=XCb[:], in_=XC[:, 0:NG]), reads=['aB4'], writes=['aXCb'])
                S.dma('sp', bdst[:], lru_bd[l, pc].rearrange("w a b -> a w b"), writes=['abdst'])
                S.op('pool', lambda e: e.tensor_copy(out=bd[:], in_=bdst[:]), reads=['abdst'], writes=['abd'])
                for dr in range(2):
                    R, I, A = B[0], B[1], B[2]
                    for gi, g0 in enumerate(range(0, NG, 512)):
                        n = min(512, NG - g0)
                        for wh, (dst, dk) in enumerate(((R, 'aB0'), (I, 'aB1'))):
                            b = 2 * (gi % 2) + wh
                            S.op('pe', lambda e, b=b, wh=wh, g0=g0, n=n: e.matmul(PS[b][:, 0:n], lhsT=bd[:, 2 * dr + wh, :], rhs=XCb[:, g0:g0 + n], start=True, stop=True),
                                 reads=['abd', 'aXCb'], writes=[pk(b)])
                            bi_ = wh * 4 + dr * 2 + pc
                            act(dst[:, g0:g0 + n], PS[b][:, 0:n], AF.Sigmoid, [pk(b), 'alb'], [dk], bias=lbias[:, bi_:bi_ + 1], scale=1.0)
                    ci = dr * 2 + pc
                    act(A[:, 0:NG], R[:, 0:NG], AF.Exp, ['aB0', 'alam'], ['aB2'], scale=lam[:, ci:ci + 1])
                    act(R[:, 0:NG], A[:, 0:NG], AF.Square, ['aB2'], ['aB0'])
                    act(R[:, 0:NG], R[:, 0:NG], AF.Ln, ['aB0'], ['aB0'], bias=onec[:], scale=-1.0)
                    act(R[:, 0:NG], R[:, 0:NG], AF.Exp, ['aB0'], ['aB0'], scale=0.5)
                    S.op('dve', lambda e: e.tensor_tensor(out=I[:, 0:NG], in0=I[:, 0:NG], in1=XC[:, 0:NG], op=ALU.mult), reads=['aB1', 'aB4'], writes=['aB1'])
                    S.op('dve', lambda e: e.tensor_tensor(out=I[:, 0:NG], in0=I[:, 0:NG], in1=R[:, 0:NG], op=ALU.mult), reads=['aB1', 'aB0'], writes=['aB1'])
                    H, hk = (B[3], 'aB3') if dr == 0 else (B[0], 'aB0')
                    if dr == 0:
                        S.op('dve', lambda e, H=H: e.tensor_tensor_scan(out=H[:, 0:256], data0=A[:, 0:256], data1=I[:, 0:256], initial=0.0, op0=ALU.mult, op1=ALU.add),
                             reads=['aB2', 'aB1'], writes=[hk])
                        S.op('dve', lambda e, H=H: e.tensor_tensor_scan(out=H[:, 259:NG], data0=A[:, 259:NG], data1=I[:, 259:NG], initial=H[:, 255:256],
                                                                         op0=ALU.mult, op1=ALU.add), reads=['aB2', 'aB1', hk], writes=[hk])
                    else:
                        rv = lambda X, a, b_: X[:, a:b_][:, ::-1]
                        S.op('dve', lambda e, H=H: e.tensor_tensor_scan(out=rv(H, 0, 256), data0=rv(A, 0, 256), data1=rv(I, 0, 256), initial=0.0, op0=ALU.mult, op1=ALU.add),
                             reads=['aB2', 'aB1'], writes=[hk])
                        S.op('dve', lambda e, H=H: e.tensor_tensor_scan(out=rv(H, 259, NG), data0=rv(A, 259, NG), data1=rv(I, 259, NG), initial=H[:, 0:1],
                                                                         op0=ALU.mult, op1=ALU.add), reads=['aB2', 'aB1', hk], writes=[hk])
                        S.op('dve', lambda e: e.tensor_tensor(out=B[3][:, 0:NG], in0=B[3][:, 0:NG], in1=B[0][:, 0:NG], op=ALU.add), reads=['aB3', 'aB0'], writes=['aB3'])
                S.op('dve', lambda e: e.tensor_tensor(out=Yb[:, 0:256], in0=B[3][:, 0:256], in1=SGt[:, 0:256], op=ALU.mult), reads=['aB3', 'aSG'], writes=['aXCb'])
                S.op('dve', lambda e: e.tensor_tensor(out=Yb[:, 256:T], in0=B[3][:, 259:NG], in1=SGt[:, 256:T], op=ALU.mult), reads=['aB3', 'aSG'], writes=['aXCb'])
                S.dma('pool', ymix[pc * 128:(pc + 1) * 128, :], Yb[:, 0:T], reads=['aXCb'], writes=['ymix'])
        S.barrier()

    NP = T + 60

    def stageC(l, last):
        with ExitStack() as cx:
            W = sb("cW", [128, 8, 768], BF16, cx)
            stgs = [(sb(f"cstg{i}", [128, 8, 256], F32, cx), f'cstg{i}') for i in range(2)]
            Yp = [sb(f"cYp{i}", [128, NP], BF16, cx) for i in range(2)]
            SG = sb("cSG", [128, 2, T], BF16, cx)
            Dg = sb("cDg", [128, 2, 31, 128], BF16, cx)
            cwt = sb("ccw", [128, 62], F32, cx)
            cbt = sb("ccb", [128, 6], F32, cx)
            sgm = [sb(f"csgm{i}", [128, 512], F32, cx) for i in range(2)]
            Cf = sb("cCf", [128, 2, 512], F32, cx)
            Cb = sb("cCb", [128, 2, 512], BF16, cx)
            Cq = sb("cCq", [128, 2, 512], BF16, cx)
            mean = sb("cmean", [128, 512], F32, cx)
            var = sb("cvar", [128, 512], F32, cx)
            dd = [sb(f"cdd{i}", [128, 512], F32, cx) for i in range(2)]
            yo = [sb(f"cyo{i}", [128, 512], BF16, cx) for i in range(2)]
            load_w(W, l, w_in, 1792, 768, stgs, 'cW')
            S.dma('sp', cwt[:], cf_w[l], writes=['ccw'])
            S.dma('sp', cbt[:], cf_b[l], writes=['ccb'])
            for pc in range(2):
                i0 = identb[:].unsqueeze(1).broadcast_to([128, 31, 128])
                i1 = cwt[:, pc * 31:(pc + 1) * 31].unsqueeze(2).broadcast_to([128, 31, 128])
                S.op('dve', lambda e, pc=pc, i0=i0, i1=i1: e.tensor_tensor(out=Dg[:, pc, :, :], in0=i0, in1=i1, op=ALU.mult), reads=['identb', 'ccw'], writes=['cDg'])
                S.op('pool', lambda e, pc=pc: e.memset(Yp[pc][:], 0.0), writes=[f'cYp{pc}'])
            for bi, (t0, n) in enumerate(BLK):
                p0 = 15 + t0 if t0 < 256 else 301 + (t0 - 256)
                for pc in range(2):
                    q = (bi * 2 + pc) % 2
                    proj_fm(0 + q, W, 'cW', pc * 128, t0, n)
                    proj_fm(2 + q, W, 'cW', 256 + pc * 128, t0, n)
                    act(sgm[q][:, 0:n], PS[2 + q][:, 0:n], AF.Sigmoid, [pk(2 + q)], [f'csgm{q}'])
                    S.op('dve', lambda e, pc=pc, q=q, p0=p0, n=n: e.tensor_tensor(out=Yp[pc][:, p0:p0 + n], in0=PS[q][:, 0:n], in1=sgm[q][:, 0:n], op=ALU.mult),
                         reads=[pk(q), f'csgm{q}'], writes=[f'cYp{pc}'])
                    proj_fm(4 + q, W, 'cW', 512 + pc * 128, t0, n)
                    act(SG[:, pc, t0:t0 + n], PS[4 + q][:, 0:n], AF.Silu, [pk(4 + q)], ['cSG'])
            for bi, (t0, n) in enumerate(BLK):
                if last and bi == 0:
                    continue
                p0 = 15 + t0 if t0 < 256 else 301 + (t0 - 256)
                for pc in range(2):
                    b = pc
                    for k in range(31):
                        S.op('pe', lambda e, k=k, pc=pc, b=b: e.matmul(PS[b][:, 0:n], lhsT=Dg[:, pc, k, :], rhs=Yp[pc][:, p0 + k - 15:p0 + k - 15 + n], start=(k == 0), stop=(k == 30)),
                             reads=['cDg', f'cYp{pc}'], writes=[pk(b)], sig=(k == 30))
                    S.op('dve', lambda e, pc=pc, b=b: e.tensor_scalar(out=Cf[:, pc, 0:n], in0=PS[b][:, 0:n], scalar1=cbt[:, pc:pc + 1], scalar2=None, op0=ALU.add),
                         reads=[pk(b), 'ccb'], writes=['cCf'])
                    S.op('pool', lambda e, pc=pc: e.tensor_copy(out=Cb[:, pc, 0:n], in_=Cf[:, pc, 0:n]), reads=['cCf'], writes=['cCb'])
                    S.op('pool', lambda e, pc=pc: e.tensor_tensor(out=Cq[:, pc, 0:n], in0=Cf[:, pc, 0:n], in1=Cf[:, pc, 0:n], op=ALU.mult), reads=['cCf'], writes=['cCq'])
                for pc in range(2):
                    S.op('pe', lambda e, pc=pc: e.matmul(PS[2][:, 0:n], lhsT=ones256[:], rhs=Cb[:, pc, 0:n], start=(pc == 0), stop=(pc == 1)),
                         reads=['ones256', 'cCb'], writes=[pk(2)], sig=(pc == 1))
                for pc in range(2):
                    S.op('pe', lambda e, pc=pc: e.matmul(PS[3][:, 0:n], lhsT=ones256[:], rhs=Cq[:, pc, 0:n], start=(pc == 0), stop=(pc == 1)),
                         reads=['ones256', 'cCq'], writes=[pk(3)], sig=(pc == 1))
                S.op('dve', lambda e: e.tensor_copy(out=mean[:, 0:n], in_=PS[2][:, 0:n]), reads=[pk(2)], writes=['cmean'])
                S.op('dve', lambda e: e.tensor_tensor(out=var[:, 0:n], in0=mean[:, 0:n], in1=mean[:, 0:n], op=ALU.mult), reads=['cmean'], writes=['cvar'])
                S.op('dve', lambda e: e.tensor_tensor(out=var[:, 0:n], in0=PS[3][:, 0:n], in1=var[:, 0:n], op=ALU.subtract), reads=[pk(3), 'cvar'], writes=['cvar'])
                act(var[:, 0:n], var[:, 0:n], AF.Ln, ['cvar'], ['cvar'], bias=epsl[:], scale=1.0)
                act(var[:, 0:n], var[:, 0:n], AF.Exp, ['cvar'], ['cvar'], scale=-0.5)
                for pc in range(2):
                    d_, y_ = dd[pc], yo[pc]
                    S.op('dve', lambda e, pc=pc, d_=d_: e.tensor_tensor(out=d_[:, 0:n], in0=Cf[:, pc, 0:n], in1=mean[:, 0:n], op=ALU.subtract), reads=['cCf', 'cmean'], writes=[f'cdd{pc}'])
                    S.op('dve', lambda e, pc=pc, d_=d_: e.tensor_tensor(out=d_[:, 0:n], in0=d_[:, 0:n], in1=var[:, 0:n], op=ALU.mult), reads=[f'cdd{pc}', 'cvar'], writes=[f'cdd{pc}'])
                    act(d_[:, 0:n], d_[:, 0:n], AF.Silu, [f'cdd{pc}', 'ccb'], [f'cdd{pc}'], bias=cbt[:, 4 + pc:5 + pc], scale=cbt[:, 2 + pc:3 + pc])
                    S.op('dve', lambda e, pc=pc, d_=d_, y_=y_: e.tensor_tensor(out=y_[:, 0:n], in0=d_[:, 0:n], in1=SG[:, pc, t0:t0 + n], op=ALU.mult),
                         reads=[f'cdd{pc}', 'cSG'], writes=[f'cyo{pc}'])
                    S.dma('pool', ymix[512 + pc * 128:512 + (pc + 1) * 128, t0:t0 + n], y_[:, 0:n], reads=[f'cyo{pc}'], writes=['ymix'])
        S.barrier()

    def stageD(l, last):
        lam_init = 0.8 - 0.6 * math.exp(-0.3 * l)
        scale = 32 ** -0.5
        with ExitStack() as cx:
            KT = sb("dKT", [128, 2, T], BF16, cx)
            QT = sb("dQT", [128, 2, T], BF16, cx)
            V = sb("dV", [128, NT, 384], BF16, cx)
            SG = sb("dSG", [128, 2, T], BF16, cx)
            lmt = sb("dlm", [128, 128], F32, cx)
            lms = sb("dls", [128, 4], F32, cx)
            gn = sb("dgn", [128, 1], F32, cx)
            S.dma('sp', lmt[:], df_lam[l:l + 1, :].partition_broadcast(128), writes=['dlm'])
            S.dma('sp', gn[:], df_g[l], writes=['dgn'])
            lm4 = lmt[:].rearrange("p (a d) -> p a d", d=32)
            S.op('dve', lambda e: e.tensor_tensor(out=lm4[:, 0, :], in0=lm4[:, 0, :], in1=lm4[:, 1, :], op=ALU.mult), reads=['dlm'], writes=['dlm'])
            S.op('dve', lambda e: e.tensor_tensor(out=lm4[:, 2, :], in0=lm4[:, 2, :], in1=lm4[:, 3, :], op=ALU.mult), reads=['dlm'], writes=['dlm'])
            S.op('dve', lambda e: e.reduce_sum(out=lms[:, 0:1], in_=lm4[:, 0, :], axis=AX.X), reads=['dlm'], writes=['dls'])
            S.op('dve', lambda e: e.reduce_sum(out=lms[:, 1:2], in_=lm4[:, 2, :], axis=AX.X), reads=['dlm'], writes=['dls'])
            act(lms[:, 0:2], lms[:, 0:2], AF.Exp, ['dls'], ['dls'])
            S.op('dve', lambda e: e.tensor_tensor(out=lms[:, 2:3], in0=lms[:, 1:2], in1=lms[:, 0:1], op=ALU.subtract), reads=['dls'], writes=['dls'])
            S.op('dve', lambda e: e.tensor_scalar(out=lms[:, 2:3], in0=lms[:, 2:3], scalar1=-lam_init, scalar2=None, op0=ALU.add), reads=['dls'], writes=['dls'])
            S.op('dve', lambda e: e.tensor_scalar(out=gn[:], in0=gn[:], scalar1=(1.0 - lam_init), scalar2=None, op0=ALU.mult), reads=['dgn'], writes=['dgn'])
            with ExitStack() as c1:
                W = sb("dW", [128, 8, 1024], BF16, c1)
                Wsw = sb("dWsw", [128, 8, 512], BF16, c1)
                stgs = [(sb(f"dstg{i}", [128, 8, 128], F32, c1), f'dstg{i}') for i in range(2)]
                cs = [sb(f"dcs{i}", [128, 512], F32, c1) for i in range(2)]
                sn = [sb(f"dsn{i}", [128, 512], F32, c1) for i in range(2)]
                t1 = [sb(f"dt1{i}", [128, 512], F32, c1) for i in range(2)]
                t2 = [sb(f"dt2{i}", [128, 512], F32, c1) for i in range(2)]
                S.op('pool', lambda e: e.memset(V[:], 1.0), writes=['dV'])
                load_w(W, l, w_in, 2560, 1024, stgs, 'dW', slab=128)
                for k in range(8):
                    wv_ = W[:, k, 0:512].rearrange("p (g two j) -> p g two j", two=2, j=16)
                    sv_ = Wsw[:, k, :].rearrange("p (g two j) -> p g two j", two=2, j=16)
                    for h in range(2):
                        S.op('pool', lambda e, wv_=wv_, sv_=sv_, h=h: e.tensor_copy(out=sv_[:, :, 1 - h, :], in_=wv_[:, :, h, :]), reads=['dW'], writes=['dWsw'])
                ci = 0
                for bi, (t0, n) in enumerate(BLK):
                    lat = t0 >= 256
                    cp = bi % 2
                    if lat:
                        S.dma('sp', cs[cp][:, 0:n], c_cos[:, t0 - 256:t0 - 256 + n], writes=[f'dcs{cp}'])
                        S.dma('sp', sn[cp][:, 0:n], c_sin[:, t0 - 256:t0 - 256 + n], writes=[f'dsn{cp}'])
                    for which, (dst, dk) in enumerate(((QT, 'dQT'), (KT, 'dKT'))):
                        for ck in range(2):
                            q = ci % 2
                            ci += 1
                            proj_fm(q, W, 'dW', which * 256 + ck * 128, t0, n)
                            if not lat:
                                S.op('dve', lambda e, dst=dst, ck=ck, q=q: e.tensor_copy(out=dst[:, ck, t0:t0 + n], in_=PS[q][:, 0:n]), reads=[pk(q)], writes=[dk])
                            else:
                                proj_fm(2 + q, Wsw, 'dWsw', which * 256 + ck * 128, t0, n)
                                S.op('dve', lambda e, q=q: e.tensor_tensor(out=t1[q][:, 0:n], in0=PS[q][:, 0:n], in1=cs[cp][:, 0:n], op=ALU.mult),
                                     reads=[pk(q), f'dcs{cp}'], writes=[f'dt1{q}'])
                                S.op('dve', lambda e, q=q: e.tensor_tensor(out=t2[q][:, 0:n], in0=PS[2 + q][:, 0:n], in1=sn[cp][:, 0:n], op=ALU.mult),
                                     reads=[pk(2 + q), f'dsn{cp}'], writes=[f'dt2{q}'])
                                S.op('pool', lambda e, dst=dst, ck=ck, q=q: e.tensor_tensor(out=dst[:, ck, t0:t0 + n], in0=t1[q][:, 0:n], in1=t2[q][:, 0:n], op=ALU.add),
                                     reads=[f'dt1{q}', f'dt2{q}'], writes=[dk])
                    for ck in range(2):
                        b = 4 + ck
                        proj_fm(b, W, 'dW', 768 + ck * 128, t0, n)
                        act(SG[:, ck, t0:t0 + n], PS[b][:, 0:n], AF.Silu, [pk(b)], ['dSG'])
                    for tt in range(n // 128):
                        i = t0 // 128 + tt
                        b = 6 + i % 2
                        proj_tm(PS[b][:, 0:256], b, W, 'dW', 512, 256, i)
                        vv = V[:, i, :].rearrange("p (g c) -> p g c", c=192)
                        pv4 = PS[b][:, 0:256].rearrange("p (g r c) -> p g r c", r=2, c=64)
                        for r_ in range(2):
                            S.op('dve', lambda e, vv=vv, pv4=pv4, r_=r_: e.tensor_copy(out=vv[:, :, 128 * r_:128 * r_ + 64], in_=pv4[:, :, r_, :]), reads=[pk(b)], writes=['dV'])
            S.barrier()
            with ExitStack() as c2:
                Pb = [sb(f"dP{i}", [128, 512], BF16, c2) for i in range(6)]
                Qz = [sb(f"dQz{i}", [128, 4, 512], BF16, c2) for i in range(2)]
                RL = [sb(f"dRL{i}", [128, 512], F32, c2) for i in range(2)]
                Nn = [sb(f"dN{i}", [128, 512], F32, c2) for i in range(2)]
                Oc = [sb(f"dOc{i}", [128, 512], F32, c2) for i in range(4)]
                Oh = sb("dOh", [128, 512], F32, c2)
                Osq = sb("dOsq", [128, 512], BF16, c2)
                rs = sb("drs", [128, 512], F32, c2)
                Yo = [sb(f"dYo{i}", [128, 512], BF16, c2) for i in range(2)]
                pending = []
                state = dict(n=0)
                for i_ in range(2):
                    S.op('pool', lambda e, i_=i_: e.memset(Qz[i_][:], 0.0), writes=[f'dQz{i_}'])

                def prep_q(pi, q0, nq, hp):
                    pp = pi % 2
                    for s_ in range(4):
                        S.op('pool', lambda e, s_=s_: e.tensor_copy(out=Qz[pp][32 * s_:32 * s_ + 32, s_, 0:nq], in_=QT[32 * s_:32 * s_ + 32, hp, q0:q0 + nq]),
                             reads=['dQT'], writes=[f'dQz{pp}'])

                def finalize(q0, nq, hp, yp):
                    steps = []
                    for s_ in range(4):
                        steps.append(lambda s_=s_: S.op('dve', lambda e: e.tensor_copy(out=Oc[s_][:, 0:nq], in_=PS[3 + s_][:, 0:nq]), reads=[pk(3 + s_)], writes=[f'dOc{s_}']))
                    for s_ in range(4):
                        hh, w = s_ // 2, s_ % 2
                        lo, ll = 64 * hh, 64 * (1 - hh)
                        steps.append(lambda s_=s_, w=w, lo=lo, ll=ll: S.op('dve', lambda e: e.reciprocal(out=RL[w][lo:lo + 64, 0:nq], in_=Oc[s_][ll:ll + 64, 0:nq]),
                                                                     reads=[f'dOc{s_}'], writes=[f'dRL{w}']))
                        steps.append(lambda s_=s_, w=w, lo=lo: S.op('dve', lambda e: e.tensor_tensor(out=Nn[w][lo:lo + 64, 0:nq], in0=Oc[s_][lo:lo + 64, 0:nq], in1=RL[w][lo:lo + 64, 0:nq], op=ALU.mult),
                                                              reads=[f'dOc{s_}', f'dRL{w}'], writes=[f'dN{w}']))
                    steps.append(lambda: S.op('dve', lambda e: e.scalar_tensor_tensor(out=Oh[:, 0:nq], in0=Nn[1][:, 0:nq], scalar=lms[:, 2:3], in1=Nn[0][:, 0:nq], op0=ALU.mult, op1=ALU.add),
                                              reads=['dN0', 'dN1', 'dls'], writes=['dOh']))
                    steps.append(lambda: S.op('pool', lambda e: e.tensor_tensor(out=Osq[:, 0:nq], in0=Oh[:, 0:nq], in1=Oh[:, 0:nq], op=ALU.mult), reads=['dOh'], writes=['dOsq']))
                    steps.append(lambda: S.op('pe', lambda e: e.matmul(PS[7][:, 0:nq], lhsT=bd64[:], rhs=Osq[:, 0:nq], start=True, stop=True), reads=['bd64', 'dOsq'], writes=[pk(7)]))
                    steps.append(lambda: rstd_from_ss(rs[:, 0:nq], PS[7][:, 0:nq], 1.0 / 64, epsr[:], 'drs', pk(7)))
                    steps.append(lambda: S.op('dve', lambda e: e.tensor_tensor(out=Oh[:, 0:nq], in0=Oh[:, 0:nq], in1=rs[:, 0:nq], op=ALU.mult), reads=['dOh', 'drs'], writes=['dOh']))
                    steps.append(lambda: S.op('dve', lambda e: e.scalar_tensor_tensor(out=Yo[yp][:, 0:nq], in0=Oh[:, 0:nq], scalar=gn[:, 0:1], in1=SG[:, hp, q0:q0 + nq], op0=ALU.mult, op1=ALU.mult),
                                              reads=['dOh', 'dgn', 'dSG'], writes=[f'dYo{yp}']))
                    steps.append(lambda: S.dma('pool', ymix[768 + hp * 128:768 + (hp + 1) * 128, q0:q0 + nq], Yo[yp][:, 0:nq], reads=[f'dYo{yp}'], writes=['ymix']))
                    return steps

                passes = []
                if not last:
                    for hp in range(2):
                        passes.append((0, 256, [0, 1], hp))
                for qb in range(8):
                    for hp in range(2):
                        passes.append((256 + qb * 512, 512, list(range(NT)), hp))

                def attn_pass(pi):
                    q0, nq, kts, hp = passes[pi]
                    pp = pi % 2
                    seq = [(kt, s_) for kt in kts for s_ in range(4)]
                    nseq = len(seq)
                    base = state['n']

                    def qk(m):
                        kt, s_ = seq[m]
                        b_ = (base + m) % 3
                        S.op('pe', lambda e: e.matmul(PS[b_][:, 0:nq], lhsT=KT[:, hp, kt * 128:(kt + 1) * 128], rhs=Qz[pp][:, s_, 0:nq], start=True, stop=True),
                             reads=['dKT', f'dQz{pp}'], writes=[pk(b_)])

                    def ex(m):
                        g = base + m
                        act(Pb[g % 6][:, 0:nq], PS[g % 3][:, 0:nq], AF.Exp, [pk(g % 3)], [f'dP{g % 6}'], scale=scale)

                    def pv(m):
                        kt, s_ = seq[m]
                        pb = (base + m) % 6
                        hh = s_ // 2
                        c0 = hp * 192 + hh * 64
                        S.op('pe', lambda e: e.matmul(PS[3 + s_][:, 0:nq], lhsT=V[:, kt, c0:c0 + 128], rhs=Pb[pb][:, 0:nq], start=(kt == kts[0]), stop=(kt == kts[-1])),
                             reads=['dV', f'dP{pb}'], writes=[pk(3 + s_)])

                    for m in range(min(3, nseq)):
                        qk(m)
                    if pi + 1 < len(passes):
                        prep_q(pi + 1, passes[pi + 1][0], passes[pi + 1][1], passes[pi + 1][3])
                    for m in range(nseq):
                        ex(m)
                        pv(m)
                        if m + 3 < nseq:
                            qk(m + 3)
                        if pending and m % 2 == 1:
                            pending.pop(0)()
                    state['n'] = base + nseq
                    while pending:
                        pending.pop(0)()
                    fs = finalize(q0, nq, hp, pi % 2)
                    for f_ in fs[:4]:
                        f_()
                    pending.extend(fs[4:])

                prep_q(0, passes[0][0], passes[0][1], passes[0][3])
                for pi in range(len(passes)):
                    attn_pass(pi)
                while pending:
                    pending.pop(0)()
        S.barrier()

    def stageB(l):
        with ExitStack() as cx:
            W = sb("bW", [128, 8, 640], BF16, cx)
            stgs = [(sb(f"bstg{i}", [128, 8, 128], F32, cx), f'bstg{i}') for i in range(2)]
            Vt = sb("bV", [128, NT, 128], BF16, cx)
            OT = sb("bOT", [128, T], F32, cx)
            QTl = sb("bQT", [128, T], BF16, cx)
            KTl = sb("bKT", [128, T], BF16, cx)
            Kt = sb("bKt", [128, NT, 128], BF16, cx)
            Sb = sb("bSb", [128, NCH, 64], BF16, cx)
            KH = Sb[:].rearrange("p c e -> p (c e)")
            M0 = sb("bM0", [128, T], BF16, cx)
            Gc = sb("bGc", [128, T], F32, cx)
            Bt = Gc[:].rearrange("p (c e) -> p c e", e=64)
            T1 = [sb(f"bT1{i}", [128, 512], F32, cx) for i in range(2)]
            T2 = [sb(f"bT2{i}", [128, 512], F32, cx) for i in range(2)]
            sm = sb("bsm", [128, 6, NCH], F32, cx)
            SCm = [sb(f"bSC{i}", [128, 256], BF16, cx) for i in range(4)]
            osq = sb("bosq", [128, 512], BF16, cx)
            ors = sb("bors", [128, 512], F32, cx)
            oy = [sb(f"boy{i}", [128, 512], BF16, cx) for i in range(2)]
            hgn = sb("bhgn", [128, 1], F32, cx)
            S.dma('sp', hgn[:], hg_g[l], writes=['bhgn'])
            groups = [[0, 1]] + [list(range(2 + 8 * g, 10 + 8 * g)) for g in range(4)]
            for pc in range(2):
                for j in range(5):
                    load_w(W, l, w_in, 512 + j * 256 + pc * 128, 128, stgs, 'bW', off=j * 128, slab=128)
                S.op('pool', lambda e: e.memset(OT[:], 0.0), writes=['bOT'])
                for i in range(NT):
                    b = 6 + i % 2
                    proj_tm(PS[b][:, 0:128], b, W, 'bW', 128, 128, i)
                    S.op('dve', lambda e, b=b, i=i: e.tensor_copy(out=Vt[:, i, :], in_=PS[b][:, 0:128]), reads=[pk(b)], writes=['bV'])
                _ck(1)
                for dr in range(2):
                    li = (pc * 2 + dr)
                    lb_ap = LBt[:, 0, li, l:l + 1]
                    oml_ap = LBt[:, 1, li, l:l + 1]
                    S.op('pool', lambda e: e.memset(M0[:], 1.0), writes=['bM0'])
                    m3 = M0[:].rearrange("p (c j) -> p c j", j=64)
                    zc = 0 if dr == 0 else 63
                    S.op('pool', lambda e, zc=zc: e.memset(m3[:, :, zc:zc + 1], 0.0), writes=['bM0'])
                    for bi, (t0, n) in enumerate(BLK):
                        q = bi % 2
                        proj_fm(q, W, 'bW', (2 + dr) * 128, t0, n)
                        act(T1[q][:, 0:n], PS[q][:, 0:n], AF.Exp, [pk(q)], [f'bT1{q}'], scale=-1.0)
                        act(T2[q][:, 0:n], T1[q][:, 0:n], AF.Ln, [f'bT1{q}', 'LBt'], [f'bT2{q}'], bias=onec[:], scale=lb_ap)
                        act(T1[q][:, 0:n], T1[q][:, 0:n], AF.Ln, [f'bT1{q}'], [f'bT1{q}'], bias=onec[:], scale=1.0)
                        S.op('dve', lambda e, q=q, t0=t0, n=n: e.tensor_tensor(out=Gc[:, t0:t0 + n], in0=T2[q][:, 0:n], in1=T1[q][:, 0:n], op=ALU.subtract),
                             reads=[f'bT1{q}', f'bT2{q}'], writes=['bGc'])
                    if dr == 0:
                        S.op('dve', lambda e: e.tensor_tensor_scan(out=Gc[:], data0=M0[:], data1=Gc[:], initial=0.0, op0=ALU.mult, op1=ALU.add),
                             reads=['bGc', 'bM0'], writes=['bGc'])
                    else:
                        S.op('dve', lambda e: e.tensor_tensor_scan(out=Gc[:][:, ::-1], data0=M0[:][:, ::-1], data1=Gc[:][:, ::-1], initial=0.0, op0=ALU.mult, op1=ALU.add),
                             reads=['bGc', 'bM0'], writes=['bGc'])
                    _ck(2)
                    g3 = Gc[:].rearrange("p (c j) -> p c j", j=64)
                    mid = 31 if dr == 0 else 32
                    end = 63 if dr == 0 else 0
                    S.op('dve', lambda e: e.tensor_copy(out=sm[:, 0, :], in_=g3[:, :, mid]), reads=['bGc'], writes=['bsm'])
                    S.op('dve', lambda e: e.tensor_copy(out=sm[:, 1, :], in_=g3[:, :, end]), reads=['bGc'], writes=['bsm'])
                    S.op('dve', lambda e: e.tensor_tensor(out=sm[:, 5, :], in0=sm[:, 1, :], in1=sm[:, 0, :], op=ALU.subtract), reads=['bsm'], writes=['bsm'])
                    act(sm[:, 2, :], sm[:, 1, :], AF.Exp, ['bsm'], ['bsm'])
                    act(sm[:, 3, :], sm[:, 5, :], AF.Exp, ['bsm'], ['bsm'])
                    act(sm[:, 4, :], sm[:, 0, :], AF.Exp, ['bsm'], ['bsm'])
                    S.op('dve', lambda e: e.tensor_tensor(out=g3, in0=g3, in1=sm[:, 0, :].unsqueeze(2).broadcast_to([128, NCH, 64]), op=ALU.subtract),
                         reads=['bGc', 'bsm'], writes=['bGc'])
                    _ck(3)
                    for bi, (t0, n) in enumerate(BLK):
                        q = bi % 2
                        c0, nc_ = t0 // 64, n // 64
                        proj_fm(q, W, 'bW', 0, t0, n)
                        act(T1[q][:, 0:n], PS[q][:, 0:n], AF.Exp, [pk(q)], [f'bT1{q}'], scale=-1.0)
                        act(T1[q][:, 0:n], T1[q][:, 0:n], AF.Ln, [f'bT1{q}'], [f'bT1{q}'], bias=onec[:], scale=1.0)
                        S.op('dve', lambda e, q=q, t0=t0, n=n: e.tensor_tensor(out=T1[q][:, 0:n], in0=Gc[:, t0:t0 + n], in1=T1[q][:, 0:n], op=ALU.subtract),
                             reads=['bGc', f'bT1{q}'], writes=[f'bT1{q}'])
                        act(T1[q][:, 0:n], T1[q][:, 0:n], AF.Exp, [f'bT1{q}'], [f'bT1{q}'])
                        S.op('dve', lambda e, q=q, t0=t0, n=n: e.tensor_tensor(out=QTl[:, t0:t0 + n], in0=PS[q][:, 0:n], in1=T1[q][:, 0:n], op=ALU.mult),
                             reads=[pk(q), f'bT1{q}'], writes=['bQT'])
                        proj_fm(2 + q, W, 'bW', (2 + dr) * 128, t0, n)
                        act(T2[q][:, 0:n], PS[2 + q][:, 0:n], AF.Exp, [pk(2 + q)], [f'bT2{q}'])
                        act(T2[q][:, 0:n], T2[q][:, 0:n], AF.Ln, [f'bT2{q}'], [f'bT2{q}'], bias=onec[:], scale=1.0)
                        S.op('dve', lambda e, q=q, t0=t0, n=n: e.tensor_tensor(out=T2[q][:, 0:n], in0=Gc[:, t0:t0 + n], in1=T2[q][:, 0:n], op=ALU.add),
                             reads=['bGc', f'bT2{q}'], writes=[f'bT2{q}'])
                        act(T2[q][:, 0:n], T2[q][:, 0:n], AF.Exp, [f'bT2{q}'], [f'bT2{q}'], scale=-1.0)
                        S.op('dve', lambda e, q=q, t0=t0, n=n: e.tensor_scalar(out=KTl[:, t0:t0 + n], in0=T2[q][:, 0:n], scalar1=oml_ap, scalar2=None, op0=ALU.mult),
                             reads=[f'bT2{q}', 'LBt'], writes=['bKT'])
                        kv3 = KTl[:, t0:t0 + n].rearrange("p (c j) -> p c j", j=64)
                        kh3 = KH[:, t0:t0 + n].rearrange("p (c j) -> p c j", j=64)
                        S.op('dve', lambda e, kv3=kv3, kh3=kh3, c0=c0, nc_=nc_: e.tensor_tensor(out=kh3, in0=kv3, in1=sm[:, 3, c0:c0 + nc_].unsqueeze(2).broadcast_to([128, nc_, 64]), op=ALU.mult),
                             reads=['bKT', 'bsm'], writes=['bSb'])
                    _ck(4)
                    for i in range(NT):
                        b = 4 + i % 2
                        psb = PS[b][:].bitcast(BF16)
                        S.op('pe', lambda e, i=i, psb=psb: e.transpose(out=psb[:, 0:128], in_=KH[:, i * 128:(i + 1) * 128], identity=identb[:]),
                             reads=['bSb', 'identb'], writes=[pk(b)])
                        S.op('dve', lambda e, i=i, psb=psb: e.tensor_copy(out=Kt[:, i, :], in_=psb[:, 0:128]), reads=[pk(b)], writes=['bKt'])
                    _ck(5)
                    for g, tiles in enumerate(groups):
                        for ti, i in enumerate(tiles):
                            for cp in range(2):
                                bb = 2 * (g % 2) + cp
                                for h2 in range(2):
                                    S.op('pe', lambda e, ti=ti, i=i, cp=cp, h2=h2, bb=bb: e.matmul(PS[bb][64 * h2:64 * h2 + 64, ti * 64:(ti + 1) * 64],
                                                                                               lhsT=Kt[64 * cp:64 * cp + 64, i, 64 * h2:64 * h2 + 64],
                                                                                               rhs=Vt[64 * cp:64 * cp + 64, i, 64 * h2:64 * h2 + 64],
                                                                                               start=True, stop=True, tile_position=(64 * cp, 64 * h2)),
                                         reads=['bKt', 'bV'], writes=[pk(bb)], sig=(ti == len(tiles) - 1 and h2 == 1))
                        nt_ = len(tiles)
                        for cp in range(2):
                            bb = 2 * (g % 2) + cp
                            cfirst = 2 * tiles[0] + cp
                            if dr == 0:
                                dst = Bt[:, cfirst:cfirst + 2 * (nt_ - 1) + 1:2, :]
                            else:
                                pfirst = (3 - cfirst) if g == 0 else (71 - cfirst)
                                stop = pfirst - 2 * (nt_ - 1) - 1
                                dst = Bt[:, pfirst:(stop if stop >= 0 else None):-2, :]
                            S.op('dve', lambda e, bb=bb, dst=dst, nt_=nt_: e.tensor_copy(out=dst, in_=PS[bb][:, 0:nt_ * 64].rearrange("p (t e) -> p t e", e=64)),
                                 reads=[pk(bb)], writes=['bGc'])
                    _ck(6)
                    if dr == 0:
                        lam_ap = sm[:, 2, :]
                    else:
                        S.op('dve', lambda e: e.tensor_copy(out=sm[:, 5, 0:4], in_=sm[:, 2, 0:4][:, ::-1]), reads=['bsm'], writes=['bsm'])
                        S.op('dve', lambda e: e.tensor_copy(out=sm[:, 5, 4:NCH], in_=sm[:, 2, 4:NCH][:, ::-1]), reads=['bsm'], writes=['bsm'])
                        lam_ap = sm[:, 5, :]
                    for e_ in range(64):
                        S.op('dve', lambda e, e_=e_: e.tensor_tensor_scan(out=Bt[:, :, e_], data0=lam_ap, data1=Bt[:, :, e_], initial=0.0, op0=ALU.mult, op1=ALU.add),
                             reads=['bsm', 'bGc'], writes=['bGc'])
                    _ck(7)
                    rho = sm[:, 4, :]
                    if dr == 0:
                        S.op('dve', lambda e: e.memset(Sb[:, 0:1, :], 0.0), writes=['bSb'])
                        S.op('dve', lambda e: e.tensor_tensor(out=Sb[:, 1:NCH, :], in0=Bt[:, 0:NCH - 1, :], in1=rho[:, 1:NCH].unsqueeze(2).broadcast_to([128, NCH - 1, 64]), op=ALU.mult),
                             reads=['bGc', 'bsm'], writes=['bSb'])
                    else:
                        S.op('dve', lambda e: e.memset(Sb[:, 3:4, :], 0.0), writes=['bSb'])
                        S.op('dve', lambda e: e.tensor_tensor(out=Sb[:, 0:3, :], in0=Bt[:, 2::-1, :], in1=rho[:, 0:3].unsqueeze(2).broadcast_to([128, 3, 64]), op=ALU.mult),
                             reads=['bGc', 'bsm'], writes=['bSb'])
                        S.op('dve', lambda e: e.tensor_tensor(out=Sb[:, 4:NCH, :], in0=Bt[:, 66:2:-1, :], in1=rho[:, 4:NCH].unsqueeze(2).broadcast_to([128, NCH - 4, 64]), op=ALU.mult),
                             reads=['bGc', 'bsm'], writes=['bSb'])
                    _ck(8)
                    mk = maskt[:, 64 * dr:64 * dr + 64]
                    for tgi, tg in enumerate(range(0, NT, 4)):
                        tiles = list(range(tg, min(tg + 4, NT)))
                        nt_ = len(tiles)
                        q = tgi % 2
                        for ti, i in enumerate(tiles):
                            for cp in range(2):
                                c = 2 * i + cp
                                for h2 in range(2):
                                    bb = 2 * q + h2
                                    for jb in range(2):
                                        full = (jb == 0) if dr == 0 else (jb == 1)
                                        i0, ni = (0, 64) if full else ((32, 32) if dr == 0 else (0, 32))
                                        S.op('pe', lambda e, c=c, cp=cp, h2=h2, bb=bb, ti=ti, jb=jb, i0=i0, ni=ni: e.matmul(
                                            PS[bb][64 * cp + 32 * jb:64 * cp + 32 * jb + 32, ti * 64 + i0:ti * 64 + i0 + ni],
                                            lhsT=KTl[64 * h2:64 * h2 + 64, c * 64 + 32 * jb:c * 64 + 32 * jb + 32],
                                            rhs=QTl[64 * h2:64 * h2 + 64, c * 64 + i0:c * 64 + i0 + ni],
                                            start=True, stop=True, tile_position=(64 * h2, 64 * cp + 32 * jb)),
                                             reads=['bKT', 'bQT'], writes=[pk(bb)], sig=(ti == nt_ - 1 and cp == 1 and jb == 1))
                        for h2 in range(2):
                            bb = 2 * q + h2
                            scv = SCm[2 * q + h2][:, 0:nt_ * 64].rearrange("p (t i) -> p t i", i=64)
                            S.op('dve', lambda e, bb=bb, scv=scv, nt_=nt_: e.tensor_tensor(out=scv, in0=PS[bb][:, 0:nt_ * 64].rearrange("p (t i) -> p t i", i=64),
                                                                                       in1=mk.unsqueeze(1).broadcast_to([128, nt_, 64]), op=ALU.mult),
                                 reads=[pk(bb), 'maskt'], writes=[f'bSC{2 * q + h2}'])
                        for ti, i in enumerate(tiles):
                            for cp in range(2):
                                for h2 in range(2):
                                    S.op('pe', lambda e, cp=cp, h2=h2, ti=ti, i=i, q=q: e.matmul(PS[4 + cp][64 * h2:64 * h2 + 64, ti * 64:(ti + 1) * 64],
                                                                                             lhsT=Vt[64 * cp:64 * cp + 64, i, 64 * h2:64 * h2 + 64],
                                                                                             rhs=SCm[2 * q + h2][64 * cp:64 * cp + 64, ti * 64:(ti + 1) * 64],
                                                                                             start=True, stop=True, tile_position=(64 * cp, 64 * h2)),
                                         reads=['bV', f'bSC{2 * q + h2}'], writes=[pk(4 + cp)], sig=(ti == nt_ - 1 and h2 == 1))
                        for ti, i in enumerate(tiles):
                            for cp in range(2):
                                c = 2 * i + cp
                                for h2 in range(2):
                                    S.op('pe', lambda e, c=c, cp=cp, h2=h2, ti=ti: e.matmul(PS[6][64 * h2:64 * h2 + 64, (ti * 2 + cp) * 64:(ti * 2 + cp + 1) * 64],
                                                                                       lhsT=Sb[64 * h2:64 * h2 + 64, c, :],
                                                                                       rhs=QTl[64 * h2:64 * h2 + 64, c * 64:(c + 1) * 64],
                                                                                       start=True, stop=True, tile_position=(64 * h2, 64 * h2)),
                                         reads=['bSb', 'bQT'], writes=[pk(6)], sig=(ti == nt_ - 1 and cp == 1 and h2 == 1))
                        otv = OT[:, tg * 128:(tg + nt_) * 128].rearrange("p (t c i) -> p t c i", c=2, i=64)
                        for cp in range(2):
                            S.op('dve', lambda e, cp=cp, otv=otv, nt_=nt_: e.tensor_tensor(out=otv[:, :, cp, :], in0=PS[4 + cp][:, 0:nt_ * 64].rearrange("p (t i) -> p t i", i=64),
                                                                                       in1=otv[:, :, cp, :], op=ALU.add),
                                 reads=[pk(4 + cp), 'bOT'], writes=['bOT'])
                        S.op('dve', lambda e, tg=tg, nt_=nt_: e.tensor_tensor(out=OT[:, tg * 128:(tg + nt_) * 128], in0=PS[6][:, 0:nt_ * 128], in1=OT[:, tg * 128:(tg + nt_) * 128], op=ALU.add),
                             reads=[pk(6), 'bOT'], writes=['bOT'])
                _ck(9)
                for bi, (t0, n) in enumerate(BLK):
                    q = bi % 2
                    S.op('pool', lambda e, t0=t0, n=n: e.tensor_tensor(out=osq[:, 0:n], in0=OT[:, t0:t0 + n], in1=OT[:, t0:t0 + n], op=ALU.mult), reads=['bOT'], writes=['bosq'])
                    S.op('pe', lambda e, n=n: e.matmul(PS[4][:, 0:n], lhsT=bd64[:], rhs=osq[:, 0:n], start=True, stop=True), reads=['bd64', 'bosq'], writes=[pk(4)])
                    rstd_from_ss(ors[:, 0:n], PS[4][:, 0:n], 1.0 / 64, epsr[:], 'bors', pk(4))
                    proj_fm(5, W, 'bW', 4 * 128, t0, n)
                    act(T1[q][:, 0:n], PS[5][:, 0:n], AF.Exp, [pk(5)], [f'bT1{q}'], scale=-1.0)
                    act(T1[q][:, 0:n], T1[q][:, 0:n], AF.Ln, [f'bT1{q}'], [f'bT1{q}'], bias=onec[:], scale=1.0)
                    act(T1[q][:, 0:n], T1[q][:, 0:n], AF.Exp, [f'bT1{q}'], [f'bT1{q}'], scale=-1.0)
                    S.op('dve', lambda e, q=q, n=n: e.tensor_tensor(out=T1[q][:, 0:n], in0=PS[5][:, 0:n], in1=T1[q][:, 0:n], op=ALU.mult), reads=[pk(5), f'bT1{q}'], writes=[f'bT1{q}'])
                    S.op('dve', lambda e, t0=t0, n=n: e.tensor_tensor(out=ors[:, 0:n], in0=ors[:, 0:n], in1=OT[:, t0:t0 + n], op=ALU.mult), reads=['bors', 'bOT'], writes=['bors'])
                    S.op('dve', lambda e, q=q, n=n: e.scalar_tensor_tensor(out=oy[q][:, 0:n], in0=ors[:, 0:n], scalar=hgn[:, 0:1], in1=T1[q][:, 0:n], op0=ALU.mult, op1=ALU.mult),
                         reads=['bors', 'bhgn', f'bT1{q}'], writes=[f'boy{q}'])
                    S.dma('pool', ymix[256 + pc * 128:256 + (pc + 1) * 128, t0:t0 + n], oy[q][:, 0:n], reads=[f'boy{q}'], writes=['ymix'])
        S.barrier()

    S.barrier()
    prologue()
    for l in range(nlayers):
        last = (l == DEPTH - 1)
        if 'H' in stages and (l == 0 or 'O' not in stages):
            stageH(l)
        if 'A' in stages:
            stageA(l)
        if 'B' in stages:
            try:
                stageB(l)
            except _Stop:
                S.barrier()
        if 'C' in stages:
            stageC(l, last)
        if 'D' in stages:
            stageD(l, last)
        if 'O' in stages:
            stageO(l, last, fuse_next=(l + 1 < nlayers and 'H' in stages))
    S.barrier()
    if 'BSTOP' not in _os.environ:
        es.close()
    return nc


def _consts():
    ident = np.eye(128, dtype=np.float32).astype(ml_dtypes.bfloat16)
    n_freq = 8
    inv_freq = (10000.0 ** (-np.arange(n_freq, dtype=np.float32) / n_freq)).astype(np.float32)
    row = np.repeat(np.arange(64, dtype=np.float32), 64)
    col = np.tile(np.arange(64, dtype=np.float32), 64)
    ang = np.concatenate([row[:, None] * inv_freq, col[:, None] * inv_freq], axis=-1).astype(np.float32)
    cos, sin = np.cos(ang).astype(np.float32), np.sin(ang).astype(np.float32)
    c32 = np.concatenate([cos, cos], axis=1).T
    s32 = np.concatenate([-sin, sin], axis=1).T
    cosT = np.ascontiguousarray(np.tile(c32, (4, 1)))
    sinT = np.ascontiguousarray(np.tile(s32, (4, 1)))
    j = np.arange(64)[:, None]
    i = np.arange(64)[None, :]
    fwd = (j <= i).astype(np.float32)
    bwd = (j >= i).astype(np.float32)
    mask = np.concatenate([np.tile(fwd, (2, 1)), np.tile(bwd, (2, 1))], axis=1)
    bd = np.zeros((128, 128), np.float32)
    bd[:64, :64] = 1
    bd[64:, 64:] = 1
    return dict(c_ident=ident, c_cos=cosT, c_sin=sinT, c_mask=np.ascontiguousarray(mask), c_bd64=bd.astype(ml_dtypes.bfloat16))


def _layout(inp, b):
    f = lambda a: np.ascontiguousarray(np.asarray(a, dtype=np.float32))
    m = {}
    m["x_b"] = f(inp["x"][b])
    m["ctx_b"] = f(inp["ctx"][b])
    cc = np.stack([np.asarray(inp["c"][b]), np.asarray(inp["c_ctx"])], -1)
    m["cfm"] = f(cc.reshape(8, 128, 2).transpose(1, 0, 2).reshape(128, 16))
    m["w_mod"] = f(inp["w_mod"])
    m["b_mod"] = f(inp["b_mod"])
    m["b_mod_fm"] = f(np.asarray(inp["b_mod"]).reshape(DEPTH, 24, 128).transpose(0, 2, 1))
    m["g_pre_fm"] = f(np.asarray(inp["g_pre"]).reshape(DEPTH, 8, 128).transpose(0, 2, 1))
    m["g_post"] = f(inp["g_post"])
    m["w_in"] = f(inp["w_in"])
    m["w_out"] = f(inp["w_out"])
    m["lru_cw"] = f(np.asarray(inp["lru_conv_w"]).reshape(DEPTH, 4, 2, 128).transpose(0, 3, 2, 1).reshape(DEPTH, 128, 8))
    m["lru_cb"] = f(np.asarray(inp["lru_conv_b"]).reshape(DEPTH, 2, 128).transpose(0, 2, 1))
    wr, wi = np.asarray(inp["lru_w_r"]), np.asarray(inp["lru_w_i"])
    bd = np.zeros((DEPTH, 2, 4, 128, 128), np.float32)
    for pc in range(2):
        for dr in range(2):
            for wh, w in enumerate((wr, wi)):
                for h2 in range(2):
                    bd[:, pc, 2 * dr + wh, 64 * h2:64 * h2 + 64, 64 * h2:64 * h2 + 64] = w[:, dr, 2 * pc + h2]
    m["lru_bd"] = bd
    br, bi_ = np.asarray(inp["lru_b_r"]), np.asarray(inp["lru_b_i"])
    lb = np.stack([br, bi_], 1).reshape(DEPTH, 2, 2, 2, 128)
    m["lru_b"] = f(lb.transpose(0, 4, 1, 2, 3).reshape(DEPTH, 128, 8))
    m["lru_lam"] = f(np.asarray(inp["lru_lambda"]).reshape(DEPTH, 2, 2, 128).transpose(0, 3, 1, 2).reshape(DEPTH, 128, 4))
    hl = np.asarray(inp["hgrn_lb"]).reshape(DEPTH, 2, 2, 128)
    m["hg_lb"] = f(hl.transpose(3, 2, 1, 0).reshape(128, 16))
    m["hg_g"] = f(np.tile(np.asarray(inp["hgrn_norm_g"]), (1, 2)).reshape(DEPTH, 128, 1))
    m["cf_w"] = f(np.asarray(inp["conf_conv_w"]).reshape(DEPTH, 31, 2, 128).transpose(0, 3, 2, 1).reshape(DEPTH, 128, 62))
    cb = np.stack([np.asarray(inp["conf_conv_b"]), np.asarray(inp["conf_ln_g"]), np.asarray(inp["conf_ln_b"])], 1).reshape(DEPTH, 3, 2, 128)
    m["cf_b"] = f(cb.transpose(0, 3, 1, 2).reshape(DEPTH, 128, 6))
    m["df_lam"] = f(np.concatenate([np.asarray(inp[k]) for k in ("diff_lam_q1", "diff_lam_k1", "diff_lam_q2", "diff_lam_k2")], axis=1))
    m["df_g"] = f(np.tile(np.asarray(inp["diff_norm_g"]), (1, 2)).reshape(DEPTH, 128, 1))
    return m


def kernel(**inputs):
    n = 8
    nc = bass.Bass("TRN2", target_bir_lowering=False)
    build(nc)
    consts = _consts()
    in_maps = []
    for b in range(n):
        m = _layout(inputs, b)
        m.update(consts)
        in_maps.append(m)
    res = run_bass_kernel_spmd(nc, in_maps, core_ids=list(range(n)))
    return np.stack([np.asarray(r["out"], dtype=np.float32) for r in res.results], axis=0)
```

```python
import math
import numpy as np
import ml_dtypes
from contextlib import ExitStack
import concourse.bass as bass
import concourse.mybir as mybir
from concourse.bass_utils import run_bass_kernel_spmd

F32 = mybir.dt.float32
BF16 = mybir.dt.bfloat16
ALU = mybir.AluOpType
AF = mybir.ActivationFunctionType
AX = mybir.AxisListType

D = 1024
LC = 256
LL = 4096
T = LC + LL
NT = T // 128
DEPTH = 4
RMS_EPS = 1e-6
LN_EPS = 1e-5
BLK = [(0, 256)] + [(256 + 512 * j, 512) for j in range(8)]
NCH = T // 64


import os as _os
class _Stop(Exception):
    pass


def _ck(n):
    if int(_os.environ.get('BSTOP', '99')) == n:
        raise _Stop()


class Sched:
    ENG = ('pe', 'act', 'dve', 'pool', 'sp')

    def __init__(s, nc, es, ndsem=24):
        s.nc = nc
        s.eng = dict(pe=nc.tensor, act=nc.scalar, dve=nc.vector, pool=nc.gpsimd, sp=nc.sync)
        s.sem = {e: es.enter_context(nc.semaphore("sem_" + e)) for e in s.ENG}
        s.cnt = {e: 0 for e in s.ENG}
        s.seen = {e: {} for e in s.ENG}
        s.dsem = [es.enter_context(nc.semaphore(f"dsem{i}")) for i in range(ndsem)]
        s.dcnt = [0] * ndsem
        s.dnext = 0
        s.lastw = {}
        s.readers = {}
        s.unsig = False

    def _need(s, e, tok):
        kind, who, val = tok
        if kind == 'e' and who == e and e == 'pe':
            return
        key = (kind, who)
        if s.seen[e].get(key, 0) >= val:
            return
        if kind == 'e':
            assert val <= s.cnt[who], f"wait on unsignaled op {tok} cnt={s.cnt[who]}"
            s.eng[e].wait_ge(s.sem[who], val)
        else:
            s.eng[e].wait_ge(s.dsem[who], val)
        s.seen[e][key] = val

    def _deps(s, e, reads, writes):
        for r in reads:
            t = s.lastw.get(r)
            if t is not None:
                s._need(e, t)
        for w in writes:
            t = s.lastw.get(w)
            if t is not None:
                s._need(e, t)
            for (k, who), val in s.readers.get(w, {}).items():
                s._need(e, (k, who, val))

    def _reg(s, tok, reads, writes):
        for r in reads:
            d = s.readers.setdefault(r, {})
            key = (tok[0], tok[1])
            if d.get(key, 0) < tok[2]:
                d[key] = tok[2]
        for w in writes:
            s.lastw[w] = tok
            s.readers[w] = {}

    def op(s, e, fn, reads=(), writes=(), sig=True):
        s._deps(e, reads, writes)
        ins = fn(s.eng[e])
        if sig:
            s.cnt[e] += 1
            ins.then_inc(s.sem[e], 1)
            tok = ('e', e, s.cnt[e])
            if e == 'pe':
                s.unsig = False
        else:
            assert e == 'pe'
            tok = ('e', e, s.cnt[e] + 1)
            s.unsig = True
        s._reg(tok, reads, writes)

    def dma(s, q, out, in_, reads=(), writes=()):
        s._deps(q, reads, writes)
        i = s.dnext
        s.dnext = (s.dnext + 1) % len(s.dsem)
        if s.dcnt[i] > 0:
            s._need(q, ('d', i, 16 * s.dcnt[i]))
        s.dcnt[i] += 1
        s.eng[q].dma_start(out=out, in_=in_).then_inc(s.dsem[i], 16)
        s._reg(('d', i, 16 * s.dcnt[i]), reads, writes)

    def barrier(s):
        assert not s.unsig
        toks = [('e', e, s.cnt[e]) for e in s.ENG if s.cnt[e] > 0]
        toks += [('d', i, 16 * s.dcnt[i]) for i in range(len(s.dsem)) if s.dcnt[i] > 0]
        for e in s.ENG:
            for t in toks:
                s._need(e, t)
        s.lastw.clear()
        s.readers.clear()


def build(nc, nlayers=DEPTH, dbg=False, stages="HABCDO"):
    es = ExitStack()
    S = Sched(nc, es)

    def din(name, shape, dt=F32):
        return nc.dram_tensor(name, list(shape), dt, kind="ExternalInput").ap()

    x_in = din("x_b", [LL, D])
    c_in = din("ctx_b", [LC, D])
    cfm = din("cfm", [128, 16])
    w_mod = din("w_mod", [DEPTH, D, 3 * D])
    b_mod = din("b_mod", [DEPTH, 3 * D])
    b_mod_fm = din("b_mod_fm", [DEPTH, 128, 24])
    g_pre_fm = din("g_pre_fm", [DEPTH, 128, 8])
    g_post = din("g_post", [DEPTH, D])
    w_in = din("w_in", [DEPTH, D, 3584])
    w_out = din("w_out", [DEPTH, D, D])
    lru_cw = din("lru_cw", [DEPTH, 128, 8])
    lru_cb = din("lru_cb", [DEPTH, 128, 2])
    lru_bd = din("lru_bd", [DEPTH, 2, 4, 128, 128])
    lru_b = din("lru_b", [DEPTH, 128, 8])
    lru_lam = din("lru_lam", [DEPTH, 128, 4])
    hg_lb = din("hg_lb", [128, 16])
    hg_g = din("hg_g", [DEPTH, 128, 1])
    cf_w = din("cf_w", [DEPTH, 128, 62])
    cf_b = din("cf_b", [DEPTH, 128, 6])
    df_lam = din("df_lam", [DEPTH, 128])
    df_g = din("df_g", [DEPTH, 128, 1])
    c_ident = din("c_ident", [128, 128], BF16)
    c_cos = din("c_cos", [128, LL])
    c_sin = din("c_sin", [128, LL])
    c_mask = din("c_mask", [128, 128])
    c_bd64 = din("c_bd64", [128, 128], BF16)

    out = nc.dram_tensor("out", [LL, D], F32, kind="ExternalOutput").ap()
    xcs = nc.dram_tensor("xcs", [LC, D], F32).ap()
    ymix = nc.dram_tensor("ymix", [D, T], BF16, kind="ExternalOutput" if dbg else "Internal").ap()
    ggd = nc.dram_tensor("ggd", [DEPTH, 2, D], F32, kind="ExternalOutput" if dbg else "Internal").ap()

    uid = [0]

    def sb(name, shape, dt=F32, ctx=None):
        uid[0] += 1
        return (ctx or es).enter_context(nc.sbuf_tensor(f"{name}_{uid[0]}", list(shape), dt))

    PS = [es.enter_context(nc.psum_tensor(f"ps{i}", [128, 512], F32)) for i in range(8)]

    def pk(i):
        return ('ps', i)

    hT = sb("hT", [128, 8, T], BF16)
    identb = sb("identb", [128, 128], BF16)
    bd64 = sb("bd64", [128, 128], BF16)
    ones64 = sb("ones64", [128, 64], BF16)
    ones256 = sb("ones256", [128, 128], BF16)
    maskt = sb("maskt", [128, 128], F32)
    GS = sb("GS", [128, DEPTH, 4, 8], F32)
    LBt = sb("LBt", [128, 3, 4, 4], F32)
    epsr = sb("epsr", [128, 1], F32)
    epsl = sb("epsl", [128, 1], F32)
    onec = sb("onec", [128, 1], F32)

    S.dma('sp', identb[:], c_ident[:, :], writes=['identb'])
    S.dma('sp', bd64[:], c_bd64[:, :], writes=['bd64'])
    S.dma('sp', maskt[:], c_mask[:, :], writes=['maskt'])
    S.op('dve', lambda e: e.memset(ones64[:], 1.0), writes=['ones64'])
    S.op('dve', lambda e: e.memset(ones256[:], 1.0 / 256.0), writes=['ones256'])
    S.op('dve', lambda e: e.memset(epsr[:], RMS_EPS), writes=['epsr'])
    S.op('dve', lambda e: e.memset(epsl[:], LN_EPS), writes=['epsl'])
    S.op('dve', lambda e: e.memset(onec[:], 1.0), writes=['onec'])
    for i_ in range(8):
        S.op('dve', lambda e, i_=i_: e.memset(PS[i_][:, :], 0.0), writes=[pk(i_)])

    def act(out_, in_, func, reads, writes, bias=None, scale=None):
        kw = {}
        if bias is not None:
            kw['bias'] = bias
        if scale is not None:
            kw['scale'] = scale
        S.op('act', lambda e: e.activation(out=out_, in_=in_, func=func, **kw), reads=reads, writes=writes)

    def rstd_from_ss(rs, ss, n_inv, eps_ap, key_rs, key_ss):
        act(rs, ss, AF.Ln, [key_ss], [key_rs], bias=eps_ap, scale=n_inv)
        act(rs, rs, AF.Exp, [key_rs], [key_rs], scale=-0.5)

    def prologue():
        with ExitStack() as cx:
            cf = sb("cf", [128, 16], F32, cx)
            sc = sb("sc", [128, 16], F32, cx)
            rep = sb("rep", [128, 2, 8, 128], F32, cx)
            wms = [sb(f"wm{i}", [128, 8, 512], F32, cx) for i in range(2)]
            bfm = sb("bfm", [128, 24], F32, cx)
            gpf = sb("gpf", [128, 8], F32, cx)
            bg = sb("bg", [128, 1024], F32, cx)
            gp = sb("gp", [128, 1024], F32, cx)
            tmpg = sb("tmpg", [128, 1024], F32, cx)
            tmp = sb("ptmp", [128, 16], F32, cx)
            hl = sb("hl", [128, 16], F32, cx)
            hs = sb("hs", [128, 4], F32, cx)
            S.dma('sp', cf[:], cfm[:, :], writes=['cf'])
            S.dma('sp', hl[:], hg_lb[:, :], writes=['hl'])
            act(sc[:], cf[:], AF.Silu, ['cf'], ['sc'])
            for j in range(2):
                src = sc[:].rearrange("p (k j) -> p k j", j=2)[:, :, j:j + 1].broadcast_to([128, 8, 128])
                S.op('dve', lambda e, j=j, src=src: e.tensor_copy(out=rep[:, j, :, :], in_=src), reads=['sc'], writes=['rep'])
            act(hl[:], hl[:], AF.Exp, ['hl'], ['hl'])
            hl3 = hl[:].rearrange("p (a l) -> p a l", l=4)
            S.op('dve', lambda e: e.reduce_sum(out=hs[:], in_=hl3, axis=AX.X), reads=['hl'], writes=['hs'])
            S.op('dve', lambda e: e.reciprocal(out=hs[:], in_=hs[:]), reads=['hs'], writes=['hs'])
            S.op('dve', lambda e: e.tensor_tensor(out=hl3, in0=hl3, in1=hs[:].unsqueeze(2).broadcast_to([128, 4, 4]), op=ALU.mult),
                 reads=['hl', 'hs'], writes=['hl'])
            S.op('dve', lambda e: e.memset(LBt[:, 0, :, 0:1], 0.0), writes=['LBt'])
            S.op('dve', lambda e: e.tensor_copy(out=LBt[:, 0, :, 1:2], in_=hl3[:, :, 1:2]), reads=['hl', 'LBt'], writes=['LBt'])
            for l in (2, 3):
                S.op('dve', lambda e, l=l: e.tensor_tensor(out=LBt[:, 0, :, l:l + 1], in0=LBt[:, 0, :, l - 1:l], in1=hl3[:, :, l:l + 1], op=ALU.add),
                     reads=['hl', 'LBt'], writes=['LBt'])
            S.op('dve', lambda e: e.tensor_scalar(out=LBt[:, 1, :, :], in0=LBt[:, 0, :, :], scalar1=-1.0, scalar2=1.0, op0=ALU.mult, op1=ALU.add),
                 reads=['LBt'], writes=['LBt'])
            S.op('dve', lambda e: e.tensor_scalar(out=LBt[:, 2, :, :], in0=LBt[:, 1, :, :], scalar1=-1.0, scalar2=None, op0=ALU.mult),
                 reads=['LBt'], writes=['LBt'])
            for l in range(nlayers):
                S.dma('sp', bfm[:], b_mod_fm[l], writes=['bfm'])
                S.dma('sp', gpf[:], g_pre_fm[l], writes=['gpf'])
                S.dma('sp', bg[:], b_mod[l:l + 1, 2048:3072].partition_broadcast(128), writes=['bg'])
                S.dma('sp', gp[:], g_post[l:l + 1, :].partition_broadcast(128), writes=['gp'])
                wv = w_mod[l].rearrange("(k p) n -> p k n", p=128)
                for sl in range(6):
                    wm = wms[sl % 2]
                    wk = f'wm{sl % 2}'
                    S.dma('sp', wm[:], wv[:, :, sl * 512:(sl + 1) * 512], writes=[wk])
                    if sl < 4:
                        for c in range(4):
                            cc = sl * 4 + c
                            for k in range(8):
                                S.op('pe', lambda e, k=k, c=c, cc=cc, wm=wm: e.matmul(PS[0][:, 2 * cc:2 * cc + 2], lhsT=wm[:, k, c * 128:(c + 1) * 128],
                                                                                   rhs=sc[:, 2 * k:2 * k + 2], start=(k == 0), stop=(k == 7)),
                                     reads=[wk, 'sc'], writes=[pk(0)], sig=(k == 7))
                    else:
                        half = sl - 4
                        for j in range(2):
                            for k in range(8):
                                S.op('pe', lambda e, k=k, j=j, half=half, wm=wm: e.matmul(PS[1 + 2 * j + half][:, :], lhsT=rep[:, j, k, :], rhs=wm[:, k, :],
                                                                                        start=(k == 0), stop=(k == 7)),
                                     reads=[wk, 'rep'], writes=[pk(1 + 2 * j + half)], sig=(k == 7))
                psv = PS[0][:, 0:32].rearrange("p (w k j) -> p w k j", w=2, k=8, j=2)
                for j in range(2):
                    S.op('dve', lambda e, j=j: e.tensor_tensor(out=GS[:, l, 1 + 2 * j, :], in0=psv[:, 0, :, j], in1=bfm[:, 0:8], op=ALU.add),
                         reads=[pk(0), 'bfm'], writes=['GS'])
                    S.op('dve', lambda e, j=j: e.tensor_tensor(out=tmp[:, 0:8], in0=psv[:, 1, :, j], in1=bfm[:, 8:16], op=ALU.add),
                         reads=[pk(0), 'bfm'], writes=['ptmp'])
                    S.op('dve', lambda e, j=j: e.scalar_tensor_tensor(out=GS[:, l, 2 * j, :], in0=tmp[:, 0:8], scalar=1.0, in1=gpf[:], op0=ALU.add, op1=ALU.mult),
                         reads=['ptmp', 'gpf'], writes=['GS'])
                for j in range(2):
                    for half in range(2):
                        S.op('dve', lambda e, j=j, half=half: e.tensor_tensor(out=tmpg[:, half * 512:(half + 1) * 512], in0=PS[1 + 2 * j + half][:, :],
                                                                            in1=bg[:, half * 512:(half + 1) * 512], op=ALU.add),
                             reads=[pk(1 + 2 * j + half), 'bg'], writes=['tmpg'])
                    S.op('dve', lambda e: e.tensor_tensor(out=tmpg[:], in0=tmpg[:], in1=gp[:], op=ALU.mult), reads=['tmpg', 'gp'], writes=['tmpg'])
                    S.dma('pool', ggd[l, j:j + 1, :], tmpg[0:1, :], reads=['tmpg'], writes=['ggd'])
        S.barrier()

    wslab = {}

    def wkeys(wkey, c0, n):
        if wkey not in wslab:
            return [wkey]
        sl = wslab[wkey]
        return [(wkey, j) for j in range(c0 // sl, (c0 + n - 1) // sl + 1)]

    def load_w(dst, l, src, col0, ncols, stgs, dkey, off=0, slab=256):
        wv = src[l].rearrange("(k p) n -> p k n", p=128)
        i = 0
        for c in range(0, ncols, slab):
            n = min(slab, ncols - c)
            st, sk = stgs[i % len(stgs)]
            i += 1
            S.dma('sp', st[:, :, 0:n], wv[:, :, col0 + c:col0 + c + n], writes=[sk])
            wslab[dkey] = slab
            dk_ = (dkey, (off + c) // slab)
            if i % 2 == 0:
                S.op('dve', lambda e, st=st, c=c, n=n: e.tensor_copy(out=dst[:, :, off + c:off + c + n], in_=st[:, :, 0:n]), reads=[sk], writes=[dk_])
            else:
                act(dst[:, :, off + c:off + c + n], st[:, :, 0:n], AF.Identity, [sk], [dk_])

    def proj_fm(b, W, wkey, wc0, t0, n, wn=128):
        for k in range(8):
            S.op('pe', lambda e, k=k: e.matmul(PS[b][0:wn, 0:n], lhsT=W[:, k, wc0:wc0 + wn], rhs=hT[:, k, t0:t0 + n], start=(k == 0), stop=(k == 7)),
                 reads=wkeys(wkey, wc0, wn) + ['hT'], writes=[pk(b)], sig=(k == 7))

    def proj_tm(ps_ap, b, W, wkey, wc0, ncols, tile):
        for k in range(8):
            S.op('pe', lambda e, k=k: e.matmul(ps_ap, lhsT=hT[:, k, tile * 128:(tile + 1) * 128], rhs=W[:, k, wc0:wc0 + ncols], start=(k == 0), stop=(k == 7)),
                 reads=wkeys(wkey, wc0, ncols) + ['hT'], writes=[pk(b)], sig=(k == 7))

    def res_src(l, i):
        if i < 2:
            base = c_in if l == 0 else xcs
            return base[i * 128:(i + 1) * 128, :]
        base = x_in if l == 0 else out
        return base[(i - 2) * 128:(i - 1) * 128, :]

    def res_dst(i):
        if i < 2:
            return xcs[i * 128:(i + 1) * 128, :]
        return out[(i - 2) * 128:(i - 1) * 128, :]

    def stageH(l):
        with ExitStack() as cx:
            xts = [sb(f"hx{i}", [128, 1024], F32, cx) for i in range(2)]
            sq = sb("hsq", [128, 1024], F32, cx)
            xss = [sb(f"hxs{i}", [128, 1024], BF16, cx) for i in range(2)]
            st = sb("hst", [128, 4], F32, cx)
            for i in range(NT):
                p = i % 2
                xt, xs = xts[p], xss[p]
                jj = 0 if i >= 2 else 2
                S.dma('sp', xt[:], res_src(l, i), reads=[('res', i)], writes=[f'hx{p}'])
                act(sq[:], xt[:], AF.Square, [f'hx{p}'], ['hsq'])
                S.op('dve', lambda e, p=p: e.reduce_sum(out=st[:, p:p + 1], in_=sq[:], axis=AX.X), reads=['hsq'], writes=[f'hss{p}'])
                rstd_from_ss(st[:, 2 + p:3 + p], st[:, p:p + 1], 1.0 / D, epsr[:], f'hrs{p}', f'hss{p}')
                S.op('dve', lambda e, p=p, xt=xt, xs=xs: e.tensor_scalar(out=xs[:], in0=xt[:], scalar1=st[:, 2 + p:3 + p], scalar2=None, op0=ALU.mult),
                     reads=[f'hx{p}', f'hrs{p}'], writes=[f'hxs{p}'])
                b = 6 + p
                psb = PS[b][:].bitcast(BF16)
                for k in range(8):
                    S.op('pe', lambda e, k=k, xs=xs, psb=psb: e.transpose(out=psb[:, k * 128:(k + 1) * 128], in_=xs[:, k * 128:(k + 1) * 128], identity=identb[:]),
                         reads=[f'hxs{p}', 'identb'], writes=[pk(b)], sig=(k == 7))
                for k in range(8):
                    S.op('dve', lambda e, k=k, psb=psb, i=i, jj=jj: e.tensor_scalar(out=hT[:, k, i * 128:(i + 1) * 128], in0=psb[:, k * 128:(k + 1) * 128],
                                                                             scalar1=GS[:, l, jj, k:k + 1], scalar2=GS[:, l, jj + 1, k:k + 1],
                                                                             op0=ALU.mult, op1=ALU.add),
                         reads=[pk(b), 'GS'], writes=['hT'])
        S.barrier()

    def stageO(l, last, fuse_next=False):
        with ExitStack() as cx:
            wo = sb("wo", [128, 8, 1024], BF16, cx)
            stgs = [(sb(f"ostg{i}", [128, 8, 256], F32, cx), f'ostg{i}') for i in range(2)]
            GG = [sb(f"GG{j}", [128, 1024], F32, cx) for j in range(2)]
            yms = [sb(f"oym{i}", [128, 8, 512], BF16, cx) for i in range(2)]
            xts = [sb(f"ox{i}", [128, 1024], F32, cx) for i in range(2)]
            sq = sb("osq", [128, 1024], F32, cx)
            tts = [sb(f"ot{i}", [128, 1024], F32, cx) for i in range(2)]
            st = sb("ost", [128, 8], F32, cx)
            if fuse_next:
                sq2 = sb("osq2", [128, 1024], F32, cx)
                xss = [sb(f"oxs{i}", [128, 1024], BF16, cx) for i in range(2)]
            load_w(wo, l, w_out, 0, 1024, stgs, 'wo')
            for j in range(2):
                S.dma('sp', GG[j][:], ggd[l, j:j + 1, :].partition_broadcast(128), reads=['ggd'], writes=[f'GG{j}'])
            ymv = ymix.rearrange("(k p) t -> p k t", p=128)
            it = 0
            hq = [None, None]
            for bi, (t0, n) in enumerate(BLK):
                if last and bi == 0:
                    continue
                ym = yms[bi % 2]
                yk = f'oym{bi % 2}'
                S.dma('sp', ym[:, :, 0:n], ymv[:, :, t0:t0 + n], reads=['ymix'], writes=[yk])
                for tt in range(n // 128):
                    i = (t0 // 128) + tt
                    j = 0 if i >= 2 else 1
                    p = it % 2
                    it += 1
                    xt, tq = xts[p], tts[p]
                    S.dma('sp', xt[:], res_src(l, i), reads=[('res', i)], writes=[f'ox{p}'])
                    for half in range(2):
                        b = 2 * p + half
                        for k in range(8):
                            S.op('pe', lambda e, k=k, b=b, half=half, ym=ym, tt=tt: e.matmul(PS[b][:, :], lhsT=ym[:, k, tt * 128:(tt + 1) * 128],
                                                                                       rhs=wo[:, k, half * 512:(half + 1) * 512], start=(k == 0), stop=(k == 7)),
                                 reads=[yk] + wkeys('wo', half * 512, 512), writes=[pk(b)], sig=(k == 7))
                        act(sq[:, half * 512:(half + 1) * 512], PS[b][:, :], AF.Square, [pk(b)], ['osq'])
                    S.op('dve', lambda e, p=p: e.reduce_sum(out=st[:, p:p + 1], in_=sq[:], axis=AX.X), reads=['osq'], writes=[f'oss{p}'])
                    rstd_from_ss(st[:, 2 + p:3 + p], st[:, p:p + 1], 1.0 / D, epsr[:], f'ors{p}', f'oss{p}')
                    for half in range(2):
                        b = 2 * p + half
                        S.op('dve', lambda e, b=b, half=half, tq=tq, p=p, j=j: e.scalar_tensor_tensor(out=tq[:, half * 512:(half + 1) * 512], in0=PS[b][:, :],
                                                                                                scalar=st[:, 2 + p:3 + p], in1=GG[j][:, half * 512:(half + 1) * 512],
                                                                                                op0=ALU.mult, op1=ALU.mult),
                             reads=[pk(b), f'ors{p}', f'GG{j}'], writes=[f'ot{p}'])
                    S.op('pool', lambda e, tq=tq, xt=xt: e.tensor_tensor(out=tq[:], in0=tq[:], in1=xt[:], op=ALU.add), reads=[f'ot{p}', f'ox{p}'], writes=[f'ot{p}'])
                    S.dma('pool', res_dst(i), tq[:], reads=[f'ot{p}'], writes=[('res', i)])
                    if fuse_next:
                        def hpart(i=i, p=p, tq=tq):
                            ln = l + 1
                            jj = 0 if i >= 2 else 2
                            xs = xss[p]
                            act(sq2[:], tq[:], AF.Square, [f'ot{p}'], ['osq2'])
                            S.op('dve', lambda e, p=p: e.reduce_sum(out=st[:, 4 + p:5 + p], in_=sq2[:], axis=AX.X), reads=['osq2'], writes=[f'oss2{p}'])
                            rstd_from_ss(st[:, 6 + p:7 + p], st[:, 4 + p:5 + p], 1.0 / D, epsr[:], f'ors2{p}', f'oss2{p}')
                            act(xs[:], tq[:], AF.Identity, [f'ot{p}', f'ors2{p}'], [f'oxs{p}'], scale=st[:, 6 + p:7 + p])
                        def hpartB(i=i, p=p):
                            ln = l + 1
                            jj = 0 if i >= 2 else 2
                            xs = xss[p]
                            for k in range(8):
                                b = (4 + p) if k < 4 else (6 + p)
                                psb = PS[b][:].bitcast(BF16)
                                S.op('pe', lambda e, k=k, xs=xs, psb=psb: e.transpose(out=psb[:, (k % 4) * 128:(k % 4 + 1) * 128], in_=xs[:, k * 128:(k + 1) * 128], identity=identb[:]),
                                     reads=[f'oxs{p}', 'identb'], writes=[pk(b)], sig=(k % 4 == 3))
                            for k in range(8):
                                b = (4 + p) if k < 4 else (6 + p)
                                psb = PS[b][:].bitcast(BF16)
                                if k < 4:
                                    act(hT[:, k, i * 128:(i + 1) * 128], psb[:, (k % 4) * 128:(k % 4 + 1) * 128], AF.Identity, [pk(b), 'GS'], [('hTa', k)],
                                        bias=GS[:, ln, jj + 1, k:k + 1], scale=GS[:, ln, jj, k:k + 1])
                                else:
                                    S.op('dve', lambda e, k=k, psb=psb, i=i, jj=jj, ln=ln: e.tensor_scalar(out=hT[:, k, i * 128:(i + 1) * 128], in0=psb[:, (k % 4) * 128:(k % 4 + 1) * 128],
                                                                                                scalar1=GS[:, ln, jj, k:k + 1], scalar2=GS[:, ln, jj + 1, k:k + 1],
                                                                                                op0=ALU.mult, op1=ALU.add),
                                         reads=[pk(b), 'GS'], writes=[('hTd', k)])
                        if hq[1] is not None:
                            hq[1]()
                            hq[1] = None
                        if hq[0] is not None:
                            hq[0][0]()
                            hq[1] = hq[0][1]
                        hq[0] = (hpart, hpartB)
            if fuse_next:
                if hq[1] is not None:
                    hq[1]()
                if hq[0] is not None:
                    hq[0][0]()
                    hq[0][1]()
        S.barrier()

    NG = T + 3

    def stageA(l):
        with ExitStack() as cx:
            W = sb("aW", [128, 8, 512], BF16, cx)
            stgs = [(sb(f"astg{i}", [128, 8, 256], F32, cx), f'astg{i}') for i in range(2)]
            bdst = sb("abdst", [128, 4, 128], F32, cx)
            bd = sb("abd", [128, 4, 128], BF16, cx)
            cw = sb("acw", [128, 8], F32, cx)
            cb = sb("acb", [128, 2], F32, cx)
            lbias = sb("alb", [128, 8], F32, cx)
            lam = sb("alam", [128, 4], F32, cx)
            B = [sb(f"aB{i}", [128, NG + 3], F32, cx) for i in range(5)]
            XCb = sb("aXCb", [128, NG], BF16, cx)
            SGt = sb("aSG", [128, T], BF16, cx)
            Yb = XCb
            load_w(W, l, w_in, 0, 512, stgs, 'aW')
            S.dma('sp', cw[:], lru_cw[l], writes=['acw'])
            S.dma('sp', cb[:], lru_cb[l], writes=['acb'])
            S.dma('sp', lbias[:], lru_b[l], writes=['alb'])
            S.dma('sp', lam[:], lru_lam[l], writes=['alam'])
            act(lam[:], lam[:], AF.Exp, ['alam'], ['alam'], scale=-1.0)
            act(lam[:], lam[:], AF.Ln, ['alam'], ['alam'], bias=onec[:], scale=1.0)
            S.op('dve', lambda e: e.tensor_scalar(out=lam[:], in0=lam[:], scalar1=-8.0, scalar2=None, op0=ALU.mult), reads=['alam'], writes=['alam'])
            for pc in range(2):
                UX, XC = B[0], B[4]
                for (a, b_) in ((0, 2), (258, 261), (NG + 2, NG + 3)):
                    S.op('dve', lambda e, a=a, b_=b_: e.memset(UX[:, a:b_], 0.0), writes=['aB0'])
                for bi, (t0, n) in enumerate(BLK):
                    ux0 = 2 + t0 if t0 < 256 else 261 + (t0 - 256)
                    b = bi % 2
                    proj_fm(b, W, 'aW', pc * 128, t0, n)
                    S.op('dve', lambda e, b=b, ux0=ux0, n=n: e.tensor_copy(out=UX[:, ux0:ux0 + n], in_=PS[b][:, 0:n]), reads=[pk(b)], writes=['aB0'])
                    b2 = 2 + bi % 2
                    proj_fm(b2, W, 'aW', 256 + pc * 128, t0, n)
                    act(SGt[:, t0:t0 + n], PS[b2][:, 0:n], AF.Silu, [pk(b2)], ['aSG'])
                S.op('dve', lambda e: e.tensor_scalar(out=XC[:, 0:NG], in0=UX[:, 0:NG], scalar1=cw[:, pc * 4:pc * 4 + 1], scalar2=cb[:, pc:pc + 1],
                                                      op0=ALU.mult, op1=ALU.add), reads=['aB0', 'acw', 'acb'], writes=['aB4'])
                for k in range(1, 4):
                    S.op('dve', lambda e, k=k: e.scalar_tensor_tensor(out=XC[:, 0:NG], in0=UX[:, k:k + NG], scalar=cw[:, pc * 4 + k:pc * 4 + k + 1], in1=XC[:, 0:NG],
                                                                      op0=ALU.mult, op1=ALU.add), reads=['aB0', 'aB4', 'acw'], writes=['aB4'])
                S.op('pool', lambda e: e.tensor_copy(out=XCb[:], in_=XC[:, 0:NG]), reads=['aB4'], writes=['aXCb'])
                S.dma('sp', bdst[:], lru_bd[l, pc].rearrange("w a b -> a w b"), writes=['abdst'])
                S.op('pool', lambda e: e.tensor_copy(out=bd[:], in_=bdst[:]), reads=['abdst'], writes=['abd'])
                for dr in range(2):
                    R, I, A = B[0], B[1], B[2]
                    for gi, g0 in enumerate(range(0, NG, 512)):
                        n = min(512, NG - g0)
                        for wh, (dst, dk) in enumerate(((R, 'aB0'), (I, 'aB1'))):
                            b = 2 * (gi % 2) + wh
                            S.op('pe', lambda e, b=b, wh=wh, g0=g0, n=n: e.matmul(PS[b][:, 0:n], lhsT=bd[:, 2 * dr + wh, :], rhs=XCb[:, g0:g0 + n], start=True, stop=True),
                                 reads=['abd', 'aXCb'], writes=[pk(b)])
                            bi_ = wh * 4 + dr * 2 + pc
                            act(dst[:, g0:g0 + n], PS[b][:, 0:n], AF.Sigmoid, [pk(b), 'alb'], [dk], bias=lbias[:, bi_:bi_ + 1], scale=1.0)
                    ci = dr * 2 + pc
                    act(A[:, 0:NG], R[:, 0:NG], AF.Exp, ['aB0', 'alam'], ['aB2'], scale=lam[:, ci:ci + 1])
                    act(R[:, 0:NG], A[:, 0:NG], AF.Square, ['aB2'], ['aB0'])
                    act(R[:, 0:NG], R[:, 0:NG], AF.Ln, ['aB0'], ['aB0'], bias=onec[:], scale=-1.0)
                    act(R[:, 0:NG], R[:, 0:NG], AF.Exp, ['aB0'], ['aB0'], scale=0.5)
                    S.op('dve', lambda e: e.tensor_tensor(out=I[:, 0:NG], in0=I[:, 0:NG], in1=XC[:, 0:NG], op=ALU.mult), reads=['aB1', 'aB4'], writes=['aB1'])
                    S.op('dve', lambda e: e.tensor_tensor(out=I[:, 0:NG], in0=I[:, 0:NG], in1=R[:, 0:NG], op=ALU.mult), reads=['aB1', 'aB0'], writes=['aB1'])
                    H, hk = (B[3], 'aB3') if dr == 0 else (B[0], 'aB0')
                    if dr == 0:
                        S.op('dve', lambda e, H=H: e.tensor_tensor_scan(out=H[:, 0:256], data0=A[:, 0:256], data1=I[:, 0:256], initial=0.0, op0=ALU.mult, op1=ALU.add),
                             reads=['aB2', 'aB1'], writes=[hk])
                        S.op('dve', lambda e, H=H: e.tensor_tensor_scan(out=H[:, 259:NG], data0=A[:, 259:NG], data1=I[:, 259:NG], initial=H[:, 255:256],
                                                                         op0=ALU.mult, op1=ALU.add), reads=['aB2', 'aB1', hk], writes=[hk])
                    else:
                        rv = lambda X, a, b_: X[:, a:b_][:, ::-1]
                        S.op('dve', lambda e, H=H: e.tensor_tensor_scan(out=rv(H, 0, 256), data0=rv(A, 0, 256), data1=rv(I, 0, 256), initial=0.0, op0=ALU.mult, op1=ALU.add),
                             reads=['aB2', 'aB1'], writes=[hk])
                        S.op('dve', lambda e, H=H: e.tensor_tensor_scan(out=rv(H, 259, NG), data0=rv(A, 259, NG), data1=rv(I, 259, NG), initial=H[:, 0:1],
                                                                         op0=ALU.mult, op1=ALU.add), reads=['aB2', 'aB1', hk], writes=[hk])
                        S.op('dve', lambda e: e.tensor_tensor(out=B[3][:, 0:NG], in0=B[3][:, 0:NG], in1=B[0][:, 0:NG], op=ALU.add), reads=['aB3', 'aB0'], writes=['aB3'])
                S.op('dve', lambda e: e.tensor_tensor(out=Yb[:, 0:256], in0=B[3][:, 0:256], in1=SGt[:, 0:256], op=ALU.mult), reads=['aB3', 'aSG'], writes=['aXCb'])
                S.op('dve', lambda e: e.tensor_tensor(out=Yb[:, 256:T], in0=B[3][:, 259:NG], in1=SGt[:, 256:T], op=ALU.mult), reads=['aB3', 'aSG'], writes=['aXCb'])
                S.dma('pool', ymix[pc * 128:(pc + 1) * 128, :], Yb[:, 0:T], reads=['aXCb'], writes=['ymix'])
        S.barrier()

    NP = T + 60

    def stageC(l, last):
        with ExitStack() as cx:
            W = sb("cW", [128, 8, 768], BF16, cx)
            stgs = [(sb(f"cstg{i}", [128, 8, 256], F32, cx), f'cstg{i}') for i in range(2)]
            Yp = [sb(f"cYp{i}", [128, NP], BF16, cx) for i in range(2)]
            SG = sb("cSG", [128, 2, T], BF16, cx)
            Dg = sb("cDg", [128, 2, 31, 128], BF16, cx)
            cwt = sb("ccw", [128, 62], F32, cx)
            cbt = sb("ccb", [128, 6], F32, cx)
            sgm = [sb(f"csgm{i}", [128, 512], F32, cx) for i in range(2)]
            Cf = sb("cCf", [128, 2, 512], F32, cx)
            Cb = sb("cCb", [128, 2, 512], BF16, cx)
            Cq = sb("cCq", [128, 2, 512], BF16, cx)
            mean = sb("cmean", [128, 512], F32, cx)
            var = sb("cvar", [128, 512], F32, cx)
            dd = [sb(f"cdd{i}", [128, 512], F32, cx) for i in range(2)]
            yo = [sb(f"cyo{i}", [128, 512], BF16, cx) for i in range(2)]
            load_w(W, l, w_in, 1792, 768, stgs, 'cW')
            S.dma('sp', cwt[:], cf_w[l], writes=['ccw'])
            S.dma('sp', cbt[:], cf_b[l], writes=['ccb'])
            for pc in range(2):
                i0 = identb[:].unsqueeze(1).broadcast_to([128, 31, 128])
                i1 = cwt[:, pc * 31:(pc + 1) * 31].unsqueeze(2).broadcast_to([128, 31, 128])
                S.op('dve', lambda e, pc=pc, i0=i0, i1=i1: e.tensor_tensor(out=Dg[:, pc, :, :], in0=i0, in1=i1, op=ALU.mult), reads=['identb', 'ccw'], writes=['cDg'])
                S.op('pool', lambda e, pc=pc: e.memset(Yp[pc][:], 0.0), writes=[f'cYp{pc}'])
            for bi, (t0, n) in enumerate(BLK):
                p0 = 15 + t0 if t0 < 256 else 301 + (t0 - 256)
                for pc in range(2):
                    q = (bi * 2 + pc) % 2
                    proj_fm(0 + q, W, 'cW', pc * 128, t0, n)
                    proj_fm(2 + q, W, 'cW', 256 + pc * 128, t0, n)
                    act(sgm[q][:, 0:n], PS[2 + q][:, 0:n], AF.Sigmoid, [pk(2 + q)], [f'csgm{q}'])
                    S.op('dve', lambda e, pc=pc, q=q, p0=p0, n=n: e.tensor_tensor(out=Yp[pc][:, p0:p0 + n], in0=PS[q][:, 0:n], in1=sgm[q][:, 0:n], op=ALU.mult),
                         reads=[pk(q), f'csgm{q}'], writes=[f'cYp{pc}'])
                    proj_fm(4 + q, W, 'cW', 512 + pc * 128, t0, n)
                    act(SG[:, pc, t0:t0 + n], PS[4 + q][:, 0:n], AF.Silu, [pk(4 + q)], ['cSG'])
            for bi, (t0, n) in enumerate(BLK):
                if last and bi == 0:
                    continue
                p0 = 15 + t0 if t0 < 256 else 301 + (t0 - 256)
                for pc in range(2):
                    b = pc
                    for k in range(31):
                        S.op('pe', lambda e, k=k, pc=pc, b=b: e.matmul(PS[b][:, 0:n], lhsT=Dg[:, pc, k, :], rhs=Yp[pc][:, p0 + k - 15:p0 + k - 15 + n], start=(k == 0), stop=(k == 30)),
                             reads=['cDg', f'cYp{pc}'], writes=[pk(b)], sig=(k == 30))
                    S.op('dve', lambda e, pc=pc, b=b: e.tensor_scalar(out=Cf[:, pc, 0:n], in0=PS[b][:, 0:n], scalar1=cbt[:, pc:pc + 1], scalar2=None, op0=ALU.add),
                         reads=[pk(b), 'ccb'], writes=['cCf'])
                    S.op('pool', lambda e, pc=pc: e.tensor_copy(out=Cb[:, pc, 0:n], in_=Cf[:, pc, 0:n]), reads=['cCf'], writes=['cCb'])
                    S.op('pool', lambda e, pc=pc: e.tensor_tensor(out=Cq[:, pc, 0:n], in0=Cf[:, pc, 0:n], in1=Cf[:, pc, 0:n], op=ALU.mult), reads=['cCf'], writes=['cCq'])
                for pc in range(2):
                    S.op('pe', lambda e, pc=pc: e.matmul(PS[2][:, 0:n], lhsT=ones256[:], rhs=Cb[:, pc, 0:n], start=(pc == 0), stop=(pc == 1)),
                         reads=['ones256', 'cCb'], writes=[pk(2)], sig=(pc == 1))
                for pc in range(2):
                    S.op('pe', lambda e, pc=pc: e.matmul(PS[3][:, 0:n], lhsT=ones256[:], rhs=Cq[:, pc, 0:n], start=(pc == 0), stop=(pc == 1)),
                         reads=['ones256', 'cCq'], writes=[pk(3)], sig=(pc == 1))
                S.op('dve', lambda e: e.tensor_copy(out=mean[:, 0:n], in_=PS[2][:, 0:n]), reads=[pk(2)], writes=['cmean'])
                S.op('dve', lambda e: e.tensor_tensor(out=var[:, 0:n], in0=mean[:, 0:n], in1=mean[:, 0:n], op=ALU.mult), reads=['cmean'], writes=['cvar'])
                S.op('dve', lambda e: e.tensor_tensor(out=var[:, 0:n], in0=PS[3][:, 0:n], in1=var[:, 0:n], op=ALU.subtract), reads=[pk(3), 'cvar'], writes=['cvar'])
                act(var[:, 0:n], var[:, 0:n], AF.Ln, ['cvar'], ['cvar'], bias=epsl[:], scale=1.0)
                act(var[:, 0:n], var[:, 0:n], AF.Exp, ['cvar'], ['cvar'], scale=-0.5)
                for pc in range(2):
                    d_, y_ = dd[pc], yo[pc]
                    S.op('dve', lambda e, pc=pc, d_=d_: e.tensor_tensor(out=d_[:, 0:n], in0=Cf[:, pc, 0:n], in1=mean[:, 0:n], op=ALU.subtract), reads=['cCf', 'cmean'], writes=[f'cdd{pc}'])
                    S.op('dve', lambda e, pc=pc, d_=d_: e.tensor_tensor(out=d_[:, 0:n], in0=d_[:, 0:n], in1=var[:, 0:n], op=ALU.mult), reads=[f'cdd{pc}', 'cvar'], writes=[f'cdd{pc}'])
                    act(d_[:, 0:n], d_[:, 0:n], AF.Silu, [f'cdd{pc}', 'ccb'], [f'cdd{pc}'], bias=cbt[:, 4 + pc:5 + pc], scale=cbt[:, 2 + pc:3 + pc])
                    S.op('dve', lambda e, pc=pc, d_=d_, y_=y_: e.tensor_tensor(out=y_[:, 0:n], in0=d_[:, 0:n], in1=SG[:, pc, t0:t0 + n], op=ALU.mult),
                         reads=[f'cdd{pc}', 'cSG'], writes=[f'cyo{pc}'])
                    S.dma('pool', ymix[512 + pc * 128:512 + (pc + 1) * 128, t0:t0 + n], y_[:, 0:n], reads=[f'cyo{pc}'], writes=['ymix'])
        S.barrier()

    def stageD(l, last):
        lam_init = 0.8 - 0.6 * math.exp(-0.3 * l)
        scale = 32 ** -0.5
        with ExitStack() as cx:
            KT = sb("dKT", [128, 2, T], BF16, cx)
            QT = sb("dQT", [128, 2, T], BF16, cx)
            V = sb("dV", [128, NT, 384], BF16, cx)
            SG = sb("dSG", [128, 2, T], BF16, cx)
            lmt = sb("dlm", [128, 128], F32, cx)
            lms = sb("dls", [128, 4], F32, cx)
            gn = sb("dgn", [128, 1], F32, cx)
            S.dma('sp', lmt[:], df_lam[l:l + 1, :].partition_broadcast(128), writes=['dlm'])
            S.dma('sp', gn[:], df_g[l], writes=['dgn'])
            lm4 = lmt[:].rearrange("p (a d) -> p a d", d=32)
            S.op('dve', lambda e: e.tensor_tensor(out=lm4[:, 0, :], in0=lm4[:, 0, :], in1=lm4[:, 1, :], op=ALU.mult), reads=['dlm'], writes=['dlm'])
            S.op('dve', lambda e: e.tensor_tensor(out=lm4[:, 2, :], in0=lm4[:, 2, :], in1=lm4[:, 3, :], op=ALU.mult), reads=['dlm'], writes=['dlm'])
            S.op('dve', lambda e: e.reduce_sum(out=lms[:, 0:1], in_=lm4[:, 0, :], axis=AX.X), reads=['dlm'], writes=['dls'])
            S.op('dve', lambda e: e.reduce_sum(out=lms[:, 1:2], in_=lm4[:, 2, :], axis=AX.X), reads=['dlm'], writes=['dls'])
            act(lms[:, 0:2], lms[:, 0:2], AF.Exp, ['dls'], ['dls'])
            S.op('dve', lambda e: e.tensor_tensor(out=lms[:, 2:3], in0=lms[:, 1:2], in1=lms[:, 0:1], op=ALU.subtract), reads=['dls'], writes=['dls'])
            S.op('dve', lambda e: e.tensor_scalar(out=lms[:, 2:3], in0=lms[:, 2:3], scalar1=-lam_init, scalar2=None, op0=ALU.add), reads=['dls'], writes=['dls'])
            S.op('dve', lambda e: e.tensor_scalar(out=gn[:], in0=gn[:], scalar1=(1.0 - lam_init), scalar2=None, op0=ALU.mult), reads=['dgn'], writes=['dgn'])
            with ExitStack() as c1:
                W = sb("dW", [128, 8, 1024], BF16, c1)
                Wsw = sb("dWsw", [128, 8, 512], BF16, c1)
                stgs = [(sb(f"dstg{i}", [128, 8, 128], F32, c1), f'dstg{i}') for i in range(2)]
                cs = [sb(f"dcs{i}", [128, 512], F32, c1) for i in range(2)]
                sn = [sb(f"dsn{i}", [128, 512], F32, c1) for i in range(2)]
                t1 = [sb(f"dt1{i}", [128, 512], F32, c1) for i in range(2)]
                t2 = [sb(f"dt2{i}", [128, 512], F32, c1) for i in range(2)]
                S.op('pool', lambda e: e.memset(V[:], 1.0), writes=['dV'])
                load_w(W, l, w_in, 2560, 1024, stgs, 'dW', slab=128)
                for k in range(8):
                    wv_ = W[:, k, 0:512].rearrange("p (g two j) -> p g two j", two=2, j=16)
                    sv_ = Wsw[:, k, :].rearrange("p (g two j) -> p g two j", two=2, j=16)
                    for h in range(2):
                        S.op('pool', lambda e, wv_=wv_, sv_=sv_, h=h: e.tensor_copy(out=sv_[:, :, 1 - h, :], in_=wv_[:, :, h, :]), reads=wkeys('dW', 0, 512), writes=['dWsw'])
                ci = 0
                for bi, (t0, n) in enumerate(BLK):
                    lat = t0 >= 256
                    cp = bi % 2
                    if lat:
                        S.dma('sp', cs[cp][:, 0:n], c_cos[:, t0 - 256:t0 - 256 + n], writes=[f'dcs{cp}'])
                        S.dma('sp', sn[cp][:, 0:n], c_sin[:, t0 - 256:t0 - 256 + n], writes=[f'dsn{cp}'])
                    for which, (dst, dk) in enumerate(((QT, 'dQT'), (KT, 'dKT'))):
                        for ck in range(2):
                            q = ci % 2
                            ci += 1
                            proj_fm(q, W, 'dW', which * 256 + ck * 128, t0, n)
                            if not lat:
                                S.op('dve', lambda e, dst=dst, ck=ck, q=q: e.tensor_copy(out=dst[:, ck, t0:t0 + n], in_=PS[q][:, 0:n]), reads=[pk(q)], writes=[dk])
                            else:
                                proj_fm(2 + q, Wsw, 'dWsw', which * 256 + ck * 128, t0, n)
                                S.op('dve', lambda e, q=q: e.tensor_tensor(out=t1[q][:, 0:n], in0=PS[q][:, 0:n], in1=cs[cp][:, 0:n], op=ALU.mult),
                                     reads=[pk(q), f'dcs{cp}'], writes=[f'dt1{q}'])
                                S.op('dve', lambda e, q=q: e.tensor_tensor(out=t2[q][:, 0:n], in0=PS[2 + q][:, 0:n], in1=sn[cp][:, 0:n], op=ALU.mult),
                                     reads=[pk(2 + q), f'dsn{cp}'], writes=[f'dt2{q}'])
                                S.op('pool', lambda e, dst=dst, ck=ck, q=q: e.tensor_tensor(out=dst[:, ck, t0:t0 + n], in0=t1[q][:, 0:n], in1=t2[q][:, 0:n], op=ALU.add),
                                     reads=[f'dt1{q}', f'dt2{q}'], writes=[dk])
                    for ck in range(2):
                        b = 4 + ck
                        proj_fm(b, W, 'dW', 768 + ck * 128, t0, n)
                        act(SG[:, ck, t0:t0 + n], PS[b][:, 0:n], AF.Silu, [pk(b)], ['dSG'])
                    for tt in range(n // 128):
                        i = t0 // 128 + tt
                        b = 6 + i % 2
                        proj_tm(PS[b][:, 0:256], b, W, 'dW', 512, 256, i)
                        vv = V[:, i, :].rearrange("p (g c) -> p g c", c=192)
                        pv4 = PS[b][:, 0:256].rearrange("p (g r c) -> p g r c", r=2, c=64)
                        for r_ in range(2):
                            S.op('dve', lambda e, vv=vv, pv4=pv4, r_=r_: e.tensor_copy(out=vv[:, :, 128 * r_:128 * r_ + 64], in_=pv4[:, :, r_, :]), reads=[pk(b)], writes=['dV'])
            S.barrier()
            with ExitStack() as c2:
                Pb = [sb(f"dP{i}", [128, 512], BF16, c2) for i in range(6)]
                Qz = [sb(f"dQz{i}", [128, 4, 512], BF16, c2) for i in range(2)]
                RL = [sb(f"dRL{i}", [128, 512], F32, c2) for i in range(2)]
                Nn = [sb(f"dN{i}", [128, 512], F32, c2) for i in range(2)]
                Oc = [sb(f"dOc{i}", [128, 512], F32, c2) for i in range(4)]
                Oh = sb("dOh", [128, 512], F32, c2)
                Osq = sb("dOsq", [128, 512], BF16, c2)
                rs = sb("drs", [128, 512], F32, c2)
                Yo = [sb(f"dYo{i}", [128, 512], BF16, c2) for i in range(2)]
                pending = []
                state = dict(n=0)
                for i_ in range(2):
                    S.op('pool', lambda e, i_=i_: e.memset(Qz[i_][:], 0.0), writes=[f'dQz{i_}'])

                def prep_q(pi, q0, nq, hp):
                    pp = pi % 2
                    for s_ in range(4):
                        S.op('pool', lambda e, s_=s_: e.tensor_copy(out=Qz[pp][32 * s_:32 * s_ + 32, s_, 0:nq], in_=QT[32 * s_:32 * s_ + 32, hp, q0:q0 + nq]),
                             reads=['dQT'], writes=[f'dQz{pp}'])

                def finalize(q0, nq, hp, yp):
                    steps = []
                    for s_ in range(4):
                        steps.append(lambda s_=s_: S.op('dve', lambda e: e.tensor_copy(out=Oc[s_][:, 0:nq], in_=PS[3 + s_][:, 0:nq]), reads=[pk(3 + s_)], writes=[f'dOc{s_}']))
                    for s_ in range(4):
                        hh, w = s_ // 2, s_ % 2
                        lo, ll = 64 * hh, 64 * (1 - hh)
                        steps.append(lambda s_=s_, w=w, lo=lo, ll=ll: S.op('dve', lambda e: e.reciprocal(out=RL[w][lo:lo + 64, 0:nq], in_=Oc[s_][ll:ll + 64, 0:nq]),
                                                                     reads=[f'dOc{s_}'], writes=[f'dRL{w}']))
                        steps.append(lambda s_=s_, w=w, lo=lo: S.op('dve', lambda e: e.tensor_tensor(out=Nn[w][lo:lo + 64, 0:nq], in0=Oc[s_][lo:lo + 64, 0:nq], in1=RL[w][lo:lo + 64, 0:nq], op=ALU.mult),
                                                              reads=[f'dOc{s_}', f'dRL{w}'], writes=[f'dN{w}']))
                    steps.append(lambda: S.op('dve', lambda e: e.scalar_tensor_tensor(out=Oh[:, 0:nq], in0=Nn[1][:, 0:nq], scalar=lms[:, 2:3], in1=Nn[0][:, 0:nq], op0=ALU.mult, op1=ALU.add),
                                              reads=['dN0', 'dN1', 'dls'], writes=['dOh']))
                    steps.append(lambda: S.op('pool', lambda e: e.tensor_tensor(out=Osq[:, 0:nq], in0=Oh[:, 0:nq], in1=Oh[:, 0:nq], op=ALU.mult), reads=['dOh'], writes=['dOsq']))
                    steps.append(lambda: S.op('pe', lambda e: e.matmul(PS[7][:, 0:nq], lhsT=bd64[:], rhs=Osq[:, 0:nq], start=True, stop=True), reads=['bd64', 'dOsq'], writes=[pk(7)]))
                    steps.append(lambda: rstd_from_ss(rs[:, 0:nq], PS[7][:, 0:nq], 1.0 / 64, epsr[:], 'drs', pk(7)))
                    steps.append(lambda: S.op('dve', lambda e: e.tensor_tensor(out=Oh[:, 0:nq], in0=Oh[:, 0:nq], in1=rs[:, 0:nq], op=ALU.mult), reads=['dOh', 'drs'], writes=['dOh']))
                    steps.append(lambda: S.op('dve', lambda e: e.scalar_tensor_tensor(out=Yo[yp][:, 0:nq], in0=Oh[:, 0:nq], scalar=gn[:, 0:1], in1=SG[:, hp, q0:q0 + nq], op0=ALU.mult, op1=ALU.mult),
                                              reads=['dOh', 'dgn', 'dSG'], writes=[f'dYo{yp}']))
                    steps.append(lambda: S.dma('pool', ymix[768 + hp * 128:768 + (hp + 1) * 128, q0:q0 + nq], Yo[yp][:, 0:nq], reads=[f'dYo{yp}'], writes=['ymix']))
                    return steps

                passes = []
                if not last:
                    for hp in range(2):
                        passes.append((0, 256, [0, 1], hp))
                for qb in range(8):
                    for hp in range(2):
                        passes.append((256 + qb * 512, 512, list(range(NT)), hp))

                def attn_pass(pi):
                    q0, nq, kts, hp = passes[pi]
                    pp = pi % 2
                    seq = [(kt, s_) for kt in kts for s_ in range(4)]
                    nseq = len(seq)
                    base = state['n']

                    def qk(m):
                        kt, s_ = seq[m]
                        b_ = (base + m) % 3
                        S.op('pe', lambda e: e.matmul(PS[b_][:, 0:nq], lhsT=KT[:, hp, kt * 128:(kt + 1) * 128], rhs=Qz[pp][:, s_, 0:nq], start=True, stop=True),
                             reads=['dKT', f'dQz{pp}'], writes=[pk(b_)])

                    def ex(m):
                        g = base + m
                        act(Pb[g % 6][:, 0:nq], PS[g % 3][:, 0:nq], AF.Exp, [pk(g % 3)], [f'dP{g % 6}'], scale=scale)

                    def pv(m):
                        kt, s_ = seq[m]
                        pb = (base + m) % 6
                        hh = s_ // 2
                        c0 = hp * 192 + hh * 64
                        S.op('pe', lambda e: e.matmul(PS[3 + s_][:, 0:nq], lhsT=V[:, kt, c0:c0 + 128], rhs=Pb[pb][:, 0:nq], start=(kt == kts[0]), stop=(kt == kts[-1])),
                             reads=['dV', f'dP{pb}'], writes=[pk(3 + s_)])

                    for m in range(min(3, nseq)):
                        qk(m)
                    if pi + 1 < len(passes):
                        prep_q(pi + 1, passes[pi + 1][0], passes[pi + 1][1], passes[pi + 1][3])
                    for m in range(nseq):
                        ex(m)
                        pv(m)
                        if m + 3 < nseq:
                            qk(m + 3)
                        if pending and m % 2 == 1:
                            pending.pop(0)()
                    state['n'] = base + nseq
                    while pending:
                        pending.pop(0)()
                    fs = finalize(q0, nq, hp, pi % 2)
                    for f_ in fs[:4]:
                        f_()
                    pending.extend(fs[4:])

                prep_q(0, passes[0][0], passes[0][1], passes[0][3])
                for pi in range(len(passes)):
                    attn_pass(pi)
                while pending:
                    pending.pop(0)()
        S.barrier()

    def stageB(l):
        with ExitStack() as cx:
            W = sb("bW", [128, 8, 640], BF16, cx)
            stgs = [(sb(f"bstg{i}", [128, 8, 128], F32, cx), f'bstg{i}') for i in range(2)]
            Vt = sb("bV", [128, NT, 128], BF16, cx)
            OT = sb("bOT", [128, T], F32, cx)
            QTl = sb("bQT", [128, T], BF16, cx)
            KTl = sb("bKT", [128, T], BF16, cx)
            Kt = sb("bKt", [128, NT, 128], BF16, cx)
            Sb = sb("bSb", [128, NCH, 64], BF16, cx)
            KH = Sb[:].rearrange("p c e -> p (c e)")
            M0 = sb("bM0", [128, T], BF16, cx)
            Gc = sb("bGc", [128, T], F32, cx)
            Bt = Gc[:].rearrange("p (c e) -> p c e", e=64)
            T1 = [sb(f"bT1{i}", [128, 512], F32, cx) for i in range(2)]
            T2 = [sb(f"bT2{i}", [128, 512], F32, cx) for i in range(2)]
            sm = sb("bsm", [128, 6, NCH], F32, cx)
            SCm = [sb(f"bSC{i}", [128, 256], BF16, cx) for i in range(4)]
            osq = sb("bosq", [128, 512], BF16, cx)
            ors = sb("bors", [128, 512], F32, cx)
            oy = [sb(f"boy{i}", [128, 512], BF16, cx) for i in range(2)]
            hgn = sb("bhgn", [128, 1], F32, cx)
            S.dma('sp', hgn[:], hg_g[l], writes=['bhgn'])
            groups = [[0, 1]] + [list(range(2 + 8 * g, 10 + 8 * g)) for g in range(4)]
            for pc in range(2):
                for j in range(5):
                    load_w(W, l, w_in, 512 + j * 256 + pc * 128, 128, stgs, 'bW', off=j * 128, slab=128)
                S.op('pool', lambda e: e.memset(OT[:], 0.0), writes=['bOT'])
                for i in range(NT):
                    b = 6 + i % 2
                    proj_tm(PS[b][:, 0:128], b, W, 'bW', 128, 128, i)
                    S.op('dve', lambda e, b=b, i=i: e.tensor_copy(out=Vt[:, i, :], in_=PS[b][:, 0:128]), reads=[pk(b)], writes=['bV'])
                _ck(1)
                for dr in range(2):
                    li = (pc * 2 + dr)
                    lb_ap = LBt[:, 0, li, l:l + 1]
                    oml_ap = LBt[:, 1, li, l:l + 1]
                    S.op('pool', lambda e: e.memset(M0[:], 1.0), writes=['bM0'])
                    m3 = M0[:].rearrange("p (c j) -> p c j", j=64)
                    zc = 0 if dr == 0 else 63
                    S.op('pool', lambda e, zc=zc: e.memset(m3[:, :, zc:zc + 1], 0.0), writes=['bM0'])
                    for bi, (t0, n) in enumerate(BLK):
                        q = bi % 2
                        proj_fm(q, W, 'bW', (2 + dr) * 128, t0, n)
                        act(T1[q][:, 0:n], PS[q][:, 0:n], AF.Exp, [pk(q)], [f'bT1{q}'], scale=-1.0)
                        act(T2[q][:, 0:n], T1[q][:, 0:n], AF.Ln, [f'bT1{q}', 'LBt'], [f'bT2{q}'], bias=onec[:], scale=lb_ap)
                        act(T1[q][:, 0:n], T1[q][:, 0:n], AF.Ln, [f'bT1{q}'], [f'bT1{q}'], bias=onec[:], scale=1.0)
                        S.op('dve', lambda e, q=q, t0=t0, n=n: e.tensor_tensor(out=Gc[:, t0:t0 + n], in0=T2[q][:, 0:n], in1=T1[q][:, 0:n], op=ALU.subtract),
                             reads=[f'bT1{q}', f'bT2{q}'], writes=['bGc'])
                    if dr == 0:
                        S.op('dve', lambda e: e.tensor_tensor_scan(out=Gc[:], data0=M0[:], data1=Gc[:], initial=0.0, op0=ALU.mult, op1=ALU.add),
                             reads=['bGc', 'bM0'], writes=['bGc'])
                    else:
                        S.op('dve', lambda e: e.tensor_tensor_scan(out=Gc[:][:, ::-1], data0=M0[:][:, ::-1], data1=Gc[:][:, ::-1], initial=0.0, op0=ALU.mult, op1=ALU.add),
                             reads=['bGc', 'bM0'], writes=['bGc'])
                    _ck(2)
                    g3 = Gc[:].rearrange("p (c j) -> p c j", j=64)
                    mid = 31 if dr == 0 else 32
                    end = 63 if dr == 0 else 0
                    S.op('dve', lambda e: e.tensor_copy(out=sm[:, 0, :], in_=g3[:, :, mid]), reads=['bGc'], writes=['bsm'])
                    S.op('dve', lambda e: e.tensor_copy(out=sm[:, 1, :], in_=g3[:, :, end]), reads=['bGc'], writes=['bsm'])
                    S.op('dve', lambda e: e.tensor_tensor(out=sm[:, 5, :], in0=sm[:, 1, :], in1=sm[:, 0, :], op=ALU.subtract), reads=['bsm'], writes=['bsm'])
                    act(sm[:, 2, :], sm[:, 1, :], AF.Exp, ['bsm'], ['bsm'])
                    act(sm[:, 3, :], sm[:, 5, :], AF.Exp, ['bsm'], ['bsm'])
                    act(sm[:, 4, :], sm[:, 0, :], AF.Exp, ['bsm'], ['bsm'])
                    S.op('dve', lambda e: e.tensor_tensor(out=g3, in0=g3, in1=sm[:, 0, :].unsqueeze(2).broadcast_to([128, NCH, 64]), op=ALU.subtract),
                         reads=['bGc', 'bsm'], writes=['bGc'])
                    _ck(3)
                    for bi, (t0, n) in enumerate(BLK):
                        q = bi % 2
                        c0, nc_ = t0 // 64, n // 64
                        proj_fm(q, W, 'bW', 0, t0, n)
                        act(T1[q][:, 0:n], PS[q][:, 0:n], AF.Exp, [pk(q)], [f'bT1{q}'], scale=-1.0)
                        act(T1[q][:, 0:n], T1[q][:, 0:n], AF.Ln, [f'bT1{q}'], [f'bT1{q}'], bias=onec[:], scale=1.0)
                        S.op('dve', lambda e, q=q, t0=t0, n=n: e.tensor_tensor(out=T1[q][:, 0:n], in0=Gc[:, t0:t0 + n], in1=T1[q][:, 0:n], op=ALU.subtract),
                             reads=['bGc', f'bT1{q}'], writes=[f'bT1{q}'])
                        act(T1[q][:, 0:n], T1[q][:, 0:n], AF.Exp, [f'bT1{q}'], [f'bT1{q}'])
                        S.op('dve', lambda e, q=q, t0=t0, n=n: e.tensor_tensor(out=QTl[:, t0:t0 + n], in0=PS[q][:, 0:n], in1=T1[q][:, 0:n], op=ALU.mult),
                             reads=[pk(q), f'bT1{q}'], writes=['bQT'])
                        proj_fm(2 + q, W, 'bW', (2 + dr) * 128, t0, n)
                        act(T2[q][:, 0:n], PS[2 + q][:, 0:n], AF.Exp, [pk(2 + q)], [f'bT2{q}'])
                        act(T2[q][:, 0:n], T2[q][:, 0:n], AF.Ln, [f'bT2{q}'], [f'bT2{q}'], bias=onec[:], scale=1.0)
                        S.op('dve', lambda e, q=q, t0=t0, n=n: e.tensor_tensor(out=T2[q][:, 0:n], in0=Gc[:, t0:t0 + n], in1=T2[q][:, 0:n], op=ALU.add),
                             reads=['bGc', f'bT2{q}'], writes=[f'bT2{q}'])
                        act(T2[q][:, 0:n], T2[q][:, 0:n], AF.Exp, [f'bT2{q}'], [f'bT2{q}'], scale=-1.0)
                        S.op('dve', lambda e, q=q, t0=t0, n=n: e.tensor_scalar(out=KTl[:, t0:t0 + n], in0=T2[q][:, 0:n], scalar1=oml_ap, scalar2=None, op0=ALU.mult),
                             reads=[f'bT2{q}', 'LBt'], writes=['bKT'])
                        kv3 = KTl[:, t0:t0 + n].rearrange("p (c j) -> p c j", j=64)
                        kh3 = KH[:, t0:t0 + n].rearrange("p (c j) -> p c j", j=64)
                        S.op('dve', lambda e, kv3=kv3, kh3=kh3, c0=c0, nc_=nc_: e.tensor_tensor(out=kh3, in0=kv3, in1=sm[:, 3, c0:c0 + nc_].unsqueeze(2).broadcast_to([128, nc_, 64]), op=ALU.mult),
                             reads=['bKT', 'bsm'], writes=['bSb'])
                    _ck(4)
                    for i in range(NT):
                        b = 4 + i % 2
                        psb = PS[b][:].bitcast(BF16)
                        S.op('pe', lambda e, i=i, psb=psb: e.transpose(out=psb[:, 0:128], in_=KH[:, i * 128:(i + 1) * 128], identity=identb[:]),
                             reads=['bSb', 'identb'], writes=[pk(b)])
                        S.op('dve', lambda e, i=i, psb=psb: e.tensor_copy(out=Kt[:, i, :], in_=psb[:, 0:128]), reads=[pk(b)], writes=['bKt'])
                    _ck(5)
                    for g, tiles in enumerate(groups):
                        for ti, i in enumerate(tiles):
                            for cp in range(2):
                                bb = 2 * (g % 2) + cp
                                for h2 in range(2):
                                    S.op('pe', lambda e, ti=ti, i=i, cp=cp, h2=h2, bb=bb: e.matmul(PS[bb][64 * h2:64 * h2 + 64, ti * 64:(ti + 1) * 64],
                                                                                               lhsT=Kt[64 * cp:64 * cp + 64, i, 64 * h2:64 * h2 + 64],
                                                                                               rhs=Vt[64 * cp:64 * cp + 64, i, 64 * h2:64 * h2 + 64],
                                                                                               start=True, stop=True, tile_position=(64 * cp, 64 * h2)),
                                         reads=['bKt', 'bV'], writes=[pk(bb)], sig=(ti == len(tiles) - 1 and h2 == 1))
                        nt_ = len(tiles)
                        for cp in range(2):
                            bb = 2 * (g % 2) + cp
                            cfirst = 2 * tiles[0] + cp
                            if dr == 0:
                                dst = Bt[:, cfirst:cfirst + 2 * (nt_ - 1) + 1:2, :]
                            else:
                                pfirst = (3 - cfirst) if g == 0 else (71 - cfirst)
                                stop = pfirst - 2 * (nt_ - 1) - 1
                                dst = Bt[:, pfirst:(stop if stop >= 0 else None):-2, :]
                            S.op('dve', lambda e, bb=bb, dst=dst, nt_=nt_: e.tensor_copy(out=dst, in_=PS[bb][:, 0:nt_ * 64].rearrange("p (t e) -> p t e", e=64)),
                                 reads=[pk(bb)], writes=['bGc'])
                    _ck(6)
                    if dr == 0:
                        lam_ap = sm[:, 2, :]
                    else:
                        S.op('dve', lambda e: e.tensor_copy(out=sm[:, 5, 0:4], in_=sm[:, 2, 0:4][:, ::-1]), reads=['bsm'], writes=['bsm'])
                        S.op('dve', lambda e: e.tensor_copy(out=sm[:, 5, 4:NCH], in_=sm[:, 2, 4:NCH][:, ::-1]), reads=['bsm'], writes=['bsm'])
                        lam_ap = sm[:, 5, :]
                    for e_ in range(64):
                        S.op('dve', lambda e, e_=e_: e.tensor_tensor_scan(out=Bt[:, :, e_], data0=lam_ap, data1=Bt[:, :, e_], initial=0.0, op0=ALU.mult, op1=ALU.add),
                             reads=['bsm', 'bGc'], writes=['bGc'])
                    _ck(7)
                    rho = sm[:, 4, :]
                    if dr == 0:
                        S.op('dve', lambda e: e.memset(Sb[:, 0:1, :], 0.0), writes=['bSb'])
                        S.op('dve', lambda e: e.tensor_tensor(out=Sb[:, 1:NCH, :], in0=Bt[:, 0:NCH - 1, :], in1=rho[:, 1:NCH].unsqueeze(2).broadcast_to([128, NCH - 1, 64]), op=ALU.mult),
                             reads=['bGc', 'bsm'], writes=['bSb'])
                    else:
                        S.op('dve', lambda e: e.memset(Sb[:, 3:4, :], 0.0), writes=['bSb'])
                        S.op('dve', lambda e: e.tensor_tensor(out=Sb[:, 0:3, :], in0=Bt[:, 2::-1, :], in1=rho[:, 0:3].unsqueeze(2).broadcast_to([128, 3, 64]), op=ALU.mult),
                             reads=['bGc', 'bsm'], writes=['bSb'])
                        S.op('dve', lambda e: e.tensor_tensor(out=Sb[:, 4:NCH, :], in0=Bt[:, 66:2:-1, :], in1=rho[:, 4:NCH].unsqueeze(2).broadcast_to([128, NCH - 4, 64]), op=ALU.mult),
                             reads=['bGc', 'bsm'], writes=['bSb'])
                    _ck(8)
                    mk = maskt[:, 64 * dr:64 * dr + 64]
                    tgs = list(enumerate(range(0, NT, 4)))

                    def b_scores(tgi, tg):
                            tiles = list(range(tg, min(tg + 4, NT)))
                            nt_ = len(tiles)
                            q = tgi % 2
                            for ti, i in enumerate(tiles):
                                for cp in range(2):
                                    c = 2 * i + cp
                                    for h2 in range(2):
                                        bb = 2 * q + h2
                                        for jb in range(2):
                                            full = (jb == 0) if dr == 0 else (jb == 1)
                                            i0, ni = (0, 64) if full else ((32, 32) if dr == 0 else (0, 32))
                                            S.op('pe', lambda e, c=c, cp=cp, h2=h2, bb=bb, ti=ti, jb=jb, i0=i0, ni=ni: e.matmul(
                                                PS[bb][64 * cp + 32 * jb:64 * cp + 32 * jb + 32, ti * 64 + i0:ti * 64 + i0 + ni],
                                                lhsT=KTl[64 * h2:64 * h2 + 64, c * 64 + 32 * jb:c * 64 + 32 * jb + 32],
                                                rhs=QTl[64 * h2:64 * h2 + 64, c * 64 + i0:c * 64 + i0 + ni],
                                                start=True, stop=True, tile_position=(64 * h2, 64 * cp + 32 * jb)),
                                                 reads=['bKT', 'bQT'], writes=[pk(bb)], sig=(ti == nt_ - 1 and cp == 1 and jb == 1))
                            for h2 in range(2):
                                bb = 2 * q + h2
                                scv = SCm[2 * q + h2][:, 0:nt_ * 64].rearrange("p (t i) -> p t i", i=64)
                                S.op('dve', lambda e, bb=bb, scv=scv, nt_=nt_: e.tensor_tensor(out=scv, in0=PS[bb][:, 0:nt_ * 64].rearrange("p (t i) -> p t i", i=64),
                                                                                           in1=mk.unsqueeze(1).broadcast_to([128, nt_, 64]), op=ALU.mult),
                                     reads=[pk(bb), 'maskt'], writes=[f'bSC{2 * q + h2}'])

                    def b_rest(tgi, tg):
                            tiles = list(range(tg, min(tg + 4, NT)))
                            nt_ = len(tiles)
                            q = tgi % 2
                            for ti, i in enumerate(tiles):
                                for cp in range(2):
                                    for h2 in range(2):
                                        S.op('pe', lambda e, cp=cp, h2=h2, ti=ti, i=i, q=q: e.matmul(PS[4 + cp][64 * h2:64 * h2 + 64, ti * 64:(ti + 1) * 64],
                                                                                                 lhsT=Vt[64 * cp:64 * cp + 64, i, 64 * h2:64 * h2 + 64],
                                                                                                 rhs=SCm[2 * q + h2][64 * cp:64 * cp + 64, ti * 64:(ti + 1) * 64],
                                                                                                 start=True, stop=True, tile_position=(64 * cp, 64 * h2)),
                                             reads=['bV', f'bSC{2 * q + h2}'], writes=[pk(4 + cp)], sig=(ti == nt_ - 1 and h2 == 1))
                            for ti, i in enumerate(tiles):
                                for cp in range(2):
                                    c = 2 * i + cp
                                    for h2 in range(2):
                                        S.op('pe', lambda e, c=c, cp=cp, h2=h2, ti=ti: e.matmul(PS[6][64 * h2:64 * h2 + 64, (ti * 2 + cp) * 64:(ti * 2 + cp + 1) * 64],
                                                                                           lhsT=Sb[64 * h2:64 * h2 + 64, c, :],
                                                                                           rhs=QTl[64 * h2:64 * h2 + 64, c * 64:(c + 1) * 64],
                                                                                           start=True, stop=True, tile_position=(64 * h2, 64 * h2)),
                                             reads=['bSb', 'bQT'], writes=[pk(6)], sig=(ti == nt_ - 1 and cp == 1 and h2 == 1))
                            otv = OT[:, tg * 128:(tg + nt_) * 128].rearrange("p (t c i) -> p t c i", c=2, i=64)
                            for cp in range(2):
                                S.op('dve', lambda e, cp=cp, otv=otv, nt_=nt_: e.tensor_tensor(out=otv[:, :, cp, :], in0=PS[4 + cp][:, 0:nt_ * 64].rearrange("p (t i) -> p t i", i=64),
                                                                                           in1=otv[:, :, cp, :], op=ALU.add),
                                     reads=[pk(4 + cp), 'bOT'], writes=['bOT'])
                            S.op('dve', lambda e, tg=tg, nt_=nt_: e.tensor_tensor(out=OT[:, tg * 128:(tg + nt_) * 128], in0=PS[6][:, 0:nt_ * 128], in1=OT[:, tg * 128:(tg + nt_) * 128], op=ALU.add),
                                 reads=[pk(6), 'bOT'], writes=['bOT'])

                    b_scores(*tgs[0])
                    for gi_ in range(len(tgs)):
                        if gi_ + 1 < len(tgs):
                            b_scores(*tgs[gi_ + 1])
                        b_rest(*tgs[gi_])
                _ck(9)
                for bi, (t0, n) in enumerate(BLK):
                    q = bi % 2
                    S.op('pool', lambda e, t0=t0, n=n: e.tensor_tensor(out=osq[:, 0:n], in0=OT[:, t0:t0 + n], in1=OT[:, t0:t0 + n], op=ALU.mult), reads=['bOT'], writes=['bosq'])
                    S.op('pe', lambda e, n=n: e.matmul(PS[4][:, 0:n], lhsT=bd64[:], rhs=osq[:, 0:n], start=True, stop=True), reads=['bd64', 'bosq'], writes=[pk(4)])
                    rstd_from_ss(ors[:, 0:n], PS[4][:, 0:n], 1.0 / 64, epsr[:], 'bors', pk(4))
                    proj_fm(5, W, 'bW', 4 * 128, t0, n)
                    act(T1[q][:, 0:n], PS[5][:, 0:n], AF.Exp, [pk(5)], [f'bT1{q}'], scale=-1.0)
                    act(T1[q][:, 0:n], T1[q][:, 0:n], AF.Ln, [f'bT1{q}'], [f'bT1{q}'], bias=onec[:], scale=1.0)
                    act(T1[q][:, 0:n], T1[q][:, 0:n], AF.Exp, [f'bT1{q}'], [f'bT1{q}'], scale=-1.0)
                    S.op('dve', lambda e, q=q, n=n: e.tensor_tensor(out=T1[q][:, 0:n], in0=PS[5][:, 0:n], in1=T1[q][:, 0:n], op=ALU.mult), reads=[pk(5), f'bT1{q}'], writes=[f'bT1{q}'])
                    S.op('dve', lambda e, t0=t0, n=n: e.tensor_tensor(out=ors[:, 0:n], in0=ors[:, 0:n], in1=OT[:, t0:t0 + n], op=ALU.mult), reads=['bors', 'bOT'], writes=['bors'])
                    S.op('dve', lambda e, q=q, n=n: e.scalar_tensor_tensor(out=oy[q][:, 0:n], in0=ors[:, 0:n], scalar=hgn[:, 0:1], in1=T1[q][:, 0:n], op0=ALU.mult, op1=ALU.mult),
                         reads=['bors', 'bhgn', f'bT1{q}'], writes=[f'boy{q}'])
                    S.dma('pool', ymix[256 + pc * 128:256 + (pc + 1) * 128, t0:t0 + n], oy[q][:, 0:n], reads=[f'boy{q}'], writes=['ymix'])
        S.barrier()

    S.barrier()
    prologue()
    for l in range(nlayers):
        last = (l == DEPTH - 1)
        if 'H' in stages and (l == 0 or 'O' not in stages):
            stageH(l)
        if 'A' in stages:
            stageA(l)
        if 'B' in stages:
            try:
                stageB(l)
            except _Stop:
                S.barrier()
        if 'C' in stages:
            stageC(l, last)
        if 'D' in stages:
            stageD(l, last)
        if 'O' in stages:
            stageO(l, last, fuse_next=(l + 1 < nlayers and 'H' in stages))
    S.barrier()
    if 'BSTOP' not in _os.environ:
        es.close()
    return nc


def _consts():
    ident = np.eye(128, dtype=np.float32).astype(ml_dtypes.bfloat16)
    n_freq = 8
    inv_freq = (10000.0 ** (-np.arange(n_freq, dtype=np.float32) / n_freq)).astype(np.float32)
    row = np.repeat(np.arange(64, dtype=np.float32), 64)
    col = np.tile(np.arange(64, dtype=np.float32), 64)
    ang = np.concatenate([row[:, None] * inv_freq, col[:, None] * inv_freq], axis=-1).astype(np.float32)
    cos, sin = np.cos(ang).astype(np.float32), np.sin(ang).astype(np.float32)
    c32 = np.concatenate([cos, cos], axis=1).T
    s32 = np.concatenate([-sin, sin], axis=1).T
    cosT = np.ascontiguousarray(np.tile(c32, (4, 1)))
    sinT = np.ascontiguousarray(np.tile(s32, (4, 1)))
    j = np.arange(64)[:, None]
    i = np.arange(64)[None, :]
    fwd = (j <= i).astype(np.float32)
    bwd = (j >= i).astype(np.float32)
    mask = np.concatenate([np.tile(fwd, (2, 1)), np.tile(bwd, (2, 1))], axis=1)
    bd = np.zeros((128, 128), np.float32)
    bd[:64, :64] = 1
    bd[64:, 64:] = 1
    return dict(c_ident=ident, c_cos=cosT, c_sin=sinT, c_mask=np.ascontiguousarray(mask), c_bd64=bd.astype(ml_dtypes.bfloat16))


def _layout(inp, b):
    f = lambda a: np.ascontiguousarray(np.asarray(a, dtype=np.float32))
    m = {}
    m["x_b"] = f(inp["x"][b])
    m["ctx_b"] = f(inp["ctx"][b])
    cc = np.stack([np.asarray(inp["c"][b]), np.asarray(inp["c_ctx"])], -1)
    m["cfm"] = f(cc.reshape(8, 128, 2).transpose(1, 0, 2).reshape(128, 16))
    m["w_mod"] = f(inp["w_mod"])
    m["b_mod"] = f(inp["b_mod"])
    m["b_mod_fm"] = f(np.asarray(inp["b_mod"]).reshape(DEPTH, 24, 128).transpose(0, 2, 1))
    m["g_pre_fm"] = f(np.asarray(inp["g_pre"]).reshape(DEPTH, 8, 128).transpose(0, 2, 1))
    m["g_post"] = f(inp["g_post"])
    m["w_in"] = f(inp["w_in"])
    m["w_out"] = f(inp["w_out"])
    m["lru_cw"] = f(np.asarray(inp["lru_conv_w"]).reshape(DEPTH, 4, 2, 128).transpose(0, 3, 2, 1).reshape(DEPTH, 128, 8))
    m["lru_cb"] = f(np.asarray(inp["lru_conv_b"]).reshape(DEPTH, 2, 128).transpose(0, 2, 1))
    wr, wi = np.asarray(inp["lru_w_r"]), np.asarray(inp["lru_w_i"])
    bd = np.zeros((DEPTH, 2, 4, 128, 128), np.float32)
    for pc in range(2):
        for dr in range(2):
            for wh, w in enumerate((wr, wi)):
                for h2 in range(2):
                    bd[:, pc, 2 * dr + wh, 64 * h2:64 * h2 + 64, 64 * h2:64 * h2 + 64] = w[:, dr, 2 * pc + h2]
    m["lru_bd"] = bd
    br, bi_ = np.asarray(inp["lru_b_r"]), np.asarray(inp["lru_b_i"])
    lb = np.stack([br, bi_], 1).reshape(DEPTH, 2, 2, 2, 128)
    m["lru_b"] = f(lb.transpose(0, 4, 1, 2, 3).reshape(DEPTH, 128, 8))
    m["lru_lam"] = f(np.asarray(inp["lru_lambda"]).reshape(DEPTH, 2, 2, 128).transpose(0, 3, 1, 2).reshape(DEPTH, 128, 4))
    hl = np.asarray(inp["hgrn_lb"]).reshape(DEPTH, 2, 2, 128)
    m["hg_lb"] = f(hl.transpose(3, 2, 1, 0).reshape(128, 16))
    m["hg_g"] = f(np.tile(np.asarray(inp["hgrn_norm_g"]), (1, 2)).reshape(DEPTH, 128, 1))
    m["cf_w"] = f(np.asarray(inp["conf_conv_w"]).reshape(DEPTH, 31, 2, 128).transpose(0, 3, 2, 1).reshape(DEPTH, 128, 62))
    cb = np.stack([np.asarray(inp["conf_conv_b"]), np.asarray(inp["conf_ln_g"]), np.asarray(inp["conf_ln_b"])], 1).reshape(DEPTH, 3, 2, 128)
    m["cf_b"] = f(cb.transpose(0, 3, 1, 2).reshape(DEPTH, 128, 6))
    m["df_lam"] = f(np.concatenate([np.asarray(inp[k]) for k in ("diff_lam_q1", "diff_lam_k1", "diff_lam_q2", "diff_lam_k2")], axis=1))
    m["df_g"] = f(np.tile(np.asarray(inp["diff_norm_g"]), (1, 2)).reshape(DEPTH, 128, 1))
    return m


def kernel(**inputs):
    n = 8
    nc = bass.Bass("TRN2", target_bir_lowering=False)
    build(nc)
    consts = _consts()
    in_maps = []
    for b in range(n):
        m = _layout(inputs, b)
        m.update(consts)
        in_maps.append(m)
    res = run_bass_kernel_spmd(nc, in_maps, core_ids=list(range(n)))
    return np.stack([np.asarray(r["out"], dtype=np.float32) for r in res.results], axis=0)
```

```python
import math
import numpy as np
import ml_dtypes
from contextlib import ExitStack
import concourse.bass as bass
import concourse.mybir as mybir
from concourse.bass_utils import run_bass_kernel_spmd

F32 = mybir.dt.float32
BF16 = mybir.dt.bfloat16
ALU = mybir.AluOpType
AF = mybir.ActivationFunctionType
AX = mybir.AxisListType

D = 1024
LC = 256
LL = 4096
T = LC + LL
NT = T // 128
DEPTH = 4
RMS_EPS = 1e-6
LN_EPS = 1e-5
BLK = [(0, 256)] + [(256 + 512 * j, 512) for j in range(8)]
NCH = T // 64


import os as _os
class _Stop(Exception):
    pass


def _ck(n):
    if int(_os.environ.get('BSTOP', '99')) == n:
        raise _Stop()


class Sched:
    ENG = ('pe', 'act', 'dve', 'pool', 'sp')

    def __init__(s, nc, es, ndsem=24):
        s.nc = nc
        s.eng = dict(pe=nc.tensor, act=nc.scalar, dve=nc.vector, pool=nc.gpsimd, sp=nc.sync)
        s.sem = {e: es.enter_context(nc.semaphore("sem_" + e)) for e in s.ENG}
        s.cnt = {e: 0 for e in s.ENG}
        s.seen = {e: {} for e in s.ENG}
        s.dsem = [es.enter_context(nc.semaphore(f"dsem{i}")) for i in range(ndsem)]
        s.dcnt = [0] * ndsem
        s.dnext = 0
        s.lastw = {}
        s.readers = {}
        s.unsig = False

    def _need(s, e, tok):
        kind, who, val = tok
        if kind == 'e' and who == e and e == 'pe':
            return
        key = (kind, who)
        if s.seen[e].get(key, 0) >= val:
            return
        if kind == 'e':
            assert val <= s.cnt[who], f"wait on unsignaled op {tok} cnt={s.cnt[who]}"
            s.eng[e].wait_ge(s.sem[who], val)
        else:
            s.eng[e].wait_ge(s.dsem[who], val)
        s.seen[e][key] = val

    def _deps(s, e, reads, writes):
        for r in reads:
            t = s.lastw.get(r)
            if t is not None:
                s._need(e, t)
        for w in writes:
            t = s.lastw.get(w)
            if t is not None:
                s._need(e, t)
            for (k, who), val in s.readers.get(w, {}).items():
                s._need(e, (k, who, val))

    def _reg(s, tok, reads, writes):
        for r in reads:
            d = s.readers.setdefault(r, {})
            key = (tok[0], tok[1])
            if d.get(key, 0) < tok[2]:
                d[key] = tok[2]
        for w in writes:
            s.lastw[w] = tok
            s.readers[w] = {}

    def op(s, e, fn, reads=(), writes=(), sig=True):
        s._deps(e, reads, writes)
        ins = fn(s.eng[e])
        if sig:
            s.cnt[e] += 1
            ins.then_inc(s.sem[e], 1)
            tok = ('e', e, s.cnt[e])
            if e == 'pe':
                s.unsig = False
        else:
            assert e == 'pe'
            tok = ('e', e, s.cnt[e] + 1)
            s.unsig = True
        s._reg(tok, reads, writes)

    def dma(s, q, out, in_, reads=(), writes=()):
        s._deps(q, reads, writes)
        i = s.dnext
        s.dnext = (s.dnext + 1) % len(s.dsem)
        if s.dcnt[i] > 0:
            s._need(q, ('d', i, 16 * s.dcnt[i]))
        s.dcnt[i] += 1
        s.eng[q].dma_start(out=out, in_=in_).then_inc(s.dsem[i], 16)
        s._reg(('d', i, 16 * s.dcnt[i]), reads, writes)

    def barrier(s):
        assert not s.unsig
        toks = [('e', e, s.cnt[e]) for e in s.ENG if s.cnt[e] > 0]
        toks += [('d', i, 16 * s.dcnt[i]) for i in range(len(s.dsem)) if s.dcnt[i] > 0]
        for e in s.ENG:
            for t in toks:
                s._need(e, t)
        s.lastw.clear()
        s.readers.clear()


def build(nc, nlayers=DEPTH, dbg=False, stages="HABCDO"):
    es = ExitStack()
    S = Sched(nc, es)

    def din(name, shape, dt=F32):
        return nc.dram_tensor(name, list(shape), dt, kind="ExternalInput").ap()

    x_in = din("x_b", [LL, D])
    c_in = din("ctx_b", [LC, D])
    cfm = din("cfm", [128, 16])
    w_mod = din("w_mod", [DEPTH, D, 3 * D])
    b_mod = din("b_mod", [DEPTH, 3 * D])
    b_mod_fm = din("b_mod_fm", [DEPTH, 128, 24])
    g_pre_fm = din("g_pre_fm", [DEPTH, 128, 8])
    g_post = din("g_post", [DEPTH, D])
    w_in = din("w_in", [DEPTH, D, 3584])
    w_out = din("w_out", [DEPTH, D, D])
    lru_cw = din("lru_cw", [DEPTH, 128, 8])
    lru_cb = din("lru_cb", [DEPTH, 128, 2])
    lru_bd = din("lru_bd", [DEPTH, 2, 4, 128, 128])
    lru_b = din("lru_b", [DEPTH, 128, 8])
    lru_lam = din("lru_lam", [DEPTH, 128, 4])
    hg_lb = din("hg_lb", [128, 16])
    hg_g = din("hg_g", [DEPTH, 128, 1])
    cf_w = din("cf_w", [DEPTH, 128, 62])
    cf_b = din("cf_b", [DEPTH, 128, 6])
    df_lam = din("df_lam", [DEPTH, 128])
    df_g = din("df_g", [DEPTH, 128, 1])
    c_ident = din("c_ident", [128, 128], BF16)
    c_cos = din("c_cos", [128, LL])
    c_sin = din("c_sin", [128, LL])
    c_mask = din("c_mask", [128, 128])
    c_bd64 = din("c_bd64", [128, 128], BF16)

    out = nc.dram_tensor("out", [LL, D], F32, kind="ExternalOutput").ap()
    xcs = nc.dram_tensor("xcs", [LC, D], F32).ap()
    ymix = nc.dram_tensor("ymix", [D, T], BF16, kind="ExternalOutput" if dbg else "Internal").ap()
    ggd = nc.dram_tensor("ggd", [DEPTH, 2, D], F32, kind="ExternalOutput" if dbg else "Internal").ap()

    uid = [0]

    def sb(name, shape, dt=F32, ctx=None):
        uid[0] += 1
        return (ctx or es).enter_context(nc.sbuf_tensor(f"{name}_{uid[0]}", list(shape), dt))

    PS = [es.enter_context(nc.psum_tensor(f"ps{i}", [128, 512], F32)) for i in range(8)]

    def pk(i):
        return ('ps', i)

    hT = sb("hT", [128, 8, T], BF16)
    identb = sb("identb", [128, 128], BF16)
    bd64 = sb("bd64", [128, 128], BF16)
    ones64 = sb("ones64", [128, 64], BF16)
    ones256 = sb("ones256", [128, 128], BF16)
    maskt = sb("maskt", [128, 128], F32)
    GS = sb("GS", [128, DEPTH, 4, 8], F32)
    LBt = sb("LBt", [128, 3, 4, 4], F32)
    epsr = sb("epsr", [128, 1], F32)
    epsl = sb("epsl", [128, 1], F32)
    onec = sb("onec", [128, 1], F32)

    S.dma('sp', identb[:], c_ident[:, :], writes=['identb'])
    S.dma('sp', bd64[:], c_bd64[:, :], writes=['bd64'])
    S.dma('sp', maskt[:], c_mask[:, :], writes=['maskt'])
    S.op('dve', lambda e: e.memset(ones64[:], 1.0), writes=['ones64'])
    S.op('dve', lambda e: e.memset(ones256[:], 1.0 / 256.0), writes=['ones256'])
    S.op('dve', lambda e: e.memset(epsr[:], RMS_EPS), writes=['epsr'])
    S.op('dve', lambda e: e.memset(epsl[:], LN_EPS), writes=['epsl'])
    S.op('dve', lambda e: e.memset(onec[:], 1.0), writes=['onec'])
    for i_ in range(8):
        S.op('dve', lambda e, i_=i_: e.memset(PS[i_][:, :], 0.0), writes=[pk(i_)])

    def act(out_, in_, func, reads, writes, bias=None, scale=None):
        kw = {}
        if bias is not None:
            kw['bias'] = bias
        if scale is not None:
            kw['scale'] = scale
        S.op('act', lambda e: e.activation(out=out_, in_=in_, func=func, **kw), reads=reads, writes=writes)

    def rstd_from_ss(rs, ss, n_inv, eps_ap, key_rs, key_ss):
        act(rs, ss, AF.Ln, [key_ss], [key_rs], bias=eps_ap, scale=n_inv)
        act(rs, rs, AF.Exp, [key_rs], [key_rs], scale=-0.5)

    def prologue():
        with ExitStack() as cx:
            cf = sb("cf", [128, 16], F32, cx)
            sc = sb("sc", [128, 16], F32, cx)
            rep = sb("rep", [128, 2, 8, 128], F32, cx)
            wms = [sb(f"wm{i}", [128, 8, 512], F32, cx) for i in range(2)]
            bfm = sb("bfm", [128, 24], F32, cx)
            gpf = sb("gpf", [128, 8], F32, cx)
            bg = sb("bg", [128, 1024], F32, cx)
            gp = sb("gp", [128, 1024], F32, cx)
            tmpg = sb("tmpg", [128, 1024], F32, cx)
            tmp = sb("ptmp", [128, 16], F32, cx)
            hl = sb("hl", [128, 16], F32, cx)
            hs = sb("hs", [128, 4], F32, cx)
            S.dma('sp', cf[:], cfm[:, :], writes=['cf'])
            S.dma('sp', hl[:], hg_lb[:, :], writes=['hl'])
            act(sc[:], cf[:], AF.Silu, ['cf'], ['sc'])
            for j in range(2):
                src = sc[:].rearrange("p (k j) -> p k j", j=2)[:, :, j:j + 1].broadcast_to([128, 8, 128])
                S.op('dve', lambda e, j=j, src=src: e.tensor_copy(out=rep[:, j, :, :], in_=src), reads=['sc'], writes=['rep'])
            act(hl[:], hl[:], AF.Exp, ['hl'], ['hl'])
            hl3 = hl[:].rearrange("p (a l) -> p a l", l=4)
            S.op('dve', lambda e: e.reduce_sum(out=hs[:], in_=hl3, axis=AX.X), reads=['hl'], writes=['hs'])
            S.op('dve', lambda e: e.reciprocal(out=hs[:], in_=hs[:]), reads=['hs'], writes=['hs'])
            S.op('dve', lambda e: e.tensor_tensor(out=hl3, in0=hl3, in1=hs[:].unsqueeze(2).broadcast_to([128, 4, 4]), op=ALU.mult),
                 reads=['hl', 'hs'], writes=['hl'])
            S.op('dve', lambda e: e.memset(LBt[:, 0, :, 0:1], 0.0), writes=['LBt'])
            S.op('dve', lambda e: e.tensor_copy(out=LBt[:, 0, :, 1:2], in_=hl3[:, :, 1:2]), reads=['hl', 'LBt'], writes=['LBt'])
            for l in (2, 3):
                S.op('dve', lambda e, l=l: e.tensor_tensor(out=LBt[:, 0, :, l:l + 1], in0=LBt[:, 0, :, l - 1:l], in1=hl3[:, :, l:l + 1], op=ALU.add),
                     reads=['hl', 'LBt'], writes=['LBt'])
            S.op('dve', lambda e: e.tensor_scalar(out=LBt[:, 1, :, :], in0=LBt[:, 0, :, :], scalar1=-1.0, scalar2=1.0, op0=ALU.mult, op1=ALU.add),
                 reads=['LBt'], writes=['LBt'])
            S.op('dve', lambda e: e.tensor_scalar(out=LBt[:, 2, :, :], in0=LBt[:, 1, :, :], scalar1=-1.0, scalar2=None, op0=ALU.mult),
                 reads=['LBt'], writes=['LBt'])
            for l in range(nlayers):
                S.dma('sp', bfm[:], b_mod_fm[l], writes=['bfm'])
                S.dma('sp', gpf[:], g_pre_fm[l], writes=['gpf'])
                S.dma('sp', bg[:], b_mod[l:l + 1, 2048:3072].partition_broadcast(128), writes=['bg'])
                S.dma('sp', gp[:], g_post[l:l + 1, :].partition_broadcast(128), writes=['gp'])
                wv = w_mod[l].rearrange("(k p) n -> p k n", p=128)
                for sl in range(6):
                    wm = wms[sl % 2]
                    wk = f'wm{sl % 2}'
                    S.dma('sp', wm[:], wv[:, :, sl * 512:(sl + 1) * 512], writes=[wk])
                    if sl < 4:
                        for c in range(4):
                            cc = sl * 4 + c
                            for k in range(8):
                                S.op('pe', lambda e, k=k, c=c, cc=cc, wm=wm: e.matmul(PS[0][:, 2 * cc:2 * cc + 2], lhsT=wm[:, k, c * 128:(c + 1) * 128],
                                                                                   rhs=sc[:, 2 * k:2 * k + 2], start=(k == 0), stop=(k == 7)),
                                     reads=[wk, 'sc'], writes=[pk(0)], sig=(k == 7))
                    else:
                        half = sl - 4
                        for j in range(2):
                            for k in range(8):
                                S.op('pe', lambda e, k=k, j=j, half=half, wm=wm: e.matmul(PS[1 + 2 * j + half][:, :], lhsT=rep[:, j, k, :], rhs=wm[:, k, :],
                                                                                        start=(k == 0), stop=(k == 7)),
                                     reads=[wk, 'rep'], writes=[pk(1 + 2 * j + half)], sig=(k == 7))
                psv = PS[0][:, 0:32].rearrange("p (w k j) -> p w k j", w=2, k=8, j=2)
                for j in range(2):
                    S.op('dve', lambda e, j=j: e.tensor_tensor(out=GS[:, l, 1 + 2 * j, :], in0=psv[:, 0, :, j], in1=bfm[:, 0:8], op=ALU.add),
                         reads=[pk(0), 'bfm'], writes=['GS'])
                    S.op('dve', lambda e, j=j: e.tensor_tensor(out=tmp[:, 0:8], in0=psv[:, 1, :, j], in1=bfm[:, 8:16], op=ALU.add),
                         reads=[pk(0), 'bfm'], writes=['ptmp'])
                    S.op('dve', lambda e, j=j: e.scalar_tensor_tensor(out=GS[:, l, 2 * j, :], in0=tmp[:, 0:8], scalar=1.0, in1=gpf[:], op0=ALU.add, op1=ALU.mult),
                         reads=['ptmp', 'gpf'], writes=['GS'])
                for j in range(2):
                    for half in range(2):
                        S.op('dve', lambda e, j=j, half=half: e.tensor_tensor(out=tmpg[:, half * 512:(half + 1) * 512], in0=PS[1 + 2 * j + half][:, :],
                                                                            in1=bg[:, half * 512:(half + 1) * 512], op=ALU.add),
                             reads=[pk(1 + 2 * j + half), 'bg'], writes=['tmpg'])
                    S.op('dve', lambda e: e.tensor_tensor(out=tmpg[:], in0=tmpg[:], in1=gp[:], op=ALU.mult), reads=['tmpg', 'gp'], writes=['tmpg'])
                    S.dma('sp', ggd[l, j:j + 1, :], tmpg[0:1, :], reads=['tmpg'], writes=['ggd'])
        S.barrier()

    wslab = {}

    def wkeys(wkey, c0, n):
        if wkey not in wslab:
            return [wkey]
        sl = wslab[wkey]
        return [(wkey, j) for j in range(c0 // sl, (c0 + n - 1) // sl + 1)]

    def load_w(dst, l, src, col0, ncols, stgs, dkey, off=0, slab=256):
        wv = src[l].rearrange("(k p) n -> p k n", p=128)
        i = 0
        for c in range(0, ncols, slab):
            n = min(slab, ncols - c)
            st, sk = stgs[i % len(stgs)]
            i += 1
            S.dma('sp', st[:, :, 0:n], wv[:, :, col0 + c:col0 + c + n], writes=[sk])
            wslab[dkey] = slab
            dk_ = (dkey, (off + c) // slab)
            if i % 2 == 0:
                S.op('dve', lambda e, st=st, c=c, n=n: e.tensor_copy(out=dst[:, :, off + c:off + c + n], in_=st[:, :, 0:n]), reads=[sk], writes=[dk_])
            else:
                act(dst[:, :, off + c:off + c + n], st[:, :, 0:n], AF.Identity, [sk], [dk_])

    def proj_fm(b, W, wkey, wc0, t0, n, wn=128):
        for k in range(8):
            S.op('pe', lambda e, k=k: e.matmul(PS[b][0:wn, 0:n], lhsT=W[:, k, wc0:wc0 + wn], rhs=hT[:, k, t0:t0 + n], start=(k == 0), stop=(k == 7)),
                 reads=wkeys(wkey, wc0, wn) + ['hT'], writes=[pk(b)], sig=(k == 7))

    def proj_tm(ps_ap, b, W, wkey, wc0, ncols, tile):
        for k in range(8):
            S.op('pe', lambda e, k=k: e.matmul(ps_ap, lhsT=hT[:, k, tile * 128:(tile + 1) * 128], rhs=W[:, k, wc0:wc0 + ncols], start=(k == 0), stop=(k == 7)),
                 reads=wkeys(wkey, wc0, ncols) + ['hT'], writes=[pk(b)], sig=(k == 7))

    def res_src(l, i):
        if i < 2:
            base = c_in if l == 0 else xcs
            return base[i * 128:(i + 1) * 128, :]
        base = x_in if l == 0 else out
        return base[(i - 2) * 128:(i - 1) * 128, :]

    def res_dst(i):
        if i < 2:
            return xcs[i * 128:(i + 1) * 128, :]
        return out[(i - 2) * 128:(i - 1) * 128, :]

    def stageH(l):
        with ExitStack() as cx:
            xts = [sb(f"hx{i}", [128, 1024], F32, cx) for i in range(2)]
            sq = sb("hsq", [128, 1024], F32, cx)
            xss = [sb(f"hxs{i}", [128, 1024], BF16, cx) for i in range(2)]
            st = sb("hst", [128, 4], F32, cx)
            for i in range(NT):
                p = i % 2
                xt, xs = xts[p], xss[p]
                jj = 0 if i >= 2 else 2
                S.dma('sp', xt[:], res_src(l, i), reads=[('res', i)], writes=[f'hx{p}'])
                act(sq[:], xt[:], AF.Square, [f'hx{p}'], ['hsq'])
                S.op('dve', lambda e, p=p: e.reduce_sum(out=st[:, p:p + 1], in_=sq[:], axis=AX.X), reads=['hsq'], writes=[f'hss{p}'])
                rstd_from_ss(st[:, 2 + p:3 + p], st[:, p:p + 1], 1.0 / D, epsr[:], f'hrs{p}', f'hss{p}')
                S.op('dve', lambda e, p=p, xt=xt, xs=xs: e.tensor_scalar(out=xs[:], in0=xt[:], scalar1=st[:, 2 + p:3 + p], scalar2=None, op0=ALU.mult),
                     reads=[f'hx{p}', f'hrs{p}'], writes=[f'hxs{p}'])
                b = 6 + p
                psb = PS[b][:].bitcast(BF16)
                for k in range(8):
                    S.op('pe', lambda e, k=k, xs=xs, psb=psb: e.transpose(out=psb[:, k * 128:(k + 1) * 128], in_=xs[:, k * 128:(k + 1) * 128], identity=identb[:]),
                         reads=[f'hxs{p}', 'identb'], writes=[pk(b)], sig=(k == 7))
                for k in range(8):
                    S.op('dve', lambda e, k=k, psb=psb, i=i, jj=jj: e.tensor_scalar(out=hT[:, k, i * 128:(i + 1) * 128], in0=psb[:, k * 128:(k + 1) * 128],
                                                                             scalar1=GS[:, l, jj, k:k + 1], scalar2=GS[:, l, jj + 1, k:k + 1],
                                                                             op0=ALU.mult, op1=ALU.add),
                         reads=[pk(b), 'GS'], writes=['hT'])
        S.barrier()

    def stageO(l, last, fuse_next=False):
        with ExitStack() as cx:
            wo = sb("wo", [128, 8, 1024], BF16, cx)
            stgs = [(sb(f"ostg{i}", [128, 8, 256], F32, cx), f'ostg{i}') for i in range(2)]
            GG = [sb(f"GG{j}", [128, 1024], F32, cx) for j in range(2)]
            yms = [sb(f"oym{i}", [128, 8, 512], BF16, cx) for i in range(2)]
            xts = [sb(f"ox{i}", [128, 1024], F32, cx) for i in range(2)]
            sq = sb("osq", [128, 1024], F32, cx)
            tts = [sb(f"ot{i}", [128, 1024], F32, cx) for i in range(2)]
            st = sb("ost", [128, 8], F32, cx)
            if fuse_next:
                sq2 = sb("osq2", [128, 1024], F32, cx)
                xss = [sb(f"oxs{i}", [128, 1024], BF16, cx) for i in range(2)]
            load_w(wo, l, w_out, 0, 1024, stgs, 'wo')
            for j in range(2):
                S.dma('sp', GG[j][:], ggd[l, j:j + 1, :].partition_broadcast(128), reads=['ggd'], writes=[f'GG{j}'])
            ymv = ymix.rearrange("(k p) t -> p k t", p=128)
            hq = [None, None]
            blks = [(t0, n) for bi, (t0, n) in enumerate(BLK) if not (last and bi == 0)]
            otiles = [(bx, tt, (t0 // 128) + tt) for bx, (t0, n) in enumerate(blks) for tt in range(n // 128)]

            def load_ym(bx):
                t0, n = blks[bx]
                S.dma('sp', yms[bx % 2][:, :, 0:n], ymv[:, :, t0:t0 + n], reads=['ymix'], writes=[f'oym{bx % 2}'])

            def load_xt(ti):
                S.dma('sp', xts[ti % 2][:], res_src(l, otiles[ti][2]), reads=[('res', otiles[ti][2])], writes=[f'ox{ti % 2}'])

            load_ym(0)
            load_xt(0)
            for ti, (bx, tt, i) in enumerate(otiles):
                if True:
                    if tt == 0 and bx + 1 < len(blks):
                        load_ym(bx + 1)
                    if ti + 1 < len(otiles):
                        load_xt(ti + 1)
                    ym = yms[bx % 2]
                    yk = f'oym{bx % 2}'
                    j = 0 if i >= 2 else 1
                    p = ti % 2
                    xt, tq = xts[p], tts[p]
                    for half in range(2):
                        b = 2 * p + half
                        for k in range(8):
                            S.op('pe', lambda e, k=k, b=b, half=half, ym=ym, tt=tt: e.matmul(PS[b][:, :], lhsT=ym[:, k, tt * 128:(tt + 1) * 128],
                                                                                       rhs=wo[:, k, half * 512:(half + 1) * 512], start=(k == 0), stop=(k == 7)),
                                 reads=[yk] + wkeys('wo', half * 512, 512), writes=[pk(b)], sig=(k == 7))
                        act(sq[:, half * 512:(half + 1) * 512], PS[b][:, :], AF.Square, [pk(b)], ['osq'])
                    S.op('dve', lambda e, p=p: e.reduce_sum(out=st[:, p:p + 1], in_=sq[:], axis=AX.X), reads=['osq'], writes=[f'oss{p}'])
                    rstd_from_ss(st[:, 2 + p:3 + p], st[:, p:p + 1], 1.0 / D, epsr[:], f'ors{p}', f'oss{p}')
                    for half in range(2):
                        b = 2 * p + half
                        S.op('dve', lambda e, b=b, half=half, tq=tq, p=p, j=j: e.scalar_tensor_tensor(out=tq[:, half * 512:(half + 1) * 512], in0=PS[b][:, :],
                                                                                                scalar=st[:, 2 + p:3 + p], in1=GG[j][:, half * 512:(half + 1) * 512],
                                                                                                op0=ALU.mult, op1=ALU.mult),
                             reads=[pk(b), f'ors{p}', f'GG{j}'], writes=[f'ot{p}'])
                    S.op('pool', lambda e, tq=tq, xt=xt: e.tensor_tensor(out=tq[:], in0=tq[:], in1=xt[:], op=ALU.add), reads=[f'ot{p}', f'ox{p}'], writes=[f'ot{p}'])
                    S.dma('sp', res_dst(i), tq[:], reads=[f'ot{p}'], writes=[('res', i)])
                    if fuse_next:
                        def hpart(i=i, p=p, tq=tq):
                            ln = l + 1
                            jj = 0 if i >= 2 else 2
                            xs = xss[p]
                            act(sq2[:], tq[:], AF.Square, [f'ot{p}'], ['osq2'])
                            S.op('dve', lambda e, p=p: e.reduce_sum(out=st[:, 4 + p:5 + p], in_=sq2[:], axis=AX.X), reads=['osq2'], writes=[f'oss2{p}'])
                            rstd_from_ss(st[:, 6 + p:7 + p], st[:, 4 + p:5 + p], 1.0 / D, epsr[:], f'ors2{p}', f'oss2{p}')
                            act(xs[:], tq[:], AF.Identity, [f'ot{p}', f'ors2{p}'], [f'oxs{p}'], scale=st[:, 6 + p:7 + p])
                        def hpartB(i=i, p=p):
                            ln = l + 1
                            jj = 0 if i >= 2 else 2
                            xs = xss[p]
                            for k in range(8):
                                b = (4 + p) if k < 4 else (6 + p)
                                psb = PS[b][:].bitcast(BF16)
                                S.op('pe', lambda e, k=k, xs=xs, psb=psb: e.transpose(out=psb[:, (k % 4) * 128:(k % 4 + 1) * 128], in_=xs[:, k * 128:(k + 1) * 128], identity=identb[:]),
                                     reads=[f'oxs{p}', 'identb'], writes=[pk(b)], sig=(k % 4 == 3))
                            for k in range(8):
                                b = (4 + p) if k < 4 else (6 + p)
                                psb = PS[b][:].bitcast(BF16)
                                if k < 4:
                                    act(hT[:, k, i * 128:(i + 1) * 128], psb[:, (k % 4) * 128:(k % 4 + 1) * 128], AF.Identity, [pk(b), 'GS'], [('hTa', k)],
                                        bias=GS[:, ln, jj + 1, k:k + 1], scale=GS[:, ln, jj, k:k + 1])
                                else:
                                    S.op('dve', lambda e, k=k, psb=psb, i=i, jj=jj, ln=ln: e.tensor_scalar(out=hT[:, k, i * 128:(i + 1) * 128], in0=psb[:, (k % 4) * 128:(k % 4 + 1) * 128],
                                                                                                scalar1=GS[:, ln, jj, k:k + 1], scalar2=GS[:, ln, jj + 1, k:k + 1],
                                                                                                op0=ALU.mult, op1=ALU.add),
                                         reads=[pk(b), 'GS'], writes=[('hTd', k)])
                        if hq[1] is not None:
                            hq[1]()
                            hq[1] = None
                        if hq[0] is not None:
                            hq[0][0]()
                            hq[1] = hq[0][1]
                        hq[0] = (hpart, hpartB)
            if fuse_next:
                if hq[1] is not None:
                    hq[1]()
                if hq[0] is not None:
                    hq[0][0]()
                    hq[0][1]()
        S.barrier()

    NG = T + 3

    def stageA(l):
        with ExitStack() as cx:
            W = sb("aW", [128, 8, 512], BF16, cx)
            stgs = [(sb(f"astg{i}", [128, 8, 256], F32, cx), f'astg{i}') for i in range(2)]
            bdst = sb("abdst", [128, 4, 128], F32, cx)
            bd = sb("abd", [128, 4, 128], BF16, cx)
            cw = sb("acw", [128, 8], F32, cx)
            cb = sb("acb", [128, 2], F32, cx)
            lbias = sb("alb", [128, 8], F32, cx)
            lam = sb("alam", [128, 4], F32, cx)
            B = [sb(f"aB{i}", [128, NG + 3], F32, cx) for i in range(5)]
            XCb = sb("aXCb", [128, NG], BF16, cx)
            SGt = sb("aSG", [128, T], BF16, cx)
            Yb = XCb
            load_w(W, l, w_in, 0, 512, stgs, 'aW')
            S.dma('sp', cw[:], lru_cw[l], writes=['acw'])
            S.dma('sp', cb[:], lru_cb[l], writes=['acb'])
            S.dma('sp', lbias[:], lru_b[l], writes=['alb'])
            S.dma('sp', lam[:], lru_lam[l], writes=['alam'])
            act(lam[:], lam[:], AF.Exp, ['alam'], ['alam'], scale=-1.0)
            act(lam[:], lam[:], AF.Ln, ['alam'], ['alam'], bias=onec[:], scale=1.0)
            S.op('dve', lambda e: e.tensor_scalar(out=lam[:], in0=lam[:], scalar1=-8.0, scalar2=None, op0=ALU.mult), reads=['alam'], writes=['alam'])
            for pc in range(2):
                UX, XC = B[0], B[4]
                for (a, b_) in ((0, 2), (258, 261), (NG + 2, NG + 3)):
                    S.op('dve', lambda e, a=a, b_=b_: e.memset(UX[:, a:b_], 0.0), writes=['aB0'])
                for bi, (t0, n) in enumerate(BLK):
                    ux0 = 2 + t0 if t0 < 256 else 261 + (t0 - 256)
                    b = bi % 2
                    proj_fm(b, W, 'aW', pc * 128, t0, n)
                    S.op('dve', lambda e, b=b, ux0=ux0, n=n: e.tensor_copy(out=UX[:, ux0:ux0 + n], in_=PS[b][:, 0:n]), reads=[pk(b)], writes=['aB0'])
                    b2 = 2 + bi % 2
                    proj_fm(b2, W, 'aW', 256 + pc * 128, t0, n)
                    act(SGt[:, t0:t0 + n], PS[b2][:, 0:n], AF.Silu, [pk(b2)], ['aSG'])
                S.op('dve', lambda e: e.tensor_scalar(out=XC[:, 0:NG], in0=UX[:, 0:NG], scalar1=cw[:, pc * 4:pc * 4 + 1], scalar2=cb[:, pc:pc + 1],
                                                      op0=ALU.mult, op1=ALU.add), reads=['aB0', 'acw', 'acb'], writes=['aB4'])
                for k in range(1, 4):
                    S.op('dve', lambda e, k=k: e.scalar_tensor_tensor(out=XC[:, 0:NG], in0=UX[:, k:k + NG], scalar=cw[:, pc * 4 + k:pc * 4 + k + 1], in1=XC[:, 0:NG],
                                                                      op0=ALU.mult, op1=ALU.add), reads=['aB0', 'aB4', 'acw'], writes=['aB4'])
                S.op('pool', lambda e: e.tensor_copy(out=XCb[:], in_=XC[:, 0:NG]), reads=['aB4'], writes=['aXCb'])
                S.dma('sp', bdst[:], lru_bd[l, pc].rearrange("w a b -> a w b"), writes=['abdst'])
                S.op('pool', lambda e: e.tensor_copy(out=bd[:], in_=bdst[:]), reads=['abdst'], writes=['abd'])
                for dr in range(2):
                    R, I, A = B[0], B[1], B[2]
                    for gi, g0 in enumerate(range(0, NG, 512)):
                        n = min(512, NG - g0)
                        for wh, (dst, dk) in enumerate(((R, 'aB0'), (I, 'aB1'))):
                            b = 2 * (gi % 2) + wh
                            S.op('pe', lambda e, b=b, wh=wh, g0=g0, n=n: e.matmul(PS[b][:, 0:n], lhsT=bd[:, 2 * dr + wh, :], rhs=XCb[:, g0:g0 + n], start=True, stop=True),
                                 reads=['abd', 'aXCb'], writes=[pk(b)])
                            bi_ = wh * 4 + dr * 2 + pc
                            act(dst[:, g0:g0 + n], PS[b][:, 0:n], AF.Sigmoid, [pk(b), 'alb'], [dk], bias=lbias[:, bi_:bi_ + 1], scale=1.0)
                    ci = dr * 2 + pc
                    act(A[:, 0:NG], R[:, 0:NG], AF.Exp, ['aB0', 'alam'], ['aB2'], scale=lam[:, ci:ci + 1])
                    act(R[:, 0:NG], A[:, 0:NG], AF.Square, ['aB2'], ['aB0'])
                    act(R[:, 0:NG], R[:, 0:NG], AF.Ln, ['aB0'], ['aB0'], bias=onec[:], scale=-1.0)
                    act(R[:, 0:NG], R[:, 0:NG], AF.Exp, ['aB0'], ['aB0'], scale=0.5)
                    S.op('dve', lambda e: e.tensor_tensor(out=I[:, 0:NG], in0=I[:, 0:NG], in1=XC[:, 0:NG], op=ALU.mult), reads=['aB1', 'aB4'], writes=['aB1'])
                    S.op('dve', lambda e: e.tensor_tensor(out=I[:, 0:NG], in0=I[:, 0:NG], in1=R[:, 0:NG], op=ALU.mult), reads=['aB1', 'aB0'], writes=['aB1'])
                    H, hk = (B[3], 'aB3') if dr == 0 else (B[0], 'aB0')
                    if dr == 0:
                        S.op('dve', lambda e, H=H: e.tensor_tensor_scan(out=H[:, 0:256], data0=A[:, 0:256], data1=I[:, 0:256], initial=0.0, op0=ALU.mult, op1=ALU.add),
                             reads=['aB2', 'aB1'], writes=[hk])
                        S.op('dve', lambda e, H=H: e.tensor_tensor_scan(out=H[:, 259:NG], data0=A[:, 259:NG], data1=I[:, 259:NG], initial=H[:, 255:256],
                                                                         op0=ALU.mult, op1=ALU.add), reads=['aB2', 'aB1', hk], writes=[hk])
                    else:
                        rv = lambda X, a, b_: X[:, a:b_][:, ::-1]
                        S.op('dve', lambda e, H=H: e.tensor_tensor_scan(out=rv(H, 0, 256), data0=rv(A, 0, 256), data1=rv(I, 0, 256), initial=0.0, op0=ALU.mult, op1=ALU.add),
                             reads=['aB2', 'aB1'], writes=[hk])
                        S.op('dve', lambda e, H=H: e.tensor_tensor_scan(out=rv(H, 259, NG), data0=rv(A, 259, NG), data1=rv(I, 259, NG), initial=H[:, 0:1],
                                                                         op0=ALU.mult, op1=ALU.add), reads=['aB2', 'aB1', hk], writes=[hk])
                        S.op('dve', lambda e: e.tensor_tensor(out=B[3][:, 0:NG], in0=B[3][:, 0:NG], in1=B[0][:, 0:NG], op=ALU.add), reads=['aB3', 'aB0'], writes=['aB3'])
                S.op('dve', lambda e: e.tensor_tensor(out=Yb[:, 0:256], in0=B[3][:, 0:256], in1=SGt[:, 0:256], op=ALU.mult), reads=['aB3', 'aSG'], writes=['aXCb'])
                S.op('dve', lambda e: e.tensor_tensor(out=Yb[:, 256:T], in0=B[3][:, 259:NG], in1=SGt[:, 256:T], op=ALU.mult), reads=['aB3', 'aSG'], writes=['aXCb'])
                S.dma('sp', ymix[pc * 128:(pc + 1) * 128, :], Yb[:, 0:T], reads=['aXCb'], writes=['ymix'])
        S.barrier()

    NP = T + 60

    def stageC(l, last):
        with ExitStack() as cx:
            W = sb("cW", [128, 8, 768], BF16, cx)
            stgs = [(sb(f"cstg{i}", [128, 8, 256], F32, cx), f'cstg{i}') for i in range(2)]
            Yp = [sb(f"cYp{i}", [128, NP], BF16, cx) for i in range(2)]
            SG = sb("cSG", [128, 2, T], BF16, cx)
            Dg = sb("cDg", [128, 2, 31, 128], BF16, cx)
            cwt = sb("ccw", [128, 62], F32, cx)
            cbt = sb("ccb", [128, 6], F32, cx)
            sgm = [sb(f"csgm{i}", [128, 512], F32, cx) for i in range(2)]
            Cf = sb("cCf", [128, 2, 512], F32, cx)
            Cb = sb("cCb", [128, 2, 512], BF16, cx)
            Cq = sb("cCq", [128, 2, 512], BF16, cx)
            mean = sb("cmean", [128, 512], F32, cx)
            var = sb("cvar", [128, 512], F32, cx)
            dd = [sb(f"cdd{i}", [128, 512], F32, cx) for i in range(2)]
            yo = [sb(f"cyo{i}", [128, 512], BF16, cx) for i in range(2)]
            load_w(W, l, w_in, 1792, 768, stgs, 'cW')
            S.dma('sp', cwt[:], cf_w[l], writes=['ccw'])
            S.dma('sp', cbt[:], cf_b[l], writes=['ccb'])
            for pc in range(2):
                i0 = identb[:].unsqueeze(1).broadcast_to([128, 31, 128])
                i1 = cwt[:, pc * 31:(pc + 1) * 31].unsqueeze(2).broadcast_to([128, 31, 128])
                S.op('dve', lambda e, pc=pc, i0=i0, i1=i1: e.tensor_tensor(out=Dg[:, pc, :, :], in0=i0, in1=i1, op=ALU.mult), reads=['identb', 'ccw'], writes=['cDg'])
                S.op('pool', lambda e, pc=pc: e.memset(Yp[pc][:], 0.0), writes=[f'cYp{pc}'])
            for bi, (t0, n) in enumerate(BLK):
                p0 = 15 + t0 if t0 < 256 else 301 + (t0 - 256)
                for pc in range(2):
                    q = (bi * 2 + pc) % 2
                    proj_fm(0 + q, W, 'cW', pc * 128, t0, n)
                    proj_fm(2 + q, W, 'cW', 256 + pc * 128, t0, n)
                    act(sgm[q][:, 0:n], PS[2 + q][:, 0:n], AF.Sigmoid, [pk(2 + q)], [f'csgm{q}'])
                    S.op('dve', lambda e, pc=pc, q=q, p0=p0, n=n: e.tensor_tensor(out=Yp[pc][:, p0:p0 + n], in0=PS[q][:, 0:n], in1=sgm[q][:, 0:n], op=ALU.mult),
                         reads=[pk(q), f'csgm{q}'], writes=[f'cYp{pc}'])
                    proj_fm(4 + q, W, 'cW', 512 + pc * 128, t0, n)
                    act(SG[:, pc, t0:t0 + n], PS[4 + q][:, 0:n], AF.Silu, [pk(4 + q)], ['cSG'])
            for bi, (t0, n) in enumerate(BLK):
                if last and bi == 0:
                    continue
                p0 = 15 + t0 if t0 < 256 else 301 + (t0 - 256)
                for pc in range(2):
                    b = pc
                    for k in range(31):
                        S.op('pe', lambda e, k=k, pc=pc, b=b: e.matmul(PS[b][:, 0:n], lhsT=Dg[:, pc, k, :], rhs=Yp[pc][:, p0 + k - 15:p0 + k - 15 + n], start=(k == 0), stop=(k == 30)),
                             reads=['cDg', f'cYp{pc}'], writes=[pk(b)], sig=(k == 30))
                    S.op('dve', lambda e, pc=pc, b=b: e.tensor_scalar(out=Cf[:, pc, 0:n], in0=PS[b][:, 0:n], scalar1=cbt[:, pc:pc + 1], scalar2=None, op0=ALU.add),
                         reads=[pk(b), 'ccb'], writes=['cCf'])
                    S.op('pool', lambda e, pc=pc: e.tensor_copy(out=Cb[:, pc, 0:n], in_=Cf[:, pc, 0:n]), reads=['cCf'], writes=['cCb'])
                    S.op('pool', lambda e, pc=pc: e.tensor_tensor(out=Cq[:, pc, 0:n], in0=Cf[:, pc, 0:n], in1=Cf[:, pc, 0:n], op=ALU.mult), reads=['cCf'], writes=['cCq'])
                for pc in range(2):
                    S.op('pe', lambda e, pc=pc: e.matmul(PS[2][:, 0:n], lhsT=ones256[:], rhs=Cb[:, pc, 0:n], start=(pc == 0), stop=(pc == 1)),
                         reads=['ones256', 'cCb'], writes=[pk(2)], sig=(pc == 1))
                for pc in range(2):
                    S.op('pe', lambda e, pc=pc: e.matmul(PS[3][:, 0:n], lhsT=ones256[:], rhs=Cq[:, pc, 0:n], start=(pc == 0), stop=(pc == 1)),
                         reads=['ones256', 'cCq'], writes=[pk(3)], sig=(pc == 1))
                S.op('dve', lambda e: e.tensor_copy(out=mean[:, 0:n], in_=PS[2][:, 0:n]), reads=[pk(2)], writes=['cmean'])
                S.op('dve', lambda e: e.tensor_tensor(out=var[:, 0:n], in0=mean[:, 0:n], in1=mean[:, 0:n], op=ALU.mult), reads=['cmean'], writes=['cvar'])
                S.op('dve', lambda e: e.tensor_tensor(out=var[:, 0:n], in0=PS[3][:, 0:n], in1=var[:, 0:n], op=ALU.subtract), reads=[pk(3), 'cvar'], writes=['cvar'])
                act(var[:, 0:n], var[:, 0:n], AF.Ln, ['cvar'], ['cvar'], bias=epsl[:], scale=1.0)
                act(var[:, 0:n], var[:, 0:n], AF.Exp, ['cvar'], ['cvar'], scale=-0.5)
                for pc in range(2):
                    d_, y_ = dd[pc], yo[pc]
                    S.op('dve', lambda e, pc=pc, d_=d_: e.tensor_tensor(out=d_[:, 0:n], in0=Cf[:, pc, 0:n], in1=mean[:, 0:n], op=ALU.subtract), reads=['cCf', 'cmean'], writes=[f'cdd{pc}'])
                    S.op('dve', lambda e, pc=pc, d_=d_: e.tensor_tensor(out=d_[:, 0:n], in0=d_[:, 0:n], in1=var[:, 0:n], op=ALU.mult), reads=[f'cdd{pc}', 'cvar'], writes=[f'cdd{pc}'])
                    act(d_[:, 0:n], d_[:, 0:n], AF.Silu, [f'cdd{pc}', 'ccb'], [f'cdd{pc}'], bias=cbt[:, 4 + pc:5 + pc], scale=cbt[:, 2 + pc:3 + pc])
                    S.op('dve', lambda e, pc=pc, d_=d_, y_=y_: e.tensor_tensor(out=y_[:, 0:n], in0=d_[:, 0:n], in1=SG[:, pc, t0:t0 + n], op=ALU.mult),
                         reads=[f'cdd{pc}', 'cSG'], writes=[f'cyo{pc}'])
                    S.dma('sp', ymix[512 + pc * 128:512 + (pc + 1) * 128, t0:t0 + n], y_[:, 0:n], reads=[f'cyo{pc}'], writes=['ymix'])
        S.barrier()

    def stageD(l, last):
        lam_init = 0.8 - 0.6 * math.exp(-0.3 * l)
        scale = 32 ** -0.5
        with ExitStack() as cx:
            KT = sb("dKT", [128, 2, T], BF16, cx)
            QT = sb("dQT", [128, 2, T], BF16, cx)
            V = sb("dV", [128, NT, 384], BF16, cx)
            SG = sb("dSG", [128, 2, T], BF16, cx)
            lmt = sb("dlm", [128, 128], F32, cx)
            lms = sb("dls", [128, 4], F32, cx)
            gn = sb("dgn", [128, 1], F32, cx)
            S.dma('sp', lmt[:], df_lam[l:l + 1, :].partition_broadcast(128), writes=['dlm'])
            S.dma('sp', gn[:], df_g[l], writes=['dgn'])
            lm4 = lmt[:].rearrange("p (a d) -> p a d", d=32)
            S.op('dve', lambda e: e.tensor_tensor(out=lm4[:, 0, :], in0=lm4[:, 0, :], in1=lm4[:, 1, :], op=ALU.mult), reads=['dlm'], writes=['dlm'])
            S.op('dve', lambda e: e.tensor_tensor(out=lm4[:, 2, :], in0=lm4[:, 2, :], in1=lm4[:, 3, :], op=ALU.mult), reads=['dlm'], writes=['dlm'])
            S.op('dve', lambda e: e.reduce_sum(out=lms[:, 0:1], in_=lm4[:, 0, :], axis=AX.X), reads=['dlm'], writes=['dls'])
            S.op('dve', lambda e: e.reduce_sum(out=lms[:, 1:2], in_=lm4[:, 2, :], axis=AX.X), reads=['dlm'], writes=['dls'])
            act(lms[:, 0:2], lms[:, 0:2], AF.Exp, ['dls'], ['dls'])
            S.op('dve', lambda e: e.tensor_tensor(out=lms[:, 2:3], in0=lms[:, 1:2], in1=lms[:, 0:1], op=ALU.subtract), reads=['dls'], writes=['dls'])
            S.op('dve', lambda e: e.tensor_scalar(out=lms[:, 2:3], in0=lms[:, 2:3], scalar1=-lam_init, scalar2=None, op0=ALU.add), reads=['dls'], writes=['dls'])
            S.op('dve', lambda e: e.tensor_scalar(out=gn[:], in0=gn[:], scalar1=(1.0 - lam_init), scalar2=None, op0=ALU.mult), reads=['dgn'], writes=['dgn'])
            with ExitStack() as c1:
                W = sb("dW", [128, 8, 1024], BF16, c1)
                Wsw = sb("dWsw", [128, 8, 512], BF16, c1)
                stgs = [(sb(f"dstg{i}", [128, 8, 128], F32, c1), f'dstg{i}') for i in range(2)]
                cs = [sb(f"dcs{i}", [128, 512], F32, c1) for i in range(2)]
                sn = [sb(f"dsn{i}", [128, 512], F32, c1) for i in range(2)]
                t1 = [sb(f"dt1{i}", [128, 512], F32, c1) for i in range(2)]
                t2 = [sb(f"dt2{i}", [128, 512], F32, c1) for i in range(2)]
                S.op('pool', lambda e: e.memset(V[:], 1.0), writes=['dV'])
                load_w(W, l, w_in, 2560, 1024, stgs, 'dW', slab=128)
                for k in range(8):
                    wv_ = W[:, k, 0:512].rearrange("p (g two j) -> p g two j", two=2, j=16)
                    sv_ = Wsw[:, k, :].rearrange("p (g two j) -> p g two j", two=2, j=16)
                    for h in range(2):
                        S.op('pool', lambda e, wv_=wv_, sv_=sv_, h=h: e.tensor_copy(out=sv_[:, :, 1 - h, :], in_=wv_[:, :, h, :]), reads=wkeys('dW', 0, 512), writes=['dWsw'])
                ci = 0
                for bi, (t0, n) in enumerate(BLK):
                    lat = t0 >= 256
                    cp = bi % 2
                    if lat:
                        S.dma('sp', cs[cp][:, 0:n], c_cos[:, t0 - 256:t0 - 256 + n], writes=[f'dcs{cp}'])
                        S.dma('sp', sn[cp][:, 0:n], c_sin[:, t0 - 256:t0 - 256 + n], writes=[f'dsn{cp}'])
                    for which, (dst, dk) in enumerate(((QT, 'dQT'), (KT, 'dKT'))):
                        for ck in range(2):
                            q = ci % 2
                            ci += 1
                            proj_fm(q, W, 'dW', which * 256 + ck * 128, t0, n)
                            if not lat:
                                S.op('dve', lambda e, dst=dst, ck=ck, q=q: e.tensor_copy(out=dst[:, ck, t0:t0 + n], in_=PS[q][:, 0:n]), reads=[pk(q)], writes=[dk])
                            else:
                                proj_fm(2 + q, Wsw, 'dWsw', which * 256 + ck * 128, t0, n)
                                S.op('dve', lambda e, q=q: e.tensor_tensor(out=t1[q][:, 0:n], in0=PS[q][:, 0:n], in1=cs[cp][:, 0:n], op=ALU.mult),
                                     reads=[pk(q), f'dcs{cp}'], writes=[f'dt1{q}'])
                                S.op('dve', lambda e, q=q: e.tensor_tensor(out=t2[q][:, 0:n], in0=PS[2 + q][:, 0:n], in1=sn[cp][:, 0:n], op=ALU.mult),
                                     reads=[pk(2 + q), f'dsn{cp}'], writes=[f'dt2{q}'])
                                S.op('pool', lambda e, dst=dst, ck=ck, q=q: e.tensor_tensor(out=dst[:, ck, t0:t0 + n], in0=t1[q][:, 0:n], in1=t2[q][:, 0:n], op=ALU.add),
                                     reads=[f'dt1{q}', f'dt2{q}'], writes=[dk])
                    for ck in range(2):
                        b = 4 + ck
                        proj_fm(b, W, 'dW', 768 + ck * 128, t0, n)
                        act(SG[:, ck, t0:t0 + n], PS[b][:, 0:n], AF.Silu, [pk(b)], ['dSG'])
                    for tt in range(n // 128):
                        i = t0 // 128 + tt
                        b = 6 + i % 2
                        proj_tm(PS[b][:, 0:256], b, W, 'dW', 512, 256, i)
                        vv = V[:, i, :].rearrange("p (g c) -> p g c", c=192)
                        pv4 = PS[b][:, 0:256].rearrange("p (g r c) -> p g r c", r=2, c=64)
                        for r_ in range(2):
                            S.op('dve', lambda e, vv=vv, pv4=pv4, r_=r_: e.tensor_copy(out=vv[:, :, 128 * r_:128 * r_ + 64], in_=pv4[:, :, r_, :]), reads=[pk(b)], writes=['dV'])
            S.barrier()
            with ExitStack() as c2:
                Pb = [sb(f"dP{i}", [128, 512], BF16, c2) for i in range(6)]
                Qz = [sb(f"dQz{i}", [128, 4, 512], BF16, c2) for i in range(2)]
                RL = [sb(f"dRL{i}", [128, 512], F32, c2) for i in range(2)]
                Nn = [sb(f"dN{i}", [128, 512], F32, c2) for i in range(2)]
                Oc = [sb(f"dOc{i}", [128, 512], F32, c2) for i in range(4)]
                Oh = sb("dOh", [128, 512], F32, c2)
                Osq = sb("dOsq", [128, 512], BF16, c2)
                rs = sb("drs", [128, 512], F32, c2)
                Yo = [sb(f"dYo{i}", [128, 512], BF16, c2) for i in range(2)]
                pending = []
                state = dict(n=0)
                for i_ in range(2):
                    S.op('pool', lambda e, i_=i_: e.memset(Qz[i_][:], 0.0), writes=[f'dQz{i_}'])

                def prep_q(pi, q0, nq, hp):
                    pp = pi % 2
                    for s_ in range(4):
                        S.op('pool', lambda e, s_=s_: e.tensor_copy(out=Qz[pp][32 * s_:32 * s_ + 32, s_, 0:nq], in_=QT[32 * s_:32 * s_ + 32, hp, q0:q0 + nq]),
                             reads=['dQT'], writes=[f'dQz{pp}'])

                def finalize(q0, nq, hp, yp):
                    steps = []
                    for s_ in range(4):
                        steps.append(lambda s_=s_: S.op('dve', lambda e: e.tensor_copy(out=Oc[s_][:, 0:nq], in_=PS[3 + s_][:, 0:nq]), reads=[pk(3 + s_)], writes=[f'dOc{s_}']))
                    for s_ in range(4):
                        hh, w = s_ // 2, s_ % 2
                        lo, ll = 64 * hh, 64 * (1 - hh)
                        steps.append(lambda s_=s_, w=w, lo=lo, ll=ll: S.op('dve', lambda e: e.reciprocal(out=RL[w][lo:lo + 64, 0:nq], in_=Oc[s_][ll:ll + 64, 0:nq]),
                                                                     reads=[f'dOc{s_}'], writes=[f'dRL{w}']))
                        steps.append(lambda s_=s_, w=w, lo=lo: S.op('dve', lambda e: e.tensor_tensor(out=Nn[w][lo:lo + 64, 0:nq], in0=Oc[s_][lo:lo + 64, 0:nq], in1=RL[w][lo:lo + 64, 0:nq], op=ALU.mult),
                                                              reads=[f'dOc{s_}', f'dRL{w}'], writes=[f'dN{w}']))
                    steps.append(lambda: S.op('dve', lambda e: e.scalar_tensor_tensor(out=Oh[:, 0:nq], in0=Nn[1][:, 0:nq], scalar=lms[:, 2:3], in1=Nn[0][:, 0:nq], op0=ALU.mult, op1=ALU.add),
                                              reads=['dN0', 'dN1', 'dls'], writes=['dOh']))
                    steps.append(lambda: S.op('pool', lambda e: e.tensor_tensor(out=Osq[:, 0:nq], in0=Oh[:, 0:nq], in1=Oh[:, 0:nq], op=ALU.mult), reads=['dOh'], writes=['dOsq']))
                    steps.append(lambda: S.op('pe', lambda e: e.matmul(PS[7][:, 0:nq], lhsT=bd64[:], rhs=Osq[:, 0:nq], start=True, stop=True), reads=['bd64', 'dOsq'], writes=[pk(7)]))
                    steps.append(lambda: rstd_from_ss(rs[:, 0:nq], PS[7][:, 0:nq], 1.0 / 64, epsr[:], 'drs', pk(7)))
                    steps.append(lambda: S.op('dve', lambda e: e.tensor_tensor(out=Oh[:, 0:nq], in0=Oh[:, 0:nq], in1=rs[:, 0:nq], op=ALU.mult), reads=['dOh', 'drs'], writes=['dOh']))
                    steps.append(lambda: S.op('dve', lambda e: e.scalar_tensor_tensor(out=Yo[yp][:, 0:nq], in0=Oh[:, 0:nq], scalar=gn[:, 0:1], in1=SG[:, hp, q0:q0 + nq], op0=ALU.mult, op1=ALU.mult),
                                              reads=['dOh', 'dgn', 'dSG'], writes=[f'dYo{yp}']))
                    steps.append(lambda: S.dma('sp', ymix[768 + hp * 128:768 + (hp + 1) * 128, q0:q0 + nq], Yo[yp][:, 0:nq], reads=[f'dYo{yp}'], writes=['ymix']))
                    return steps

                passes = []
                if not last:
                    for hp in range(2):
                        passes.append((0, 256, [0, 1], hp))
                for qb in range(8):
                    for hp in range(2):
                        passes.append((256 + qb * 512, 512, list(range(NT)), hp))

                def attn_pass(pi):
                    q0, nq, kts, hp = passes[pi]
                    pp = pi % 2
                    seq = [(kt, s_) for kt in kts for s_ in range(4)]
                    nseq = len(seq)
                    base = state['n']

                    def qk(m):
                        kt, s_ = seq[m]
                        b_ = (base + m) % 3
                        S.op('pe', lambda e: e.matmul(PS[b_][:, 0:nq], lhsT=KT[:, hp, kt * 128:(kt + 1) * 128], rhs=Qz[pp][:, s_, 0:nq], start=True, stop=True),
                             reads=['dKT', f'dQz{pp}'], writes=[pk(b_)])

                    def ex(m):
                        g = base + m
                        act(Pb[g % 6][:, 0:nq], PS[g % 3][:, 0:nq], AF.Exp, [pk(g % 3)], [f'dP{g % 6}'], scale=scale)

                    def pv(m):
                        kt, s_ = seq[m]
                        pb = (base + m) % 6
                        hh = s_ // 2
                        c0 = hp * 192 + hh * 64
                        S.op('pe', lambda e: e.matmul(PS[3 + s_][:, 0:nq], lhsT=V[:, kt, c0:c0 + 128], rhs=Pb[pb][:, 0:nq], start=(kt == kts[0]), stop=(kt == kts[-1])),
                             reads=['dV', f'dP{pb}'], writes=[pk(3 + s_)])

                    for m in range(min(3, nseq)):
                        qk(m)
                    if pi + 1 < len(passes):
                        prep_q(pi + 1, passes[pi + 1][0], passes[pi + 1][1], passes[pi + 1][3])
                    for m in range(nseq):
                        ex(m)
                        pv(m)
                        if m + 3 < nseq:
                            qk(m + 3)
                        if pending and m % 2 == 1:
                            pending.pop(0)()
                    state['n'] = base + nseq
                    while pending:
                        pending.pop(0)()
                    fs = finalize(q0, nq, hp, pi % 2)
                    for f_ in fs[:4]:
                        f_()
                    pending.extend(fs[4:])

                prep_q(0, passes[0][0], passes[0][1], passes[0][3])
                for pi in range(len(passes)):
                    attn_pass(pi)
                while pending:
                    pending.pop(0)()
        S.barrier()

    def stageB(l):
        with ExitStack() as cx:
            W = sb("bW", [128, 8, 640], BF16, cx)
            stgs = [(sb(f"bstg{i}", [128, 8, 128], F32, cx), f'bstg{i}') for i in range(2)]
            Vt = sb("bV", [128, NT, 128], BF16, cx)
            OT = sb("bOT", [128, T], F32, cx)
            QTl = sb("bQT", [128, T], BF16, cx)
            KTl = sb("bKT", [128, T], BF16, cx)
            Kt = sb("bKt", [128, NT, 128], BF16, cx)
            Sb = sb("bSb", [128, NCH, 64], BF16, cx)
            KH = Sb[:].rearrange("p c e -> p (c e)")
            M0 = sb("bM0", [128, T], BF16, cx)
            Gc = sb("bGc", [128, T], F32, cx)
            Bt = Gc[:].rearrange("p (c e) -> p c e", e=64)
            T1 = [sb(f"bT1{i}", [128, 512], F32, cx) for i in range(2)]
            T2 = [sb(f"bT2{i}", [128, 512], F32, cx) for i in range(2)]
            sm = sb("bsm", [128, 6, NCH], F32, cx)
            SCm = [sb(f"bSC{i}", [128, 256], BF16, cx) for i in range(4)]
            osq = sb("bosq", [128, 512], BF16, cx)
            ors = sb("bors", [128, 512], F32, cx)
            oy = [sb(f"boy{i}", [128, 512], BF16, cx) for i in range(2)]
            hgn = sb("bhgn", [128, 1], F32, cx)
            S.dma('sp', hgn[:], hg_g[l], writes=['bhgn'])
            groups = [[0, 1]] + [list(range(2 + 8 * g, 10 + 8 * g)) for g in range(4)]
            for pc in range(2):
                for j in range(5):
                    load_w(W, l, w_in, 512 + j * 256 + pc * 128, 128, stgs, 'bW', off=j * 128, slab=128)
                S.op('pool', lambda e: e.memset(OT[:], 0.0), writes=['bOT'])
                for i in range(NT):
                    b = 6 + i % 2
                    proj_tm(PS[b][:, 0:128], b, W, 'bW', 128, 128, i)
                    S.op('dve', lambda e, b=b, i=i: e.tensor_copy(out=Vt[:, i, :], in_=PS[b][:, 0:128]), reads=[pk(b)], writes=['bV'])
                _ck(1)
                for dr in range(2):
                    li = (pc * 2 + dr)
                    lb_ap = LBt[:, 0, li, l:l + 1]
                    oml_ap = LBt[:, 1, li, l:l + 1]
                    S.op('pool', lambda e: e.memset(M0[:], 1.0), writes=['bM0'])
                    m3 = M0[:].rearrange("p (c j) -> p c j", j=64)
                    zc = 0 if dr == 0 else 63
                    S.op('pool', lambda e, zc=zc: e.memset(m3[:, :, zc:zc + 1], 0.0), writes=['bM0'])
                    for bi, (t0, n) in enumerate(BLK):
                        q = bi % 2
                        proj_fm(q, W, 'bW', (2 + dr) * 128, t0, n)
                        act(T1[q][:, 0:n], PS[q][:, 0:n], AF.Exp, [pk(q)], [f'bT1{q}'], scale=-1.0)
                        act(T2[q][:, 0:n], T1[q][:, 0:n], AF.Ln, [f'bT1{q}', 'LBt'], [f'bT2{q}'], bias=onec[:], scale=lb_ap)
                        act(T1[q][:, 0:n], T1[q][:, 0:n], AF.Ln, [f'bT1{q}'], [f'bT1{q}'], bias=onec[:], scale=1.0)
                        S.op('dve', lambda e, q=q, t0=t0, n=n: e.tensor_tensor(out=Gc[:, t0:t0 + n], in0=T2[q][:, 0:n], in1=T1[q][:, 0:n], op=ALU.subtract),
                             reads=[f'bT1{q}', f'bT2{q}'], writes=['bGc'])
                    if dr == 0:
                        S.op('dve', lambda e: e.tensor_tensor_scan(out=Gc[:], data0=M0[:], data1=Gc[:], initial=0.0, op0=ALU.mult, op1=ALU.add),
                             reads=['bGc', 'bM0'], writes=['bGc'])
                    else:
                        S.op('dve', lambda e: e.tensor_tensor_scan(out=Gc[:][:, ::-1], data0=M0[:][:, ::-1], data1=Gc[:][:, ::-1], initial=0.0, op0=ALU.mult, op1=ALU.add),
                             reads=['bGc', 'bM0'], writes=['bGc'])
                    _ck(2)
                    g3 = Gc[:].rearrange("p (c j) -> p c j", j=64)
                    mid = 31 if dr == 0 else 32
                    end = 63 if dr == 0 else 0
                    S.op('dve', lambda e: e.tensor_copy(out=sm[:, 0, :], in_=g3[:, :, mid]), reads=['bGc'], writes=['bsm'])
                    S.op('dve', lambda e: e.tensor_copy(out=sm[:, 1, :], in_=g3[:, :, end]), reads=['bGc'], writes=['bsm'])
                    S.op('dve', lambda e: e.tensor_tensor(out=sm[:, 5, :], in0=sm[:, 1, :], in1=sm[:, 0, :], op=ALU.subtract), reads=['bsm'], writes=['bsm'])
                    act(sm[:, 2, :], sm[:, 1, :], AF.Exp, ['bsm'], ['bsm'])
                    act(sm[:, 3, :], sm[:, 5, :], AF.Exp, ['bsm'], ['bsm'])
                    act(sm[:, 4, :], sm[:, 0, :], AF.Exp, ['bsm'], ['bsm'])
                    S.op('dve', lambda e: e.tensor_tensor(out=g3, in0=g3, in1=sm[:, 0, :].unsqueeze(2).broadcast_to([128, NCH, 64]), op=ALU.subtract),
                         reads=['bGc', 'bsm'], writes=['bGc'])
                    _ck(3)
                    for bi, (t0, n) in enumerate(BLK):
                        q = bi % 2
                        c0, nc_ = t0 // 64, n // 64
                        proj_fm(q, W, 'bW', 0, t0, n)
                        act(T1[q][:, 0:n], PS[q][:, 0:n], AF.Exp, [pk(q)], [f'bT1{q}'], scale=-1.0)
                        act(T1[q][:, 0:n], T1[q][:, 0:n], AF.Ln, [f'bT1{q}'], [f'bT1{q}'], bias=onec[:], scale=1.0)
                        S.op('dve', lambda e, q=q, t0=t0, n=n: e.tensor_tensor(out=T1[q][:, 0:n], in0=Gc[:, t0:t0 + n], in1=T1[q][:, 0:n], op=ALU.subtract),
                             reads=['bGc', f'bT1{q}'], writes=[f'bT1{q}'])
                        act(T1[q][:, 0:n], T1[q][:, 0:n], AF.Exp, [f'bT1{q}'], [f'bT1{q}'])
                        S.op('dve', lambda e, q=q, t0=t0, n=n: e.tensor_tensor(out=QTl[:, t0:t0 + n], in0=PS[q][:, 0:n], in1=T1[q][:, 0:n], op=ALU.mult),
                             reads=[pk(q), f'bT1{q}'], writes=['bQT'])
                        proj_fm(2 + q, W, 'bW', (2 + dr) * 128, t0, n)
                        act(T2[q][:, 0:n], PS[2 + q][:, 0:n], AF.Exp, [pk(2 + q)], [f'bT2{q}'])
                        act(T2[q][:, 0:n], T2[q][:, 0:n], AF.Ln, [f'bT2{q}'], [f'bT2{q}'], bias=onec[:], scale=1.0)
                        S.op('dve', lambda e, q=q, t0=t0, n=n: e.tensor_tensor(out=T2[q][:, 0:n], in0=Gc[:, t0:t0 + n], in1=T2[q][:, 0:n], op=ALU.add),
                             reads=['bGc', f'bT2{q}'], writes=[f'bT2{q}'])
                        act(T2[q][:, 0:n], T2[q][:, 0:n], AF.Exp, [f'bT2{q}'], [f'bT2{q}'], scale=-1.0)
                        S.op('dve', lambda e, q=q, t0=t0, n=n: e.tensor_scalar(out=KTl[:, t0:t0 + n], in0=T2[q][:, 0:n], scalar1=oml_ap, scalar2=None, op0=ALU.mult),
                             reads=[f'bT2{q}', 'LBt'], writes=['bKT'])
                        kv3 = KTl[:, t0:t0 + n].rearrange("p (c j) -> p c j", j=64)
                        kh3 = KH[:, t0:t0 + n].rearrange("p (c j) -> p c j", j=64)
                        S.op('dve', lambda e, kv3=kv3, kh3=kh3, c0=c0, nc_=nc_: e.tensor_tensor(out=kh3, in0=kv3, in1=sm[:, 3, c0:c0 + nc_].unsqueeze(2).broadcast_to([128, nc_, 64]), op=ALU.mult),
                             reads=['bKT', 'bsm'], writes=['bSb'])
                    _ck(4)
                    for i in range(NT):
                        b = 4 + i % 2
                        psb = PS[b][:].bitcast(BF16)
                        S.op('pe', lambda e, i=i, psb=psb: e.transpose(out=psb[:, 0:128], in_=KH[:, i * 128:(i + 1) * 128], identity=identb[:]),
                             reads=['bSb', 'identb'], writes=[pk(b)])
                        S.op('dve', lambda e, i=i, psb=psb: e.tensor_copy(out=Kt[:, i, :], in_=psb[:, 0:128]), reads=[pk(b)], writes=['bKt'])
                    _ck(5)
                    for g, tiles in enumerate(groups):
                        for ti, i in enumerate(tiles):
                            for cp in range(2):
                                bb = 2 * (g % 2) + cp
                                for h2 in range(2):
                                    S.op('pe', lambda e, ti=ti, i=i, cp=cp, h2=h2, bb=bb: e.matmul(PS[bb][64 * h2:64 * h2 + 64, ti * 64:(ti + 1) * 64],
                                                                                               lhsT=Kt[64 * cp:64 * cp + 64, i, 64 * h2:64 * h2 + 64],
                                                                                               rhs=Vt[64 * cp:64 * cp + 64, i, 64 * h2:64 * h2 + 64],
                                                                                               start=True, stop=True, tile_position=(64 * cp, 64 * h2)),
                                         reads=['bKt', 'bV'], writes=[pk(bb)], sig=(ti == len(tiles) - 1 and h2 == 1))
                        nt_ = len(tiles)
                        for cp in range(2):
                            bb = 2 * (g % 2) + cp
                            cfirst = 2 * tiles[0] + cp
                            if dr == 0:
                                dst = Bt[:, cfirst:cfirst + 2 * (nt_ - 1) + 1:2, :]
                            else:
                                pfirst = (3 - cfirst) if g == 0 else (71 - cfirst)
                                stop = pfirst - 2 * (nt_ - 1) - 1
                                dst = Bt[:, pfirst:(stop if stop >= 0 else None):-2, :]
                            S.op('dve', lambda e, bb=bb, dst=dst, nt_=nt_: e.tensor_copy(out=dst, in_=PS[bb][:, 0:nt_ * 64].rearrange("p (t e) -> p t e", e=64)),
                                 reads=[pk(bb)], writes=['bGc'])
                    _ck(6)
                    if dr == 0:
                        lam_ap = sm[:, 2, :]
                    else:
                        S.op('dve', lambda e: e.tensor_copy(out=sm[:, 5, 0:4], in_=sm[:, 2, 0:4][:, ::-1]), reads=['bsm'], writes=['bsm'])
                        S.op('dve', lambda e: e.tensor_copy(out=sm[:, 5, 4:NCH], in_=sm[:, 2, 4:NCH][:, ::-1]), reads=['bsm'], writes=['bsm'])
                        lam_ap = sm[:, 5, :]
                    for e_ in range(64):
                        S.op('dve', lambda e, e_=e_: e.tensor_tensor_scan(out=Bt[:, :, e_], data0=lam_ap, data1=Bt[:, :, e_], initial=0.0, op0=ALU.mult, op1=ALU.add),
                             reads=['bsm', 'bGc'], writes=['bGc'])
                    _ck(7)
                    rho = sm[:, 4, :]
                    if dr == 0:
                        S.op('dve', lambda e: e.memset(Sb[:, 0:1, :], 0.0), writes=['bSb'])
                        S.op('dve', lambda e: e.tensor_tensor(out=Sb[:, 1:NCH, :], in0=Bt[:, 0:NCH - 1, :], in1=rho[:, 1:NCH].unsqueeze(2).broadcast_to([128, NCH - 1, 64]), op=ALU.mult),
                             reads=['bGc', 'bsm'], writes=['bSb'])
                    else:
                        S.op('dve', lambda e: e.memset(Sb[:, 3:4, :], 0.0), writes=['bSb'])
                        S.op('dve', lambda e: e.tensor_tensor(out=Sb[:, 0:3, :], in0=Bt[:, 2::-1, :], in1=rho[:, 0:3].unsqueeze(2).broadcast_to([128, 3, 64]), op=ALU.mult),
                             reads=['bGc', 'bsm'], writes=['bSb'])
                        S.op('dve', lambda e: e.tensor_tensor(out=Sb[:, 4:NCH, :], in0=Bt[:, 66:2:-1, :], in1=rho[:, 4:NCH].unsqueeze(2).broadcast_to([128, NCH - 4, 64]), op=ALU.mult),
                             reads=['bGc', 'bsm'], writes=['bSb'])
                    _ck(8)
                    mk = maskt[:, 64 * dr:64 * dr + 64]
                    tgs = list(enumerate(range(0, NT, 4)))

                    def b_scores(tgi, tg):
                            tiles = list(range(tg, min(tg + 4, NT)))
                            nt_ = len(tiles)
                            q = tgi % 2
                            for ti, i in enumerate(tiles):
                                for cp in range(2):
                                    c = 2 * i + cp
                                    for h2 in range(2):
                                        bb = 2 * q + h2
                                        for jb in range(2):
                                            full = (jb == 0) if dr == 0 else (jb == 1)
                                            i0, ni = (0, 64) if full else ((32, 32) if dr == 0 else (0, 32))
                                            S.op('pe', lambda e, c=c, cp=cp, h2=h2, bb=bb, ti=ti, jb=jb, i0=i0, ni=ni: e.matmul(
                                                PS[bb][64 * cp + 32 * jb:64 * cp + 32 * jb + 32, ti * 64 + i0:ti * 64 + i0 + ni],
                                                lhsT=KTl[64 * h2:64 * h2 + 64, c * 64 + 32 * jb:c * 64 + 32 * jb + 32],
                                                rhs=QTl[64 * h2:64 * h2 + 64, c * 64 + i0:c * 64 + i0 + ni],
                                                start=True, stop=True, tile_position=(64 * h2, 64 * cp + 32 * jb)),
                                                 reads=['bKT', 'bQT'], writes=[pk(bb)], sig=(ti == nt_ - 1 and cp == 1 and jb == 1))
                            for h2 in range(2):
                                bb = 2 * q + h2
                                scv = SCm[2 * q + h2][:, 0:nt_ * 64].rearrange("p (t i) -> p t i", i=64)
                                S.op('dve', lambda e, bb=bb, scv=scv, nt_=nt_: e.tensor_tensor(out=scv, in0=PS[bb][:, 0:nt_ * 64].rearrange("p (t i) -> p t i", i=64),
                                                                                           in1=mk.unsqueeze(1).broadcast_to([128, nt_, 64]), op=ALU.mult),
                                     reads=[pk(bb), 'maskt'], writes=[f'bSC{2 * q + h2}'])

                    def b_rest(tgi, tg):
                            tiles = list(range(tg, min(tg + 4, NT)))
                            nt_ = len(tiles)
                            q = tgi % 2
                            for ti, i in enumerate(tiles):
                                for cp in range(2):
                                    for h2 in range(2):
                                        S.op('pe', lambda e, cp=cp, h2=h2, ti=ti, i=i, q=q: e.matmul(PS[4 + cp][64 * h2:64 * h2 + 64, ti * 64:(ti + 1) * 64],
                                                                                                 lhsT=Vt[64 * cp:64 * cp + 64, i, 64 * h2:64 * h2 + 64],
                                                                                                 rhs=SCm[2 * q + h2][64 * cp:64 * cp + 64, ti * 64:(ti + 1) * 64],
                                                                                                 start=True, stop=True, tile_position=(64 * cp, 64 * h2)),
                                             reads=['bV', f'bSC{2 * q + h2}'], writes=[pk(4 + cp)], sig=(ti == nt_ - 1 and h2 == 1))
                            for ti, i in enumerate(tiles):
                                for cp in range(2):
                                    c = 2 * i + cp
                                    for h2 in range(2):
                                        S.op('pe', lambda e, c=c, cp=cp, h2=h2, ti=ti: e.matmul(PS[6][64 * h2:64 * h2 + 64, (ti * 2 + cp) * 64:(ti * 2 + cp + 1) * 64],
                                                                                           lhsT=Sb[64 * h2:64 * h2 + 64, c, :],
                                                                                           rhs=QTl[64 * h2:64 * h2 + 64, c * 64:(c + 1) * 64],
                                                                                           start=True, stop=True, tile_position=(64 * h2, 64 * h2)),
                                             reads=['bSb', 'bQT'], writes=[pk(6)], sig=(ti == nt_ - 1 and cp == 1 and h2 == 1))
                            otv = OT[:, tg * 128:(tg + nt_) * 128].rearrange("p (t c i) -> p t c i", c=2, i=64)
                            for cp in range(2):
                                S.op('dve', lambda e, cp=cp, otv=otv, nt_=nt_: e.tensor_tensor(out=otv[:, :, cp, :], in0=PS[4 + cp][:, 0:nt_ * 64].rearrange("p (t i) -> p t i", i=64),
                                                                                           in1=otv[:, :, cp, :], op=ALU.add),
                                     reads=[pk(4 + cp), 'bOT'], writes=['bOT'])
                            S.op('dve', lambda e, tg=tg, nt_=nt_: e.tensor_tensor(out=OT[:, tg * 128:(tg + nt_) * 128], in0=PS[6][:, 0:nt_ * 128], in1=OT[:, tg * 128:(tg + nt_) * 128], op=ALU.add),
                                 reads=[pk(6), 'bOT'], writes=['bOT'])

                    b_scores(*tgs[0])
                    for gi_ in range(len(tgs)):
                        if gi_ + 1 < len(tgs):
                            b_scores(*tgs[gi_ + 1])
                        b_rest(*tgs[gi_])
                _ck(9)
                for bi, (t0, n) in enumerate(BLK):
                    q = bi % 2
                    S.op('pool', lambda e, t0=t0, n=n: e.tensor_tensor(out=osq[:, 0:n], in0=OT[:, t0:t0 + n], in1=OT[:, t0:t0 + n], op=ALU.mult), reads=['bOT'], writes=['bosq'])
                    S.op('pe', lambda e, n=n: e.matmul(PS[4][:, 0:n], lhsT=bd64[:], rhs=osq[:, 0:n], start=True, stop=True), reads=['bd64', 'bosq'], writes=[pk(4)])
                    rstd_from_ss(ors[:, 0:n], PS[4][:, 0:n], 1.0 / 64, epsr[:], 'bors', pk(4))
                    proj_fm(5, W, 'bW', 4 * 128, t0, n)
                    act(T1[q][:, 0:n], PS[5][:, 0:n], AF.Exp, [pk(5)], [f'bT1{q}'], scale=-1.0)
                    act(T1[q][:, 0:n], T1[q][:, 0:n], AF.Ln, [f'bT1{q}'], [f'bT1{q}'], bias=onec[:], scale=1.0)
                    act(T1[q][:, 0:n], T1[q][:, 0:n], AF.Exp, [f'bT1{q}'], [f'bT1{q}'], scale=-1.0)
                    S.op('dve', lambda e, q=q, n=n: e.tensor_tensor(out=T1[q][:, 0:n], in0=PS[5][:, 0:n], in1=T1[q][:, 0:n], op=ALU.mult), reads=[pk(5), f'bT1{q}'], writes=[f'bT1{q}'])
                    S.op('dve', lambda e, t0=t0, n=n: e.tensor_tensor(out=ors[:, 0:n], in0=ors[:, 0:n], in1=OT[:, t0:t0 + n], op=ALU.mult), reads=['bors', 'bOT'], writes=['bors'])
                    S.op('dve', lambda e, q=q, n=n: e.scalar_tensor_tensor(out=oy[q][:, 0:n], in0=ors[:, 0:n], scalar=hgn[:, 0:1], in1=T1[q][:, 0:n], op0=ALU.mult, op1=ALU.mult),
                         reads=['bors', 'bhgn', f'bT1{q}'], writes=[f'boy{q}'])
                    S.dma('sp', ymix[256 + pc * 128:256 + (pc + 1) * 128, t0:t0 + n], oy[q][:, 0:n], reads=[f'boy{q}'], writes=['ymix'])
        S.barrier()

    S.barrier()
    prologue()
    for l in range(nlayers):
        last = (l == DEPTH - 1)
        if 'H' in stages and (l == 0 or 'O' not in stages):
            stageH(l)
        if 'A' in stages:
            stageA(l)
        if 'B' in stages:
            try:
                stageB(l)
            except _Stop:
                S.barrier()
        if 'C' in stages:
            stageC(l, last)
        if 'D' in stages:
            stageD(l, last)
        if 'O' in stages:
            stageO(l, last, fuse_next=(l + 1 < nlayers and 'H' in stages))
    S.barrier()
    if 'BSTOP' not in _os.environ:
        es.close()
    return nc


def _consts():
    ident = np.eye(128, dtype=np.float32).astype(ml_dtypes.bfloat16)
    n_freq = 8
    inv_freq = (10000.0 ** (-np.arange(n_freq, dtype=np.float32) / n_freq)).astype(np.float32)
    row = np.repeat(np.arange(64, dtype=np.float32), 64)
    col = np.tile(np.arange(64, dtype=np.float32), 64)
    ang = np.concatenate([row[:, None] * inv_freq, col[:, None] * inv_freq], axis=-1).astype(np.float32)
    cos, sin = np.cos(ang).astype(np.float32), np.sin(ang).astype(np.float32)
    c32 = np.concatenate([cos, cos], axis=1).T
    s32 = np.concatenate([-sin, sin], axis=1).T
    cosT = np.ascontiguousarray(np.tile(c32, (4, 1)))
    sinT = np.ascontiguousarray(np.tile(s32, (4, 1)))
    j = np.arange(64)[:, None]
    i = np.arange(64)[None, :]
    fwd = (j <= i).astype(np.float32)
    bwd = (j >= i).astype(np.float32)
    mask = np.concatenate([np.tile(fwd, (2, 1)), np.tile(bwd, (2, 1))], axis=1)
    bd = np.zeros((128, 128), np.float32)
    bd[:64, :64] = 1
    bd[64:, 64:] = 1
    return dict(c_ident=ident, c_cos=cosT, c_sin=sinT, c_mask=np.ascontiguousarray(mask), c_bd64=bd.astype(ml_dtypes.bfloat16))


def _layout(inp, b):
    f = lambda a: np.ascontiguousarray(np.asarray(a, dtype=np.float32))
    m = {}
    m["x_b"] = f(inp["x"][b])
    m["ctx_b"] = f(inp["ctx"][b])
    cc = np.stack([np.asarray(inp["c"][b]), np.asarray(inp["c_ctx"])], -1)
    m["cfm"] = f(cc.reshape(8, 128, 2).transpose(1, 0, 2).reshape(128, 16))
    m["w_mod"] = f(inp["w_mod"])
    m["b_mod"] = f(inp["b_mod"])
    m["b_mod_fm"] = f(np.asarray(inp["b_mod"]).reshape(DEPTH, 24, 128).transpose(0, 2, 1))
    m["g_pre_fm"] = f(np.asarray(inp["g_pre"]).reshape(DEPTH, 8, 128).transpose(0, 2, 1))
    m["g_post"] = f(inp["g_post"])
    m["w_in"] = f(inp["w_in"])
    m["w_out"] = f(inp["w_out"])
    m["lru_cw"] = f(np.asarray(inp["lru_conv_w"]).reshape(DEPTH, 4, 2, 128).transpose(0, 3, 2, 1).reshape(DEPTH, 128, 8))
    m["lru_cb"] = f(np.asarray(inp["lru_conv_b"]).reshape(DEPTH, 2, 128).transpose(0, 2, 1))
    wr, wi = np.asarray(inp["lru_w_r"]), np.asarray(inp["lru_w_i"])
    bd = np.zeros((DEPTH, 2, 4, 128, 128), np.float32)
    for pc in range(2):
        for dr in range(2):
            for wh, w in enumerate((wr, wi)):
                for h2 in range(2):
                    bd[:, pc, 2 * dr + wh, 64 * h2:64 * h2 + 64, 64 * h2:64 * h2 + 64] = w[:, dr, 2 * pc + h2]
    m["lru_bd"] = bd
    br, bi_ = np.asarray(inp["lru_b_r"]), np.asarray(inp["lru_b_i"])
    lb = np.stack([br, bi_], 1).reshape(DEPTH, 2, 2, 2, 128)
    m["lru_b"] = f(lb.transpose(0, 4, 1, 2, 3).reshape(DEPTH, 128, 8))
    m["lru_lam"] = f(np.asarray(inp["lru_lambda"]).reshape(DEPTH, 2, 2, 128).transpose(0, 3, 1, 2).reshape(DEPTH, 128, 4))
    hl = np.asarray(inp["hgrn_lb"]).reshape(DEPTH, 2, 2, 128)
    m["hg_lb"] = f(hl.transpose(3, 2, 1, 0).reshape(128, 16))
    m["hg_g"] = f(np.tile(np.asarray(inp["hgrn_norm_g"]), (1, 2)).reshape(DEPTH, 128, 1))
    m["cf_w"] = f(np.asarray(inp["conf_conv_w"]).reshape(DEPTH, 31, 2, 128).transpose(0, 3, 2, 1).reshape(DEPTH, 128, 62))
    cb = np.stack([np.asarray(inp["conf_conv_b"]), np.asarray(inp["conf_ln_g"]), np.asarray(inp["conf_ln_b"])], 1).reshape(DEPTH, 3, 2, 128)
    m["cf_b"] = f(cb.transpose(0, 3, 1, 2).reshape(DEPTH, 128, 6))
    m["df_lam"] = f(np.concatenate([np.asarray(inp[k]) for k in ("diff_lam_q1", "diff_lam_k1", "diff_lam_q2", "diff_lam_k2")], axis=1))
    m["df_g"] = f(np.tile(np.asarray(inp["diff_norm_g"]), (1, 2)).reshape(DEPTH, 128, 1))
    return m


def kernel(**inputs):
    n = 8
    nc = bass.Bass("TRN2", target_bir_lowering=False)
    build(nc)
    consts = _consts()
    in_maps = []
    for b in range(n):
        m = _layout(inputs, b)
        m.update(consts)
        in_maps.append(m)
    res = run_bass_kernel_spmd(nc, in_maps, core_ids=list(range(n)))
    return np.stack([np.asarray(r["out"], dtype=np.float32) for r in res.results], axis=0)
```

```python
import math
import numpy as np
import ml_dtypes
from contextlib import ExitStack
import concourse.bass as bass
import concourse.mybir as mybir
from concourse.bass_utils import run_bass_kernel_spmd

F32 = mybir.dt.float32
BF16 = mybir.dt.bfloat16
ALU = mybir.AluOpType
AF = mybir.ActivationFunctionType
AX = mybir.AxisListType

D = 1024
LC = 256
LL = 4096
T = LC + LL
NT = T // 128
DEPTH = 4
RMS_EPS = 1e-6
LN_EPS = 1e-5
BLK = [(0, 256)] + [(256 + 512 * j, 512) for j in range(8)]
NCH = T // 64


import os as _os
class _Stop(Exception):
    pass


def _ck(n):
    if int(_os.environ.get('BSTOP', '99')) == n:
        raise _Stop()


class Sched:
    ENG = ('pe', 'act', 'dve', 'pool', 'sp')

    def __init__(s, nc, es, ndsem=24):
        s.nc = nc
        s.eng = dict(pe=nc.tensor, act=nc.scalar, dve=nc.vector, pool=nc.gpsimd, sp=nc.sync)
        s.sem = {e: es.enter_context(nc.semaphore("sem_" + e)) for e in s.ENG}
        s.cnt = {e: 0 for e in s.ENG}
        s.seen = {e: {} for e in s.ENG}
        s.dsem = [es.enter_context(nc.semaphore(f"dsem{i}")) for i in range(ndsem)]
        s.dcnt = [0] * ndsem
        s.dnext = 0
        s.lastw = {}
        s.readers = {}
        s.unsig = False

    def _need(s, e, tok):
        kind, who, val = tok
        if kind == 'e' and who == e and e == 'pe':
            return
        key = (kind, who)
        if s.seen[e].get(key, 0) >= val:
            return
        if kind == 'e':
            assert val <= s.cnt[who], f"wait on unsignaled op {tok} cnt={s.cnt[who]}"
            s.eng[e].wait_ge(s.sem[who], val)
        else:
            s.eng[e].wait_ge(s.dsem[who], val)
        s.seen[e][key] = val

    def _deps(s, e, reads, writes):
        for r in reads:
            t = s.lastw.get(r)
            if t is not None:
                s._need(e, t)
        for w in writes:
            t = s.lastw.get(w)
            if t is not None:
                s._need(e, t)
            for (k, who), val in s.readers.get(w, {}).items():
                s._need(e, (k, who, val))

    def _reg(s, tok, reads, writes):
        for r in reads:
            d = s.readers.setdefault(r, {})
            key = (tok[0], tok[1])
            if d.get(key, 0) < tok[2]:
                d[key] = tok[2]
        for w in writes:
            s.lastw[w] = tok
            s.readers[w] = {}

    def op(s, e, fn, reads=(), writes=(), sig=True):
        s._deps(e, reads, writes)
        ins = fn(s.eng[e])
        if sig:
            s.cnt[e] += 1
            ins.then_inc(s.sem[e], 1)
            tok = ('e', e, s.cnt[e])
            if e == 'pe':
                s.unsig = False
        else:
            assert e == 'pe'
            tok = ('e', e, s.cnt[e] + 1)
            s.unsig = True
        s._reg(tok, reads, writes)

    def dma(s, q, out, in_, reads=(), writes=()):
        s._deps(q, reads, writes)
        i = s.dnext
        s.dnext = (s.dnext + 1) % len(s.dsem)
        if s.dcnt[i] > 0:
            s._need(q, ('d', i, 16 * s.dcnt[i]))
        s.dcnt[i] += 1
        s.eng[q].dma_start(out=out, in_=in_).then_inc(s.dsem[i], 16)
        s._reg(('d', i, 16 * s.dcnt[i]), reads, writes)

    def barrier(s):
        assert not s.unsig
        toks = [('e', e, s.cnt[e]) for e in s.ENG if s.cnt[e] > 0]
        toks += [('d', i, 16 * s.dcnt[i]) for i in range(len(s.dsem)) if s.dcnt[i] > 0]
        for e in s.ENG:
            for t in toks:
                s._need(e, t)
        s.lastw.clear()
        s.readers.clear()


def build(nc, nlayers=DEPTH, dbg=False, stages="HABCDO"):
    es = ExitStack()
    S = Sched(nc, es)

    def din(name, shape, dt=F32):
        return nc.dram_tensor(name, list(shape), dt, kind="ExternalInput").ap()

    x_in = din("x_b", [LL, D])
    c_in = din("ctx_b", [LC, D])
    cfm = din("cfm", [128, 16])
    w_mod = din("w_mod", [DEPTH, D, 3 * D])
    b_mod = din("b_mod", [DEPTH, 3 * D])
    b_mod_fm = din("b_mod_fm", [DEPTH, 128, 24])
    g_pre_fm = din("g_pre_fm", [DEPTH, 128, 8])
    g_post = din("g_post", [DEPTH, D])
    w_in = din("w_in", [DEPTH, D, 3584])
    w_out = din("w_out", [DEPTH, D, D])
    lru_cw = din("lru_cw", [DEPTH, 128, 8])
    lru_cb = din("lru_cb", [DEPTH, 128, 2])
    lru_bd = din("lru_bd", [DEPTH, 2, 4, 128, 128])
    lru_b = din("lru_b", [DEPTH, 128, 8])
    lru_lam = din("lru_lam", [DEPTH, 128, 4])
    hg_lb = din("hg_lb", [128, 16])
    hg_g = din("hg_g", [DEPTH, 128, 1])
    cf_w = din("cf_w", [DEPTH, 128, 62])
    cf_b = din("cf_b", [DEPTH, 128, 6])
    df_lam = din("df_lam", [DEPTH, 128])
    df_g = din("df_g", [DEPTH, 128, 1])
    c_ident = din("c_ident", [128, 128], BF16)
    c_cos = din("c_cos", [128, LL])
    c_sin = din("c_sin", [128, LL])
    c_mask = din("c_mask", [128, 128])
    c_bd64 = din("c_bd64", [128, 128], BF16)

    out = nc.dram_tensor("out", [LL, D], F32, kind="ExternalOutput").ap()
    xcs = nc.dram_tensor("xcs", [LC, D], F32).ap()
    ymix = nc.dram_tensor("ymix", [D, T], BF16, kind="ExternalOutput" if dbg else "Internal").ap()
    ggd = nc.dram_tensor("ggd", [DEPTH, 2, D], F32, kind="ExternalOutput" if dbg else "Internal").ap()

    uid = [0]

    def sb(name, shape, dt=F32, ctx=None):
        uid[0] += 1
        return (ctx or es).enter_context(nc.sbuf_tensor(f"{name}_{uid[0]}", list(shape), dt))

    PS = [es.enter_context(nc.psum_tensor(f"ps{i}", [128, 512], F32)) for i in range(8)]

    def pk(i):
        return ('ps', i)

    hT = sb("hT", [128, 8, T], BF16)
    identb = sb("identb", [128, 128], BF16)
    bd64 = sb("bd64", [128, 128], BF16)
    ones64 = sb("ones64", [128, 64], BF16)
    ones256 = sb("ones256", [128, 128], BF16)
    maskt = sb("maskt", [128, 128], F32)
    GS = sb("GS", [128, DEPTH, 4, 8], F32)
    LBt = sb("LBt", [128, 3, 4, 4], F32)
    epsr = sb("epsr", [128, 1], F32)
    epsl = sb("epsl", [128, 1], F32)
    onec = sb("onec", [128, 1], F32)

    S.dma('sp', identb[:], c_ident[:, :], writes=['identb'])
    S.dma('sp', bd64[:], c_bd64[:, :], writes=['bd64'])
    S.dma('sp', maskt[:], c_mask[:, :], writes=['maskt'])
    S.op('dve', lambda e: e.memset(ones64[:], 1.0), writes=['ones64'])
    S.op('dve', lambda e: e.memset(ones256[:], 1.0 / 256.0), writes=['ones256'])
    S.op('dve', lambda e: e.memset(epsr[:], RMS_EPS), writes=['epsr'])
    S.op('dve', lambda e: e.memset(epsl[:], LN_EPS), writes=['epsl'])
    S.op('dve', lambda e: e.memset(onec[:], 1.0), writes=['onec'])
    for i_ in range(8):
        S.op('dve', lambda e, i_=i_: e.memset(PS[i_][:, :], 0.0), writes=[pk(i_)])

    def act(out_, in_, func, reads, writes, bias=None, scale=None):
        kw = {}
        if bias is not None:
            kw['bias'] = bias
        if scale is not None:
            kw['scale'] = scale
        S.op('act', lambda e: e.activation(out=out_, in_=in_, func=func, **kw), reads=reads, writes=writes)

    def rstd_from_ss(rs, ss, n_inv, eps_ap, key_rs, key_ss):
        act(rs, ss, AF.Ln, [key_ss], [key_rs], bias=eps_ap, scale=n_inv)
        act(rs, rs, AF.Exp, [key_rs], [key_rs], scale=-0.5)

    def prologue():
        with ExitStack() as cx:
            cf = sb("cf", [128, 16], F32, cx)
            sc = sb("sc", [128, 16], F32, cx)
            rep = sb("rep", [128, 2, 8, 128], F32, cx)
            wms = [sb(f"wm{i}", [128, 8, 512], F32, cx) for i in range(2)]
            bfm = sb("bfm", [128, 24], F32, cx)
            gpf = sb("gpf", [128, 8], F32, cx)
            bg = sb("bg", [128, 1024], F32, cx)
            gp = sb("gp", [128, 1024], F32, cx)
            tmpg = sb("tmpg", [128, 1024], F32, cx)
            tmp = sb("ptmp", [128, 16], F32, cx)
            hl = sb("hl", [128, 16], F32, cx)
            hs = sb("hs", [128, 4], F32, cx)
            S.dma('sp', cf[:], cfm[:, :], writes=['cf'])
            S.dma('sp', hl[:], hg_lb[:, :], writes=['hl'])
            act(sc[:], cf[:], AF.Silu, ['cf'], ['sc'])
            for j in range(2):
                src = sc[:].rearrange("p (k j) -> p k j", j=2)[:, :, j:j + 1].broadcast_to([128, 8, 128])
                S.op('dve', lambda e, j=j, src=src: e.tensor_copy(out=rep[:, j, :, :], in_=src), reads=['sc'], writes=['rep'])
            act(hl[:], hl[:], AF.Exp, ['hl'], ['hl'])
            hl3 = hl[:].rearrange("p (a l) -> p a l", l=4)
            S.op('dve', lambda e: e.reduce_sum(out=hs[:], in_=hl3, axis=AX.X), reads=['hl'], writes=['hs'])
            S.op('dve', lambda e: e.reciprocal(out=hs[:], in_=hs[:]), reads=['hs'], writes=['hs'])
            S.op('dve', lambda e: e.tensor_tensor(out=hl3, in0=hl3, in1=hs[:].unsqueeze(2).broadcast_to([128, 4, 4]), op=ALU.mult),
                 reads=['hl', 'hs'], writes=['hl'])
            S.op('dve', lambda e: e.memset(LBt[:, 0, :, 0:1], 0.0), writes=['LBt'])
            S.op('dve', lambda e: e.tensor_copy(out=LBt[:, 0, :, 1:2], in_=hl3[:, :, 1:2]), reads=['hl', 'LBt'], writes=['LBt'])
            for l in (2, 3):
                S.op('dve', lambda e, l=l: e.tensor_tensor(out=LBt[:, 0, :, l:l + 1], in0=LBt[:, 0, :, l - 1:l], in1=hl3[:, :, l:l + 1], op=ALU.add),
                     reads=['hl', 'LBt'], writes=['LBt'])
            S.op('dve', lambda e: e.tensor_scalar(out=LBt[:, 1, :, :], in0=LBt[:, 0, :, :], scalar1=-1.0, scalar2=1.0, op0=ALU.mult, op1=ALU.add),
                 reads=['LBt'], writes=['LBt'])
            S.op('dve', lambda e: e.tensor_scalar(out=LBt[:, 2, :, :], in0=LBt[:, 1, :, :], scalar1=-1.0, scalar2=None, op0=ALU.mult),
                 reads=['LBt'], writes=['LBt'])
            for l in range(nlayers):
                S.dma('sp', bfm[:], b_mod_fm[l], writes=['bfm'])
                S.dma('sp', gpf[:], g_pre_fm[l], writes=['gpf'])
                S.dma('sp', bg[:], b_mod[l:l + 1, 2048:3072].partition_broadcast(128), writes=['bg'])
                S.dma('sp', gp[:], g_post[l:l + 1, :].partition_broadcast(128), writes=['gp'])
                wv = w_mod[l].rearrange("(k p) n -> p k n", p=128)
                for sl in range(6):
                    wm = wms[sl % 2]
                    wk = f'wm{sl % 2}'
                    S.dma('sp', wm[:], wv[:, :, sl * 512:(sl + 1) * 512], writes=[wk])
                    if sl < 4:
                        for c in range(4):
                            cc = sl * 4 + c
                            for k in range(8):
                                S.op('pe', lambda e, k=k, c=c, cc=cc, wm=wm: e.matmul(PS[0][:, 2 * cc:2 * cc + 2], lhsT=wm[:, k, c * 128:(c + 1) * 128],
                                                                                   rhs=sc[:, 2 * k:2 * k + 2], start=(k == 0), stop=(k == 7)),
                                     reads=[wk, 'sc'], writes=[pk(0)], sig=(k == 7))
                    else:
                        half = sl - 4
                        for j in range(2):
                            for k in range(8):
                                S.op('pe', lambda e, k=k, j=j, half=half, wm=wm: e.matmul(PS[1 + 2 * j + half][:, :], lhsT=rep[:, j, k, :], rhs=wm[:, k, :],
                                                                                        start=(k == 0), stop=(k == 7)),
                                     reads=[wk, 'rep'], writes=[pk(1 + 2 * j + half)], sig=(k == 7))
                psv = PS[0][:, 0:32].rearrange("p (w k j) -> p w k j", w=2, k=8, j=2)
                for j in range(2):
                    S.op('dve', lambda e, j=j: e.tensor_tensor(out=GS[:, l, 1 + 2 * j, :], in0=psv[:, 0, :, j], in1=bfm[:, 0:8], op=ALU.add),
                         reads=[pk(0), 'bfm'], writes=['GS'])
                    S.op('dve', lambda e, j=j: e.tensor_tensor(out=tmp[:, 0:8], in0=psv[:, 1, :, j], in1=bfm[:, 8:16], op=ALU.add),
                         reads=[pk(0), 'bfm'], writes=['ptmp'])
                    S.op('dve', lambda e, j=j: e.scalar_tensor_tensor(out=GS[:, l, 2 * j, :], in0=tmp[:, 0:8], scalar=1.0, in1=gpf[:], op0=ALU.add, op1=ALU.mult),
                         reads=['ptmp', 'gpf'], writes=['GS'])
                for j in range(2):
                    for half in range(2):
                        S.op('dve', lambda e, j=j, half=half: e.tensor_tensor(out=tmpg[:, half * 512:(half + 1) * 512], in0=PS[1 + 2 * j + half][:, :],
                                                                            in1=bg[:, half * 512:(half + 1) * 512], op=ALU.add),
                             reads=[pk(1 + 2 * j + half), 'bg'], writes=['tmpg'])
                    S.op('dve', lambda e: e.tensor_tensor(out=tmpg[:], in0=tmpg[:], in1=gp[:], op=ALU.mult), reads=['tmpg', 'gp'], writes=['tmpg'])
                    S.dma('sp', ggd[l, j:j + 1, :], tmpg[0:1, :], reads=['tmpg'], writes=['ggd'])
        S.barrier()

    wslab = {}

    def wkeys(wkey, c0, n):
        if wkey not in wslab:
            return [wkey]
        sl = wslab[wkey]
        return [(wkey, j) for j in range(c0 // sl, (c0 + n - 1) // sl + 1)]

    def load_w(dst, l, src, col0, ncols, stgs, dkey, off=0, slab=256):
        wv = src[l].rearrange("(k p) n -> p k n", p=128)
        i = 0
        for c in range(0, ncols, slab):
            n = min(slab, ncols - c)
            st, sk = stgs[i % len(stgs)]
            i += 1
            S.dma('sp', st[:, :, 0:n], wv[:, :, col0 + c:col0 + c + n], writes=[sk])
            wslab[dkey] = slab
            dk_ = (dkey, (off + c) // slab)
            if i % 2 == 0:
                S.op('dve', lambda e, st=st, c=c, n=n: e.tensor_copy(out=dst[:, :, off + c:off + c + n], in_=st[:, :, 0:n]), reads=[sk], writes=[dk_])
            else:
                act(dst[:, :, off + c:off + c + n], st[:, :, 0:n], AF.Identity, [sk], [dk_])

    def proj_fm(b, W, wkey, wc0, t0, n, wn=128):
        for k in range(8):
            S.op('pe', lambda e, k=k: e.matmul(PS[b][0:wn, 0:n], lhsT=W[:, k, wc0:wc0 + wn], rhs=hT[:, k, t0:t0 + n], start=(k == 0), stop=(k == 7)),
                 reads=wkeys(wkey, wc0, wn) + ['hT'], writes=[pk(b)], sig=(k == 7))

    def proj_tm(ps_ap, b, W, wkey, wc0, ncols, tile):
        for k in range(8):
            S.op('pe', lambda e, k=k: e.matmul(ps_ap, lhsT=hT[:, k, tile * 128:(tile + 1) * 128], rhs=W[:, k, wc0:wc0 + ncols], start=(k == 0), stop=(k == 7)),
                 reads=wkeys(wkey, wc0, ncols) + ['hT'], writes=[pk(b)], sig=(k == 7))

    def res_src(l, i):
        if i < 2:
            base = c_in if l == 0 else xcs
            return base[i * 128:(i + 1) * 128, :]
        base = x_in if l == 0 else out
        return base[(i - 2) * 128:(i - 1) * 128, :]

    def res_dst(i):
        if i < 2:
            return xcs[i * 128:(i + 1) * 128, :]
        return out[(i - 2) * 128:(i - 1) * 128, :]

    def stageH(l):
        with ExitStack() as cx:
            xts = [sb(f"hx{i}", [128, 1024], F32, cx) for i in range(2)]
            sq = sb("hsq", [128, 1024], F32, cx)
            xss = [sb(f"hxs{i}", [128, 1024], BF16, cx) for i in range(2)]
            st = sb("hst", [128, 4], F32, cx)
            for i in range(NT):
                p = i % 2
                xt, xs = xts[p], xss[p]
                jj = 0 if i >= 2 else 2
                S.dma('sp', xt[:], res_src(l, i), reads=[('res', i)], writes=[f'hx{p}'])
                act(sq[:], xt[:], AF.Square, [f'hx{p}'], ['hsq'])
                S.op('dve', lambda e, p=p: e.reduce_sum(out=st[:, p:p + 1], in_=sq[:], axis=AX.X), reads=['hsq'], writes=[f'hss{p}'])
                rstd_from_ss(st[:, 2 + p:3 + p], st[:, p:p + 1], 1.0 / D, epsr[:], f'hrs{p}', f'hss{p}')
                S.op('dve', lambda e, p=p, xt=xt, xs=xs: e.tensor_scalar(out=xs[:], in0=xt[:], scalar1=st[:, 2 + p:3 + p], scalar2=None, op0=ALU.mult),
                     reads=[f'hx{p}', f'hrs{p}'], writes=[f'hxs{p}'])
                b = 6 + p
                psb = PS[b][:].bitcast(BF16)
                for k in range(8):
                    S.op('pe', lambda e, k=k, xs=xs, psb=psb: e.transpose(out=psb[:, k * 128:(k + 1) * 128], in_=xs[:, k * 128:(k + 1) * 128], identity=identb[:]),
                         reads=[f'hxs{p}', 'identb'], writes=[pk(b)], sig=(k == 7))
                for k in range(8):
                    S.op('dve', lambda e, k=k, psb=psb, i=i, jj=jj: e.tensor_scalar(out=hT[:, k, i * 128:(i + 1) * 128], in0=psb[:, k * 128:(k + 1) * 128],
                                                                             scalar1=GS[:, l, jj, k:k + 1], scalar2=GS[:, l, jj + 1, k:k + 1],
                                                                             op0=ALU.mult, op1=ALU.add),
                         reads=[pk(b), 'GS'], writes=['hT'])
        S.barrier()

    def stageO(l, last, fuse_next=False):
        with ExitStack() as cx:
            wo = sb("wo", [128, 8, 1024], BF16, cx)
            stgs = [(sb(f"ostg{i}", [128, 8, 256], F32, cx), f'ostg{i}') for i in range(2)]
            GG = [sb(f"GG{j}", [128, 1024], F32, cx) for j in range(2)]
            yms = [sb(f"oym{i}", [128, 8, 512], BF16, cx) for i in range(2)]
            xts = [sb(f"ox{i}", [128, 1024], F32, cx) for i in range(2)]
            sq = sb("osq", [128, 1024], F32, cx)
            tts = [sb(f"ot{i}", [128, 1024], F32, cx) for i in range(2)]
            st = sb("ost", [128, 8], F32, cx)
            if fuse_next:
                sq2 = sb("osq2", [128, 1024], F32, cx)
                xss = [sb(f"oxs{i}", [128, 1024], BF16, cx) for i in range(2)]
            load_w(wo, l, w_out, 0, 1024, stgs, 'wo')
            for j in range(2):
                S.dma('sp', GG[j][:], ggd[l, j:j + 1, :].partition_broadcast(128), reads=['ggd'], writes=[f'GG{j}'])
            ymv = ymix.rearrange("(k p) t -> p k t", p=128)
            hq = [None, None]
            blks = [(t0, n) for bi, (t0, n) in enumerate(BLK) if not (last and bi == 0)]
            otiles = [(bx, tt, (t0 // 128) + tt) for bx, (t0, n) in enumerate(blks) for tt in range(n // 128)]

            def load_ym(bx):
                t0, n = blks[bx]
                S.dma('sp', yms[bx % 2][:, :, 0:n], ymv[:, :, t0:t0 + n], reads=['ymix'], writes=[f'oym{bx % 2}'])

            def load_xt(ti):
                S.dma('sp', xts[ti % 2][:], res_src(l, otiles[ti][2]), reads=[('res', otiles[ti][2])], writes=[f'ox{ti % 2}'])

            load_ym(0)
            load_xt(0)
            for ti, (bx, tt, i) in enumerate(otiles):
                if True:
                    if tt == 0 and bx + 1 < len(blks):
                        load_ym(bx + 1)
                    if ti + 1 < len(otiles):
                        load_xt(ti + 1)
                    ym = yms[bx % 2]
                    yk = f'oym{bx % 2}'
                    j = 0 if i >= 2 else 1
                    p = ti % 2
                    xt, tq = xts[p], tts[p]
                    for half in range(2):
                        b = 2 * p + half
                        for k in range(8):
                            S.op('pe', lambda e, k=k, b=b, half=half, ym=ym, tt=tt: e.matmul(PS[b][:, :], lhsT=ym[:, k, tt * 128:(tt + 1) * 128],
                                                                                       rhs=wo[:, k, half * 512:(half + 1) * 512], start=(k == 0), stop=(k == 7)),
                                 reads=[yk] + wkeys('wo', half * 512, 512), writes=[pk(b)], sig=(k == 7))
                        act(sq[:, half * 512:(half + 1) * 512], PS[b][:, :], AF.Square, [pk(b)], ['osq'])
                    S.op('dve', lambda e, p=p: e.reduce_sum(out=st[:, p:p + 1], in_=sq[:], axis=AX.X), reads=['osq'], writes=[f'oss{p}'])
                    rstd_from_ss(st[:, 2 + p:3 + p], st[:, p:p + 1], 1.0 / D, epsr[:], f'ors{p}', f'oss{p}')
                    for half in range(2):
                        b = 2 * p + half
                        S.op('dve', lambda e, b=b, half=half, tq=tq, p=p, j=j: e.scalar_tensor_tensor(out=tq[:, half * 512:(half + 1) * 512], in0=PS[b][:, :],
                                                                                                scalar=st[:, 2 + p:3 + p], in1=GG[j][:, half * 512:(half + 1) * 512],
                                                                                                op0=ALU.mult, op1=ALU.mult),
                             reads=[pk(b), f'ors{p}', f'GG{j}'], writes=[f'ot{p}'])
                    S.op('pool', lambda e, tq=tq, xt=xt: e.tensor_tensor(out=tq[:], in0=tq[:], in1=xt[:], op=ALU.add), reads=[f'ot{p}', f'ox{p}'], writes=[f'ot{p}'])
                    S.dma('sp', res_dst(i), tq[:], reads=[f'ot{p}'], writes=[('res', i)])
                    if fuse_next:
                        def hpart(i=i, p=p, tq=tq):
                            ln = l + 1
                            jj = 0 if i >= 2 else 2
                            xs = xss[p]
                            act(sq2[:], tq[:], AF.Square, [f'ot{p}'], ['osq2'])
                            S.op('dve', lambda e, p=p: e.reduce_sum(out=st[:, 4 + p:5 + p], in_=sq2[:], axis=AX.X), reads=['osq2'], writes=[f'oss2{p}'])
                            rstd_from_ss(st[:, 6 + p:7 + p], st[:, 4 + p:5 + p], 1.0 / D, epsr[:], f'ors2{p}', f'oss2{p}')
                            act(xs[:], tq[:], AF.Identity, [f'ot{p}', f'ors2{p}'], [f'oxs{p}'], scale=st[:, 6 + p:7 + p])
                        def hpartB(i=i, p=p):
                            ln = l + 1
                            jj = 0 if i >= 2 else 2
                            xs = xss[p]
                            for k in range(8):
                                b = (4 + p) if k < 4 else (6 + p)
                                psb = PS[b][:].bitcast(BF16)
                                S.op('pe', lambda e, k=k, xs=xs, psb=psb: e.transpose(out=psb[:, (k % 4) * 128:(k % 4 + 1) * 128], in_=xs[:, k * 128:(k + 1) * 128], identity=identb[:]),
                                     reads=[f'oxs{p}', 'identb'], writes=[pk(b)], sig=(k % 4 == 3))
                            for k in range(8):
                                b = (4 + p) if k < 4 else (6 + p)
                                psb = PS[b][:].bitcast(BF16)
                                if k < 4:
                                    act(hT[:, k, i * 128:(i + 1) * 128], psb[:, (k % 4) * 128:(k % 4 + 1) * 128], AF.Identity, [pk(b), 'GS'], [('hTa', k)],
                                        bias=GS[:, ln, jj + 1, k:k + 1], scale=GS[:, ln, jj, k:k + 1])
                                else:
                                    S.op('dve', lambda e, k=k, psb=psb, i=i, jj=jj, ln=ln: e.tensor_scalar(out=hT[:, k, i * 128:(i + 1) * 128], in0=psb[:, (k % 4) * 128:(k % 4 + 1) * 128],
                                                                                                scalar1=GS[:, ln, jj, k:k + 1], scalar2=GS[:, ln, jj + 1, k:k + 1],
                                                                                                op0=ALU.mult, op1=ALU.add),
                                         reads=[pk(b), 'GS'], writes=[('hTd', k)])
                        if hq[1] is not None:
                            hq[1]()
                            hq[1] = None
                        if hq[0] is not None:
                            hq[0][0]()
                            hq[1] = hq[0][1]
                        hq[0] = (hpart, hpartB)
            if fuse_next:
                if hq[1] is not None:
                    hq[1]()
                if hq[0] is not None:
                    hq[0][0]()
                    hq[0][1]()
        S.barrier()

    NG = T + 3

    def stageA(l):
        with ExitStack() as cx:
            W = sb("aW", [128, 8, 512], BF16, cx)
            stgs = [(sb(f"astg{i}", [128, 8, 256], F32, cx), f'astg{i}') for i in range(2)]
            bdst = sb("abdst", [128, 4, 128], F32, cx)
            bd = sb("abd", [128, 4, 128], BF16, cx)
            cw = sb("acw", [128, 8], F32, cx)
            cb = sb("acb", [128, 2], F32, cx)
            lbias = sb("alb", [128, 8], F32, cx)
            lam = sb("alam", [128, 4], F32, cx)
            B = [sb(f"aB{i}", [128, NG + 3], F32, cx) for i in range(5)]
            XCb = sb("aXCb", [128, NG], BF16, cx)
            SGt = sb("aSG", [128, T], BF16, cx)
            Yb = XCb
            load_w(W, l, w_in, 0, 512, stgs, 'aW')
            S.dma('sp', cw[:], lru_cw[l], writes=['acw'])
            S.dma('sp', cb[:], lru_cb[l], writes=['acb'])
            S.dma('sp', lbias[:], lru_b[l], writes=['alb'])
            S.dma('sp', lam[:], lru_lam[l], writes=['alam'])
            act(lam[:], lam[:], AF.Exp, ['alam'], ['alam'], scale=-1.0)
            act(lam[:], lam[:], AF.Ln, ['alam'], ['alam'], bias=onec[:], scale=1.0)
            S.op('dve', lambda e: e.tensor_scalar(out=lam[:], in0=lam[:], scalar1=-8.0, scalar2=None, op0=ALU.mult), reads=['alam'], writes=['alam'])
            for pc in range(2):
                UX, XC = B[0], B[4]
                for (a, b_) in ((0, 2), (258, 261), (NG + 2, NG + 3)):
                    S.op('dve', lambda e, a=a, b_=b_: e.memset(UX[:, a:b_], 0.0), writes=['aB0'])
                for bi, (t0, n) in enumerate(BLK):
                    ux0 = 2 + t0 if t0 < 256 else 261 + (t0 - 256)
                    b = bi % 2
                    proj_fm(b, W, 'aW', pc * 128, t0, n)
                    S.op('dve', lambda e, b=b, ux0=ux0, n=n: e.tensor_copy(out=UX[:, ux0:ux0 + n], in_=PS[b][:, 0:n]), reads=[pk(b)], writes=['aB0'])
                    b2 = 2 + bi % 2
                    proj_fm(b2, W, 'aW', 256 + pc * 128, t0, n)
                    act(SGt[:, t0:t0 + n], PS[b2][:, 0:n], AF.Silu, [pk(b2)], ['aSG'])
                S.op('dve', lambda e: e.tensor_scalar(out=XC[:, 0:NG], in0=UX[:, 0:NG], scalar1=cw[:, pc * 4:pc * 4 + 1], scalar2=cb[:, pc:pc + 1],
                                                      op0=ALU.mult, op1=ALU.add), reads=['aB0', 'acw', 'acb'], writes=['aB4'])
                for k in range(1, 4):
                    S.op('dve', lambda e, k=k: e.scalar_tensor_tensor(out=XC[:, 0:NG], in0=UX[:, k:k + NG], scalar=cw[:, pc * 4 + k:pc * 4 + k + 1], in1=XC[:, 0:NG],
                                                                      op0=ALU.mult, op1=ALU.add), reads=['aB0', 'aB4', 'acw'], writes=['aB4'])
                S.op('pool', lambda e: e.tensor_copy(out=XCb[:], in_=XC[:, 0:NG]), reads=['aB4'], writes=['aXCb'])
                S.dma('sp', bdst[:], lru_bd[l, pc].rearrange("w a b -> a w b"), writes=['abdst'])
                S.op('pool', lambda e: e.tensor_copy(out=bd[:], in_=bdst[:]), reads=['abdst'], writes=['abd'])
                S.op('dve', lambda e: e.memset(B[3][:, 256:259], 0.0), writes=['aB3'])
                for dr in range(2):
                    R, I, A = B[0], B[1], B[2]
                    for gi, g0 in enumerate(range(0, NG, 512)):
                        n = min(512, NG - g0)
                        for wh, (dst, dk) in enumerate(((R, 'aB0'), (I, 'aB1'))):
                            b = 2 * (gi % 2) + wh
                            S.op('pe', lambda e, b=b, wh=wh, g0=g0, n=n: e.matmul(PS[b][:, 0:n], lhsT=bd[:, 2 * dr + wh, :], rhs=XCb[:, g0:g0 + n], start=True, stop=True),
                                 reads=['abd', 'aXCb'], writes=[pk(b)])
                            bi_ = wh * 4 + dr * 2 + pc
                            act(dst[:, g0:g0 + n], PS[b][:, 0:n], AF.Sigmoid, [pk(b), 'alb'], [dk], bias=lbias[:, bi_:bi_ + 1], scale=1.0)
                    ci = dr * 2 + pc
                    act(A[:, 0:NG], R[:, 0:NG], AF.Exp, ['aB0', 'alam'], ['aB2'], scale=lam[:, ci:ci + 1])
                    act(R[:, 0:NG], A[:, 0:NG], AF.Square, ['aB2'], ['aB0'])
                    act(R[:, 0:NG], R[:, 0:NG], AF.Ln, ['aB0'], ['aB0'], bias=onec[:], scale=-1.0)
                    act(R[:, 0:NG], R[:, 0:NG], AF.Exp, ['aB0'], ['aB0'], scale=0.5)
                    S.op('dve', lambda e: e.tensor_tensor(out=I[:, 0:NG], in0=I[:, 0:NG], in1=XC[:, 0:NG], op=ALU.mult), reads=['aB1', 'aB4'], writes=['aB1'])
                    S.op('dve', lambda e: e.tensor_tensor(out=I[:, 0:NG], in0=I[:, 0:NG], in1=R[:, 0:NG], op=ALU.mult), reads=['aB1', 'aB0'], writes=['aB1'])
                    H, hk = (B[3], 'aB3') if dr == 0 else (B[0], 'aB0')
                    if dr == 0:
                        S.op('dve', lambda e, H=H: e.tensor_tensor_scan(out=H[:, 0:256], data0=A[:, 0:256], data1=I[:, 0:256], initial=0.0, op0=ALU.mult, op1=ALU.add),
                             reads=['aB2', 'aB1'], writes=[hk])
                        S.op('dve', lambda e, H=H: e.tensor_tensor_scan(out=H[:, 259:NG], data0=A[:, 259:NG], data1=I[:, 259:NG], initial=H[:, 255:256],
                                                                         op0=ALU.mult, op1=ALU.add), reads=['aB2', 'aB1', hk], writes=[hk])
                    else:
                        rv = lambda X, a, b_: X[:, a:b_][:, ::-1]
                        S.op('dve', lambda e, H=H: e.tensor_tensor_scan(out=rv(H, 0, 256), data0=rv(A, 0, 256), data1=rv(I, 0, 256), initial=0.0, op0=ALU.mult, op1=ALU.add),
                             reads=['aB2', 'aB1'], writes=[hk])
                        S.op('dve', lambda e, H=H: e.tensor_tensor_scan(out=rv(H, 259, NG), data0=rv(A, 259, NG), data1=rv(I, 259, NG), initial=H[:, 0:1],
                                                                         op0=ALU.mult, op1=ALU.add), reads=['aB2', 'aB1', hk], writes=[hk])
                        S.op('dve', lambda e: e.tensor_tensor(out=B[3][:, 0:NG], in0=B[3][:, 0:NG], in1=B[0][:, 0:NG], op=ALU.add), reads=['aB3', 'aB0'], writes=['aB3'])
                S.op('dve', lambda e: e.tensor_tensor(out=Yb[:, 0:256], in0=B[3][:, 0:256], in1=SGt[:, 0:256], op=ALU.mult), reads=['aB3', 'aSG'], writes=['aXCb'])
                S.op('dve', lambda e: e.tensor_tensor(out=Yb[:, 256:T], in0=B[3][:, 259:NG], in1=SGt[:, 256:T], op=ALU.mult), reads=['aB3', 'aSG'], writes=['aXCb'])
                S.dma('sp', ymix[pc * 128:(pc + 1) * 128, :], Yb[:, 0:T], reads=['aXCb'], writes=['ymix'])
        S.barrier()

    NP = T + 60

    def stageC(l, last):
        with ExitStack() as cx:
            W = sb("cW", [128, 8, 768], BF16, cx)
            stgs = [(sb(f"cstg{i}", [128, 8, 256], F32, cx), f'cstg{i}') for i in range(2)]
            Yp = [sb(f"cYp{i}", [128, NP], BF16, cx) for i in range(2)]
            SG = sb("cSG", [128, 2, T], BF16, cx)
            Dg = sb("cDg", [128, 2, 31, 128], BF16, cx)
            cwt = sb("ccw", [128, 62], F32, cx)
            cbt = sb("ccb", [128, 6], F32, cx)
            sgm = [sb(f"csgm{i}", [128, 512], F32, cx) for i in range(2)]
            Cf = sb("cCf", [128, 2, 512], F32, cx)
            Cb = sb("cCb", [128, 2, 512], BF16, cx)
            Cq = sb("cCq", [128, 2, 512], BF16, cx)
            mean = sb("cmean", [128, 512], F32, cx)
            var = sb("cvar", [128, 512], F32, cx)
            dd = [sb(f"cdd{i}", [128, 512], F32, cx) for i in range(2)]
            yo = [sb(f"cyo{i}", [128, 512], BF16, cx) for i in range(2)]
            load_w(W, l, w_in, 1792, 768, stgs, 'cW')
            S.dma('sp', cwt[:], cf_w[l], writes=['ccw'])
            S.dma('sp', cbt[:], cf_b[l], writes=['ccb'])
            for pc in range(2):
                i0 = identb[:].unsqueeze(1).broadcast_to([128, 31, 128])
                i1 = cwt[:, pc * 31:(pc + 1) * 31].unsqueeze(2).broadcast_to([128, 31, 128])
                S.op('dve', lambda e, pc=pc, i0=i0, i1=i1: e.tensor_tensor(out=Dg[:, pc, :, :], in0=i0, in1=i1, op=ALU.mult), reads=['identb', 'ccw'], writes=['cDg'])
                S.op('pool', lambda e, pc=pc: e.memset(Yp[pc][:], 0.0), writes=[f'cYp{pc}'])
            for bi, (t0, n) in enumerate(BLK):
                p0 = 15 + t0 if t0 < 256 else 301 + (t0 - 256)
                for pc in range(2):
                    q = (bi * 2 + pc) % 2
                    proj_fm(0 + q, W, 'cW', pc * 128, t0, n)
                    proj_fm(2 + q, W, 'cW', 256 + pc * 128, t0, n)
                    act(sgm[q][:, 0:n], PS[2 + q][:, 0:n], AF.Sigmoid, [pk(2 + q)], [f'csgm{q}'])
                    S.op('dve', lambda e, pc=pc, q=q, p0=p0, n=n: e.tensor_tensor(out=Yp[pc][:, p0:p0 + n], in0=PS[q][:, 0:n], in1=sgm[q][:, 0:n], op=ALU.mult),
                         reads=[pk(q), f'csgm{q}'], writes=[f'cYp{pc}'])
                    proj_fm(4 + q, W, 'cW', 512 + pc * 128, t0, n)
                    act(SG[:, pc, t0:t0 + n], PS[4 + q][:, 0:n], AF.Silu, [pk(4 + q)], ['cSG'])
            for bi, (t0, n) in enumerate(BLK):
                if last and bi == 0:
                    continue
                p0 = 15 + t0 if t0 < 256 else 301 + (t0 - 256)
                for pc in range(2):
                    b = pc
                    for k in range(31):
                        S.op('pe', lambda e, k=k, pc=pc, b=b: e.matmul(PS[b][:, 0:n], lhsT=Dg[:, pc, k, :], rhs=Yp[pc][:, p0 + k - 15:p0 + k - 15 + n], start=(k == 0), stop=(k == 30)),
                             reads=['cDg', f'cYp{pc}'], writes=[pk(b)], sig=(k == 30))
                    S.op('dve', lambda e, pc=pc, b=b: e.tensor_scalar(out=Cf[:, pc, 0:n], in0=PS[b][:, 0:n], scalar1=cbt[:, pc:pc + 1], scalar2=None, op0=ALU.add),
                         reads=[pk(b), 'ccb'], writes=['cCf'])
                    S.op('pool', lambda e, pc=pc: e.tensor_copy(out=Cb[:, pc, 0:n], in_=Cf[:, pc, 0:n]), reads=['cCf'], writes=['cCb'])
                    S.op('pool', lambda e, pc=pc: e.tensor_tensor(out=Cq[:, pc, 0:n], in0=Cf[:, pc, 0:n], in1=Cf[:, pc, 0:n], op=ALU.mult), reads=['cCf'], writes=['cCq'])
                for pc in range(2):
                    S.op('pe', lambda e, pc=pc: e.matmul(PS[2][:, 0:n], lhsT=ones256[:], rhs=Cb[:, pc, 0:n], start=(pc == 0), stop=(pc == 1)),
                         reads=['ones256', 'cCb'], writes=[pk(2)], sig=(pc == 1))
                for pc in range(2):
                    S.op('pe', lambda e, pc=pc: e.matmul(PS[3][:, 0:n], lhsT=ones256[:], rhs=Cq[:, pc, 0:n], start=(pc == 0), stop=(pc == 1)),
                         reads=['ones256', 'cCq'], writes=[pk(3)], sig=(pc == 1))
                S.op('dve', lambda e: e.tensor_copy(out=mean[:, 0:n], in_=PS[2][:, 0:n]), reads=[pk(2)], writes=['cmean'])
                S.op('dve', lambda e: e.tensor_tensor(out=var[:, 0:n], in0=mean[:, 0:n], in1=mean[:, 0:n], op=ALU.mult), reads=['cmean'], writes=['cvar'])
                S.op('dve', lambda e: e.tensor_tensor(out=var[:, 0:n], in0=PS[3][:, 0:n], in1=var[:, 0:n], op=ALU.subtract), reads=[pk(3), 'cvar'], writes=['cvar'])
                act(var[:, 0:n], var[:, 0:n], AF.Ln, ['cvar'], ['cvar'], bias=epsl[:], scale=1.0)
                act(var[:, 0:n], var[:, 0:n], AF.Exp, ['cvar'], ['cvar'], scale=-0.5)
                for pc in range(2):
                    d_, y_ = dd[pc], yo[pc]
                    S.op('dve', lambda e, pc=pc, d_=d_: e.tensor_tensor(out=d_[:, 0:n], in0=Cf[:, pc, 0:n], in1=mean[:, 0:n], op=ALU.subtract), reads=['cCf', 'cmean'], writes=[f'cdd{pc}'])
                    S.op('dve', lambda e, pc=pc, d_=d_: e.tensor_tensor(out=d_[:, 0:n], in0=d_[:, 0:n], in1=var[:, 0:n], op=ALU.mult), reads=[f'cdd{pc}', 'cvar'], writes=[f'cdd{pc}'])
                    act(d_[:, 0:n], d_[:, 0:n], AF.Silu, [f'cdd{pc}', 'ccb'], [f'cdd{pc}'], bias=cbt[:, 4 + pc:5 + pc], scale=cbt[:, 2 + pc:3 + pc])
                    S.op('dve', lambda e, pc=pc, d_=d_, y_=y_: e.tensor_tensor(out=y_[:, 0:n], in0=d_[:, 0:n], in1=SG[:, pc, t0:t0 + n], op=ALU.mult),
                         reads=[f'cdd{pc}', 'cSG'], writes=[f'cyo{pc}'])
                    S.dma('sp', ymix[512 + pc * 128:512 + (pc + 1) * 128, t0:t0 + n], y_[:, 0:n], reads=[f'cyo{pc}'], writes=['ymix'])
        S.barrier()

    def stageD(l, last):
        lam_init = 0.8 - 0.6 * math.exp(-0.3 * l)
        scale = 32 ** -0.5
        with ExitStack() as cx:
            KT = sb("dKT", [128, 2, T], BF16, cx)
            QT = sb("dQT", [128, 2, T], BF16, cx)
            V = sb("dV", [128, NT, 384], BF16, cx)
            SG = sb("dSG", [128, 2, T], BF16, cx)
            lmt = sb("dlm", [128, 128], F32, cx)
            lms = sb("dls", [128, 4], F32, cx)
            gn = sb("dgn", [128, 1], F32, cx)
            S.dma('sp', lmt[:], df_lam[l:l + 1, :].partition_broadcast(128), writes=['dlm'])
            S.dma('sp', gn[:], df_g[l], writes=['dgn'])
            lm4 = lmt[:].rearrange("p (a d) -> p a d", d=32)
            S.op('dve', lambda e: e.tensor_tensor(out=lm4[:, 0, :], in0=lm4[:, 0, :], in1=lm4[:, 1, :], op=ALU.mult), reads=['dlm'], writes=['dlm'])
            S.op('dve', lambda e: e.tensor_tensor(out=lm4[:, 2, :], in0=lm4[:, 2, :], in1=lm4[:, 3, :], op=ALU.mult), reads=['dlm'], writes=['dlm'])
            S.op('dve', lambda e: e.reduce_sum(out=lms[:, 0:1], in_=lm4[:, 0, :], axis=AX.X), reads=['dlm'], writes=['dls'])
            S.op('dve', lambda e: e.reduce_sum(out=lms[:, 1:2], in_=lm4[:, 2, :], axis=AX.X), reads=['dlm'], writes=['dls'])
            act(lms[:, 0:2], lms[:, 0:2], AF.Exp, ['dls'], ['dls'])
            S.op('dve', lambda e: e.tensor_tensor(out=lms[:, 2:3], in0=lms[:, 1:2], in1=lms[:, 0:1], op=ALU.subtract), reads=['dls'], writes=['dls'])
            S.op('dve', lambda e: e.tensor_scalar(out=lms[:, 2:3], in0=lms[:, 2:3], scalar1=-lam_init, scalar2=None, op0=ALU.add), reads=['dls'], writes=['dls'])
            S.op('dve', lambda e: e.tensor_scalar(out=gn[:], in0=gn[:], scalar1=(1.0 - lam_init), scalar2=None, op0=ALU.mult), reads=['dgn'], writes=['dgn'])
            with ExitStack() as c1:
                W = sb("dW", [128, 8, 1024], BF16, c1)
                Wsw = sb("dWsw", [128, 8, 512], BF16, c1)
                stgs = [(sb(f"dstg{i}", [128, 8, 128], F32, c1), f'dstg{i}') for i in range(2)]
                cs = [sb(f"dcs{i}", [128, 512], F32, c1) for i in range(2)]
                sn = [sb(f"dsn{i}", [128, 512], F32, c1) for i in range(2)]
                t1 = [sb(f"dt1{i}", [128, 512], F32, c1) for i in range(2)]
                t2 = [sb(f"dt2{i}", [128, 512], F32, c1) for i in range(2)]
                S.op('pool', lambda e: e.memset(V[:], 1.0), writes=['dV'])
                load_w(W, l, w_in, 2560, 1024, stgs, 'dW', slab=128)
                for k in range(8):
                    wv_ = W[:, k, 0:512].rearrange("p (g two j) -> p g two j", two=2, j=16)
                    sv_ = Wsw[:, k, :].rearrange("p (g two j) -> p g two j", two=2, j=16)
                    for h in range(2):
                        S.op('pool', lambda e, wv_=wv_, sv_=sv_, h=h: e.tensor_copy(out=sv_[:, :, 1 - h, :], in_=wv_[:, :, h, :]), reads=wkeys('dW', 0, 512), writes=['dWsw'])
                ci = 0
                for bi, (t0, n) in enumerate(BLK):
                    lat = t0 >= 256
                    cp = bi % 2
                    if lat:
                        S.dma('sp', cs[cp][:, 0:n], c_cos[:, t0 - 256:t0 - 256 + n], writes=[f'dcs{cp}'])
                        S.dma('sp', sn[cp][:, 0:n], c_sin[:, t0 - 256:t0 - 256 + n], writes=[f'dsn{cp}'])
                    for which, (dst, dk) in enumerate(((QT, 'dQT'), (KT, 'dKT'))):
                        for ck in range(2):
                            q = ci % 2
                            ci += 1
                            proj_fm(q, W, 'dW', which * 256 + ck * 128, t0, n)
                            if not lat:
                                S.op('dve', lambda e, dst=dst, ck=ck, q=q: e.tensor_copy(out=dst[:, ck, t0:t0 + n], in_=PS[q][:, 0:n]), reads=[pk(q)], writes=[dk])
                            else:
                                proj_fm(2 + q, Wsw, 'dWsw', which * 256 + ck * 128, t0, n)
                                S.op('dve', lambda e, q=q: e.tensor_tensor(out=t1[q][:, 0:n], in0=PS[q][:, 0:n], in1=cs[cp][:, 0:n], op=ALU.mult),
                                     reads=[pk(q), f'dcs{cp}'], writes=[f'dt1{q}'])
                                S.op('dve', lambda e, q=q: e.tensor_tensor(out=t2[q][:, 0:n], in0=PS[2 + q][:, 0:n], in1=sn[cp][:, 0:n], op=ALU.mult),
                                     reads=[pk(2 + q), f'dsn{cp}'], writes=[f'dt2{q}'])
                                S.op('pool', lambda e, dst=dst, ck=ck, q=q: e.tensor_tensor(out=dst[:, ck, t0:t0 + n], in0=t1[q][:, 0:n], in1=t2[q][:, 0:n], op=ALU.add),
                                     reads=[f'dt1{q}', f'dt2{q}'], writes=[dk])
                    for ck in range(2):
                        b = 4 + ck
                        proj_fm(b, W, 'dW', 768 + ck * 128, t0, n)
                        act(SG[:, ck, t0:t0 + n], PS[b][:, 0:n], AF.Silu, [pk(b)], ['dSG'])
                    for tt in range(n // 128):
                        i = t0 // 128 + tt
                        b = 6 + i % 2
                        proj_tm(PS[b][:, 0:256], b, W, 'dW', 512, 256, i)
                        vv = V[:, i, :].rearrange("p (g c) -> p g c", c=192)
                        pv4 = PS[b][:, 0:256].rearrange("p (g r c) -> p g r c", r=2, c=64)
                        for r_ in range(2):
                            S.op('dve', lambda e, vv=vv, pv4=pv4, r_=r_: e.tensor_copy(out=vv[:, :, 128 * r_:128 * r_ + 64], in_=pv4[:, :, r_, :]), reads=[pk(b)], writes=['dV'])
            S.barrier()
            with ExitStack() as c2:
                Pb = [sb(f"dP{i}", [128, 512], BF16, c2) for i in range(6)]
                Qz = [sb(f"dQz{i}", [128, 4, 512], BF16, c2) for i in range(2)]
                RL = [sb(f"dRL{i}", [128, 512], F32, c2) for i in range(2)]
                Nn = [sb(f"dN{i}", [128, 512], F32, c2) for i in range(2)]
                Oc = [sb(f"dOc{i}", [128, 512], F32, c2) for i in range(4)]
                Oh = sb("dOh", [128, 512], F32, c2)
                Osq = sb("dOsq", [128, 512], BF16, c2)
                rs = sb("drs", [128, 512], F32, c2)
                Yo = [sb(f"dYo{i}", [128, 512], BF16, c2) for i in range(2)]
                pending = []
                state = dict(n=0)
                for i_ in range(2):
                    S.op('pool', lambda e, i_=i_: e.memset(Qz[i_][:], 0.0), writes=[f'dQz{i_}'])

                def prep_q(pi, q0, nq, hp):
                    pp = pi % 2
                    for s_ in range(4):
                        S.op('pool', lambda e, s_=s_: e.tensor_copy(out=Qz[pp][32 * s_:32 * s_ + 32, s_, 0:nq], in_=QT[32 * s_:32 * s_ + 32, hp, q0:q0 + nq]),
                             reads=['dQT'], writes=[f'dQz{pp}'])

                def finalize(q0, nq, hp, yp):
                    steps = []
                    for s_ in range(4):
                        steps.append(lambda s_=s_: S.op('dve', lambda e: e.tensor_copy(out=Oc[s_][:, 0:nq], in_=PS[3 + s_][:, 0:nq]), reads=[pk(3 + s_)], writes=[f'dOc{s_}']))
                    for s_ in range(4):
                        hh, w = s_ // 2, s_ % 2
                        lo, ll = 64 * hh, 64 * (1 - hh)
                        steps.append(lambda s_=s_, w=w, lo=lo, ll=ll: S.op('dve', lambda e: e.reciprocal(out=RL[w][lo:lo + 64, 0:nq], in_=Oc[s_][ll:ll + 64, 0:nq]),
                                                                     reads=[f'dOc{s_}'], writes=[f'dRL{w}']))
                        steps.append(lambda s_=s_, w=w, lo=lo: S.op('dve', lambda e: e.tensor_tensor(out=Nn[w][lo:lo + 64, 0:nq], in0=Oc[s_][lo:lo + 64, 0:nq], in1=RL[w][lo:lo + 64, 0:nq], op=ALU.mult),
                                                              reads=[f'dOc{s_}', f'dRL{w}'], writes=[f'dN{w}']))
                    steps.append(lambda: S.op('dve', lambda e: e.scalar_tensor_tensor(out=Oh[:, 0:nq], in0=Nn[1][:, 0:nq], scalar=lms[:, 2:3], in1=Nn[0][:, 0:nq], op0=ALU.mult, op1=ALU.add),
                                              reads=['dN0', 'dN1', 'dls'], writes=['dOh']))
                    steps.append(lambda: S.op('pool', lambda e: e.tensor_tensor(out=Osq[:, 0:nq], in0=Oh[:, 0:nq], in1=Oh[:, 0:nq], op=ALU.mult), reads=['dOh'], writes=['dOsq']))
                    steps.append(lambda: S.op('pe', lambda e: e.matmul(PS[7][:, 0:nq], lhsT=bd64[:], rhs=Osq[:, 0:nq], start=True, stop=True), reads=['bd64', 'dOsq'], writes=[pk(7)]))
                    steps.append(lambda: rstd_from_ss(rs[:, 0:nq], PS[7][:, 0:nq], 1.0 / 64, epsr[:], 'drs', pk(7)))
                    steps.append(lambda: S.op('dve', lambda e: e.tensor_tensor(out=Oh[:, 0:nq], in0=Oh[:, 0:nq], in1=rs[:, 0:nq], op=ALU.mult), reads=['dOh', 'drs'], writes=['dOh']))
                    steps.append(lambda: S.op('dve', lambda e: e.scalar_tensor_tensor(out=Yo[yp][:, 0:nq], in0=Oh[:, 0:nq], scalar=gn[:, 0:1], in1=SG[:, hp, q0:q0 + nq], op0=ALU.mult, op1=ALU.mult),
                                              reads=['dOh', 'dgn', 'dSG'], writes=[f'dYo{yp}']))
                    steps.append(lambda: S.dma('sp', ymix[768 + hp * 128:768 + (hp + 1) * 128, q0:q0 + nq], Yo[yp][:, 0:nq], reads=[f'dYo{yp}'], writes=['ymix']))
                    return steps

                passes = []
                if not last:
                    for hp in range(2):
                        passes.append((0, 256, [0, 1], hp))
                for qb in range(8):
                    for hp in range(2):
                        passes.append((256 + qb * 512, 512, list(range(NT)), hp))

                def attn_pass(pi):
                    q0, nq, kts, hp = passes[pi]
                    pp = pi % 2
                    seq = [(kt, s_) for kt in kts for s_ in range(4)]
                    nseq = len(seq)
                    base = state['n']

                    def qk(m):
                        kt, s_ = seq[m]
                        b_ = (base + m) % 3
                        S.op('pe', lambda e: e.matmul(PS[b_][:, 0:nq], lhsT=KT[:, hp, kt * 128:(kt + 1) * 128], rhs=Qz[pp][:, s_, 0:nq], start=True, stop=True),
                             reads=['dKT', f'dQz{pp}'], writes=[pk(b_)])

                    def ex(m):
                        g = base + m
                        act(Pb[g % 6][:, 0:nq], PS[g % 3][:, 0:nq], AF.Exp, [pk(g % 3)], [f'dP{g % 6}'], scale=scale)

                    def pv(m):
                        kt, s_ = seq[m]
                        pb = (base + m) % 6
                        hh = s_ // 2
                        c0 = hp * 192 + hh * 64
                        S.op('pe', lambda e: e.matmul(PS[3 + s_][:, 0:nq], lhsT=V[:, kt, c0:c0 + 128], rhs=Pb[pb][:, 0:nq], start=(kt == kts[0]), stop=(kt == kts[-1])),
                             reads=['dV', f'dP{pb}'], writes=[pk(3 + s_)])

                    for m in range(min(3, nseq)):
                        qk(m)
                    if pi + 1 < len(passes):
                        prep_q(pi + 1, passes[pi + 1][0], passes[pi + 1][1], passes[pi + 1][3])
                    for m in range(nseq):
                        ex(m)
                        pv(m)
                        if m + 3 < nseq:
                            qk(m + 3)
                        if pending and m % 2 == 1:
                            pending.pop(0)()
                    state['n'] = base + nseq
                    while pending:
                        pending.pop(0)()
                    fs = finalize(q0, nq, hp, pi % 2)
                    for f_ in fs[:4]:
                        f_()
                    pending.extend(fs[4:])

                prep_q(0, passes[0][0], passes[0][1], passes[0][3])
                for pi in range(len(passes)):
                    attn_pass(pi)
                while pending:
                    pending.pop(0)()
        S.barrier()

    def stageB(l):
        with ExitStack() as cx:
            W = sb("bW", [128, 8, 640], BF16, cx)
            stgs = [(sb(f"bstg{i}", [128, 8, 128], F32, cx), f'bstg{i}') for i in range(2)]
            Vt = sb("bV", [128, NT, 128], BF16, cx)
            OT = sb("bOT", [128, T], F32, cx)
            QTl = sb("bQT", [128, T], BF16, cx)
            KTl = sb("bKT", [128, T], BF16, cx)
            Kt = sb("bKt", [128, NT, 128], BF16, cx)
            Sb = sb("bSb", [128, NCH, 64], BF16, cx)
            KH = Sb[:].rearrange("p c e -> p (c e)")
            M0 = sb("bM0", [128, T], BF16, cx)
            Gc = sb("bGc", [128, T], F32, cx)
            Bt = Gc[:].rearrange("p (c e) -> p c e", e=64)
            T1 = [sb(f"bT1{i}", [128, 512], F32, cx) for i in range(2)]
            T2 = [sb(f"bT2{i}", [128, 512], F32, cx) for i in range(2)]
            sm = sb("bsm", [128, 6, NCH], F32, cx)
            SCm = [sb(f"bSC{i}", [128, 256], BF16, cx) for i in range(4)]
            osq = sb("bosq", [128, 512], BF16, cx)
            ors = sb("bors", [128, 512], F32, cx)
            oy = [sb(f"boy{i}", [128, 512], BF16, cx) for i in range(2)]
            hgn = sb("bhgn", [128, 1], F32, cx)
            S.dma('sp', hgn[:], hg_g[l], writes=['bhgn'])
            groups = [[0, 1]] + [list(range(2 + 8 * g, 10 + 8 * g)) for g in range(4)]
            for pc in range(2):
                for j in range(5):
                    load_w(W, l, w_in, 512 + j * 256 + pc * 128, 128, stgs, 'bW', off=j * 128, slab=128)
                S.op('pool', lambda e: e.memset(OT[:], 0.0), writes=['bOT'])
                for i in range(NT):
                    b = 6 + i % 2
                    proj_tm(PS[b][:, 0:128], b, W, 'bW', 128, 128, i)
                    S.op('dve', lambda e, b=b, i=i: e.tensor_copy(out=Vt[:, i, :], in_=PS[b][:, 0:128]), reads=[pk(b)], writes=['bV'])
                _ck(1)
                for dr in range(2):
                    li = (pc * 2 + dr)
                    lb_ap = LBt[:, 0, li, l:l + 1]
                    oml_ap = LBt[:, 1, li, l:l + 1]
                    S.op('pool', lambda e: e.memset(M0[:], 1.0), writes=['bM0'])
                    m3 = M0[:].rearrange("p (c j) -> p c j", j=64)
                    zc = 0 if dr == 0 else 63
                    S.op('pool', lambda e, zc=zc: e.memset(m3[:, :, zc:zc + 1], 0.0), writes=['bM0'])
                    for bi, (t0, n) in enumerate(BLK):
                        q = bi % 2
                        proj_fm(q, W, 'bW', (2 + dr) * 128, t0, n)
                        act(T1[q][:, 0:n], PS[q][:, 0:n], AF.Exp, [pk(q)], [f'bT1{q}'], scale=-1.0)
                        act(T2[q][:, 0:n], T1[q][:, 0:n], AF.Ln, [f'bT1{q}', 'LBt'], [f'bT2{q}'], bias=onec[:], scale=lb_ap)
                        act(T1[q][:, 0:n], T1[q][:, 0:n], AF.Ln, [f'bT1{q}'], [f'bT1{q}'], bias=onec[:], scale=1.0)
                        S.op('dve', lambda e, q=q, t0=t0, n=n: e.tensor_tensor(out=Gc[:, t0:t0 + n], in0=T2[q][:, 0:n], in1=T1[q][:, 0:n], op=ALU.subtract),
                             reads=[f'bT1{q}', f'bT2{q}'], writes=['bGc'])
                    if dr == 0:
                        S.op('dve', lambda e: e.tensor_tensor_scan(out=Gc[:], data0=M0[:], data1=Gc[:], initial=0.0, op0=ALU.mult, op1=ALU.add),
                             reads=['bGc', 'bM0'], writes=['bGc'])
                    else:
                        S.op('dve', lambda e: e.tensor_tensor_scan(out=Gc[:][:, ::-1], data0=M0[:][:, ::-1], data1=Gc[:][:, ::-1], initial=0.0, op0=ALU.mult, op1=ALU.add),
                             reads=['bGc', 'bM0'], writes=['bGc'])
                    _ck(2)
                    g3 = Gc[:].rearrange("p (c j) -> p c j", j=64)
                    mid = 31 if dr == 0 else 32
                    end = 63 if dr == 0 else 0
                    S.op('dve', lambda e: e.tensor_copy(out=sm[:, 0, :], in_=g3[:, :, mid]), reads=['bGc'], writes=['bsm'])
                    S.op('dve', lambda e: e.tensor_copy(out=sm[:, 1, :], in_=g3[:, :, end]), reads=['bGc'], writes=['bsm'])
                    S.op('dve', lambda e: e.tensor_tensor(out=sm[:, 5, :], in0=sm[:, 1, :], in1=sm[:, 0, :], op=ALU.subtract), reads=['bsm'], writes=['bsm'])
                    act(sm[:, 2, :], sm[:, 1, :], AF.Exp, ['bsm'], ['bsm'])
                    act(sm[:, 3, :], sm[:, 5, :], AF.Exp, ['bsm'], ['bsm'])
                    act(sm[:, 4, :], sm[:, 0, :], AF.Exp, ['bsm'], ['bsm'])
                    S.op('dve', lambda e: e.tensor_tensor(out=g3, in0=g3, in1=sm[:, 0, :].unsqueeze(2).broadcast_to([128, NCH, 64]), op=ALU.subtract),
                         reads=['bGc', 'bsm'], writes=['bGc'])
                    _ck(3)
                    for bi, (t0, n) in enumerate(BLK):
                        q = bi % 2
                        c0, nc_ = t0 // 64, n // 64
                        proj_fm(q, W, 'bW', 0, t0, n)
                        act(T1[q][:, 0:n], PS[q][:, 0:n], AF.Exp, [pk(q)], [f'bT1{q}'], scale=-1.0)
                        act(T1[q][:, 0:n], T1[q][:, 0:n], AF.Ln, [f'bT1{q}'], [f'bT1{q}'], bias=onec[:], scale=1.0)
                        S.op('dve', lambda e, q=q, t0=t0, n=n: e.tensor_tensor(out=T1[q][:, 0:n], in0=Gc[:, t0:t0 + n], in1=T1[q][:, 0:n], op=ALU.subtract),
                             reads=['bGc', f'bT1{q}'], writes=[f'bT1{q}'])
                        act(T1[q][:, 0:n], T1[q][:, 0:n], AF.Exp, [f'bT1{q}'], [f'bT1{q}'])
                        S.op('dve', lambda e, q=q, t0=t0, n=n: e.tensor_tensor(out=QTl[:, t0:t0 + n], in0=PS[q][:, 0:n], in1=T1[q][:, 0:n], op=ALU.mult),
                             reads=[pk(q), f'bT1{q}'], writes=['bQT'])
                        proj_fm(2 + q, W, 'bW', (2 + dr) * 128, t0, n)
                        act(T2[q][:, 0:n], PS[2 + q][:, 0:n], AF.Exp, [pk(2 + q)], [f'bT2{q}'])
                        act(T2[q][:, 0:n], T2[q][:, 0:n], AF.Ln, [f'bT2{q}'], [f'bT2{q}'], bias=onec[:], scale=1.0)
                        S.op('dve', lambda e, q=q, t0=t0, n=n: e.tensor_tensor(out=T2[q][:, 0:n], in0=Gc[:, t0:t0 + n], in1=T2[q][:, 0:n], op=ALU.add),
                             reads=['bGc', f'bT2{q}'], writes=[f'bT2{q}'])
                        act(T2[q][:, 0:n], T2[q][:, 0:n], AF.Exp, [f'bT2{q}'], [f'bT2{q}'], scale=-1.0)
                        S.op('dve', lambda e, q=q, t0=t0, n=n: e.tensor_scalar(out=KTl[:, t0:t0 + n], in0=T2[q][:, 0:n], scalar1=oml_ap, scalar2=None, op0=ALU.mult),
                             reads=[f'bT2{q}', 'LBt'], writes=['bKT'])
                        kv3 = KTl[:, t0:t0 + n].rearrange("p (c j) -> p c j", j=64)
                        kh3 = KH[:, t0:t0 + n].rearrange("p (c j) -> p c j", j=64)
                        S.op('dve', lambda e, kv3=kv3, kh3=kh3, c0=c0, nc_=nc_: e.tensor_tensor(out=kh3, in0=kv3, in1=sm[:, 3, c0:c0 + nc_].unsqueeze(2).broadcast_to([128, nc_, 64]), op=ALU.mult),
                             reads=['bKT', 'bsm'], writes=['bSb'])
                    _ck(4)
                    for i in range(NT):
                        b = 4 + i % 2
                        psb = PS[b][:].bitcast(BF16)
                        S.op('pe', lambda e, i=i, psb=psb: e.transpose(out=psb[:, 0:128], in_=KH[:, i * 128:(i + 1) * 128], identity=identb[:]),
                             reads=['bSb', 'identb'], writes=[pk(b)])
                        S.op('dve', lambda e, i=i, psb=psb: e.tensor_copy(out=Kt[:, i, :], in_=psb[:, 0:128]), reads=[pk(b)], writes=['bKt'])
                    _ck(5)
                    for g, tiles in enumerate(groups):
                        for ti, i in enumerate(tiles):
                            for cp in range(2):
                                bb = 2 * (g % 2) + cp
                                for h2 in range(2):
                                    S.op('pe', lambda e, ti=ti, i=i, cp=cp, h2=h2, bb=bb: e.matmul(PS[bb][64 * h2:64 * h2 + 64, ti * 64:(ti + 1) * 64],
                                                                                               lhsT=Kt[64 * cp:64 * cp + 64, i, 64 * h2:64 * h2 + 64],
                                                                                               rhs=Vt[64 * cp:64 * cp + 64, i, 64 * h2:64 * h2 + 64],
                                                                                               start=True, stop=True, tile_position=(64 * cp, 64 * h2)),
                                         reads=['bKt', 'bV'], writes=[pk(bb)], sig=(ti == len(tiles) - 1 and h2 == 1))
                        nt_ = len(tiles)
                        for cp in range(2):
                            bb = 2 * (g % 2) + cp
                            cfirst = 2 * tiles[0] + cp
                            if dr == 0:
                                dst = Bt[:, cfirst:cfirst + 2 * (nt_ - 1) + 1:2, :]
                            else:
                                pfirst = (3 - cfirst) if g == 0 else (71 - cfirst)
                                stop = pfirst - 2 * (nt_ - 1) - 1
                                dst = Bt[:, pfirst:(stop if stop >= 0 else None):-2, :]
                            S.op('dve', lambda e, bb=bb, dst=dst, nt_=nt_: e.tensor_copy(out=dst, in_=PS[bb][:, 0:nt_ * 64].rearrange("p (t e) -> p t e", e=64)),
                                 reads=[pk(bb)], writes=['bGc'])
                    _ck(6)
                    if dr == 0:
                        lam_ap = sm[:, 2, :]
                    else:
                        S.op('dve', lambda e: e.tensor_copy(out=sm[:, 5, 0:4], in_=sm[:, 2, 0:4][:, ::-1]), reads=['bsm'], writes=['bsm'])
                        S.op('dve', lambda e: e.tensor_copy(out=sm[:, 5, 4:NCH], in_=sm[:, 2, 4:NCH][:, ::-1]), reads=['bsm'], writes=['bsm'])
                        lam_ap = sm[:, 5, :]
                    for e_ in range(64):
                        S.op('dve', lambda e, e_=e_: e.tensor_tensor_scan(out=Bt[:, :, e_], data0=lam_ap, data1=Bt[:, :, e_], initial=0.0, op0=ALU.mult, op1=ALU.add),
                             reads=['bsm', 'bGc'], writes=['bGc'])
                    _ck(7)
                    rho = sm[:, 4, :]
                    if dr == 0:
                        S.op('dve', lambda e: e.memset(Sb[:, 0:1, :], 0.0), writes=['bSb'])
                        S.op('dve', lambda e: e.tensor_tensor(out=Sb[:, 1:NCH, :], in0=Bt[:, 0:NCH - 1, :], in1=rho[:, 1:NCH].unsqueeze(2).broadcast_to([128, NCH - 1, 64]), op=ALU.mult),
                             reads=['bGc', 'bsm'], writes=['bSb'])
                    else:
                        S.op('dve', lambda e: e.memset(Sb[:, 3:4, :], 0.0), writes=['bSb'])
                        S.op('dve', lambda e: e.tensor_tensor(out=Sb[:, 0:3, :], in0=Bt[:, 2::-1, :], in1=rho[:, 0:3].unsqueeze(2).broadcast_to([128, 3, 64]), op=ALU.mult),
                             reads=['bGc', 'bsm'], writes=['bSb'])
                        S.op('dve', lambda e: e.tensor_tensor(out=Sb[:, 4:NCH, :], in0=Bt[:, 66:2:-1, :], in1=rho[:, 4:NCH].unsqueeze(2).broadcast_to([128, NCH - 4, 64]), op=ALU.mult),
                             reads=['bGc', 'bsm'], writes=['bSb'])
                    _ck(8)
                    mk = maskt[:, 64 * dr:64 * dr + 64]
                    tgs = list(enumerate(range(0, NT, 4)))

                    def b_scores(tgi, tg):
                            tiles = list(range(tg, min(tg + 4, NT)))
                            nt_ = len(tiles)
                            q = tgi % 2
                            for ti, i in enumerate(tiles):
                                for cp in range(2):
                                    c = 2 * i + cp
                                    for h2 in range(2):
                                        bb = 2 * q + h2
                                        for jb in range(2):
                                            full = (jb == 0) if dr == 0 else (jb == 1)
                                            i0, ni = (0, 64) if full else ((32, 32) if dr == 0 else (0, 32))
                                            S.op('pe', lambda e, c=c, cp=cp, h2=h2, bb=bb, ti=ti, jb=jb, i0=i0, ni=ni: e.matmul(
                                                PS[bb][64 * cp + 32 * jb:64 * cp + 32 * jb + 32, ti * 64 + i0:ti * 64 + i0 + ni],
                                                lhsT=KTl[64 * h2:64 * h2 + 64, c * 64 + 32 * jb:c * 64 + 32 * jb + 32],
                                                rhs=QTl[64 * h2:64 * h2 + 64, c * 64 + i0:c * 64 + i0 + ni],
                                                start=True, stop=True, tile_position=(64 * h2, 64 * cp + 32 * jb)),
                                                 reads=['bKT', 'bQT'], writes=[pk(bb)], sig=(ti == nt_ - 1 and cp == 1 and jb == 1))
                            for h2 in range(2):
                                bb = 2 * q + h2
                                scv = SCm[2 * q + h2][:, 0:nt_ * 64].rearrange("p (t i) -> p t i", i=64)
                                S.op('dve', lambda e, bb=bb, scv=scv, nt_=nt_: e.tensor_tensor(out=scv, in0=PS[bb][:, 0:nt_ * 64].rearrange("p (t i) -> p t i", i=64),
                                                                                           in1=mk.unsqueeze(1).broadcast_to([128, nt_, 64]), op=ALU.mult),
                                     reads=[pk(bb), 'maskt'], writes=[f'bSC{2 * q + h2}'])

                    def b_rest(tgi, tg):
                            tiles = list(range(tg, min(tg + 4, NT)))
                            nt_ = len(tiles)
                            q = tgi % 2
                            for ti, i in enumerate(tiles):
                                for cp in range(2):
                                    for h2 in range(2):
                                        S.op('pe', lambda e, cp=cp, h2=h2, ti=ti, i=i, q=q: e.matmul(PS[4 + cp][64 * h2:64 * h2 + 64, ti * 64:(ti + 1) * 64],
                                                                                                 lhsT=Vt[64 * cp:64 * cp + 64, i, 64 * h2:64 * h2 + 64],
                                                                                                 rhs=SCm[2 * q + h2][64 * cp:64 * cp + 64, ti * 64:(ti + 1) * 64],
                                                                                                 start=True, stop=True, tile_position=(64 * cp, 64 * h2)),
                                             reads=['bV', f'bSC{2 * q + h2}'], writes=[pk(4 + cp)], sig=(ti == nt_ - 1 and h2 == 1))
                            for ti, i in enumerate(tiles):
                                for cp in range(2):
                                    c = 2 * i + cp
                                    for h2 in range(2):
                                        S.op('pe', lambda e, c=c, cp=cp, h2=h2, ti=ti: e.matmul(PS[6][64 * h2:64 * h2 + 64, (ti * 2 + cp) * 64:(ti * 2 + cp + 1) * 64],
                                                                                           lhsT=Sb[64 * h2:64 * h2 + 64, c, :],
                                                                                           rhs=QTl[64 * h2:64 * h2 + 64, c * 64:(c + 1) * 64],
                                                                                           start=True, stop=True, tile_position=(64 * h2, 64 * h2)),
                                             reads=['bSb', 'bQT'], writes=[pk(6)], sig=(ti == nt_ - 1 and cp == 1 and h2 == 1))
                            otv = OT[:, tg * 128:(tg + nt_) * 128].rearrange("p (t c i) -> p t c i", c=2, i=64)
                            for cp in range(2):
                                S.op('dve', lambda e, cp=cp, otv=otv, nt_=nt_: e.tensor_tensor(out=otv[:, :, cp, :], in0=PS[4 + cp][:, 0:nt_ * 64].rearrange("p (t i) -> p t i", i=64),
                                                                                           in1=otv[:, :, cp, :], op=ALU.add),
                                     reads=[pk(4 + cp), 'bOT'], writes=['bOT'])
                            S.op('dve', lambda e, tg=tg, nt_=nt_: e.tensor_tensor(out=OT[:, tg * 128:(tg + nt_) * 128], in0=PS[6][:, 0:nt_ * 128], in1=OT[:, tg * 128:(tg + nt_) * 128], op=ALU.add),
                                 reads=[pk(6), 'bOT'], writes=['bOT'])

                    b_scores(*tgs[0])
                    for gi_ in range(len(tgs)):
                        if gi_ + 1 < len(tgs):
                            b_scores(*tgs[gi_ + 1])
                        b_rest(*tgs[gi_])
                _ck(9)
                for bi, (t0, n) in enumerate(BLK):
                    q = bi % 2
                    S.op('pool', lambda e, t0=t0, n=n: e.tensor_tensor(out=osq[:, 0:n], in0=OT[:, t0:t0 + n], in1=OT[:, t0:t0 + n], op=ALU.mult), reads=['bOT'], writes=['bosq'])
                    S.op('pe', lambda e, n=n: e.matmul(PS[4][:, 0:n], lhsT=bd64[:], rhs=osq[:, 0:n], start=True, stop=True), reads=['bd64', 'bosq'], writes=[pk(4)])
                    rstd_from_ss(ors[:, 0:n], PS[4][:, 0:n], 1.0 / 64, epsr[:], 'bors', pk(4))
                    proj_fm(5, W, 'bW', 4 * 128, t0, n)
                    act(T1[q][:, 0:n], PS[5][:, 0:n], AF.Exp, [pk(5)], [f'bT1{q}'], scale=-1.0)
                    act(T1[q][:, 0:n], T1[q][:, 0:n], AF.Ln, [f'bT1{q}'], [f'bT1{q}'], bias=onec[:], scale=1.0)
                    act(T1[q][:, 0:n], T1[q][:, 0:n], AF.Exp, [f'bT1{q}'], [f'bT1{q}'], scale=-1.0)
                    S.op('dve', lambda e, q=q, n=n: e.tensor_tensor(out=T1[q][:, 0:n], in0=PS[5][:, 0:n], in1=T1[q][:, 0:n], op=ALU.mult), reads=[pk(5), f'bT1{q}'], writes=[f'bT1{q}'])
                    S.op('dve', lambda e, t0=t0, n=n: e.tensor_tensor(out=ors[:, 0:n], in0=ors[:, 0:n], in1=OT[:, t0:t0 + n], op=ALU.mult), reads=['bors', 'bOT'], writes=['bors'])
                    S.op('dve', lambda e, q=q, n=n: e.scalar_tensor_tensor(out=oy[q][:, 0:n], in0=ors[:, 0:n], scalar=hgn[:, 0:1], in1=T1[q][:, 0:n], op0=ALU.mult, op1=ALU.mult),
                         reads=['bors', 'bhgn', f'bT1{q}'], writes=[f'boy{q}'])
                    S.dma('sp', ymix[256 + pc * 128:256 + (pc + 1) * 128, t0:t0 + n], oy[q][:, 0:n], reads=[f'boy{q}'], writes=['ymix'])
        S.barrier()

    S.barrier()
    prologue()
    for l in range(nlayers):
        last = (l == DEPTH - 1)
        if 'H' in stages and (l == 0 or 'O' not in stages):
            stageH(l)
        if 'A' in stages:
            stageA(l)
        if 'B' in stages:
            try:
                stageB(l)
            except _Stop:
                S.barrier()
        if 'C' in stages:
            stageC(l, last)
        if 'D' in stages:
            stageD(l, last)
        if 'O' in stages:
            stageO(l, last, fuse_next=(l + 1 < nlayers and 'H' in stages))
    S.barrier()
    if 'BSTOP' not in _os.environ:
        es.close()
    return nc


def _consts():
    ident = np.eye(128, dtype=np.float32).astype(ml_dtypes.bfloat16)
    n_freq = 8
    inv_freq = (10000.0 ** (-np.arange(n_freq, dtype=np.float32) / n_freq)).astype(np.float32)
    row = np.repeat(np.arange(64, dtype=np.float32), 64)
    col = np.tile(np.arange(64, dtype=np.float32), 64)
    ang = np.concatenate([row[:, None] * inv_freq, col[:, None] * inv_freq], axis=-1).astype(np.float32)
    cos, sin = np.cos(ang).astype(np.float32), np.sin(ang).astype(np.float32)
    c32 = np.concatenate([cos, cos], axis=1).T
    s32 = np.concatenate([-sin, sin], axis=1).T
    cosT = np.ascontiguousarray(np.tile(c32, (4, 1)))
    sinT = np.ascontiguousarray(np.tile(s32, (4, 1)))
    j = np.arange(64)[:, None]
    i = np.arange(64)[None, :]
    fwd = (j <= i).astype(np.float32)
    bwd = (j >= i).astype(np.float32)
    mask = np.concatenate([np.tile(fwd, (2, 1)), np.tile(bwd, (2, 1))], axis=1)
    bd = np.zeros((128, 128), np.float32)
    bd[:64, :64] = 1
    bd[64:, 64:] = 1
    return dict(c_ident=ident, c_cos=cosT, c_sin=sinT, c_mask=np.ascontiguousarray(mask), c_bd64=bd.astype(ml_dtypes.bfloat16))


def _layout(inp, b):
    f = lambda a: np.ascontiguousarray(np.asarray(a, dtype=np.float32))
    m = {}
    m["x_b"] = f(inp["x"][b])
    m["ctx_b"] = f(inp["ctx"][b])
    cc = np.stack([np.asarray(inp["c"][b]), np.asarray(inp["c_ctx"])], -1)
    m["cfm"] = f(cc.reshape(8, 128, 2).transpose(1, 0, 2).reshape(128, 16))
    m["w_mod"] = f(inp["w_mod"])
    m["b_mod"] = f(inp["b_mod"])
    m["b_mod_fm"] = f(np.asarray(inp["b_mod"]).reshape(DEPTH, 24, 128).transpose(0, 2, 1))
    m["g_pre_fm"] = f(np.asarray(inp["g_pre"]).reshape(DEPTH, 8, 128).transpose(0, 2, 1))
    m["g_post"] = f(inp["g_post"])
    m["w_in"] = f(inp["w_in"])
    m["w_out"] = f(inp["w_out"])
    m["lru_cw"] = f(np.asarray(inp["lru_conv_w"]).reshape(DEPTH, 4, 2, 128).transpose(0, 3, 2, 1).reshape(DEPTH, 128, 8))
    m["lru_cb"] = f(np.asarray(inp["lru_conv_b"]).reshape(DEPTH, 2, 128).transpose(0, 2, 1))
    wr, wi = np.asarray(inp["lru_w_r"]), np.asarray(inp["lru_w_i"])
    bd = np.zeros((DEPTH, 2, 4, 128, 128), np.float32)
    for pc in range(2):
        for dr in range(2):
            for wh, w in enumerate((wr, wi)):
                for h2 in range(2):
                    bd[:, pc, 2 * dr + wh, 64 * h2:64 * h2 + 64, 64 * h2:64 * h2 + 64] = w[:, dr, 2 * pc + h2]
    m["lru_bd"] = bd
    br, bi_ = np.asarray(inp["lru_b_r"]), np.asarray(inp["lru_b_i"])
    lb = np.stack([br, bi_], 1).reshape(DEPTH, 2, 2, 2, 128)
    m["lru_b"] = f(lb.transpose(0, 4, 1, 2, 3).reshape(DEPTH, 128, 8))
    m["lru_lam"] = f(np.asarray(inp["lru_lambda"]).reshape(DEPTH, 2, 2, 128).transpose(0, 3, 1, 2).reshape(DEPTH, 128, 4))
    hl = np.asarray(inp["hgrn_lb"]).reshape(DEPTH, 2, 2, 128)
    m["hg_lb"] = f(hl.transpose(3, 2, 1, 0).reshape(128, 16))
    m["hg_g"] = f(np.tile(np.asarray(inp["hgrn_norm_g"]), (1, 2)).reshape(DEPTH, 128, 1))
    m["cf_w"] = f(np.asarray(inp["conf_conv_w"]).reshape(DEPTH, 31, 2, 128).transpose(0, 3, 2, 1).reshape(DEPTH, 128, 62))
    cb = np.stack([np.asarray(inp["conf_conv_b"]), np.asarray(inp["conf_ln_g"]), np.asarray(inp["conf_ln_b"])], 1).reshape(DEPTH, 3, 2, 128)
    m["cf_b"] = f(cb.transpose(0, 3, 1, 2).reshape(DEPTH, 128, 6))
    m["df_lam"] = f(np.concatenate([np.asarray(inp[k]) for k in ("diff_lam_q1", "diff_lam_k1", "diff_lam_q2", "diff_lam_k2")], axis=1))
    m["df_g"] = f(np.tile(np.asarray(inp["diff_norm_g"]), (1, 2)).reshape(DEPTH, 128, 1))
    return m


def kernel(**inputs):
    n = 8
    nc = bass.Bass("TRN2", target_bir_lowering=False)
    build(nc)
    consts = _consts()
    in_maps = []
    for b in range(n):
        m = _layout(inputs, b)
        m.update(consts)
        in_maps.append(m)
    res = run_bass_kernel_spmd(nc, in_maps, core_ids=list(range(n)))
    return np.stack([np.asarray(r["out"], dtype=np.float32) for r in res.results], axis=0)
```

```python
import math
import numpy as np
import ml_dtypes
from contextlib import ExitStack
import concourse.bass as bass
import concourse.mybir as mybir
from concourse.bass_utils import run_bass_kernel_spmd

F32 = mybir.dt.float32
BF16 = mybir.dt.bfloat16
ALU = mybir.AluOpType
AF = mybir.ActivationFunctionType
AX = mybir.AxisListType

D = 1024
LC = 256
LL = 4096
T = LC + LL
NT = T // 128
DEPTH = 4
RMS_EPS = 1e-6
LN_EPS = 1e-5
BLK = [(0, 256)] + [(256 + 512 * j, 512) for j in range(8)]
NCH = T // 64


import os as _os
class _Stop(Exception):
    pass


def _ck(n):
    if int(_os.environ.get('BSTOP', '99')) == n:
        raise _Stop()


class Sched:
    ENG = ('pe', 'act', 'dve', 'pool', 'sp')

    def __init__(s, nc, es, ndsem=24):
        s.nc = nc
        s.eng = dict(pe=nc.tensor, act=nc.scalar, dve=nc.vector, pool=nc.gpsimd, sp=nc.sync)
        s.sem = {e: es.enter_context(nc.semaphore("sem_" + e)) for e in s.ENG}
        s.cnt = {e: 0 for e in s.ENG}
        s.seen = {e: {} for e in s.ENG}
        s.dsem = [es.enter_context(nc.semaphore(f"dsem{i}")) for i in range(ndsem)]
        s.dcnt = [0] * ndsem
        s.dnext = 0
        s.lastw = {}
        s.readers = {}
        s.unsig = False

    def _need(s, e, tok):
        kind, who, val = tok
        if kind == 'e' and who == e and e == 'pe':
            return
        key = (kind, who)
        if s.seen[e].get(key, 0) >= val:
            return
        if kind == 'e':
            assert val <= s.cnt[who], f"wait on unsignaled op {tok} cnt={s.cnt[who]}"
            s.eng[e].wait_ge(s.sem[who], val)
        else:
            s.eng[e].wait_ge(s.dsem[who], val)
        s.seen[e][key] = val

    def _deps(s, e, reads, writes):
        for r in reads:
            t = s.lastw.get(r)
            if t is not None:
                s._need(e, t)
        for w in writes:
            t = s.lastw.get(w)
            if t is not None:
                s._need(e, t)
            for (k, who), val in s.readers.get(w, {}).items():
                s._need(e, (k, who, val))

    def _reg(s, tok, reads, writes):
        for r in reads:
            d = s.readers.setdefault(r, {})
            key = (tok[0], tok[1])
            if d.get(key, 0) < tok[2]:
                d[key] = tok[2]
        for w in writes:
            s.lastw[w] = tok
            s.readers[w] = {}

    def op(s, e, fn, reads=(), writes=(), sig=True):
        s._deps(e, reads, writes)
        ins = fn(s.eng[e])
        if sig:
            s.cnt[e] += 1
            ins.then_inc(s.sem[e], 1)
            tok = ('e', e, s.cnt[e])
            if e == 'pe':
                s.unsig = False
        else:
            assert e == 'pe'
            tok = ('e', e, s.cnt[e] + 1)
            s.unsig = True
        s._reg(tok, reads, writes)

    def dma(s, q, out, in_, reads=(), writes=()):
        s._deps(q, reads, writes)
        i = s.dnext
        s.dnext = (s.dnext + 1) % len(s.dsem)
        if s.dcnt[i] > 0:
            s._need(q, ('d', i, 16 * s.dcnt[i]))
        s.dcnt[i] += 1
        s.eng[q].dma_start(out=out, in_=in_).then_inc(s.dsem[i], 16)
        s._reg(('d', i, 16 * s.dcnt[i]), reads, writes)

    def barrier(s):
        assert not s.unsig
        toks = [('e', e, s.cnt[e]) for e in s.ENG if s.cnt[e] > 0]
        toks += [('d', i, 16 * s.dcnt[i]) for i in range(len(s.dsem)) if s.dcnt[i] > 0]
        for e in s.ENG:
            for t in toks:
                s._need(e, t)
        s.lastw.clear()
        s.readers.clear()


def build(nc, nlayers=DEPTH, dbg=False, stages="HABCDO"):
    es = ExitStack()
    S = Sched(nc, es)

    def din(name, shape, dt=F32):
        return nc.dram_tensor(name, list(shape), dt, kind="ExternalInput").ap()

    x_in = din("x_b", [LL, D])
    c_in = din("ctx_b", [LC, D])
    cfm = din("cfm", [128, 16])
    w_mod = din("w_mod", [DEPTH, D, 3 * D])
    b_mod = din("b_mod", [DEPTH, 3 * D])
    b_mod_fm = din("b_mod_fm", [DEPTH, 128, 24])
    g_pre_fm = din("g_pre_fm", [DEPTH, 128, 8])
    g_post = din("g_post", [DEPTH, D])
    w_in = din("w_in", [DEPTH, D, 3584])
    w_out = din("w_out", [DEPTH, D, D])
    lru_cw = din("lru_cw", [DEPTH, 128, 8])
    lru_cb = din("lru_cb", [DEPTH, 128, 2])
    lru_bd = din("lru_bd", [DEPTH, 2, 4, 128, 128])
    lru_b = din("lru_b", [DEPTH, 128, 8])
    lru_lam = din("lru_lam", [DEPTH, 128, 4])
    hg_lb = din("hg_lb", [128, 16])
    hg_g = din("hg_g", [DEPTH, 128, 1])
    cf_w = din("cf_w", [DEPTH, 128, 62])
    cf_b = din("cf_b", [DEPTH, 128, 6])
    df_lam = din("df_lam", [DEPTH, 128])
    df_g = din("df_g", [DEPTH, 128, 1])
    c_ident = din("c_ident", [128, 128], BF16)
    c_cos = din("c_cos", [128, LL])
    c_sin = din("c_sin", [128, LL])
    c_mask = din("c_mask", [128, 128])
    c_bd64 = din("c_bd64", [128, 128], BF16)
    c_perm = din("c_perm", [128, 128], BF16)

    out = nc.dram_tensor("out", [LL, D], F32, kind="ExternalOutput").ap()
    xcs = nc.dram_tensor("xcs", [LC, D], F32).ap()
    ymix = nc.dram_tensor("ymix", [D, T], BF16, kind="ExternalOutput" if dbg else "Internal").ap()
    ggd = nc.dram_tensor("ggd", [DEPTH, 2, D], F32, kind="ExternalOutput" if dbg else "Internal").ap()

    uid = [0]

    def sb(name, shape, dt=F32, ctx=None):
        uid[0] += 1
        return (ctx or es).enter_context(nc.sbuf_tensor(f"{name}_{uid[0]}", list(shape), dt))

    PS = [es.enter_context(nc.psum_tensor(f"ps{i}", [128, 512], F32)) for i in range(8)]

    def pk(i):
        return ('ps', i)

    hT = sb("hT", [128, 8, T], BF16)
    identb = sb("identb", [128, 128], BF16)
    bd64 = sb("bd64", [128, 128], BF16)
    permb = sb("permb", [128, 128], BF16)
    ones64 = sb("ones64", [128, 64], BF16)
    ones256 = sb("ones256", [128, 128], BF16)
    maskt = sb("maskt", [128, 128], F32)
    GS = sb("GS", [128, DEPTH, 4, 8], F32)
    LBt = sb("LBt", [128, 3, 4, 4], F32)
    epsr = sb("epsr", [128, 1], F32)
    epsl = sb("epsl", [128, 1], F32)
    onec = sb("onec", [128, 1], F32)

    S.dma('sp', identb[:], c_ident[:, :], writes=['identb'])
    S.dma('sp', bd64[:], c_bd64[:, :], writes=['bd64'])
    S.dma('sp', permb[:], c_perm[:, :], writes=['permb'])
    S.dma('sp', maskt[:], c_mask[:, :], writes=['maskt'])
    S.op('dve', lambda e: e.memset(ones64[:], 1.0), writes=['ones64'])
    S.op('dve', lambda e: e.memset(ones256[:], 1.0 / 256.0), writes=['ones256'])
    S.op('dve', lambda e: e.memset(epsr[:], RMS_EPS), writes=['epsr'])
    S.op('dve', lambda e: e.memset(epsl[:], LN_EPS), writes=['epsl'])
    S.op('dve', lambda e: e.memset(onec[:], 1.0), writes=['onec'])
    for i_ in range(8):
        S.op('dve', lambda e, i_=i_: e.memset(PS[i_][:, :], 0.0), writes=[pk(i_)])

    def act(out_, in_, func, reads, writes, bias=None, scale=None):
        kw = {}
        if bias is not None:
            kw['bias'] = bias
        if scale is not None:
            kw['scale'] = scale
        S.op('act', lambda e: e.activation(out=out_, in_=in_, func=func, **kw), reads=reads, writes=writes)

    def rstd_from_ss(rs, ss, n_inv, eps_ap, key_rs, key_ss):
        act(rs, ss, AF.Ln, [key_ss], [key_rs], bias=eps_ap, scale=n_inv)
        act(rs, rs, AF.Exp, [key_rs], [key_rs], scale=-0.5)

    def prologue():
        with ExitStack() as cx:
            cf = sb("cf", [128, 16], F32, cx)
            sc = sb("sc", [128, 16], F32, cx)
            rep = sb("rep", [128, 2, 8, 128], F32, cx)
            wms = [sb(f"wm{i}", [128, 8, 512], F32, cx) for i in range(2)]
            bfm = sb("bfm", [128, 24], F32, cx)
            gpf = sb("gpf", [128, 8], F32, cx)
            bg = sb("bg", [128, 1024], F32, cx)
            gp = sb("gp", [128, 1024], F32, cx)
            tmpg = sb("tmpg", [128, 1024], F32, cx)
            tmp = sb("ptmp", [128, 16], F32, cx)
            hl = sb("hl", [128, 16], F32, cx)
            hs = sb("hs", [128, 4], F32, cx)
            S.dma('sp', cf[:], cfm[:, :], writes=['cf'])
            S.dma('sp', hl[:], hg_lb[:, :], writes=['hl'])
            act(sc[:], cf[:], AF.Silu, ['cf'], ['sc'])
            for j in range(2):
                src = sc[:].rearrange("p (k j) -> p k j", j=2)[:, :, j:j + 1].broadcast_to([128, 8, 128])
                S.op('dve', lambda e, j=j, src=src: e.tensor_copy(out=rep[:, j, :, :], in_=src), reads=['sc'], writes=['rep'])
            act(hl[:], hl[:], AF.Exp, ['hl'], ['hl'])
            hl3 = hl[:].rearrange("p (a l) -> p a l", l=4)
            S.op('dve', lambda e: e.reduce_sum(out=hs[:], in_=hl3, axis=AX.X), reads=['hl'], writes=['hs'])
            S.op('dve', lambda e: e.reciprocal(out=hs[:], in_=hs[:]), reads=['hs'], writes=['hs'])
            S.op('dve', lambda e: e.tensor_tensor(out=hl3, in0=hl3, in1=hs[:].unsqueeze(2).broadcast_to([128, 4, 4]), op=ALU.mult),
                 reads=['hl', 'hs'], writes=['hl'])
            S.op('dve', lambda e: e.memset(LBt[:, 0, :, 0:1], 0.0), writes=['LBt'])
            S.op('dve', lambda e: e.tensor_copy(out=LBt[:, 0, :, 1:2], in_=hl3[:, :, 1:2]), reads=['hl', 'LBt'], writes=['LBt'])
            for l in (2, 3):
                S.op('dve', lambda e, l=l: e.tensor_tensor(out=LBt[:, 0, :, l:l + 1], in0=LBt[:, 0, :, l - 1:l], in1=hl3[:, :, l:l + 1], op=ALU.add),
                     reads=['hl', 'LBt'], writes=['LBt'])
            S.op('dve', lambda e: e.tensor_scalar(out=LBt[:, 1, :, :], in0=LBt[:, 0, :, :], scalar1=-1.0, scalar2=1.0, op0=ALU.mult, op1=ALU.add),
                 reads=['LBt'], writes=['LBt'])
            S.op('dve', lambda e: e.tensor_scalar(out=LBt[:, 2, :, :], in0=LBt[:, 1, :, :], scalar1=-1.0, scalar2=None, op0=ALU.mult),
                 reads=['LBt'], writes=['LBt'])
            for l in range(nlayers):
                S.dma('sp', bfm[:], b_mod_fm[l], writes=['bfm'])
                S.dma('sp', gpf[:], g_pre_fm[l], writes=['gpf'])
                S.dma('sp', bg[:], b_mod[l:l + 1, 2048:3072].partition_broadcast(128), writes=['bg'])
                S.dma('sp', gp[:], g_post[l:l + 1, :].partition_broadcast(128), writes=['gp'])
                wv = w_mod[l].rearrange("(k p) n -> p k n", p=128)
                for sl in range(6):
                    wm = wms[sl % 2]
                    wk = f'wm{sl % 2}'
                    S.dma('sp', wm[:], wv[:, :, sl * 512:(sl + 1) * 512], writes=[wk])
                    if sl < 4:
                        for c in range(4):
                            cc = sl * 4 + c
                            for k in range(8):
                                S.op('pe', lambda e, k=k, c=c, cc=cc, wm=wm: e.matmul(PS[0][:, 2 * cc:2 * cc + 2], lhsT=wm[:, k, c * 128:(c + 1) * 128],
                                                                                   rhs=sc[:, 2 * k:2 * k + 2], start=(k == 0), stop=(k == 7)),
                                     reads=[wk, 'sc'], writes=[pk(0)], sig=(k == 7))
                    else:
                        half = sl - 4
                        for j in range(2):
                            for k in range(8):
                                S.op('pe', lambda e, k=k, j=j, half=half, wm=wm: e.matmul(PS[1 + 2 * j + half][:, :], lhsT=rep[:, j, k, :], rhs=wm[:, k, :],
                                                                                        start=(k == 0), stop=(k == 7)),
                                     reads=[wk, 'rep'], writes=[pk(1 + 2 * j + half)], sig=(k == 7))
                psv = PS[0][:, 0:32].rearrange("p (w k j) -> p w k j", w=2, k=8, j=2)
                for j in range(2):
                    S.op('dve', lambda e, j=j: e.tensor_tensor(out=GS[:, l, 1 + 2 * j, :], in0=psv[:, 0, :, j], in1=bfm[:, 0:8], op=ALU.add),
                         reads=[pk(0), 'bfm'], writes=['GS'])
                    S.op('dve', lambda e, j=j: e.tensor_tensor(out=tmp[:, 0:8], in0=psv[:, 1, :, j], in1=bfm[:, 8:16], op=ALU.add),
                         reads=[pk(0), 'bfm'], writes=['ptmp'])
                    S.op('dve', lambda e, j=j: e.scalar_tensor_tensor(out=GS[:, l, 2 * j, :], in0=tmp[:, 0:8], scalar=1.0, in1=gpf[:], op0=ALU.add, op1=ALU.mult),
                         reads=['ptmp', 'gpf'], writes=['GS'])
                for j in range(2):
                    for half in range(2):
                        S.op('dve', lambda e, j=j, half=half: e.tensor_tensor(out=tmpg[:, half * 512:(half + 1) * 512], in0=PS[1 + 2 * j + half][:, :],
                                                                            in1=bg[:, half * 512:(half + 1) * 512], op=ALU.add),
                             reads=[pk(1 + 2 * j + half), 'bg'], writes=['tmpg'])
                    S.op('dve', lambda e: e.tensor_tensor(out=tmpg[:], in0=tmpg[:], in1=gp[:], op=ALU.mult), reads=['tmpg', 'gp'], writes=['tmpg'])
                    S.dma('sp', ggd[l, j:j + 1, :], tmpg[0:1, :], reads=['tmpg'], writes=['ggd'])
        S.barrier()

    wslab = {}

    def wkeys(wkey, c0, n):
        if wkey not in wslab:
            return [wkey]
        sl = wslab[wkey]
        return [(wkey, j) for j in range(c0 // sl, (c0 + n - 1) // sl + 1)]

    def load_w(dst, l, src, col0, ncols, stgs, dkey, off=0, slab=256):
        wv = src[l].rearrange("(k p) n -> p k n", p=128)
        i = 0
        for c in range(0, ncols, slab):
            n = min(slab, ncols - c)
            st, sk = stgs[i % len(stgs)]
            i += 1
            S.dma('sp', st[:, :, 0:n], wv[:, :, col0 + c:col0 + c + n], writes=[sk])
            wslab[dkey] = slab
            dk_ = (dkey, (off + c) // slab)
            if i % 2 == 0:
                S.op('dve', lambda e, st=st, c=c, n=n: e.tensor_copy(out=dst[:, :, off + c:off + c + n], in_=st[:, :, 0:n]), reads=[sk], writes=[dk_])
            else:
                act(dst[:, :, off + c:off + c + n], st[:, :, 0:n], AF.Identity, [sk], [dk_])

    def proj_fm(b, W, wkey, wc0, t0, n, wn=128):
        for k in range(8):
            S.op('pe', lambda e, k=k: e.matmul(PS[b][0:wn, 0:n], lhsT=W[:, k, wc0:wc0 + wn], rhs=hT[:, k, t0:t0 + n], start=(k == 0), stop=(k == 7)),
                 reads=wkeys(wkey, wc0, wn) + ['hT'], writes=[pk(b)], sig=(k == 7))

    def proj_tm(ps_ap, b, W, wkey, wc0, ncols, tile):
        for k in range(8):
            S.op('pe', lambda e, k=k: e.matmul(ps_ap, lhsT=hT[:, k, tile * 128:(tile + 1) * 128], rhs=W[:, k, wc0:wc0 + ncols], start=(k == 0), stop=(k == 7)),
                 reads=wkeys(wkey, wc0, ncols) + ['hT'], writes=[pk(b)], sig=(k == 7))

    def res_src(l, i):
        if i < 2:
            base = c_in if l == 0 else xcs
            return base[i * 128:(i + 1) * 128, :]
        base = x_in if l == 0 else out
        return base[(i - 2) * 128:(i - 1) * 128, :]

    def res_dst(i):
        if i < 2:
            return xcs[i * 128:(i + 1) * 128, :]
        return out[(i - 2) * 128:(i - 1) * 128, :]

    def stageH(l):
        with ExitStack() as cx:
            xts = [sb(f"hx{i}", [128, 1024], F32, cx) for i in range(2)]
            sq = sb("hsq", [128, 1024], F32, cx)
            xss = [sb(f"hxs{i}", [128, 1024], BF16, cx) for i in range(2)]
            st = sb("hst", [128, 4], F32, cx)
            for i in range(NT):
                p = i % 2
                xt, xs = xts[p], xss[p]
                jj = 0 if i >= 2 else 2
                S.dma('sp', xt[:], res_src(l, i), reads=[('res', i)], writes=[f'hx{p}'])
                act(sq[:], xt[:], AF.Square, [f'hx{p}'], ['hsq'])
                S.op('dve', lambda e, p=p: e.reduce_sum(out=st[:, p:p + 1], in_=sq[:], axis=AX.X), reads=['hsq'], writes=[f'hss{p}'])
                rstd_from_ss(st[:, 2 + p:3 + p], st[:, p:p + 1], 1.0 / D, epsr[:], f'hrs{p}', f'hss{p}')
                S.op('dve', lambda e, p=p, xt=xt, xs=xs: e.tensor_scalar(out=xs[:], in0=xt[:], scalar1=st[:, 2 + p:3 + p], scalar2=None, op0=ALU.mult),
                     reads=[f'hx{p}', f'hrs{p}'], writes=[f'hxs{p}'])
                b = 6 + p
                psb = PS[b][:].bitcast(BF16)
                for k in range(8):
                    S.op('pe', lambda e, k=k, xs=xs, psb=psb: e.transpose(out=psb[:, k * 128:(k + 1) * 128], in_=xs[:, k * 128:(k + 1) * 128], identity=identb[:]),
                         reads=[f'hxs{p}', 'identb'], writes=[pk(b)], sig=(k == 7))
                for k in range(8):
                    S.op('dve', lambda e, k=k, psb=psb, i=i, jj=jj: e.tensor_scalar(out=hT[:, k, i * 128:(i + 1) * 128], in0=psb[:, k * 128:(k + 1) * 128],
                                                                             scalar1=GS[:, l, jj, k:k + 1], scalar2=GS[:, l, jj + 1, k:k + 1],
                                                                             op0=ALU.mult, op1=ALU.add),
                         reads=[pk(b), 'GS'], writes=['hT'])
        S.barrier()

    def stageO(l, last, fuse_next=False):
        with ExitStack() as cx:
            wo = sb("wo", [128, 8, 1024], BF16, cx)
            stgs = [(sb(f"ostg{i}", [128, 8, 256], F32, cx), f'ostg{i}') for i in range(2)]
            GG = [sb(f"GG{j}", [128, 1024], F32, cx) for j in range(2)]
            yms = [sb(f"oym{i}", [128, 8, 512], BF16, cx) for i in range(2)]
            xts = [sb(f"ox{i}", [128, 1024], F32, cx) for i in range(2)]
            sq = sb("osq", [128, 1024], F32, cx)
            tts = [sb(f"ot{i}", [128, 1024], F32, cx) for i in range(2)]
            st = sb("ost", [128, 8], F32, cx)
            if fuse_next:
                sq2 = sb("osq2", [128, 1024], F32, cx)
                xss = [sb(f"oxs{i}", [128, 1024], BF16, cx) for i in range(2)]
            load_w(wo, l, w_out, 0, 1024, stgs, 'wo')
            for j in range(2):
                S.dma('sp', GG[j][:], ggd[l, j:j + 1, :].partition_broadcast(128), reads=['ggd'], writes=[f'GG{j}'])
            ymv = ymix.rearrange("(k p) t -> p k t", p=128)
            hq = [None, None]
            blks = [(t0, n) for bi, (t0, n) in enumerate(BLK) if not (last and bi == 0)]
            otiles = [(bx, tt, (t0 // 128) + tt) for bx, (t0, n) in enumerate(blks) for tt in range(n // 128)]

            def load_ym(bx):
                t0, n = blks[bx]
                S.dma('sp', yms[bx % 2][:, :, 0:n], ymv[:, :, t0:t0 + n], reads=['ymix'], writes=[f'oym{bx % 2}'])

            def load_xt(ti):
                S.dma('sp', xts[ti % 2][:], res_src(l, otiles[ti][2]), reads=[('res', otiles[ti][2])], writes=[f'ox{ti % 2}'])

            load_ym(0)
            load_xt(0)
            for ti, (bx, tt, i) in enumerate(otiles):
                if True:
                    if tt == 0 and bx + 1 < len(blks):
                        load_ym(bx + 1)
                    if ti + 1 < len(otiles):
                        load_xt(ti + 1)
                    ym = yms[bx % 2]
                    yk = f'oym{bx % 2}'
                    j = 0 if i >= 2 else 1
                    p = ti % 2
                    xt, tq = xts[p], tts[p]
                    for half in range(2):
                        b = 2 * p + half
                        for k in range(8):
                            S.op('pe', lambda e, k=k, b=b, half=half, ym=ym, tt=tt: e.matmul(PS[b][:, :], lhsT=ym[:, k, tt * 128:(tt + 1) * 128],
                                                                                       rhs=wo[:, k, half * 512:(half + 1) * 512], start=(k == 0), stop=(k == 7)),
                                 reads=[yk] + wkeys('wo', half * 512, 512), writes=[pk(b)], sig=(k == 7))
                        act(sq[:, half * 512:(half + 1) * 512], PS[b][:, :], AF.Square, [pk(b)], ['osq'])
                    S.op('dve', lambda e, p=p: e.reduce_sum(out=st[:, p:p + 1], in_=sq[:], axis=AX.X), reads=['osq'], writes=[f'oss{p}'])
                    rstd_from_ss(st[:, 2 + p:3 + p], st[:, p:p + 1], 1.0 / D, epsr[:], f'ors{p}', f'oss{p}')
                    for half in range(2):
                        b = 2 * p + half
                        S.op('dve', lambda e, b=b, half=half, tq=tq, p=p, j=j: e.scalar_tensor_tensor(out=tq[:, half * 512:(half + 1) * 512], in0=PS[b][:, :],
                                                                                                scalar=st[:, 2 + p:3 + p], in1=GG[j][:, half * 512:(half + 1) * 512],
                                                                                                op0=ALU.mult, op1=ALU.mult),
                             reads=[pk(b), f'ors{p}', f'GG{j}'], writes=[f'ot{p}'])
                    S.op('pool', lambda e, tq=tq, xt=xt: e.tensor_tensor(out=tq[:], in0=tq[:], in1=xt[:], op=ALU.add), reads=[f'ot{p}', f'ox{p}'], writes=[f'ot{p}'])
                    S.dma('sp', res_dst(i), tq[:], reads=[f'ot{p}'], writes=[('res', i)])
                    if fuse_next:
                        def hpart(i=i, p=p, tq=tq):
                            ln = l + 1
                            jj = 0 if i >= 2 else 2
                            xs = xss[p]
                            act(sq2[:], tq[:], AF.Square, [f'ot{p}'], ['osq2'])
                            S.op('dve', lambda e, p=p: e.reduce_sum(out=st[:, 4 + p:5 + p], in_=sq2[:], axis=AX.X), reads=['osq2'], writes=[f'oss2{p}'])
                            rstd_from_ss(st[:, 6 + p:7 + p], st[:, 4 + p:5 + p], 1.0 / D, epsr[:], f'ors2{p}', f'oss2{p}')
                            act(xs[:], tq[:], AF.Identity, [f'ot{p}', f'ors2{p}'], [f'oxs{p}'], scale=st[:, 6 + p:7 + p])
                        def hpartB(i=i, p=p):
                            ln = l + 1
                            jj = 0 if i >= 2 else 2
                            xs = xss[p]
                            for k in range(8):
                                b = (4 + p) if k < 4 else (6 + p)
                                psb = PS[b][:].bitcast(BF16)
                                S.op('pe', lambda e, k=k, xs=xs, psb=psb: e.transpose(out=psb[:, (k % 4) * 128:(k % 4 + 1) * 128], in_=xs[:, k * 128:(k + 1) * 128], identity=identb[:]),
                                     reads=[f'oxs{p}', 'identb'], writes=[pk(b)], sig=(k % 4 == 3))
                            for k in range(8):
                                b = (4 + p) if k < 4 else (6 + p)
                                psb = PS[b][:].bitcast(BF16)
                                if k < 4:
                                    act(hT[:, k, i * 128:(i + 1) * 128], psb[:, (k % 4) * 128:(k % 4 + 1) * 128], AF.Identity, [pk(b), 'GS'], [('hTa', k)],
                                        bias=GS[:, ln, jj + 1, k:k + 1], scale=GS[:, ln, jj, k:k + 1])
                                else:
                                    S.op('dve', lambda e, k=k, psb=psb, i=i, jj=jj, ln=ln: e.tensor_scalar(out=hT[:, k, i * 128:(i + 1) * 128], in0=psb[:, (k % 4) * 128:(k % 4 + 1) * 128],
                                                                                                scalar1=GS[:, ln, jj, k:k + 1], scalar2=GS[:, ln, jj + 1, k:k + 1],
                                                                                                op0=ALU.mult, op1=ALU.add),
                                         reads=[pk(b), 'GS'], writes=[('hTd', k)])
                        if hq[1] is not None:
                            hq[1]()
                            hq[1] = None
                        if hq[0] is not None:
                            hq[0][0]()
                            hq[1] = hq[0][1]
                        hq[0] = (hpart, hpartB)
            if fuse_next:
                if hq[1] is not None:
                    hq[1]()
                if hq[0] is not None:
                    hq[0][0]()
                    hq[0][1]()
        S.barrier()

    NG = T + 3

    def stageA(l):
        with ExitStack() as cx:
            W = sb("aW", [128, 8, 512], BF16, cx)
            stgs = [(sb(f"astg{i}", [128, 8, 256], F32, cx), f'astg{i}') for i in range(2)]
            bdst = sb("abdst", [128, 4, 128], F32, cx)
            bd = sb("abd", [128, 4, 128], BF16, cx)
            cw = sb("acw", [128, 8], F32, cx)
            cb = sb("acb", [128, 2], F32, cx)
            lbias = sb("alb", [128, 8], F32, cx)
            lam = sb("alam", [128, 4], F32, cx)
            B = [sb(f"aB{i}", [128, NG + 3], F32, cx) for i in range(5)]
            XCb = sb("aXCb", [128, NG], BF16, cx)
            SGt = sb("aSG", [128, T], BF16, cx)
            Yb = XCb
            load_w(W, l, w_in, 0, 512, stgs, 'aW')
            S.dma('sp', cw[:], lru_cw[l], writes=['acw'])
            S.dma('sp', cb[:], lru_cb[l], writes=['acb'])
            S.dma('sp', lbias[:], lru_b[l], writes=['alb'])
            S.dma('sp', lam[:], lru_lam[l], writes=['alam'])
            act(lam[:], lam[:], AF.Exp, ['alam'], ['alam'], scale=-1.0)
            act(lam[:], lam[:], AF.Ln, ['alam'], ['alam'], bias=onec[:], scale=1.0)
            S.op('dve', lambda e: e.tensor_scalar(out=lam[:], in0=lam[:], scalar1=-8.0, scalar2=None, op0=ALU.mult), reads=['alam'], writes=['alam'])
            for pc in range(2):
                UX, XC = B[0], B[4]
                for (a, b_) in ((0, 2), (258, 261), (NG + 2, NG + 3)):
                    S.op('dve', lambda e, a=a, b_=b_: e.memset(UX[:, a:b_], 0.0), writes=['aB0'])
                for bi, (t0, n) in enumerate(BLK):
                    ux0 = 2 + t0 if t0 < 256 else 261 + (t0 - 256)
                    b = bi % 2
                    proj_fm(b, W, 'aW', pc * 128, t0, n)
                    S.op('dve', lambda e, b=b, ux0=ux0, n=n: e.tensor_copy(out=UX[:, ux0:ux0 + n], in_=PS[b][:, 0:n]), reads=[pk(b)], writes=['aB0'])
                    b2 = 2 + bi % 2
                    proj_fm(b2, W, 'aW', 256 + pc * 128, t0, n)
                    act(SGt[:, t0:t0 + n], PS[b2][:, 0:n], AF.Silu, [pk(b2)], ['aSG'])
                S.op('dve', lambda e: e.tensor_scalar(out=XC[:, 0:NG], in0=UX[:, 0:NG], scalar1=cw[:, pc * 4:pc * 4 + 1], scalar2=cb[:, pc:pc + 1],
                                                      op0=ALU.mult, op1=ALU.add), reads=['aB0', 'acw', 'acb'], writes=['aB4'])
                for k in range(1, 4):
                    S.op('dve', lambda e, k=k: e.scalar_tensor_tensor(out=XC[:, 0:NG], in0=UX[:, k:k + NG], scalar=cw[:, pc * 4 + k:pc * 4 + k + 1], in1=XC[:, 0:NG],
                                                                      op0=ALU.mult, op1=ALU.add), reads=['aB0', 'aB4', 'acw'], writes=['aB4'])
                S.op('pool', lambda e: e.tensor_copy(out=XCb[:], in_=XC[:, 0:NG]), reads=['aB4'], writes=['aXCb'])
                S.dma('sp', bdst[:], lru_bd[l, pc].rearrange("w a b -> a w b"), writes=['abdst'])
                S.op('pool', lambda e: e.tensor_copy(out=bd[:], in_=bdst[:]), reads=['abdst'], writes=['abd'])
                S.op('dve', lambda e: e.memset(B[3][:, 256:259], 0.0), writes=['aB3'])
                for dr in range(2):
                    R, I, A = B[0], B[1], B[2]
                    for gi, g0 in enumerate(range(0, NG, 512)):
                        n = min(512, NG - g0)
                        for wh, (dst, dk) in enumerate(((R, 'aB0'), (I, 'aB1'))):
                            b = 2 * (gi % 2) + wh
                            S.op('pe', lambda e, b=b, wh=wh, g0=g0, n=n: e.matmul(PS[b][:, 0:n], lhsT=bd[:, 2 * dr + wh, :], rhs=XCb[:, g0:g0 + n], start=True, stop=True),
                                 reads=['abd', 'aXCb'], writes=[pk(b)])
                            bi_ = wh * 4 + dr * 2 + pc
                            act(dst[:, g0:g0 + n], PS[b][:, 0:n], AF.Sigmoid, [pk(b), 'alb'], [dk], bias=lbias[:, bi_:bi_ + 1], scale=1.0)
                    ci = dr * 2 + pc
                    act(A[:, 0:NG], R[:, 0:NG], AF.Exp, ['aB0', 'alam'], ['aB2'], scale=lam[:, ci:ci + 1])
                    act(R[:, 0:NG], A[:, 0:NG], AF.Square, ['aB2'], ['aB0'])
                    act(R[:, 0:NG], R[:, 0:NG], AF.Ln, ['aB0'], ['aB0'], bias=onec[:], scale=-1.0)
                    act(R[:, 0:NG], R[:, 0:NG], AF.Exp, ['aB0'], ['aB0'], scale=0.5)
                    S.op('dve', lambda e: e.tensor_tensor(out=I[:, 0:NG], in0=I[:, 0:NG], in1=XC[:, 0:NG], op=ALU.mult), reads=['aB1', 'aB4'], writes=['aB1'])
                    S.op('dve', lambda e: e.tensor_tensor(out=I[:, 0:NG], in0=I[:, 0:NG], in1=R[:, 0:NG], op=ALU.mult), reads=['aB1', 'aB0'], writes=['aB1'])
                    H, hk = (B[3], 'aB3') if dr == 0 else (B[0], 'aB0')
                    if dr == 0:
                        S.op('dve', lambda e, H=H: e.tensor_tensor_scan(out=H[:, 0:256], data0=A[:, 0:256], data1=I[:, 0:256], initial=0.0, op0=ALU.mult, op1=ALU.add),
                             reads=['aB2', 'aB1'], writes=[hk])
                        S.op('dve', lambda e, H=H: e.tensor_tensor_scan(out=H[:, 259:NG], data0=A[:, 259:NG], data1=I[:, 259:NG], initial=H[:, 255:256],
                                                                         op0=ALU.mult, op1=ALU.add), reads=['aB2', 'aB1', hk], writes=[hk])
                    else:
                        rv = lambda X, a, b_: X[:, a:b_][:, ::-1]
                        S.op('dve', lambda e, H=H: e.tensor_tensor_scan(out=rv(H, 0, 256), data0=rv(A, 0, 256), data1=rv(I, 0, 256), initial=0.0, op0=ALU.mult, op1=ALU.add),
                             reads=['aB2', 'aB1'], writes=[hk])
                        S.op('dve', lambda e, H=H: e.tensor_tensor_scan(out=rv(H, 259, NG), data0=rv(A, 259, NG), data1=rv(I, 259, NG), initial=H[:, 0:1],
                                                                         op0=ALU.mult, op1=ALU.add), reads=['aB2', 'aB1', hk], writes=[hk])
                        S.op('dve', lambda e: e.tensor_tensor(out=B[3][:, 0:NG], in0=B[3][:, 0:NG], in1=B[0][:, 0:NG], op=ALU.add), reads=['aB3', 'aB0'], writes=['aB3'])
                S.op('dve', lambda e: e.tensor_tensor(out=Yb[:, 0:256], in0=B[3][:, 0:256], in1=SGt[:, 0:256], op=ALU.mult), reads=['aB3', 'aSG'], writes=['aXCb'])
                S.op('dve', lambda e: e.tensor_tensor(out=Yb[:, 256:T], in0=B[3][:, 259:NG], in1=SGt[:, 256:T], op=ALU.mult), reads=['aB3', 'aSG'], writes=['aXCb'])
                S.dma('sp', ymix[pc * 128:(pc + 1) * 128, :], Yb[:, 0:T], reads=['aXCb'], writes=['ymix'])
        S.barrier()

    NP = T + 60

    def stageC(l, last):
        with ExitStack() as cx:
            W = sb("cW", [128, 8, 768], BF16, cx)
            stgs = [(sb(f"cstg{i}", [128, 8, 256], F32, cx), f'cstg{i}') for i in range(2)]
            Yp = [sb(f"cYp{i}", [128, NP], BF16, cx) for i in range(2)]
            SG = sb("cSG", [128, 2, T], BF16, cx)
            Dg = sb("cDg", [128, 2, 31, 128], BF16, cx)
            cwt = sb("ccw", [128, 62], F32, cx)
            cbt = sb("ccb", [128, 6], F32, cx)
            sgm = [sb(f"csgm{i}", [128, 512], F32, cx) for i in range(2)]
            Cf = sb("cCf", [128, 2, 512], F32, cx)
            Cb = sb("cCb", [128, 2, 512], BF16, cx)
            Cq = sb("cCq", [128, 2, 512], BF16, cx)
            mean = sb("cmean", [128, 512], F32, cx)
            var = sb("cvar", [128, 512], F32, cx)
            dd = [sb(f"cdd{i}", [128, 512], F32, cx) for i in range(2)]
            yo = [sb(f"cyo{i}", [128, 512], BF16, cx) for i in range(2)]
            load_w(W, l, w_in, 1792, 768, stgs, 'cW')
            S.dma('sp', cwt[:], cf_w[l], writes=['ccw'])
            S.dma('sp', cbt[:], cf_b[l], writes=['ccb'])
            for pc in range(2):
                i0 = identb[:].unsqueeze(1).broadcast_to([128, 31, 128])
                i1 = cwt[:, pc * 31:(pc + 1) * 31].unsqueeze(2).broadcast_to([128, 31, 128])
                S.op('dve', lambda e, pc=pc, i0=i0, i1=i1: e.tensor_tensor(out=Dg[:, pc, :, :], in0=i0, in1=i1, op=ALU.mult), reads=['identb', 'ccw'], writes=['cDg'])
                S.op('pool', lambda e, pc=pc: e.memset(Yp[pc][:], 0.0), writes=[f'cYp{pc}'])
            for bi, (t0, n) in enumerate(BLK):
                p0 = 15 + t0 if t0 < 256 else 301 + (t0 - 256)
                for pc in range(2):
                    q = (bi * 2 + pc) % 2
                    proj_fm(0 + q, W, 'cW', pc * 128, t0, n)
                    proj_fm(2 + q, W, 'cW', 256 + pc * 128, t0, n)
                    act(sgm[q][:, 0:n], PS[2 + q][:, 0:n], AF.Sigmoid, [pk(2 + q)], [f'csgm{q}'])
                    S.op('dve', lambda e, pc=pc, q=q, p0=p0, n=n: e.tensor_tensor(out=Yp[pc][:, p0:p0 + n], in0=PS[q][:, 0:n], in1=sgm[q][:, 0:n], op=ALU.mult),
                         reads=[pk(q), f'csgm{q}'], writes=[f'cYp{pc}'])
                    proj_fm(4 + q, W, 'cW', 512 + pc * 128, t0, n)
                    act(SG[:, pc, t0:t0 + n], PS[4 + q][:, 0:n], AF.Silu, [pk(4 + q)], ['cSG'])
            for bi, (t0, n) in enumerate(BLK):
                if last and bi == 0:
                    continue
                p0 = 15 + t0 if t0 < 256 else 301 + (t0 - 256)
                for pc in range(2):
                    b = pc
                    for k in range(31):
                        S.op('pe', lambda e, k=k, pc=pc, b=b: e.matmul(PS[b][:, 0:n], lhsT=Dg[:, pc, k, :], rhs=Yp[pc][:, p0 + k - 15:p0 + k - 15 + n], start=(k == 0), stop=(k == 30)),
                             reads=['cDg', f'cYp{pc}'], writes=[pk(b)], sig=(k == 30))
                    S.op('dve', lambda e, pc=pc, b=b: e.tensor_scalar(out=Cf[:, pc, 0:n], in0=PS[b][:, 0:n], scalar1=cbt[:, pc:pc + 1], scalar2=None, op0=ALU.add),
                         reads=[pk(b), 'ccb'], writes=['cCf'])
                    S.op('pool', lambda e, pc=pc: e.tensor_copy(out=Cb[:, pc, 0:n], in_=Cf[:, pc, 0:n]), reads=['cCf'], writes=['cCb'])
                    S.op('pool', lambda e, pc=pc: e.tensor_tensor(out=Cq[:, pc, 0:n], in0=Cf[:, pc, 0:n], in1=Cf[:, pc, 0:n], op=ALU.mult), reads=['cCf'], writes=['cCq'])
                for pc in range(2):
                    S.op('pe', lambda e, pc=pc: e.matmul(PS[2][:, 0:n], lhsT=ones256[:], rhs=Cb[:, pc, 0:n], start=(pc == 0), stop=(pc == 1)),
                         reads=['ones256', 'cCb'], writes=[pk(2)], sig=(pc == 1))
                for pc in range(2):
                    S.op('pe', lambda e, pc=pc: e.matmul(PS[3][:, 0:n], lhsT=ones256[:], rhs=Cq[:, pc, 0:n], start=(pc == 0), stop=(pc == 1)),
                         reads=['ones256', 'cCq'], writes=[pk(3)], sig=(pc == 1))
                S.op('dve', lambda e: e.tensor_copy(out=mean[:, 0:n], in_=PS[2][:, 0:n]), reads=[pk(2)], writes=['cmean'])
                S.op('dve', lambda e: e.tensor_tensor(out=var[:, 0:n], in0=mean[:, 0:n], in1=mean[:, 0:n], op=ALU.mult), reads=['cmean'], writes=['cvar'])
                S.op('dve', lambda e: e.tensor_tensor(out=var[:, 0:n], in0=PS[3][:, 0:n], in1=var[:, 0:n], op=ALU.subtract), reads=[pk(3), 'cvar'], writes=['cvar'])
                act(var[:, 0:n], var[:, 0:n], AF.Ln, ['cvar'], ['cvar'], bias=epsl[:], scale=1.0)
                act(var[:, 0:n], var[:, 0:n], AF.Exp, ['cvar'], ['cvar'], scale=-0.5)
                for pc in range(2):
                    d_, y_ = dd[pc], yo[pc]
                    S.op('dve', lambda e, pc=pc, d_=d_: e.tensor_tensor(out=d_[:, 0:n], in0=Cf[:, pc, 0:n], in1=mean[:, 0:n], op=ALU.subtract), reads=['cCf', 'cmean'], writes=[f'cdd{pc}'])
                    S.op('dve', lambda e, pc=pc, d_=d_: e.tensor_tensor(out=d_[:, 0:n], in0=d_[:, 0:n], in1=var[:, 0:n], op=ALU.mult), reads=[f'cdd{pc}', 'cvar'], writes=[f'cdd{pc}'])
                    act(d_[:, 0:n], d_[:, 0:n], AF.Silu, [f'cdd{pc}', 'ccb'], [f'cdd{pc}'], bias=cbt[:, 4 + pc:5 + pc], scale=cbt[:, 2 + pc:3 + pc])
                    S.op('dve', lambda e, pc=pc, d_=d_, y_=y_: e.tensor_tensor(out=y_[:, 0:n], in0=d_[:, 0:n], in1=SG[:, pc, t0:t0 + n], op=ALU.mult),
                         reads=[f'cdd{pc}', 'cSG'], writes=[f'cyo{pc}'])
                    S.dma('sp', ymix[512 + pc * 128:512 + (pc + 1) * 128, t0:t0 + n], y_[:, 0:n], reads=[f'cyo{pc}'], writes=['ymix'])
        S.barrier()

    def stageD(l, last):
        lam_init = 0.8 - 0.6 * math.exp(-0.3 * l)
        scale = 32 ** -0.5
        with ExitStack() as cx:
            KT = sb("dKT", [128, 2, T], BF16, cx)
            QT = sb("dQT", [128, 2, T], BF16, cx)
            V = sb("dV", [128, NT, 384], BF16, cx)
            SG = sb("dSG", [128, 2, T], BF16, cx)
            lmt = sb("dlm", [128, 128], F32, cx)
            lms = sb("dls", [128, 4], F32, cx)
            gn = sb("dgn", [128, 1], F32, cx)
            S.dma('sp', lmt[:], df_lam[l:l + 1, :].partition_broadcast(128), writes=['dlm'])
            S.dma('sp', gn[:], df_g[l], writes=['dgn'])
            lm4 = lmt[:].rearrange("p (a d) -> p a d", d=32)
            S.op('dve', lambda e: e.tensor_tensor(out=lm4[:, 0, :], in0=lm4[:, 0, :], in1=lm4[:, 1, :], op=ALU.mult), reads=['dlm'], writes=['dlm'])
            S.op('dve', lambda e: e.tensor_tensor(out=lm4[:, 2, :], in0=lm4[:, 2, :], in1=lm4[:, 3, :], op=ALU.mult), reads=['dlm'], writes=['dlm'])
            S.op('dve', lambda e: e.reduce_sum(out=lms[:, 0:1], in_=lm4[:, 0, :], axis=AX.X), reads=['dlm'], writes=['dls'])
            S.op('dve', lambda e: e.reduce_sum(out=lms[:, 1:2], in_=lm4[:, 2, :], axis=AX.X), reads=['dlm'], writes=['dls'])
            act(lms[:, 0:2], lms[:, 0:2], AF.Exp, ['dls'], ['dls'])
            S.op('dve', lambda e: e.tensor_tensor(out=lms[:, 2:3], in0=lms[:, 1:2], in1=lms[:, 0:1], op=ALU.subtract), reads=['dls'], writes=['dls'])
            S.op('dve', lambda e: e.tensor_scalar(out=lms[:, 2:3], in0=lms[:, 2:3], scalar1=-lam_init, scalar2=None, op0=ALU.add), reads=['dls'], writes=['dls'])
            S.op('dve', lambda e: e.tensor_scalar(out=gn[:], in0=gn[:], scalar1=(1.0 - lam_init), scalar2=None, op0=ALU.mult), reads=['dgn'], writes=['dgn'])
            with ExitStack() as c1:
                W = sb("dW", [128, 8, 1024], BF16, c1)
                qb = [sb(f"dqb{i}", [128, 512], BF16, c1) for i in range(2)]
                stgs = [(sb(f"dstg{i}", [128, 8, 128], F32, c1), f'dstg{i}') for i in range(2)]
                cs = [sb(f"dcs{i}", [128, 512], F32, c1) for i in range(2)]
                sn = [sb(f"dsn{i}", [128, 512], F32, c1) for i in range(2)]
                t1 = [sb(f"dt1{i}", [128, 512], F32, c1) for i in range(2)]
                t2 = [sb(f"dt2{i}", [128, 512], F32, c1) for i in range(2)]
                S.op('pool', lambda e: e.memset(V[:], 1.0), writes=['dV'])
                load_w(W, l, w_in, 2560, 1024, stgs, 'dW', slab=128)
                ci = 0
                for bi, (t0, n) in enumerate(BLK):
                    lat = t0 >= 256
                    cp = bi % 2
                    if lat:
                        S.dma('sp', cs[cp][:, 0:n], c_cos[:, t0 - 256:t0 - 256 + n], writes=[f'dcs{cp}'])
                        S.dma('sp', sn[cp][:, 0:n], c_sin[:, t0 - 256:t0 - 256 + n], writes=[f'dsn{cp}'])
                    for which, (dst, dk) in enumerate(((QT, 'dQT'), (KT, 'dKT'))):
                        for ck in range(2):
                            q = ci % 2
                            ci += 1
                            proj_fm(q, W, 'dW', which * 256 + ck * 128, t0, n)
                            if not lat:
                                S.op('dve', lambda e, dst=dst, ck=ck, q=q: e.tensor_copy(out=dst[:, ck, t0:t0 + n], in_=PS[q][:, 0:n]), reads=[pk(q)], writes=[dk])
                            else:
                                S.op('dve', lambda e, q=q: e.tensor_copy(out=qb[q][:, 0:n], in_=PS[q][:, 0:n]), reads=[pk(q)], writes=[f'dqb{q}'])
                                S.op('pe', lambda e, q=q: e.matmul(PS[2 + q][:, 0:n], lhsT=permb[:], rhs=qb[q][:, 0:n], start=True, stop=True),
                                     reads=['permb', f'dqb{q}'], writes=[pk(2 + q)])
                                S.op('dve', lambda e, q=q: e.tensor_tensor(out=t1[q][:, 0:n], in0=PS[q][:, 0:n], in1=cs[cp][:, 0:n], op=ALU.mult),
                                     reads=[pk(q), f'dcs{cp}'], writes=[f'dt1{q}'])
                                S.op('dve', lambda e, q=q: e.tensor_tensor(out=t2[q][:, 0:n], in0=PS[2 + q][:, 0:n], in1=sn[cp][:, 0:n], op=ALU.mult),
                                     reads=[pk(2 + q), f'dsn{cp}'], writes=[f'dt2{q}'])
                                S.op('pool', lambda e, dst=dst, ck=ck, q=q: e.tensor_tensor(out=dst[:, ck, t0:t0 + n], in0=t1[q][:, 0:n], in1=t2[q][:, 0:n], op=ALU.add),
                                     reads=[f'dt1{q}', f'dt2{q}'], writes=[dk])
                    for ck in range(2):
                        b = 4 + ck
                        proj_fm(b, W, 'dW', 768 + ck * 128, t0, n)
                        act(SG[:, ck, t0:t0 + n], PS[b][:, 0:n], AF.Silu, [pk(b)], ['dSG'])
                    for tt in range(n // 128):
                        i = t0 // 128 + tt
                        b = 6 + i % 2
                        proj_tm(PS[b][:, 0:256], b, W, 'dW', 512, 256, i)
                        vv = V[:, i, :].rearrange("p (g c) -> p g c", c=192)
                        pv4 = PS[b][:, 0:256].rearrange("p (g r c) -> p g r c", r=2, c=64)
                        for r_ in range(2):
                            S.op('dve', lambda e, vv=vv, pv4=pv4, r_=r_: e.tensor_copy(out=vv[:, :, 128 * r_:128 * r_ + 64], in_=pv4[:, :, r_, :]), reads=[pk(b)], writes=['dV'])
            S.barrier()
            with ExitStack() as c2:
                Pb = [sb(f"dP{i}", [128, 512], BF16, c2) for i in range(6)]
                Qz = [sb(f"dQz{i}", [128, 4, 512], BF16, c2) for i in range(2)]
                RL = [sb(f"dRL{i}", [128, 512], F32, c2) for i in range(2)]
                Nn = [sb(f"dN{i}", [128, 512], F32, c2) for i in range(2)]
                Oc = [sb(f"dOc{i}", [128, 512], F32, c2) for i in range(4)]
                Oh = sb("dOh", [128, 512], F32, c2)
                Osq = sb("dOsq", [128, 512], BF16, c2)
                rs = sb("drs", [128, 512], F32, c2)
                Yo = [sb(f"dYo{i}", [128, 512], BF16, c2) for i in range(2)]
                pending = []
                state = dict(n=0)
                for i_ in range(2):
                    S.op('pool', lambda e, i_=i_: e.memset(Qz[i_][:], 0.0), writes=[f'dQz{i_}'])

                def prep_q(pi, q0, nq, hp):
                    pp = pi % 2
                    for s_ in range(4):
                        S.op('pool', lambda e, s_=s_: e.tensor_copy(out=Qz[pp][32 * s_:32 * s_ + 32, s_, 0:nq], in_=QT[32 * s_:32 * s_ + 32, hp, q0:q0 + nq]),
                             reads=['dQT'], writes=[f'dQz{pp}'])

                def finalize(q0, nq, hp, yp):
                    steps = []
                    for s_ in range(4):
                        steps.append(lambda s_=s_: S.op('dve', lambda e: e.tensor_copy(out=Oc[s_][:, 0:nq], in_=PS[3 + s_][:, 0:nq]), reads=[pk(3 + s_)], writes=[f'dOc{s_}']))
                    for s_ in range(4):
                        hh, w = s_ // 2, s_ % 2
                        lo, ll = 64 * hh, 64 * (1 - hh)
                        steps.append(lambda s_=s_, w=w, lo=lo, ll=ll: S.op('dve', lambda e: e.reciprocal(out=RL[w][lo:lo + 64, 0:nq], in_=Oc[s_][ll:ll + 64, 0:nq]),
                                                                     reads=[f'dOc{s_}'], writes=[f'dRL{w}']))
                        steps.append(lambda s_=s_, w=w, lo=lo: S.op('dve', lambda e: e.tensor_tensor(out=Nn[w][lo:lo + 64, 0:nq], in0=Oc[s_][lo:lo + 64, 0:nq], in1=RL[w][lo:lo + 64, 0:nq], op=ALU.mult),
                                                              reads=[f'dOc{s_}', f'dRL{w}'], writes=[f'dN{w}']))
                    steps.append(lambda: S.op('dve', lambda e: e.scalar_tensor_tensor(out=Oh[:, 0:nq], in0=Nn[1][:, 0:nq], scalar=lms[:, 2:3], in1=Nn[0][:, 0:nq], op0=ALU.mult, op1=ALU.add),
                                              reads=['dN0', 'dN1', 'dls'], writes=['dOh']))
                    steps.append(lambda: S.op('pool', lambda e: e.tensor_tensor(out=Osq[:, 0:nq], in0=Oh[:, 0:nq], in1=Oh[:, 0:nq], op=ALU.mult), reads=['dOh'], writes=['dOsq']))
                    steps.append(lambda: S.op('pe', lambda e: e.matmul(PS[7][:, 0:nq], lhsT=bd64[:], rhs=Osq[:, 0:nq], start=True, stop=True), reads=['bd64', 'dOsq'], writes=[pk(7)]))
                    steps.append(lambda: rstd_from_ss(rs[:, 0:nq], PS[7][:, 0:nq], 1.0 / 64, epsr[:], 'drs', pk(7)))
                    steps.append(lambda: S.op('dve', lambda e: e.tensor_tensor(out=Oh[:, 0:nq], in0=Oh[:, 0:nq], in1=rs[:, 0:nq], op=ALU.mult), reads=['dOh', 'drs'], writes=['dOh']))
                    steps.append(lambda: S.op('dve', lambda e: e.scalar_tensor_tensor(out=Yo[yp][:, 0:nq], in0=Oh[:, 0:nq], scalar=gn[:, 0:1], in1=SG[:, hp, q0:q0 + nq], op0=ALU.mult, op1=ALU.mult),
                                              reads=['dOh', 'dgn', 'dSG'], writes=[f'dYo{yp}']))
                    steps.append(lambda: S.dma('sp', ymix[768 + hp * 128:768 + (hp + 1) * 128, q0:q0 + nq], Yo[yp][:, 0:nq], reads=[f'dYo{yp}'], writes=['ymix']))
                    return steps

                passes = []
                if not last:
                    for hp in range(2):
                        passes.append((0, 256, [0, 1], hp))
                for qb in range(8):
                    for hp in range(2):
                        passes.append((256 + qb * 512, 512, list(range(NT)), hp))

                def attn_pass(pi):
                    q0, nq, kts, hp = passes[pi]
                    pp = pi % 2
                    seq = [(kt, s_) for kt in kts for s_ in range(4)]
                    nseq = len(seq)
                    base = state['n']

                    def qk(m):
                        kt, s_ = seq[m]
                        b_ = (base + m) % 3
                        S.op('pe', lambda e: e.matmul(PS[b_][:, 0:nq], lhsT=KT[:, hp, kt * 128:(kt + 1) * 128], rhs=Qz[pp][:, s_, 0:nq], start=True, stop=True),
                             reads=['dKT', f'dQz{pp}'], writes=[pk(b_)])

                    def ex(m):
                        g = base + m
                        act(Pb[g % 6][:, 0:nq], PS[g % 3][:, 0:nq], AF.Exp, [pk(g % 3)], [f'dP{g % 6}'], scale=scale)

                    def pv(m):
                        kt, s_ = seq[m]
                        pb = (base + m) % 6
                        hh = s_ // 2
                        c0 = hp * 192 + hh * 64
                        S.op('pe', lambda e: e.matmul(PS[3 + s_][:, 0:nq], lhsT=V[:, kt, c0:c0 + 128], rhs=Pb[pb][:, 0:nq], start=(kt == kts[0]), stop=(kt == kts[-1])),
                             reads=['dV', f'dP{pb}'], writes=[pk(3 + s_)])

                    for m in range(min(3, nseq)):
                        qk(m)
                    if pi + 1 < len(passes):
                        prep_q(pi + 1, passes[pi + 1][0], passes[pi + 1][1], passes[pi + 1][3])
                    for m in range(nseq):
                        ex(m)
                        pv(m)
                        if m + 3 < nseq:
                            qk(m + 3)
                        if pending and m % 2 == 1:
                            pending.pop(0)()
                    state['n'] = base + nseq
                    while pending:
                        pending.pop(0)()
                    fs = finalize(q0, nq, hp, pi % 2)
                    for f_ in fs[:4]:
                        f_()
                    pending.extend(fs[4:])

                prep_q(0, passes[0][0], passes[0][1], passes[0][3])
                for pi in range(len(passes)):
                    attn_pass(pi)
                while pending:
                    pending.pop(0)()
        S.barrier()

    def stageB(l):
        with ExitStack() as cx:
            W = sb("bW", [128, 8, 640], BF16, cx)
            stgs = [(sb(f"bstg{i}", [128, 8, 128], F32, cx), f'bstg{i}') for i in range(2)]
            Vt = sb("bV", [128, NT, 128], BF16, cx)
            OT = sb("bOT", [128, T], F32, cx)
            QTl = sb("bQT", [128, T], BF16, cx)
            KTl = sb("bKT", [128, T], BF16, cx)
            Kt = sb("bKt", [128, NT, 128], BF16, cx)
            Sb = sb("bSb", [128, NCH, 64], BF16, cx)
            KH = Sb[:].rearrange("p c e -> p (c e)")
            M0 = sb("bM0", [128, T], BF16, cx)
            Gc = sb("bGc", [128, T], F32, cx)
            Bt = Gc[:].rearrange("p (c e) -> p c e", e=64)
            T1 = [sb(f"bT1{i}", [128, 512], F32, cx) for i in range(2)]
            T2 = [sb(f"bT2{i}", [128, 512], F32, cx) for i in range(2)]
            sm = sb("bsm", [128, 6, NCH], F32, cx)
            SCm = [sb(f"bSC{i}", [128, 256], BF16, cx) for i in range(4)]
            osq = sb("bosq", [128, 512], BF16, cx)
            ors = sb("bors", [128, 512], F32, cx)
            oy = [sb(f"boy{i}", [128, 512], BF16, cx) for i in range(2)]
            hgn = sb("bhgn", [128, 1], F32, cx)
            S.dma('sp', hgn[:], hg_g[l], writes=['bhgn'])
            groups = [[0, 1]] + [list(range(2 + 8 * g, 10 + 8 * g)) for g in range(4)]
            for pc in range(2):
                for j in range(5):
                    load_w(W, l, w_in, 512 + j * 256 + pc * 128, 128, stgs, 'bW', off=j * 128, slab=128)
                S.op('pool', lambda e: e.memset(OT[:], 0.0), writes=['bOT'])
                for i in range(NT):
                    b = 6 + i % 2
                    proj_tm(PS[b][:, 0:128], b, W, 'bW', 128, 128, i)
                    S.op('dve', lambda e, b=b, i=i: e.tensor_copy(out=Vt[:, i, :], in_=PS[b][:, 0:128]), reads=[pk(b)], writes=['bV'])
                _ck(1)
                for dr in range(2):
                    li = (pc * 2 + dr)
                    lb_ap = LBt[:, 0, li, l:l + 1]
                    oml_ap = LBt[:, 1, li, l:l + 1]
                    S.op('pool', lambda e: e.memset(M0[:], 1.0), writes=['bM0'])
                    m3 = M0[:].rearrange("p (c j) -> p c j", j=64)
                    zc = 0 if dr == 0 else 63
                    S.op('pool', lambda e, zc=zc: e.memset(m3[:, :, zc:zc + 1], 0.0), writes=['bM0'])
                    for bi, (t0, n) in enumerate(BLK):
                        q = bi % 2
                        proj_fm(q, W, 'bW', (2 + dr) * 128, t0, n)
                        act(T1[q][:, 0:n], PS[q][:, 0:n], AF.Exp, [pk(q)], [f'bT1{q}'], scale=-1.0)
                        act(T2[q][:, 0:n], T1[q][:, 0:n], AF.Ln, [f'bT1{q}', 'LBt'], [f'bT2{q}'], bias=onec[:], scale=lb_ap)
                        act(T1[q][:, 0:n], T1[q][:, 0:n], AF.Ln, [f'bT1{q}'], [f'bT1{q}'], bias=onec[:], scale=1.0)
                        S.op('dve', lambda e, q=q, t0=t0, n=n: e.tensor_tensor(out=Gc[:, t0:t0 + n], in0=T2[q][:, 0:n], in1=T1[q][:, 0:n], op=ALU.subtract),
                             reads=[f'bT1{q}', f'bT2{q}'], writes=['bGc'])
                    if dr == 0:
                        S.op('dve', lambda e: e.tensor_tensor_scan(out=Gc[:], data0=M0[:], data1=Gc[:], initial=0.0, op0=ALU.mult, op1=ALU.add),
                             reads=['bGc', 'bM0'], writes=['bGc'])
                    else:
                        S.op('dve', lambda e: e.tensor_tensor_scan(out=Gc[:][:, ::-1], data0=M0[:][:, ::-1], data1=Gc[:][:, ::-1], initial=0.0, op0=ALU.mult, op1=ALU.add),
                             reads=['bGc', 'bM0'], writes=['bGc'])
                    _ck(2)
                    g3 = Gc[:].rearrange("p (c j) -> p c j", j=64)
                    mid = 31 if dr == 0 else 32
                    end = 63 if dr == 0 else 0
                    S.op('dve', lambda e: e.tensor_copy(out=sm[:, 0, :], in_=g3[:, :, mid]), reads=['bGc'], writes=['bsm'])
                    S.op('dve', lambda e: e.tensor_copy(out=sm[:, 1, :], in_=g3[:, :, end]), reads=['bGc'], writes=['bsm'])
                    S.op('dve', lambda e: e.tensor_tensor(out=sm[:, 5, :], in0=sm[:, 1, :], in1=sm[:, 0, :], op=ALU.subtract), reads=['bsm'], writes=['bsm'])
                    act(sm[:, 2, :], sm[:, 1, :], AF.Exp, ['bsm'], ['bsm'])
                    act(sm[:, 3, :], sm[:, 5, :], AF.Exp, ['bsm'], ['bsm'])
                    act(sm[:, 4, :], sm[:, 0, :], AF.Exp, ['bsm'], ['bsm'])
                    S.op('dve', lambda e: e.tensor_tensor(out=g3, in0=g3, in1=sm[:, 0, :].unsqueeze(2).broadcast_to([128, NCH, 64]), op=ALU.subtract),
                         reads=['bGc', 'bsm'], writes=['bGc'])
                    _ck(3)
                    for bi, (t0, n) in enumerate(BLK):
                        q = bi % 2
                        c0, nc_ = t0 // 64, n // 64
                        proj_fm(q, W, 'bW', 0, t0, n)
                        act(T1[q][:, 0:n], PS[q][:, 0:n], AF.Exp, [pk(q)], [f'bT1{q}'], scale=-1.0)
                        act(T1[q][:, 0:n], T1[q][:, 0:n], AF.Ln, [f'bT1{q}'], [f'bT1{q}'], bias=onec[:], scale=1.0)
                        S.op('dve', lambda e, q=q, t0=t0, n=n: e.tensor_tensor(out=T1[q][:, 0:n], in0=Gc[:, t0:t0 + n], in1=T1[q][:, 0:n], op=ALU.subtract),
                             reads=['bGc', f'bT1{q}'], writes=[f'bT1{q}'])
                        act(T1[q][:, 0:n], T1[q][:, 0:n], AF.Exp, [f'bT1{q}'], [f'bT1{q}'])
                        S.op('dve', lambda e, q=q, t0=t0, n=n: e.tensor_tensor(out=QTl[:, t0:t0 + n], in0=PS[q][:, 0:n], in1=T1[q][:, 0:n], op=ALU.mult),
                             reads=[pk(q), f'bT1{q}'], writes=['bQT'])
                        proj_fm(2 + q, W, 'bW', (2 + dr) * 128, t0, n)
                        act(T2[q][:, 0:n], PS[2 + q][:, 0:n], AF.Exp, [pk(2 + q)], [f'bT2{q}'])
                        act(T2[q][:, 0:n], T2[q][:, 0:n], AF.Ln, [f'bT2{q}'], [f'bT2{q}'], bias=onec[:], scale=1.0)
                        S.op('dve', lambda e, q=q, t0=t0, n=n: e.tensor_tensor(out=T2[q][:, 0:n], in0=Gc[:, t0:t0 + n], in1=T2[q][:, 0:n], op=ALU.add),
                             reads=['bGc', f'bT2{q}'], writes=[f'bT2{q}'])
                        act(T2[q][:, 0:n], T2[q][:, 0:n], AF.Exp, [f'bT2{q}'], [f'bT2{q}'], scale=-1.0)
                        S.op('dve', lambda e, q=q, t0=t0, n=n: e.tensor_scalar(out=KTl[:, t0:t0 + n], in0=T2[q][:, 0:n], scalar1=oml_ap, scalar2=None, op0=ALU.mult),
                             reads=[f'bT2{q}', 'LBt'], writes=['bKT'])
                        kv3 = KTl[:, t0:t0 + n].rearrange("p (c j) -> p c j", j=64)
                        kh3 = KH[:, t0:t0 + n].rearrange("p (c j) -> p c j", j=64)
                        S.op('dve', lambda e, kv3=kv3, kh3=kh3, c0=c0, nc_=nc_: e.tensor_tensor(out=kh3, in0=kv3, in1=sm[:, 3, c0:c0 + nc_].unsqueeze(2).broadcast_to([128, nc_, 64]), op=ALU.mult),
                             reads=['bKT', 'bsm'], writes=['bSb'])
                    _ck(4)
                    for i in range(NT):
                        b = 4 + i % 2
                        psb = PS[b][:].bitcast(BF16)
                        S.op('pe', lambda e, i=i, psb=psb: e.transpose(out=psb[:, 0:128], in_=KH[:, i * 128:(i + 1) * 128], identity=identb[:]),
                             reads=['bSb', 'identb'], writes=[pk(b)])
                        S.op('dve', lambda e, i=i, psb=psb: e.tensor_copy(out=Kt[:, i, :], in_=psb[:, 0:128]), reads=[pk(b)], writes=['bKt'])
                    _ck(5)
                    for g, tiles in enumerate(groups):
                        for ti, i in enumerate(tiles):
                            for cp in range(2):
                                bb = 2 * (g % 2) + cp
                                for h2 in range(2):
                                    S.op('pe', lambda e, ti=ti, i=i, cp=cp, h2=h2, bb=bb: e.matmul(PS[bb][64 * h2:64 * h2 + 64, ti * 64:(ti + 1) * 64],
                                                                                               lhsT=Kt[64 * cp:64 * cp + 64, i, 64 * h2:64 * h2 + 64],
                                                                                               rhs=Vt[64 * cp:64 * cp + 64, i, 64 * h2:64 * h2 + 64],
                                                                                               start=True, stop=True, tile_position=(64 * cp, 64 * h2)),
                                         reads=['bKt', 'bV'], writes=[pk(bb)], sig=(ti == len(tiles) - 1 and h2 == 1))
                        nt_ = len(tiles)
                        for cp in range(2):
                            bb = 2 * (g % 2) + cp
                            cfirst = 2 * tiles[0] + cp
                            if dr == 0:
                                dst = Bt[:, cfirst:cfirst + 2 * (nt_ - 1) + 1:2, :]
                            else:
                                pfirst = (3 - cfirst) if g == 0 else (71 - cfirst)
                                stop = pfirst - 2 * (nt_ - 1) - 1
                                dst = Bt[:, pfirst:(stop if stop >= 0 else None):-2, :]
                            S.op('dve', lambda e, bb=bb, dst=dst, nt_=nt_: e.tensor_copy(out=dst, in_=PS[bb][:, 0:nt_ * 64].rearrange("p (t e) -> p t e", e=64)),
                                 reads=[pk(bb)], writes=['bGc'])
                    _ck(6)
                    if dr == 0:
                        lam_ap = sm[:, 2, :]
                    else:
                        S.op('dve', lambda e: e.tensor_copy(out=sm[:, 5, 0:4], in_=sm[:, 2, 0:4][:, ::-1]), reads=['bsm'], writes=['bsm'])
                        S.op('dve', lambda e: e.tensor_copy(out=sm[:, 5, 4:NCH], in_=sm[:, 2, 4:NCH][:, ::-1]), reads=['bsm'], writes=['bsm'])
                        lam_ap = sm[:, 5, :]
                    for e_ in range(64):
                        S.op('dve', lambda e, e_=e_: e.tensor_tensor_scan(out=Bt[:, :, e_], data0=lam_ap, data1=Bt[:, :, e_], initial=0.0, op0=ALU.mult, op1=ALU.add),
                             reads=['bsm', 'bGc'], writes=['bGc'])
                    _ck(7)
                    rho = sm[:, 4, :]
                    if dr == 0:
                        S.op('dve', lambda e: e.memset(Sb[:, 0:1, :], 0.0), writes=['bSb'])
                        S.op('dve', lambda e: e.tensor_tensor(out=Sb[:, 1:NCH, :], in0=Bt[:, 0:NCH - 1, :], in1=rho[:, 1:NCH].unsqueeze(2).broadcast_to([128, NCH - 1, 64]), op=ALU.mult),
                             reads=['bGc', 'bsm'], writes=['bSb'])
                    else:
                        S.op('dve', lambda e: e.memset(Sb[:, 3:4, :], 0.0), writes=['bSb'])
                        S.op('dve', lambda e: e.tensor_tensor(out=Sb[:, 0:3, :], in0=Bt[:, 2::-1, :], in1=rho[:, 0:3].unsqueeze(2).broadcast_to([128, 3, 64]), op=ALU.mult),
                             reads=['bGc', 'bsm'], writes=['bSb'])
                        S.op('dve', lambda e: e.tensor_tensor(out=Sb[:, 4:NCH, :], in0=Bt[:, 66:2:-1, :], in1=rho[:, 4:NCH].unsqueeze(2).broadcast_to([128, NCH - 4, 64]), op=ALU.mult),
                             reads=['bGc', 'bsm'], writes=['bSb'])
                    _ck(8)
                    mk = maskt[:, 64 * dr:64 * dr + 64]
                    tgs = list(enumerate(range(0, NT, 4)))

                    def b_scores(tgi, tg):
                            tiles = list(range(tg, min(tg + 4, NT)))
                            nt_ = len(tiles)
                            q = tgi % 2
                            for ti, i in enumerate(tiles):
                                for cp in range(2):
                                    c = 2 * i + cp
                                    for h2 in range(2):
                                        bb = 2 * q + h2
                                        for jb in range(2):
                                            full = (jb == 0) if dr == 0 else (jb == 1)
                                            i0, ni = (0, 64) if full else ((32, 32) if dr == 0 else (0, 32))
                                            S.op('pe', lambda e, c=c, cp=cp, h2=h2, bb=bb, ti=ti, jb=jb, i0=i0, ni=ni: e.matmul(
                                                PS[bb][64 * cp + 32 * jb:64 * cp + 32 * jb + 32, ti * 64 + i0:ti * 64 + i0 + ni],
                                                lhsT=KTl[64 * h2:64 * h2 + 64, c * 64 + 32 * jb:c * 64 + 32 * jb + 32],
                                                rhs=QTl[64 * h2:64 * h2 + 64, c * 64 + i0:c * 64 + i0 + ni],
                                                start=True, stop=True, tile_position=(64 * h2, 64 * cp + 32 * jb)),
                                                 reads=['bKT', 'bQT'], writes=[pk(bb)], sig=(ti == nt_ - 1 and cp == 1 and jb == 1))
                            for h2 in range(2):
                                bb = 2 * q + h2
                                scv = SCm[2 * q + h2][:, 0:nt_ * 64].rearrange("p (t i) -> p t i", i=64)
                                S.op('dve', lambda e, bb=bb, scv=scv, nt_=nt_: e.tensor_tensor(out=scv, in0=PS[bb][:, 0:nt_ * 64].rearrange("p (t i) -> p t i", i=64),
                                                                                           in1=mk.unsqueeze(1).broadcast_to([128, nt_, 64]), op=ALU.mult),
                                     reads=[pk(bb), 'maskt'], writes=[f'bSC{2 * q + h2}'])

                    def b_rest(tgi, tg):
                            tiles = list(range(tg, min(tg + 4, NT)))
                            nt_ = len(tiles)
                            q = tgi % 2
                            for ti, i in enumerate(tiles):
                                for cp in range(2):
                                    for h2 in range(2):
                                        S.op('pe', lambda e, cp=cp, h2=h2, ti=ti, i=i, q=q: e.matmul(PS[4 + cp][64 * h2:64 * h2 + 64, ti * 64:(ti + 1) * 64],
                                                                                                 lhsT=Vt[64 * cp:64 * cp + 64, i, 64 * h2:64 * h2 + 64],
                                                                                                 rhs=SCm[2 * q + h2][64 * cp:64 * cp + 64, ti * 64:(ti + 1) * 64],
                                                                                                 start=True, stop=True, tile_position=(64 * cp, 64 * h2)),
                                             reads=['bV', f'bSC{2 * q + h2}'], writes=[pk(4 + cp)], sig=(ti == nt_ - 1 and h2 == 1))
                            for ti, i in enumerate(tiles):
                                for cp in range(2):
                                    c = 2 * i + cp
                                    for h2 in range(2):
                                        S.op('pe', lambda e, c=c, cp=cp, h2=h2, ti=ti: e.matmul(PS[6][64 * h2:64 * h2 + 64, (ti * 2 + cp) * 64:(ti * 2 + cp + 1) * 64],
                                                                                           lhsT=Sb[64 * h2:64 * h2 + 64, c, :],
                                                                                           rhs=QTl[64 * h2:64 * h2 + 64, c * 64:(c + 1) * 64],
                                                                                           start=True, stop=True, tile_position=(64 * h2, 64 * h2)),
                                             reads=['bSb', 'bQT'], writes=[pk(6)], sig=(ti == nt_ - 1 and cp == 1 and h2 == 1))
                            otv = OT[:, tg * 128:(tg + nt_) * 128].rearrange("p (t c i) -> p t c i", c=2, i=64)
                            for cp in range(2):
                                S.op('dve', lambda e, cp=cp, otv=otv, nt_=nt_: e.tensor_tensor(out=otv[:, :, cp, :], in0=PS[4 + cp][:, 0:nt_ * 64].rearrange("p (t i) -> p t i", i=64),
                                                                                           in1=otv[:, :, cp, :], op=ALU.add),
                                     reads=[pk(4 + cp), 'bOT'], writes=['bOT'])
                            S.op('dve', lambda e, tg=tg, nt_=nt_: e.tensor_tensor(out=OT[:, tg * 128:(tg + nt_) * 128], in0=PS[6][:, 0:nt_ * 128], in1=OT[:, tg * 128:(tg + nt_) * 128], op=ALU.add),
                                 reads=[pk(6), 'bOT'], writes=['bOT'])

                    b_scores(*tgs[0])
                    for gi_ in range(len(tgs)):
                        if gi_ + 1 < len(tgs):
                            b_scores(*tgs[gi_ + 1])
                        b_rest(*tgs[gi_])
                _ck(9)
                for bi, (t0, n) in enumerate(BLK):
                    q = bi % 2
                    S.op('pool', lambda e, t0=t0, n=n: e.tensor_tensor(out=osq[:, 0:n], in0=OT[:, t0:t0 + n], in1=OT[:, t0:t0 + n], op=ALU.mult), reads=['bOT'], writes=['bosq'])
                    S.op('pe', lambda e, n=n: e.matmul(PS[4][:, 0:n], lhsT=bd64[:], rhs=osq[:, 0:n], start=True, stop=True), reads=['bd64', 'bosq'], writes=[pk(4)])
                    rstd_from_ss(ors[:, 0:n], PS[4][:, 0:n], 1.0 / 64, epsr[:], 'bors', pk(4))
                    proj_fm(5, W, 'bW', 4 * 128, t0, n)
                    act(T1[q][:, 0:n], PS[5][:, 0:n], AF.Exp, [pk(5)], [f'bT1{q}'], scale=-1.0)
                    act(T1[q][:, 0:n], T1[q][:, 0:n], AF.Ln, [f'bT1{q}'], [f'bT1{q}'], bias=onec[:], scale=1.0)
                    act(T1[q][:, 0:n], T1[q][:, 0:n], AF.Exp, [f'bT1{q}'], [f'bT1{q}'], scale=-1.0)
                    S.op('dve', lambda e, q=q, n=n: e.tensor_tensor(out=T1[q][:, 0:n], in0=PS[5][:, 0:n], in1=T1[q][:, 0:n], op=ALU.mult), reads=[pk(5), f'bT1{q}'], writes=[f'bT1{q}'])
                    S.op('dve', lambda e, t0=t0, n=n: e.tensor_tensor(out=ors[:, 0:n], in0=ors[:, 0:n], in1=OT[:, t0:t0 + n], op=ALU.mult), reads=['bors', 'bOT'], writes=['bors'])
                    S.op('dve', lambda e, q=q, n=n: e.scalar_tensor_tensor(out=oy[q][:, 0:n], in0=ors[:, 0:n], scalar=hgn[:, 0:1], in1=T1[q][:, 0:n], op0=ALU.mult, op1=ALU.mult),
                         reads=['bors', 'bhgn', f'bT1{q}'], writes=[f'boy{q}'])
                    S.dma('sp', ymix[256 + pc * 128:256 + (pc + 1) * 128, t0:t0 + n], oy[q][:, 0:n], reads=[f'boy{q}'], writes=['ymix'])
        S.barrier()

    S.barrier()
    prologue()
    for l in range(nlayers):
        last = (l == DEPTH - 1)
        if 'H' in stages and (l == 0 or 'O' not in stages):
            stageH(l)
        if 'A' in stages:
            stageA(l)
        if 'B' in stages:
            try:
                stageB(l)
            except _Stop:
                S.barrier()
        if 'C' in stages:
            stageC(l, last)
        if 'D' in stages:
            stageD(l, last)
        if 'O' in stages:
            stageO(l, last, fuse_next=(l + 1 < nlayers and 'H' in stages))
    S.barrier()
    if 'BSTOP' not in _os.environ:
        es.close()
    return nc


def _consts():
    ident = np.eye(128, dtype=np.float32).astype(ml_dtypes.bfloat16)
    n_freq = 8
    inv_freq = (10000.0 ** (-np.arange(n_freq, dtype=np.float32) / n_freq)).astype(np.float32)
    row = np.repeat(np.arange(64, dtype=np.float32), 64)
    col = np.tile(np.arange(64, dtype=np.float32), 64)
    ang = np.concatenate([row[:, None] * inv_freq, col[:, None] * inv_freq], axis=-1).astype(np.float32)
    cos, sin = np.cos(ang).astype(np.float32), np.sin(ang).astype(np.float32)
    c32 = np.concatenate([cos, cos], axis=1).T
    s32 = np.concatenate([-sin, sin], axis=1).T
    cosT = np.ascontiguousarray(np.tile(c32, (4, 1)))
    sinT = np.ascontiguousarray(np.tile(s32, (4, 1)))
    j = np.arange(64)[:, None]
    i = np.arange(64)[None, :]
    fwd = (j <= i).astype(np.float32)
    bwd = (j >= i).astype(np.float32)
    mask = np.concatenate([np.tile(fwd, (2, 1)), np.tile(bwd, (2, 1))], axis=1)
    bd = np.zeros((128, 128), np.float32)
    bd[:64, :64] = 1
    bd[64:, 64:] = 1
    perm = np.zeros((128, 128), np.float32)
    for m_ in range(128):
        k_ = m_ + 16 if (m_ % 32) < 16 else m_ - 16
        perm[k_, m_] = 1.0
    return dict(c_perm=perm.astype(ml_dtypes.bfloat16), c_ident=ident, c_cos=cosT, c_sin=sinT, c_mask=np.ascontiguousarray(mask), c_bd64=bd.astype(ml_dtypes.bfloat16))


def _layout(inp, b):
    f = lambda a: np.ascontiguousarray(np.asarray(a, dtype=np.float32))
    m = {}
    m["x_b"] = f(inp["x"][b])
    m["ctx_b"] = f(inp["ctx"][b])
    cc = np.stack([np.asarray(inp["c"][b]), np.asarray(inp["c_ctx"])], -1)
    m["cfm"] = f(cc.reshape(8, 128, 2).transpose(1, 0, 2).reshape(128, 16))
    m["w_mod"] = f(inp["w_mod"])
    m["b_mod"] = f(inp["b_mod"])
    m["b_mod_fm"] = f(np.asarray(inp["b_mod"]).reshape(DEPTH, 24, 128).transpose(0, 2, 1))
    m["g_pre_fm"] = f(np.asarray(inp["g_pre"]).reshape(DEPTH, 8, 128).transpose(0, 2, 1))
    m["g_post"] = f(inp["g_post"])
    m["w_in"] = f(inp["w_in"])
    m["w_out"] = f(inp["w_out"])
    m["lru_cw"] = f(np.asarray(inp["lru_conv_w"]).reshape(DEPTH, 4, 2, 128).transpose(0, 3, 2, 1).reshape(DEPTH, 128, 8))
    m["lru_cb"] = f(np.asarray(inp["lru_conv_b"]).reshape(DEPTH, 2, 128).transpose(0, 2, 1))
    wr, wi = np.asarray(inp["lru_w_r"]), np.asarray(inp["lru_w_i"])
    bd = np.zeros((DEPTH, 2, 4, 128, 128), np.float32)
    for pc in range(2):
        for dr in range(2):
            for wh, w in enumerate((wr, wi)):
                for h2 in range(2):
                    bd[:, pc, 2 * dr + wh, 64 * h2:64 * h2 + 64, 64 * h2:64 * h2 + 64] = w[:, dr, 2 * pc + h2]
    m["lru_bd"] = bd
    br, bi_ = np.asarray(inp["lru_b_r"]), np.asarray(inp["lru_b_i"])
    lb = np.stack([br, bi_], 1).reshape(DEPTH, 2, 2, 2, 128)
    m["lru_b"] = f(lb.transpose(0, 4, 1, 2, 3).reshape(DEPTH, 128, 8))
    m["lru_lam"] = f(np.asarray(inp["lru_lambda"]).reshape(DEPTH, 2, 2, 128).transpose(0, 3, 1, 2).reshape(DEPTH, 128, 4))
    hl = np.asarray(inp["hgrn_lb"]).reshape(DEPTH, 2, 2, 128)
    m["hg_lb"] = f(hl.transpose(3, 2, 1, 0).reshape(128, 16))
    m["hg_g"] = f(np.tile(np.asarray(inp["hgrn_norm_g"]), (1, 2)).reshape(DEPTH, 128, 1))
    m["cf_w"] = f(np.asarray(inp["conf_conv_w"]).reshape(DEPTH, 31, 2, 128).transpose(0, 3, 2, 1).reshape(DEPTH, 128, 62))
    cb = np.stack([np.asarray(inp["conf_conv_b"]), np.asarray(inp["conf_ln_g"]), np.asarray(inp["conf_ln_b"])], 1).reshape(DEPTH, 3, 2, 128)
    m["cf_b"] = f(cb.transpose(0, 3, 1, 2).reshape(DEPTH, 128, 6))
    m["df_lam"] = f(np.concatenate([np.asarray(inp[k]) for k in ("diff_lam_q1", "diff_lam_k1", "diff_lam_q2", "diff_lam_k2")], axis=1))
    m["df_g"] = f(np.tile(np.asarray(inp["diff_norm_g"]), (1, 2)).reshape(DEPTH, 128, 1))
    return m


def kernel(**inputs):
    n = 8
    nc = bass.Bass("TRN2", target_bir_lowering=False)
    build(nc)
    consts = _consts()
    in_maps = []
    for b in range(n):
        m = _layout(inputs, b)
        m.update(consts)
        in_maps.append(m)
    res = run_bass_kernel_spmd(nc, in_maps, core_ids=list(range(n)))
    return np.stack([np.asarray(r["out"], dtype=np.float32) for r in res.results], axis=0)
```
